# Optimizing a Trainium2 kernel written in Bass

```python
import math
import jax, jax.numpy as jnp
from jax import lax
import numpy as np

D_MODEL = 4096
BATCH = 2
SEQ = 4096
DEPTH = 2

N_META = 16
Q_BLOCK = 128
LEAD = Q_BLOCK
N_PAD = LEAD - N_META
RMS_EPS = 1e-6
NEG_INF = -1e30

MLA_V = 128
MLA_HEADS = (3 * D_MODEL // 8) // MLA_V
MLA_NOPE = 128
MLA_ROPE = 64
MLA_Q_LORA = 1536
MLA_KV_LORA = 512
ROPE_THETA = 10000.0

GLA_DV = 256
GLA_DK = 128
GLA_HEADS = (D_MODEL // 4) // GLA_DV
GLA_GATE_RANK = 16
GLA_TAU = 16.0
GLA_CHUNK = 64

FOX_DH = 128
FOX_HEADS = (3 * D_MODEL // 8) // FOX_DH

W_MLA = MLA_HEADS * MLA_V
W_GLA = GLA_HEADS * GLA_DV
W_FOX = FOX_HEADS * FOX_DH
D_MIX = W_MLA + W_GLA + W_FOX

IN_SIZES = (
    MLA_Q_LORA, MLA_KV_LORA, MLA_ROPE,
    GLA_HEADS * GLA_DK, GLA_HEADS * GLA_DK, W_GLA,
    GLA_GATE_RANK, W_GLA,
    W_FOX, W_FOX, W_FOX, FOX_HEADS,
)
D_IN = sum(IN_SIZES)

D_FF = 256 * (-(-8 * D_MODEL // (3 * 256)))
CONV_W = 3

kernel_name = "hybrid_mla_gla_fox_convffn"


def rms_norm(x, g):
    xf = x.astype(jnp.float32)
    y = xf * lax.rsqrt(jnp.mean(xf * xf, axis=-1, keepdims=True) + RMS_EPS)
    return (y * g.astype(jnp.float32)).astype(x.dtype)


def head_rms_norm(o, gain, n_heads):
    b, l = o.shape[:2]
    o = o.reshape(b, l, n_heads, -1)
    return rms_norm(o, gain.reshape(n_heads, -1)).reshape(b, l, -1)


def split_cols(x, sizes):
    out, start = [], 0
    for s in sizes:
        out.append(x[..., start:start + s])
        start += s
    return out


def rope(x, cos, sin):
    half = x.shape[-1] // 2
    x1, x2 = x[..., :half], x[..., half:]
    xf1, xf2 = x1.astype(jnp.float32), x2.astype(jnp.float32)
    return jnp.concatenate([xf1 * cos - xf2 * sin, xf2 * cos + xf1 * sin], axis=-1).astype(x.dtype)


def causal_block_attention(q, k, v, key_valid, scale, log_f_cum=None):
    L = q.shape[1]
    outs = []
    for i in range(L // Q_BLOCK):
        q0 = i * Q_BLOCK
        k_end = q0 + Q_BLOCK
        qb = q[:, q0:k_end].astype(jnp.float32)
        kb = k[:, :k_end].astype(jnp.float32)
        vb = v[:, :k_end].astype(jnp.float32)
        s = jnp.einsum('bqhd,bkhd->bhqk', qb, kb) * scale
        if log_f_cum is not None:
            c_q = jnp.transpose(log_f_cum[:, q0:k_end], (0, 2, 1))[..., :, None]
            c_k = jnp.transpose(log_f_cum[:, :k_end], (0, 2, 1))[..., None, :]
            s = s + (c_q - c_k)
        q_pos = q0 + jnp.arange(Q_BLOCK)
        k_pos = jnp.arange(k_end)
        mask = (k_pos[None, :] <= q_pos[:, None]) & key_valid[None, :k_end]
        s = jnp.where(mask, s, NEG_INF)
        p = jax.nn.softmax(s, axis=-1)
        outs.append(jnp.einsum('bhqk,bkhd->bqhd', p, vb).astype(v.dtype))
    return jnp.concatenate(outs, axis=1)


def gla_chunked(q, k, v, log_a):
    B, L, H, DK = q.shape
    DV = v.shape[-1]
    C = GLA_CHUNK
    N = L // C
    q = q.reshape(B, N, C, H, DK)
    k = k.reshape(B, N, C, H, DK)
    v = v.reshape(B, N, C, H, DV)
    log_a = log_a.reshape(B, N, C, H, DK)
    b = jnp.cumsum(log_a, axis=2)
    b_last = b[:, :, -1:]
    q_dec = q * jnp.exp(b)
    k_dec = k * jnp.exp(-b)
    causal = jnp.tril(jnp.ones((C, C), dtype=bool))
    a = jnp.einsum('bnthk,bnshk->bnhts', q_dec, k_dec)
    a = jnp.where(causal, a, 0.0)
    o_intra = jnp.einsum('bnhts,bnshv->bnthv', a, v)
    k_state = k * jnp.exp(b_last - b)
    decay = jnp.exp(b_last[:, :, 0])

    def step(state, inp):
        q_c, k_c, v_c, d_c = inp
        o_c = jnp.einsum('bthk,bhkv->bthv', q_c, state)
        state = state * d_c[..., None] + jnp.einsum('bthk,bthv->bhkv', k_c, v_c)
        return state, o_c

    xs = (jnp.moveaxis(q_dec, 1, 0), jnp.moveaxis(k_state, 1, 0),
          jnp.moveaxis(v, 1, 0), jnp.moveaxis(decay, 1, 0))
    s0 = jnp.zeros((B, H, DK, DV), jnp.float32)
    _, o_inter = lax.scan(step, s0, xs)
    o = o_intra + jnp.moveaxis(o_inter, 0, 1)
    return o.reshape(B, L, H, DV)


def hybrid_mixer(hn, valid, cos, sin, w_in, mla_q_norm, mla_w_uq, mla_kv_norm, mla_w_ukv,
                 gla_w_gate2, gla_b_gate, fox_b_f, out_norm_mla, out_norm_gla,
                 out_norm_fox, w_out):
    B, L, _ = hn.shape
    proj = hn @ w_in
    (c_q, c_kv, k_rope, g_q, g_k, g_v, g_z, g_r,
     f_q, f_k, f_v, f_z) = split_cols(proj, IN_SIZES)

    q = (rms_norm(c_q, mla_q_norm) @ mla_w_uq).reshape(B, L, MLA_HEADS, MLA_NOPE + MLA_ROPE)
    q_nope, q_pe = q[..., :MLA_NOPE], q[..., MLA_NOPE:]
    q_pe = rope(q_pe, cos[:, None, :], sin[:, None, :])
    kv = (rms_norm(c_kv, mla_kv_norm) @ mla_w_ukv).reshape(B, L, MLA_HEADS, MLA_NOPE + MLA_V)
    k_nope, v_m = kv[..., :MLA_NOPE], kv[..., MLA_NOPE:]
    k_pe = rope(k_rope, cos, sin)
    q_m = jnp.concatenate([q_nope, q_pe], axis=-1)
    k_m = jnp.concatenate(
        [k_nope, jnp.broadcast_to(k_pe[:, :, None, :], (B, L, MLA_HEADS, MLA_ROPE))], axis=-1)
    o_mla = causal_block_attention(q_m, k_m, v_m, valid, (MLA_NOPE + MLA_ROPE) ** -0.5)
    o_mla = head_rms_norm(o_mla, out_norm_mla, MLA_HEADS)

    vf = valid.astype(jnp.float32)[None, :, None, None]
    gq = g_q.astype(jnp.float32).reshape(B, L, GLA_HEADS, GLA_DK) * (GLA_DK ** -0.5)
    gk = g_k.astype(jnp.float32).reshape(B, L, GLA_HEADS, GLA_DK) * vf
    gv = g_v.astype(jnp.float32).reshape(B, L, GLA_HEADS, GLA_DV)
    gate_logit = (g_z @ gla_w_gate2 + gla_b_gate).astype(jnp.float32)
    log_a = (jax.nn.log_sigmoid(gate_logit) / GLA_TAU).reshape(B, L, GLA_HEADS, GLA_DK) * vf
    o_gla = gla_chunked(gq, gk, gv, log_a).astype(hn.dtype)
    o_gla = head_rms_norm(o_gla, out_norm_gla, GLA_HEADS) * jax.nn.silu(g_r)

    fq = f_q.reshape(B, L, FOX_HEADS, FOX_DH)
    fk = f_k.reshape(B, L, FOX_HEADS, FOX_DH)
    fv = f_v.reshape(B, L, FOX_HEADS, FOX_DH)
    log_f = jax.nn.log_sigmoid((f_z + fox_b_f).astype(jnp.float32)) \
        * valid.astype(jnp.float32)[None, :, None]
    c = jnp.cumsum(log_f, axis=1)
    o_fox = causal_block_attention(fq, fk, fv, valid, FOX_DH ** -0.5, log_f_cum=c)
    o_fox = head_rms_norm(o_fox, out_norm_fox, FOX_HEADS)

    return jnp.concatenate([o_mla, o_gla, o_fox], axis=-1) @ w_out


def conv_ffn(hn, valid, w_up, conv_w, conv_b, w_down):
    L = hn.shape[1]
    u = (hn @ w_up) * valid.astype(hn.dtype)[None, :, None]
    u_pad = jnp.pad(u, ((0, 0), (CONV_W - 1, 0), (0, 0)))
    cv = conv_b + sum(conv_w[j] * u_pad[:, j:j + L] for j in range(CONV_W))
    gate, val = cv[..., :D_FF], cv[..., D_FF:]
    return (jax.nn.silu(gate) * val) @ w_down


def setup_inputs(seed: int = 0) -> dict:
    key = jax.random.key(seed)
    ks = jax.random.split(key, 24)
    f32 = jnp.float32
    nrm = lambda k, shape, s: jax.random.normal(k, shape, f32) * s
    gain = lambda k, shape: 1.0 + 0.02 * jax.random.normal(k, shape, f32)
    return {
        "x": nrm(ks[0], (BATCH, SEQ, D_MODEL), 1.0),
        "meta_tokens": nrm(ks[1], (N_META, D_MODEL), 1.0),
        "attn_norm": gain(ks[2], (DEPTH, D_MODEL)),
        "w_in": nrm(ks[3], (DEPTH, D_MODEL, D_IN), D_MODEL ** -0.5),
        "mla_q_norm": gain(ks[4], (DEPTH, MLA_Q_LORA)),
        "mla_w_uq": nrm(ks[5], (DEPTH, MLA_Q_LORA, MLA_HEADS * (MLA_NOPE + MLA_ROPE)), MLA_Q_LORA ** -0.5),
        "mla_kv_norm": gain(ks[6], (DEPTH, MLA_KV_LORA)),
        "mla_w_ukv": nrm(ks[7], (DEPTH, MLA_KV_LORA, MLA_HEADS * (MLA_NOPE + MLA_V)), MLA_KV_LORA ** -0.5),
        "gla_w_gate2": nrm(ks[8], (DEPTH, GLA_GATE_RANK, GLA_HEADS * GLA_DK), GLA_GATE_RANK ** -0.5),
        "gla_b_gate": nrm(ks[9], (DEPTH, GLA_HEADS * GLA_DK), 0.1),
        "fox_b_f": nrm(ks[10], (DEPTH, FOX_HEADS), 0.1),
        "out_norm_mla": gain(ks[11], (DEPTH, W_MLA)),
        "out_norm_gla": gain(ks[12], (DEPTH, W_GLA)),
        "out_norm_fox": gain(ks[13], (DEPTH, W_FOX)),
        "w_out": nrm(ks[14], (DEPTH, D_MIX, D_MODEL), D_MIX ** -0.5),
        "ffn_norm": gain(ks[15], (DEPTH, D_MODEL)),
        "ffn_w_up": nrm(ks[16], (DEPTH, D_MODEL, 2 * D_FF), D_MODEL ** -0.5),
        "ffn_conv_w": nrm(ks[17], (DEPTH, CONV_W, 2 * D_FF), CONV_W ** -0.5),
        "ffn_conv_b": nrm(ks[18], (DEPTH, 2 * D_FF), 0.02),
        "ffn_w_down": nrm(ks[19], (DEPTH, D_FF, D_MODEL), D_FF ** -0.5),
        "final_norm": gain(ks[20], (D_MODEL,)),
    }


def reference(x, meta_tokens, attn_norm, w_in, mla_q_norm, mla_w_uq, mla_kv_norm, mla_w_ukv,
              gla_w_gate2, gla_b_gate, fox_b_f, out_norm_mla, out_norm_gla, out_norm_fox,
              w_out, ffn_norm, ffn_w_up, ffn_conv_w, ffn_conv_b, ffn_w_down, final_norm):
    B = x.shape[0]
    meta = jnp.broadcast_to(meta_tokens.astype(x.dtype)[None], (B, N_META, D_MODEL))
    h = jnp.concatenate([jnp.zeros((B, N_PAD, D_MODEL), x.dtype), meta, x], axis=1)
    L = h.shape[1]
    idx = jnp.arange(L)
    valid = idx >= N_PAD
    vmask = valid.astype(x.dtype)[None, :, None]
    pos = jnp.maximum(idx - N_PAD, 0).astype(jnp.float32)
    inv_freq = 1.0 / (ROPE_THETA ** (jnp.arange(0, MLA_ROPE, 2, dtype=jnp.float32) / MLA_ROPE))
    ang = pos[:, None] * inv_freq[None, :]
    cos, sin = jnp.cos(ang), jnp.sin(ang)

    for layer in range(DEPTH):
        hn = rms_norm(h, attn_norm[layer])
        h = h + hybrid_mixer(hn, valid, cos, sin, w_in[layer], mla_q_norm[layer], mla_w_uq[layer],
                             mla_kv_norm[layer], mla_w_ukv[layer], gla_w_gate2[layer],
                             gla_b_gate[layer], fox_b_f[layer], out_norm_mla[layer],
                             out_norm_gla[layer], out_norm_fox[layer], w_out[layer]) * vmask
        hn = rms_norm(h, ffn_norm[layer])
        h = h + conv_ffn(hn, valid, ffn_w_up[layer], ffn_conv_w[layer], ffn_conv_b[layer],
                         ffn_w_down[layer]) * vmask

    return rms_norm(h, final_norm)[:, LEAD:]
```

```python
import numpy as np
from contextlib import ExitStack
import ml_dtypes
import concourse.bass as bass
import concourse.mybir as mybir
from concourse.bass_utils import run_bass_kernel_spmd

F32 = mybir.dt.float32
BF16 = mybir.dt.bfloat16
AF = mybir.ActivationFunctionType
ALU = mybir.AluOpType
AX = mybir.AxisListType
NPBF = ml_dtypes.bfloat16

D = 4096
SEQ = 4096
NMETA = 16
OWN = 1024
TL = 2 + OWN + 2 + NMETA
TT = [(0, 512), (512, 512), (1024, TL - 1024)]
TM = [(i * 128, 128) for i in range(8)] + [(1024, TL - 1024)]
NT = NMETA + SEQ
EPS = 1e-6
DFF = 11008
GW = 256


def fm_groups(col0, nchunks, hf):
    per = GW // 128
    return [(col0 + g * GW, min(GW, (nchunks - g * per) * 128),
             [(j * 128, 128, hf(g * per + j)) for j in range(min(per, nchunks - g * per))])
            for g in range((nchunks + per - 1) // per)]


def tm_groups(col0, ncols, hf):
    return [(col0 + g * GW, min(GW, ncols - g * GW), hf(g * GW)) for g in range((ncols + GW - 1) // GW)]

O_CQ, O_CKV, O_KR, O_GQ, O_GK, O_GV, O_GZ, O_GR, O_FQ, O_FK, O_FV, O_FZ = (
    0, 1536, 2048, 2112, 2624, 3136, 4160, 4176, 5200, 6736, 8272, 9808)
DIN = 9820
R32_GQ, R32_GK, R32_GZ, R32_GR, R32_FZ, N32 = 0, 512, 1024, 1040, 2064, 2076
RB_Q, RB_KN, RB_KPE, RB_FQ, RB_FK, NB = 0, 2304, 3840, 3904, 5440, 6976
CB_GV, CB_FV, CB_VM, NCB = 0, 1024, 2560, 4096


class Buf:
    __slots__ = ("w", "r")

    def __init__(self):
        self.w = {}
        self.r = {}


def fresh(olds):
    b = Buf()
    for o in olds:
        for t in list(o.w.values()) + list(o.r.values()):
            Ctx._add(b.r, t)
    return b


class DSem:
    __slots__ = ("h", "v")

    def __init__(self, h):
        self.h = h
        self.v = 0


class Ctx:
    CE = ("pe", "act", "dve", "pool")

    def __init__(self, nc, es):
        self.nc = nc
        self.es = es
        self.sem_es = es
        self.eng = {"pe": nc.tensor, "act": nc.scalar, "dve": nc.vector,
                    "pool": nc.gpsimd, "sp": nc.sync}
        self.sem = {}
        self.cnt = {}
        self.nsem = 0
        self.latest = {}
        self.free_dsems = []
        self.phase_dsems = []
        self.phase = 0
        for e in self.CE:
            self._new_engine_sem(e)
        self.seen = {e: {} for e in self.eng}
        self.pe_pending = False
        self.n_inst = 0

    def _new_engine_sem(self, e):
        self.nsem += 1
        self.sem[e] = self.sem_es.enter_context(self.nc.semaphore(f"s_{e}_{self.nsem}"))
        self.cnt[e] = 0

    def nm(self, name):
        return f"{name}_p{self.phase}"

    def dsem(self, name):
        if self.free_dsems:
            d = self.free_dsems.pop()
        else:
            self.nsem += 1
            d = DSem(self.sem_es.enter_context(self.nc.semaphore(f"d_{name}_{self.nsem}")))
        self.phase_dsems.append(d)
        return d

    def barrier(self):
        assert not self.pe_pending
        for e in self.eng:
            own = id(self.sem[e]) if e in self.sem else None
            deps = {k: t for k, t in self.latest.items() if k != own}
            self._emit_waits(e, deps)

    def end_phase(self):
        self.barrier()
        self.free_dsems.extend(self.phase_dsems)
        self.phase_dsems = []
        self.phase += 1

    def allgather(self, out, in_, cs, reads=(), writes=()):
        deps = self._collect("pool", reads, writes)
        self._emit_waits("pool", deps)
        ins = self.nc.gpsimd.collective_compute("AllGather", ALU.bypass, replica_groups=[[0, 1, 2, 3], [4, 5, 6, 7]],
                                                ins=[in_], outs=[out])
        self.n_inst += 1
        cs.v += 1
        ins.then_inc(cs.h)
        t = (cs.h, cs.v)
        self.latest[id(cs.h)] = t
        self._record(t, reads, writes)
        return ins

    @staticmethod
    def _add(deps, t):
        k = id(t[0])
        if k not in deps or deps[k][1] < t[1]:
            deps[k] = t

    def _collect(self, e, reads, writes, pwrites=()):
        deps = {}
        own = id(self.sem[e]) if e in self.sem else None
        for b in pwrites:
            for t in b.r.values():
                self._add(deps, t)
        for b in reads:
            for t in b.w.values():
                if id(t[0]) == own and e == "pe":
                    continue
                self._add(deps, t)
        for b in writes:
            for t in b.w.values():
                if id(t[0]) == own:
                    continue
                self._add(deps, t)
            for t in b.r.values():
                if id(t[0]) == own:
                    continue
                self._add(deps, t)
        return deps

    def _emit_waits(self, e, deps):
        seen = self.seen[e]
        for k, (s, v) in deps.items():
            if seen.get(k, 0) >= v:
                continue
            self.eng[e].wait_ge(s, v)
            self.n_inst += 1
            seen[k] = v

    def _record(self, t, reads, writes, pwrites=()):
        k = id(t[0])
        for b in pwrites:
            if k not in b.w or b.w[k][1] < t[1]:
                b.w[k] = t
        for b in reads:
            if k not in b.r or b.r[k][1] < t[1]:
                b.r[k] = t
        for b in writes:
            b.w = {k: t}
            b.r = {}

    def op(self, e, fn, reads=(), writes=(), inc=True, pwrites=()):
        deps = self._collect(e, reads, writes, pwrites)
        self._emit_waits(e, deps)
        ins = fn()
        self.n_inst += 1
        if inc:
            if self.cnt[e] >= 30000 and not (e == "pe" and self.pe_pending):
                self._new_engine_sem(e)
            self.cnt[e] += 1
            ins.then_inc(self.sem[e], 1)
            t = (self.sem[e], self.cnt[e])
            self.latest[id(t[0])] = t
            if e == "pe":
                self.pe_pending = False
        else:
            assert e == "pe"
            t = (self.sem[e], self.cnt[e] + 1)
            self.pe_pending = True
        self._record(t, reads, writes, pwrites)
        return ins

    def dma(self, q, out, in_, ds, reads=(), writes=(), pwrites=(), **kw):
        deps = self._collect(q, reads, writes, pwrites)
        self._emit_waits(q, deps)
        ins = self.eng[q].dma_start(out=out, in_=in_, **kw)
        self.n_inst += 1
        ds.v += 16
        ins.then_inc(ds.h, 16)
        self.latest[id(ds.h)] = (ds.h, ds.v)
        self._record((ds.h, ds.v), reads, writes, pwrites)
        return ins

    @staticmethod
    def seal(ds, bufs):
        k = id(ds.h)
        for b in bufs:
            if k in b.w:
                b.w[k] = (ds.h, ds.v)

    def wait_all(self, e, bufs):
        deps = {}
        for b in bufs:
            for t in b.w.values():
                self._add(deps, t)
        self._emit_waits(e, deps)


class Ring:
    def __init__(self, c, name, n, shape, dtype, es=None):
        es = es or c.es
        self.t = [es.enter_context(c.nc.sbuf_tensor(c.nm(f"{name}{i}"), shape, dtype)) for i in range(n)]
        self.b = [Buf() for _ in range(n)]
        self.d = [c.dsem(f"{name}{i}") for i in range(n)]
        self.i = -1
        self.n = n

    def next(self):
        self.i = (self.i + 1) % self.n
        return self.t[self.i], self.b[self.i], self.d[self.i]


class Dense:
    def __init__(self, c, kcmax=32):
        nc, es = c.nc, c.es
        self.c = c
        self.ws = Ring(c, "ws", 2, [128, kcmax * GW], BF16)
        self.psA = [es.enter_context(nc.psum_tensor(c.nm(f"psA{i}"), [128, 512], F32)) for i in range(2)]
        self.psB = [es.enter_context(nc.psum_tensor(c.nm(f"psB{i}"), [128, 512], F32)) for i in range(2)]
        self.psC = es.enter_context(nc.psum_tensor(c.nm("psC"), [128, 512], F32))
        self.bA = [Buf(), Buf()]
        self.bB = [Buf(), Buf()]
        self.bC = Buf()
        self.psT = [es.enter_context(nc.psum_tensor(c.nm(f"psT{i}"), [128, 512], F32)) for i in range(2)]
        self.bT = [Buf(), Buf()]
        self.psS = es.enter_context(nc.psum_tensor(c.nm("psS"), [128, 512], F32))
        self.bS = Buf()
        self.it = 0
        self.itT = 0

    def load(self, W, KC, col0, ncols, row0=0):
        c = self.c
        t, b, d = self.ws.next()
        t = t[:, :KC * ncols].rearrange("p (k m) -> p k m", m=ncols)
        Wv = W[row0:row0 + KC * 128, col0:col0 + ncols].rearrange("(kc p) m -> p kc m", p=128)
        step = 8
        for i, k0 in enumerate(range(0, KC, step)):
            k1 = min(KC, k0 + step)
            c.dma("pool", t[:, k0:k1, :], Wv[:, k0:k1, :], d,
                  writes=[b] if i == 0 else [], pwrites=[b] if i else [])
        return t, b

    def fm(self, W, KC, act, act_b, groups, tts=TT, row0=0):
        c, nc = self.c, self.c.nc
        for (col0, ncols, chunks) in groups:
            wt, wb = self.load(W, KC, col0, ncols, row0)
            for (off, M, handler) in chunks:
                pb = self.it % 2
                self.it += 1
                pst = [(self.psA[pb], self.bA[pb]), (self.psB[pb], self.bB[pb]), (self.psC, self.bC)]
                for kc in range(KC):
                    for j, (t0, tn) in enumerate(tts):
                        last = kc == KC - 1
                        ps, pbuf = pst[j]
                        c.op("pe", lambda: nc.tensor.matmul(ps[:M, :tn], wt[:, kc, off:off + M], act[:, kc, t0:t0 + tn],
                                                            start=(kc == 0), stop=last),
                             reads=[wb, act_b[kc]], writes=[pbuf], inc=last)
                handler([(pst[j][0][:M, :tn], pst[j][1], t0, tn) for j, (t0, tn) in enumerate(tts)])

    def tm(self, W, KC, act, act_b, groups, tms=TM, row0=0):
        c, nc = self.c, self.c.nc
        for (col0, ncols, handler) in groups:
            wt, wb = self.load(W, KC, col0, ncols, row0)
            for ti, (t0, tn) in enumerate(tms):
                pb = self.itT % 2
                self.itT += 1
                ps, pbuf = self.psT[pb], self.bT[pb]
                for kc in range(KC):
                    last = kc == KC - 1
                    c.op("pe", lambda: nc.tensor.matmul(ps[:tn, :ncols], act[:, kc, t0:t0 + tn], wt[:, kc, :ncols],
                                                        start=(kc == 0), stop=last),
                         reads=[wb, act_b[kc]], writes=[pbuf], inc=last)
                handler(ps[:tn, :ncols], pbuf, ti, t0, tn)


def _consts(c, names_shapes, es=None, olds=()):
    out = {}
    es = es or c.es
    ds = c.dsem("consts")
    for name, ap, shape in names_shapes:
        t = es.enter_context(c.nc.sbuf_tensor(c.nm("k_" + name), shape, F32))
        b = fresh(olds)
        c.dma("sp", t[:], ap, ds, writes=[b])
        out[name] = (t, b)
    Ctx.seal(ds, [b for (_, b) in out.values()])
    return out


def col_stats(c, dn, acc, acc_b, rstd, rstd_b, n_feat, post_scale=1.0, tts=TT):
    nc = c.nc
    for (t0, tn) in tts:
        c.op("pe", lambda: nc.tensor.matmul(dn.psS[:, :tn], dn.ones[:], acc[:, t0:t0 + tn], start=True, stop=True),
             reads=[acc_b, dn.ones_b], writes=[dn.bS])
        c.op("dve", lambda: nc.vector.tensor_scalar(rstd[:, t0:t0 + tn], dn.psS[:, :tn], 1.0 / n_feat, EPS,
                                                    ALU.mult, ALU.add),
             reads=[dn.bS], pwrites=[rstd_b])
    c.op("act", lambda: nc.scalar.activation(rstd[:], rstd[:], AF.Sqrt, scale=float(1.0 / post_scale ** 2)),
         reads=[rstd_b], writes=[rstd_b])
    c.op("dve", lambda: nc.vector.reciprocal(rstd[:], rstd[:]), reads=[rstd_b], writes=[rstd_b])


def build_p1():
    nc = bass.Bass("TRN2", target_bir_lowering=False)
    dt = lambda n, s, d=F32, k="ExternalInput": nc.dram_tensor(n, s, d, kind=k).ap()
    hT = dt("hT", [128, 32, TL])
    w_in = dt("w_in", [D, DIN])
    w_uq = dt("w_uq", [1536, 2304])
    w_ukv = dt("w_ukv", [512, 3072])
    g_attn = dt("g_attn", [128, 32])
    g_q = dt("g_q", [128, 12])
    g_kv = dt("g_kv", [128, 4])
    cos4 = dt("cos4", [128, TL])
    sin4 = dt("sin4", [128, TL])
    o32 = dt("o32", [N32, TL], F32, "ExternalOutput")
    obf = dt("obf", [NB, TL], BF16, "ExternalOutput")
    otf = dt("otf", [TL, 512], F32, "ExternalOutput")
    otb = dt("otb", [TL, NCB], BF16, "ExternalOutput")
    with ExitStack() as es:
        c = Ctx(nc, es)
        emit_p1(c, hT, w_in, w_uq, w_ukv, g_attn, g_q, g_kv, cos4, sin4, o32, obf, otf, otb)
    return nc


def emit_p1(c, hT, w_in, w_uq, w_ukv, g_attn, g_q, g_kv, cos4, sin4, o32, obf, otf, otb):
    nc, es = c.nc, c.es
    sb = lambda n, s, d=F32, st=None: (st or es).enter_context(nc.sbuf_tensor(c.nm(n), s, d))
    dn = Dense(c)
    K = _consts(c, [("g_attn", g_attn, [128, 32]), ("g_q", g_q, [128, 12]), ("g_kv", g_kv, [128, 4])])
    dn.ones = sb("ones", [128, 128])
    dn.ones_b = Buf()
    c.op("dve", lambda: nc.vector.memset(dn.ones[:], 1.0), writes=[dn.ones_b])
    sq = Ring(c, "sq", 2, [128, TL], F32)
    st32 = Ring(c, "st32", 2, [128, TL], F32)
    stbf = Ring(c, "stbf", 3, [128, TL], BF16)
    ttf = Ring(c, "ttf", 2, [128, GW], F32)
    ttb = Ring(c, "ttb", 3, [128, GW], BF16)
    out_b = Buf()
    cqn = sb("cqn", [128, 12, TL], BF16); cqn_b = [Buf() for _ in range(12)]
    ckn = sb("ckn", [128, 4, TL], BF16); ckn_b = [Buf() for _ in range(4)]
    accq = sb("accq", [128, TL]); accq_b = Buf()
    acck = sb("acck", [128, TL]); acck_b = Buf()
    kr = [sb("kr1", [32, TL]), sb("kr2", [32, TL])]
    kr_b = [Buf(), Buf()]
    es_hn = ExitStack()
    hn = sb("hn", [128, 32, TL], BF16, es_hn)
    hn_b = [Buf() for _ in range(32)]
    with ExitStack() as es1:
        hp = Ring(c, "hp", 2, [128, 1, TL], F32, es1)
        acc = sb("acc", [128, TL], F32, es1); acc_b = Buf()
        rstd = sb("rstd", [128, TL], F32, es1); rstd_b = Buf()
        c.op("dve", lambda: nc.vector.memset(acc[:], 0.0), writes=[acc_b])
        for g in range(32):
            t, b, d = hp.next()
            c.dma("sp", t[:], hT[:, g:g + 1, :], d, writes=[b])
            for i in range(1):
                s, s_b, _ = sq.next()
                c.op("act", lambda: nc.scalar.activation(s[:], t[:, i, :], AF.Square), reads=[b], writes=[s_b])
                c.op("dve", lambda: nc.vector.tensor_tensor(acc[:], acc[:], s[:], ALU.add), reads=[acc_b, s_b], writes=[acc_b])
        col_stats(c, dn, acc, acc_b, rstd, rstd_b, D)
        ga, ga_b = K["g_attn"]
        for g in range(32):
            t, b, d = hp.next()
            c.dma("sp", t[:], hT[:, g:g + 1, :], d, writes=[b])
            for i in range(1):
                kc = g + i
                c.op("dve", lambda: nc.vector.scalar_tensor_tensor(hn[:, kc, :], t[:, i, :], ga[:, kc:kc + 1], rstd[:],
                                                                   ALU.mult, ALU.mult),
                     reads=[b, ga_b, rstd_b], writes=[hn_b[kc]])


    def out_fm(dst, row0, func=None, scale=1.0, dtype=F32):
        def h(tiles):
            t, b, d = (st32 if dtype == F32 else stbf).next()
            M = None
            for (ps, pb, t0, tn) in tiles:
                M = ps.shape[0]
                c.op("act", lambda: nc.scalar.activation(t[:M, t0:t0 + tn], ps, func or AF.Identity, scale=float(scale)),
                     reads=[pb], pwrites=[b])
            c.dma("sp", dst[row0:row0 + M, :], t[:M, :], d, reads=[b], pwrites=[out_b])
        return h

    c.op("dve", lambda: nc.vector.memset(accq[:], 0.0), writes=[accq_b])
    c.op("dve", lambda: nc.vector.memset(acck[:], 0.0), writes=[acck_b])

    def lat(dstt, dst_b, i, gain, gain_b, ac, ac_b):
        def h(tiles):
            s, s_b, _ = sq.next()
            for (ps, pb, t0, tn) in tiles:
                c.op("act", lambda: nc.scalar.activation(dstt[:, i, t0:t0 + tn], ps, AF.Identity, scale=gain[:, i:i + 1]),
                     reads=[pb, gain_b], pwrites=[dst_b[i]])
                c.op("act", lambda: nc.scalar.activation(s[:, t0:t0 + tn], ps, AF.Square), reads=[pb], pwrites=[s_b])
            c.op("dve", lambda: nc.vector.tensor_tensor(ac[:], ac[:], s[:], ALU.add), reads=[ac_b, s_b], writes=[ac_b])
        return h

    def krope(i):
        def h(tiles):
            for (ps, pb, t0, tn) in tiles:
                c.op("act", lambda: nc.scalar.copy(kr[i][:, t0:t0 + tn], ps), reads=[pb], pwrites=[kr_b[i]])
        return h

    gq, gq_b = K["g_q"]
    gk, gk_b = K["g_kv"]
    groups = []
    groups += fm_groups(O_CQ, 12, lambda i: lat(cqn, cqn_b, i, gq, gq_b, accq, accq_b))
    groups += fm_groups(O_CKV, 4, lambda i: lat(ckn, ckn_b, i, gk, gk_b, acck, acck_b))
    groups.append((O_KR, 64, [(0, 32, krope(0)), (32, 32, krope(1))]))
    groups += fm_groups(O_GQ, 4, lambda i: out_fm(o32, R32_GQ + i * 128, scale=128 ** -0.5))
    groups += fm_groups(O_GK, 4, lambda i: out_fm(o32, R32_GK + i * 128))
    groups.append((O_GZ, 16, [(0, 16, out_fm(o32, R32_GZ))]))
    groups += fm_groups(O_GR, 8, lambda i: out_fm(o32, R32_GR + i * 128, func=AF.Silu))
    groups += fm_groups(O_FQ, 12, lambda i: out_fm(obf, RB_FQ + i * 128, scale=128 ** -0.5, dtype=BF16))
    groups += fm_groups(O_FK, 12, lambda i: out_fm(obf, RB_FK + i * 128, dtype=BF16))
    groups.append((O_FZ, 12, [(0, 12, out_fm(o32, R32_FZ))]))
    dn.fm(w_in, 32, hn, hn_b, groups)


    def out_tm(dst, col0, ring):
        def h(ps, pb, ti, t0, tn):
            t, b, d = ring.next()
            n = ps.shape[1]
            c.op("act", lambda: nc.scalar.copy(t[:tn, :n], ps), reads=[pb], writes=[b])
            c.dma("sp", dst[t0:t0 + tn, col0:col0 + n], t[:tn, :n], d, reads=[b], pwrites=[out_b])
        return h

    tg = tm_groups(O_GK, 512, lambda o: out_tm(otf, o, ttf))
    tg += tm_groups(O_GV, 1024, lambda o: out_tm(otb, CB_GV + o, ttb))
    tg += tm_groups(O_FV, 1536, lambda o: out_tm(otb, CB_FV + o, ttb))
    dn.tm(w_in, 32, hn, hn_b, tg)

    es_hn.close()
    olds = hn_b + hp.b + [acc_b, rstd_b]
    K2 = _consts(c, [("cos4", cos4, [128, TL]), ("sin4", sin4, [128, TL])], olds=olds)
    cs, cs_b = K2["cos4"]
    sn, sn_b = K2["sin4"]
    tmp = [sb(f"rtmp{i}", [128, TL]) for i in range(4)]
    tmp_b = [fresh(olds) for _ in range(4)]

    def rope(x1, x1_b, x2, x2_b, P, dst_rows1, dst_rows2):
        c.op("dve", lambda: nc.vector.tensor_tensor(tmp[0][:P, :], x1, cs[:P, :], ALU.mult), reads=[x1_b, cs_b], writes=[tmp_b[0]])
        c.op("dve", lambda: nc.vector.tensor_tensor(tmp[1][:P, :], x2, sn[:P, :], ALU.mult), reads=[x2_b, sn_b], writes=[tmp_b[1]])
        c.op("dve", lambda: nc.vector.tensor_tensor(tmp[2][:P, :], x2, cs[:P, :], ALU.mult), reads=[x2_b, cs_b], writes=[tmp_b[2]])
        c.op("dve", lambda: nc.vector.tensor_tensor(tmp[3][:P, :], x1, sn[:P, :], ALU.mult), reads=[x1_b, sn_b], writes=[tmp_b[3]])
        t, b, d = stbf.next()
        c.op("dve", lambda: nc.vector.tensor_tensor(t[:P, :], tmp[0][:P, :], tmp[1][:P, :], ALU.subtract),
             reads=[tmp_b[0], tmp_b[1]], writes=[b])
        for (r0, p0, n) in dst_rows1:
            c.dma("sp", obf[r0:r0 + n, :], t[p0:p0 + n, :], d, reads=[b], pwrites=[out_b])
        t2, b2, d2 = stbf.next()
        c.op("dve", lambda: nc.vector.tensor_tensor(t2[:P, :], tmp[2][:P, :], tmp[3][:P, :], ALU.add),
             reads=[tmp_b[2], tmp_b[3]], writes=[b2])
        for (r0, p0, n) in dst_rows2:
            c.dma("sp", obf[r0:r0 + n, :], t2[p0:p0 + n, :], d2, reads=[b2], pwrites=[out_b])

    rope(kr[0][:], kr_b[0], kr[1][:], kr_b[1], 32, [(RB_KPE, 0, 32)], [(RB_KPE + 32, 0, 32)])

    rq = sb("rq", [128, TL]); rq_b = fresh(olds)
    rk = sb("rk", [128, TL]); rk_b = fresh(olds)
    col_stats(c, dn, accq, accq_b, rq, rq_b, 1536)
    col_stats(c, dn, acck, acck_b, rk, rk_b, 512)
    for i in range(12):
        c.op("dve", lambda: nc.vector.tensor_tensor(cqn[:, i, :], cqn[:, i, :], rq[:], ALU.mult),
             reads=[cqn_b[i], rq_b], writes=[cqn_b[i]])
    for i in range(4):
        c.op("dve", lambda: nc.vector.tensor_tensor(ckn[:, i, :], ckn[:, i, :], rk[:], ALU.mult),
             reads=[ckn_b[i], rk_b], writes=[ckn_b[i]])

    QS = 192 ** -0.5
    qpe = [sb(f"qpe{i}", [128, TL]) for i in range(2)]
    qpe_b = [fresh(olds), fresh(olds)]

    def qpe_h(i, j3):
        def h(tiles):
            for (ps, pb, t0, tn) in tiles:
                c.op("act", lambda: nc.scalar.activation(qpe[i][:, t0:t0 + tn], ps, AF.Identity, scale=QS), reads=[pb], pwrites=[qpe_b[i]])
            if i == 1:
                rows1 = [(RB_Q + (4 * j3 + hh) * 192 + 128, 32 * hh, 32) for hh in range(4)]
                rows2 = [(RB_Q + (4 * j3 + hh) * 192 + 160, 32 * hh, 32) for hh in range(4)]
                rope(qpe[0][:], qpe_b[0], qpe[1][:], qpe_b[1], 128, rows1, rows2)
        return h

    qg = fm_groups(0, 12, lambda i: out_fm(obf, RB_Q + i * 192, scale=QS, dtype=BF16))
    for j3 in range(3):
        qg.append((1536 + j3 * 128, 128, [(0, 128, qpe_h(0, j3))]))
        qg.append((1920 + j3 * 128, 128, [(0, 128, qpe_h(1, j3))]))
    dn.fm(w_uq, 12, cqn, cqn_b, qg)
    kg = fm_groups(0, 12, lambda i: out_fm(obf, RB_KN + i * 128, dtype=BF16))
    dn.fm(w_ukv, 4, ckn, ckn_b, kg)
    vg = tm_groups(1536, 1536, lambda o: out_tm(otb, CB_VM + o, ttb))
    dn.tm(w_ukv, 4, ckn, ckn_b, vg)
    c.wait_all("sp", [out_b])


TH = TL // 2
TTH = [(0, 512), (512, TH - 512)]


def build_p3():
    nc = bass.Bass("TRN2", target_bir_lowering=False)
    dt = lambda n, s, d=F32, k="ExternalInput": nc.dram_tensor(n, s, d, kind=k).ap()
    oT = dt("oT", [128, 32, TL], BF16)
    hT = dt("hT", [128, 32, TL])
    w_out = dt("w_out", [D, D])
    w_up = dt("w_up", [D, 2 * DFF])
    w_down = dt("w_down", [DFF, D])
    g_ffn = dt("g_ffn", [128, 32])
    g_next = dt("g_next", [128, 32])
    conv_w = dt("conv_w", [128, 172, 3])
    conv_b = dt("conv_b", [128, 172])
    h_out = dt("h_out", [128, 32, TL], F32, "ExternalOutput")
    y_out = dt("y_out", [128, 32, TL], F32, "ExternalOutput")
    h_mid = dt("h_mid", [128, 32, TL], F32, "Internal")
    act = dt("act_scr", [86, 128, TL], BF16, "Internal")
    with ExitStack() as es:
        c = Ctx(nc, es)
        emit_p3(c, oT, hT, w_out, w_up, w_down, g_ffn, g_next, conv_w, conv_b, h_out, y_out, h_mid, act)
    return nc


def emit_p3(c, oT, hT, w_out, w_up, w_down, g_ffn, g_next, conv_w, conv_b, h_out, y_out, h_mid, act):
    nc, es = c.nc, c.es
    sb = lambda n, s, d=F32, st=None: (st or es).enter_context(nc.sbuf_tensor(c.nm(n), s, d))
    dn = Dense(c)
    K = _consts(c, [("g_ffn", g_ffn, [128, 32]), ("g_next", g_next, [128, 32]),
                    ("cw", conv_w, [128, 172, 3]), ("cb", conv_b, [128, 172])])
    dn.ones = sb("ones", [128, 128]); dn.ones_b = Buf()
    c.op("dve", lambda: nc.vector.memset(dn.ones[:], 1.0), writes=[dn.ones_b])
    sq = Ring(c, "sq", 2, [128, TL], F32)
    hc = Ring(c, "hc", 2, [128, TL], F32)
    acc = sb("acc", [128, TL]); acc_b = Buf()
    rstd = sb("rstd", [128, TL]); rstd_b = Buf()
    hmid_b = [Buf() for _ in range(32)]
    act_b = [Buf() for _ in range(86)]
    hout_b = [[Buf(), Buf()] for _ in range(32)]
    y_b = Buf()
    gf, gf_b = K["g_ffn"]
    c.op("dve", lambda: nc.vector.memset(acc[:], 0.0), writes=[acc_b])

    es_hg = ExitStack()
    hg = sb("hg", [128, 32, TL], BF16, es_hg); hg_b = [Buf() for _ in range(32)]
    with ExitStack() as es1:
        osb = sb("osb", [128, 32, TL], BF16, es1); osb_b = [Buf() for _ in range(32)]
        d_o = c.dsem("oT")
        for g in range(8):
            c.dma("sp", osb[:, g * 4:(g + 1) * 4, :], oT[:, g * 4:(g + 1) * 4, :], d_o, writes=osb_b[g * 4:(g + 1) * 4])
        Ctx.seal(d_o, osb_b)

        def h3a(m):
            def h(tiles):
                t, b, d = hc.next()
                c.dma("sp", t[:], hT[:, m, :], d, writes=[b])
                for (ps, pb, t0, tn) in tiles:
                    c.op("dve", lambda: nc.vector.tensor_tensor(t[:, t0:t0 + tn], ps, t[:, t0:t0 + tn], ALU.add),
                         reads=[pb, b], pwrites=[b])
                c.dma("sp", h_mid[:, m, :], t[:], d, reads=[b], writes=[hmid_b[m]])
                s, s_b, _ = sq.next()
                c.op("act", lambda: nc.scalar.activation(s[:], t[:], AF.Square), reads=[b], writes=[s_b])
                c.op("dve", lambda: nc.vector.tensor_tensor(acc[:], acc[:], s[:], ALU.add), reads=[acc_b, s_b], writes=[acc_b])
                c.op("act", lambda: nc.scalar.activation(hg[:, m, :], t[:], AF.Identity, scale=gf[:, m:m + 1]),
                     reads=[b, gf_b], writes=[hg_b[m]])
            return h
        dn.fm(w_out, 32, osb, osb_b, fm_groups(0, 32, h3a))
    olds = list(osb_b)
    col_stats(c, dn, acc, acc_b, rstd, rstd_b, D)

    cw, cw_b = K["cw"]
    cb, cb_b = K["cb"]
    with ExitStack() as es2:
        ug = Ring(c, "ug", 2, [128, TL], F32, es2)
        uv = Ring(c, "uv", 2, [128, TL], F32, es2)
        cvg = Ring(c, "cvg", 2, [128, TL], F32, es2)
        cvv = Ring(c, "cvv", 2, [128, TL], F32, es2)
        ab = Ring(c, "ab", 2, [128, TL], BF16, es2)
        for r in (ug, uv, cvg, cvv, ab):
            r.b = [fresh(olds) for _ in r.b]
        for i in range(2):
            c.op("dve", lambda: nc.vector.memset(ab.t[i][:], 0.0), writes=[ab.b[i]])
        gate_cv = {}

        def conv(u, u_b, ch, ring):
            cv, cv_b, _ = ring.next()
            n = TL - 2
            c.op("act", lambda: nc.scalar.activation(cv[:, 2:], u[:, 2:], AF.Identity, bias=cb[:, ch:ch + 1], scale=cw[:, ch, 2:3]),
                 reads=[u_b, cb_b, cw_b], writes=[cv_b])
            c.op("dve", lambda: nc.vector.scalar_tensor_tensor(cv[:, 2:], u[:, 1:1 + n], cw[:, ch, 1:2], cv[:, 2:], ALU.mult, ALU.add),
                 reads=[u_b, cw_b, cv_b], writes=[cv_b])
            c.op("dve", lambda: nc.vector.scalar_tensor_tensor(cv[:, 2:], u[:, 0:n], cw[:, ch, 0:1], cv[:, 2:], ALU.mult, ALU.add),
                 reads=[u_b, cw_b, cv_b], writes=[cv_b])
            return cv, cv_b

        def hup(m, is_val):
            def h(tiles):
                u, u_b, _ = (uv if is_val else ug).next()
                for (ps, pb, t0, tn) in tiles:
                    c.op("dve", lambda: nc.vector.tensor_tensor(u[:, t0:t0 + tn], ps, rstd[:, t0:t0 + tn], ALU.mult),
                         reads=[pb, rstd_b], pwrites=[u_b])
                if not is_val:
                    cv, cv_b = conv(u, u_b, m, cvg)
                    c.op("act", lambda: nc.scalar.activation(cv[:, 2:], cv[:, 2:], AF.Silu), reads=[cv_b], writes=[cv_b])
                    gate_cv[m] = (cv, cv_b)
                else:
                    cv, cv_b = conv(u, u_b, 86 + m, cvv)
                    g, g_b = gate_cv.pop(m)
                    a, a_b, a_d = ab.next()
                    c.op("dve", lambda: nc.vector.tensor_tensor(a[:, 2:], g[:, 2:], cv[:, 2:], ALU.mult),
                         reads=[g_b, cv_b], pwrites=[a_b])
                    c.dma("sp", act[m], a[:], a_d, reads=[a_b], writes=[act_b[m]])
            return h

        groups = []
        for m0 in range(0, 86, 2):
            groups += [(m0 * 128, 256, [(0, 128, hup(m0, False)), (128, 128, hup(m0 + 1, False))]),
                       (DFF + m0 * 128, 256, [(0, 128, hup(m0, True)), (128, 128, hup(m0 + 1, True))])]
        dn.fm(w_up, 32, hg, hg_b, groups)
        olds = olds + hg_b + ug.b + uv.b + cvg.b + cvv.b + ab.b
    es_hg.close()

    accf = sb("accf", [128, TL]); accf_b = fresh(olds)
    c.op("dve", lambda: nc.vector.memset(accf[:], 0.0), writes=[accf_b])
    asb = sb("asb", [128, 86, TH], BF16); asb_b = [fresh(olds) for _ in range(86)]
    d_a = c.dsem("asb")
    actv = act.rearrange("m p t -> p m t")
    for half in range(2):
        c0 = half * TH
        for g in range(0, 86, 8):
            g1 = min(86, g + 8)
            c.dma("sp", asb[:, g:g1, :], actv[:, g:g1, c0:c0 + TH], d_a, reads=act_b[g:g1], writes=asb_b[g:g1])
        Ctx.seal(d_a, asb_b)
        for m in range(32):
            pb = dn.it % 2
            dn.it += 1
            pst = [(dn.psA[pb], dn.bA[pb]), (dn.psB[pb], dn.bB[pb])]
            for part in range(2):
                wt, wb = dn.load(w_down, 43, m * 128, 128, row0=part * 43 * 128)
                for k in range(43):
                    kc = part * 43 + k
                    last = kc == 85
                    for j, (t0, tn) in enumerate(TTH):
                        ps, pbuf = pst[j]
                        c.op("pe", lambda: nc.tensor.matmul(ps[:, :tn], wt[:, k, 0:128], asb[:, kc, t0:t0 + tn],
                                                            start=(kc == 0), stop=last),
                             reads=[wb, asb_b[kc]], writes=[pbuf], inc=(last or k == 42))
            t, b, d = hc.next()
            c.dma("sp", t[:, :TH], h_mid[:, m, c0:c0 + TH], d, reads=[hmid_b[m]], writes=[b])
            for j, (t0, tn) in enumerate(TTH):
                ps, pbuf = pst[j]
                c.op("dve", lambda: nc.vector.tensor_tensor(t[:, t0:t0 + tn], ps[:, :tn], t[:, t0:t0 + tn], ALU.add),
                     reads=[pbuf, b], pwrites=[b])
            c.dma("sp", h_out[:, m, c0:c0 + TH], t[:, :TH], d, reads=[b], writes=[hout_b[m][half]])
            s, s_b, _ = sq.next()
            c.op("act", lambda: nc.scalar.activation(s[:, :TH], t[:, :TH], AF.Square), reads=[b], writes=[s_b])
            c.op("dve", lambda: nc.vector.tensor_tensor(accf[:, c0:c0 + TH], accf[:, c0:c0 + TH], s[:, :TH], ALU.add),
                 reads=[accf_b, s_b], writes=[accf_b])
    col_stats(c, dn, accf, accf_b, rstd, rstd_b, D)
    gn, gn_b = K["g_next"]
    for m in range(32):
        t, b, d = hc.next()
        c.dma("sp", t[:], h_out[:, m, :], d, reads=hout_b[m], writes=[b])
        c.op("dve", lambda: nc.vector.scalar_tensor_tensor(t[:], t[:], gn[:, m:m + 1], rstd[:], ALU.mult, ALU.mult),
             reads=[b, gn_b, rstd_b], writes=[b])
        c.dma("sp", y_out[:, m, :], t[:], d, reads=[b], pwrites=[y_b])
    c.wait_all("sp", [y_b] + [x for hb in hout_b for x in hb])


QT = [(i * 512, 512) for i in range(8)] + [(4096, 16)]
NKC = 33
A_Q, A_KN, A_KPE, A_FQ, A_FK, NA = 0, 576, 960, 1024, 1408, 1792
V_VM, V_FV, V_GV, NV = 0, 384, 768, 1024
F_GQ, F_GK, F_GZ, F_GR, F_FZ, NF = 0, 128, 256, 273, 529, 532
GCH = [(0, 16)] + [(16 + 64 * i, 64) for i in range(64)]


def build_p2():
    nc = bass.Bass("TRN2", target_bir_lowering=False)
    dt = lambda n, s, d=F32, k="ExternalInput": nc.dram_tensor(n, s, d, kind=k).ap()
    a = dict(
        abf=dt("abf", [NA, NT], BF16), vbf=dt("vbf", [NT, NV], BF16), f32=dt("f32", [NF, NT]),
        gkm=dt("gkm", [NT, 128]), wg2=dt("wg2", [17, 128]), fbf=dt("fbf", [3, 1]),
        gmla=dt("gmla", [128, 3]), gfox=dt("gfox", [128, 3]), ggla=dt("ggla", [128, 2]),
        mask4=dt("mask4", [128, 4, 512], BF16), tri01=dt("tri01", [64, 64]),
        tris64=dt("tris64", [64, 65]), tris16=dt("tris16", [16, 17]), sus=dt("sus", [64, 64]),
        oT=dt("oT", [1024, NT], BF16, "ExternalOutput"),
        aug=dt("aug_scr", [3, 12, NT], BF16, "Internal"))
    with ExitStack() as es:
        c = Ctx(nc, es)
        emit_p2(c, **a)
    return nc


def emit_p2(c, abf, vbf, f32, gkm, wg2, fbf, gmla, gfox, ggla, mask4, tri01, tris64, tris16, sus, oT, aug):
    nc, es = c.nc, c.es
    sb = lambda n, s, d=F32, st=None: (st or es).enter_context(nc.sbuf_tensor(c.nm(n), s, d))
    K = _consts(c, [("wg2", wg2, [17, 128]), ("fbf", fbf, [3, 1]), ("gmla", gmla, [128, 3]), ("gfox", gfox, [128, 3]),
                    ("ggla", ggla, [128, 2]), ("tri01", tri01, [64, 64]), ("tris64", tris64, [64, 65]),
                    ("tris16", tris16, [16, 17]), ("sus", sus, [64, 64])])
    mk = sb("mk_sb", [128, 4, 512], BF16); mk_b = Buf()
    c.dma("sp", mk[:], mask4, c.dsem("mk"), writes=[mk_b])
    ones = sb("ones", [128, 512]); ones_b = Buf()
    c.op("dve", lambda: nc.vector.memset(ones[:], 1.0), writes=[ones_b])
    onesb = sb("onesb", [128, 128], BF16); onesb_b = Buf()
    c.op("dve", lambda: nc.vector.memset(onesb[:], 1.0), writes=[onesb_b])
    P = [es.enter_context(nc.psum_tensor(c.nm(f"pp{i}"), [128, 512], F32)) for i in range(7)]
    Pb = [Buf() for _ in range(7)]
    out_b = Buf()
    stb = Ring(c, "stb", 3, [128, 512], BF16)
    olds = []

    def head_norm_out(o_parts, gain, gain_b, gcol0, n, q0, row0, extra=None):
        nf = 128 * len(o_parts)
        for i, (o, o_b) in enumerate(o_parts):
            s, s_b, _ = sqr.next()
            c.op("act", lambda: nc.scalar.activation(s[:, :n], o, AF.Square), reads=[o_b], writes=[s_b])
            c.op("pe", lambda: nc.tensor.matmul(P[6][:, :n], ones[:, :128], s[:, :n], start=(i == 0), stop=(i == len(o_parts) - 1)),
                 reads=[s_b, ones_b], writes=[Pb[6]])
        r, r_b, _ = rsr.next()
        c.op("dve", lambda: nc.vector.tensor_scalar(r[:, :n], P[6][:, :n], 1.0 / nf, EPS, ALU.mult, ALU.add), reads=[Pb[6]], writes=[r_b])
        c.op("act", lambda: nc.scalar.activation(r[:, :n], r[:, :n], AF.Sqrt), reads=[r_b], writes=[r_b])
        c.op("dve", lambda: nc.vector.reciprocal(r[:, :n], r[:, :n]), reads=[r_b], writes=[r_b])
        for i, (o, o_b) in enumerate(o_parts):
            t, b, d = stb.next()
            if extra is None:
                c.op("dve", lambda: nc.vector.scalar_tensor_tensor(t[:, :n], o, gain[:, gcol0 + i:gcol0 + i + 1], r[:, :n], ALU.mult, ALU.mult),
                     reads=[o_b, gain_b, r_b], writes=[b])
            else:
                ex, ex_b = extra[i]
                c.op("dve", lambda: nc.vector.scalar_tensor_tensor(o, o, gain[:, gcol0 + i:gcol0 + i + 1], r[:, :n], ALU.mult, ALU.mult),
                     reads=[o_b, gain_b, r_b], writes=[o_b])
                c.op("dve", lambda: nc.vector.tensor_tensor(t[:, :n], o, ex, ALU.mult), reads=[o_b, ex_b], writes=[b])
            c.dma("sp", oT[row0 + i * 128:row0 + (i + 1) * 128, q0:q0 + n], t[:, :n], d, reads=[b], pwrites=[out_b])

    sqr = Ring(c, "sqr", 2, [128, 512], F32)
    rsr = Ring(c, "rsr", 2, [128, 512], F32)

    with ExitStack() as eg:
        wg, wg_b = K["wg2"]
        tri, tri_b = K["tri01"]
        ts64, ts64_b = K["tris64"]
        ts16, ts16_b = K["tris16"]
        su, su_b = K["sus"]
        gv = sb("gv", [64, 65, 256], BF16, eg); gv_b = Buf()
        d_gv = c.dsem("gv")
        c.dma("sp", gv[:16, 0, :], vbf[0:16, V_GV:V_GV + 256], d_gv, writes=[gv_b])
        for i in range(4):
            c.dma("sp", gv[:, 1 + 16 * i:17 + 16 * i, :],
                  vbf[16 + 1024 * i:16 + 1024 * (i + 1), V_GV:V_GV + 256].rearrange("(n p) d -> p n d", p=64), d_gv, pwrites=[gv_b])
        qdec = sb("qdec", [128, NT], BF16, eg); qdec_b = [Buf() for _ in range(65)]
        kst = sb("kst", [64, 65, 128], BF16, eg); kst_b = [Buf() for _ in range(65)]
        Aall = sb("Aall", [64, 65, 64], BF16, eg); A_b = [Buf() for _ in range(65)]
        dec = sb("dec", [128, 65], F32, eg); dec_b = [Buf() for _ in range(65)]
        oall = sb("oall_sb", [128, 2, NT], F32, eg); oall_b = [Buf() for _ in range(9)]
        S = sb("S", [128, 256], F32, eg); S_b = Buf()
        Sbf = sb("Sbf", [128, 256], BF16, eg); Sbf_b = Buf()
        gzr = Ring(c, "gzr", 2, [17, 512], F32, eg)
        gqr = Ring(c, "gqr", 2, [128, 512], F32, eg)
        gkr = Ring(c, "gkr", 2, [128, 512], F32, eg)
        gkmr = Ring(c, "gkmr", 2, [64, 8, 128], F32, eg)
        spr = Ring(c, "spr", 2, [64, 8, 128], F32, eg)
        e4 = Ring(c, "e4", 2, [128, 64], F32, eg)
        ek = Ring(c, "ek", 2, [64, 128], F32, eg)
        kdr = Ring(c, "kdr", 2, [128, 64], BF16, eg)
        supers = [(0, 16, [0])] + [(16 + 512 * j, 512, list(range(1 + 8 * j, 9 + 8 * j))) for j in range(8)]
        for (r0, rn, chunks) in supers:
            gz, gz_b, gz_d = gzr.next()
            c.dma("sp", gz[:, :rn], f32[F_GZ:F_GZ + 17, r0:r0 + rn], gz_d, writes=[gz_b])
            gq, gq_b, gq_d = gqr.next()
            c.dma("sp", gq[:, :rn], f32[F_GQ:F_GQ + 128, r0:r0 + rn], gq_d, writes=[gq_b])
            gk, gk_b, gk_d = gkr.next()
            c.dma("sp", gk[:, :rn], f32[F_GK:F_GK + 128, r0:r0 + rn], gk_d, writes=[gk_b])
            gm, gm_b, gm_d = gkmr.next()
            C = GCH[chunks[0]][1]
            nch = len(chunks)
            c.dma("sp", gm[:C, :nch, :], gkm[r0:r0 + rn, :].rearrange("(n p) d -> p n d", p=C), gm_d, writes=[gm_b])
            sp, sp_b, _ = spr.next()
            for g4 in range(0, nch, 4):
                n4 = min(4, nch - g4)
                for i in range(n4):
                    o0 = (g4 + i) * C
                    c.op("pe", lambda: nc.tensor.matmul(P[0][:C, i * 128:(i + 1) * 128], gz[:, o0:o0 + C], wg[:], start=True, stop=True),
                         reads=[gz_b, wg_b], writes=[Pb[0]])
                c.op("act", lambda: nc.scalar.activation(sp[:C, g4:g4 + n4, :], P[0][:C, :n4 * 128].rearrange("p (a b) -> p a b", b=128), AF.Exp, scale=-1.0),
                     reads=[Pb[0]], pwrites=[sp_b])
            c.op("act", lambda: nc.scalar.activation(sp[:C, :nch, :], sp[:C, :nch, :], AF.Ln, bias=1.0), reads=[sp_b], writes=[sp_b])
            for i, n in enumerate(chunks):
                s0, C = GCH[n]
                l0 = i * C
                tsx, tsx_b = (ts64, ts64_b) if C == 64 else (ts16, ts16_b)
                c.op("pe", lambda: nc.tensor.matmul(P[1][:, :C + 1], sp[:C, i, :], tsx[:C, :C + 1], start=True, stop=True),
                     reads=[sp_b, tsx_b], writes=[Pb[1]])
                c.op("pe", lambda: nc.tensor.matmul(P[2][:C, :128], su[:C, :C], sp[:C, i, :], start=True, stop=True),
                     reads=[sp_b, su_b], writes=[Pb[2]])
                eb, eb_b, _ = e4.next()
                c.op("act", lambda: nc.scalar.activation(eb[:, :C], P[1][:, :C], AF.Exp), reads=[Pb[1]], writes=[eb_b])
                c.op("dve", lambda: nc.vector.tensor_tensor(qdec[:, s0:s0 + C], gq[:, l0:l0 + C], eb[:, :C], ALU.mult),
                     reads=[gq_b, eb_b], writes=[qdec_b[n]])
                en, en_b, _ = e4.next()
                c.op("act", lambda: nc.scalar.activation(en[:, :C], P[1][:, :C], AF.Exp, scale=-1.0), reads=[Pb[1]], writes=[en_b])
                kd, kd_b, _ = kdr.next()
                c.op("dve", lambda: nc.vector.tensor_tensor(kd[:, :C], gk[:, l0:l0 + C], en[:, :C], ALU.mult),
                     reads=[gk_b, en_b], writes=[kd_b])
                c.op("act", lambda: nc.scalar.activation(dec[:, n:n + 1], P[1][:, C:C + 1], AF.Exp), reads=[Pb[1]], writes=[dec_b[n]])
                ekt, ek_b, _ = ek.next()
                c.op("act", lambda: nc.scalar.activation(ekt[:C, :], P[2][:C, :128], AF.Exp), reads=[Pb[2]], writes=[ek_b])
                c.op("dve", lambda: nc.vector.tensor_tensor(kst[:C, n, :], gm[:C, i, :], ekt[:C, :], ALU.mult),
                     reads=[gm_b, ek_b], writes=[kst_b[n]])
                c.op("pe", lambda: nc.tensor.matmul(P[3][:C, :C], kd[:, :C], qdec[:, s0:s0 + C], start=True, stop=True),
                     reads=[kd_b, qdec_b[n]], writes=[Pb[3]])
                c.op("dve", lambda: nc.vector.tensor_tensor(Aall[:C, n, :C], P[3][:C, :C], tri[:C, :C], ALU.mult),
                     reads=[Pb[3], tri_b], writes=[A_b[n]])
        c.op("dve", lambda: nc.vector.memset(S[:], 0.0), writes=[S_b])
        for n, (s0, C) in enumerate(GCH):
            po, po_b = P[4 + n % 2], Pb[4 + n % 2]
            for half in range(2):
                c.op("pe", lambda: nc.tensor.matmul(po[:, half * 64:half * 64 + C], gv[:C, n, half * 128:(half + 1) * 128], Aall[:C, n, :C],
                                                    start=True, stop=(n == 0)),
                     reads=[gv_b, A_b[n]], writes=[po_b])
                if n > 0:
                    c.op("pe", lambda: nc.tensor.matmul(po[:, half * 64:half * 64 + C], Sbf[:, half * 128:(half + 1) * 128], qdec[:, s0:s0 + C],
                                                        start=False, stop=True),
                         reads=[Sbf_b, qdec_b[n]], writes=[po_b])
            ti = 0 if n == 0 else 0 + (s0 // 512)
            for half in range(2):
                c.op("act", lambda: nc.scalar.copy(oall[:, half, s0:s0 + C], po[:, half * 64:half * 64 + C]),
                     reads=[po_b], pwrites=[oall_b[min(8, s0 // 512)], oall_b[min(8, (s0 + C - 1) // 512)]])
            c.op("pe", lambda: nc.tensor.matmul(P[0][:, :256], kst[:C, n, :], gv[:C, n, :], start=True, stop=True),
                 reads=[kst_b[n], gv_b], writes=[Pb[0]])
            c.op("dve", lambda: nc.vector.scalar_tensor_tensor(S[:], S[:], dec[:, n:n + 1], P[0][:, :256], ALU.mult, ALU.add),
                 reads=[S_b, dec_b[n], Pb[0]], writes=[S_b])
            c.op("act", lambda: nc.scalar.copy(Sbf[:], S[:]), reads=[S_b], writes=[Sbf_b])
        gg, gg_b = K["ggla"]
        grr = Ring(c, "grr", 2, [128, 2, 512], F32, eg)
        for ti, (q0, n) in enumerate(QT):
            gr, gr_b, gr_d = grr.next()
            c.dma("sp", gr[:, :, :n], f32[F_GR:F_GR + 256, q0:q0 + n].rearrange("(h p) t -> p h t", p=128), gr_d, writes=[gr_b])
            head_norm_out([(oall[:, hh, q0:q0 + n], oall_b[ti]) for hh in range(2)], gg, gg_b, 0, n, q0, 384,
                          extra=[(gr[:, hh, :n], gr_b) for hh in range(2)])
        olds = [gv_b, Sbf_b, S_b] + qdec_b + kst_b + A_b + dec_b + oall_b + gzr.b + gqr.b + gkr.b + gkmr.b + spr.b + e4.b + ek.b + kdr.b + grr.b

    with ExitStack() as ea:
        qn = Ring(c, "qn", 2, [128, NT], BF16, ea)
        qp = Ring(c, "qp", 2, [64, NT], BF16, ea)
        kn = Ring(c, "kn", 2, [128, NT], BF16, ea)
        vv = Ring(c, "vv", 2, [128, NKC, 128], BF16, ea)
        kpe = sb("kpe", [64, NT], BF16, ea); kpe_b = fresh(olds)
        pT = Ring(c, "pT", 3, [128, 512], BF16, ea)
        osb = Ring(c, "osb", 2, [128, 512], F32, ea)
        rl = Ring(c, "rl", 2, [128, 512], F32, ea)
        exr = Ring(c, "exr", 2, [128, 512], F32, ea)
        mx = sb("mx", [128, 4], F32, ea); mx_b = fresh(olds)
        negm = sb("negm", [128, 1], F32, ea); negm_b = fresh(olds)
        nfb = sb("nfb", [3, 1], F32, ea); nfb_b = fresh(olds)
        aq = Ring(c, "aq", 1, [6, NT], BF16, ea)
        ak = Ring(c, "ak", 1, [6, NT], BF16, ea)
        for r in (qn, qp, kn, vv, pT, osb, rl, exr, aq, ak):
            r.b = [fresh(olds) for _ in r.b]
        c.dma("sp", kpe[:], abf[A_KPE:A_KPE + 64, :], c.dsem("kpe"), writes=[kpe_b])
        aug_b = Buf()
        with ExitStack() as ef:
            fz = sb("fz", [3, NT], F32, ef); fz_b = fresh(olds)
            cs = sb("cs", [3, NT], F32, ef); cs_b = fresh(olds)
            spl = Ring(c, "spl", 2, [3, NT], BF16, ef)
            spl.b = [fresh(olds) for _ in spl.b]
            fb, fb_b = K["fbf"]
            c.dma("sp", fz[:], f32[F_FZ:F_FZ + 3, :], c.dsem("fz"), writes=[fz_b])
            t1, t1_b, t1_d = spl.next()
            c.op("dve", lambda: nc.vector.memset(t1[:], 1.0), writes=[t1_b])
            for r in range(3, 9):
                c.dma("sp", aug[:, r, :], t1[:], t1_d, reads=[t1_b], pwrites=[aug_b])
            c.op("dve", lambda: nc.vector.tensor_scalar(nfb[:], fb[:], -1.0, None, ALU.mult), reads=[fb_b], writes=[nfb_b])
            c.op("act", lambda: nc.scalar.activation(fz[:], fz[:], AF.Exp, bias=nfb[:], scale=-1.0), reads=[fz_b, nfb_b], writes=[fz_b])
            c.op("act", lambda: nc.scalar.activation(fz[:], fz[:], AF.Ln, bias=1.0), reads=[fz_b], writes=[fz_b])
            c.op("dve", lambda: nc.vector.tensor_scalar(fz[:], fz[:], -1.0, None, ALU.mult), reads=[fz_b], writes=[fz_b])
            for j, (q0, n) in enumerate(QT):
                init = 0.0 if j == 0 else cs[:, q0 - 1:q0]
                c.op("dve", lambda: nc.vector.tensor_tensor_scan(cs[:, q0:q0 + n], ones[:3, :n], fz[:, q0:q0 + n], init, ALU.mult, ALU.add),
                     reads=[ones_b, fz_b, cs_b], writes=[cs_b])
            for i in range(3):
                t1, t1_b, t1_d = spl.next()
                c.op("dve", lambda: nc.vector.tensor_copy(t1[:], cs[:]), reads=[cs_b], writes=[t1_b])
                c.dma("sp", aug[:, i, :], t1[:], t1_d, reads=[t1_b], pwrites=[aug_b])
                if i < 2:
                    c.op("dve", lambda: nc.vector.tensor_tensor(cs[:], cs[:], t1[:], ALU.subtract), reads=[cs_b, t1_b], writes=[cs_b])
                t2, t2_b, t2_d = spl.next()
                c.op("act", lambda: nc.scalar.mul(t2[:], t1[:], -1.0), reads=[t1_b], writes=[t2_b])
                c.dma("sp", aug[:, 9 + i, :], t2[:], t2_d, reads=[t2_b], pwrites=[aug_b])
            olds = olds + [fz_b, cs_b] + spl.b

        heads = [("mla", h) for h in range(3)] + [("fox", h) for h in range(3)]
        for (kind, h) in heads:
            q_t, q_b, q_d = qn.next()
            k_t, k_b, k_d = kn.next()
            v_t, v_b, v_d = vv.next()
            parts = []
            if kind == "mla":
                qrow, krow, vcol, orow = A_Q + h * 192, A_KN + h * 128, V_VM + h * 128, h * 128
                gain, gain_b = K["gmla"]
                p_t, p_b, p_d = qp.next()
                c.dma("sp", p_t[:], abf[qrow + 128:qrow + 192, :], p_d, writes=[p_b])
                parts = [(k_t, k_b, q_t, q_b, 128), (kpe, kpe_b, p_t, p_b, 64)]
            else:
                qrow, krow, vcol, orow = A_FQ + h * 128, A_FK + h * 128, V_FV + h * 128, 640 + h * 128
                gain, gain_b = K["gfox"]
                a_q, a_qb, a_qd = aq.next()
                a_k, a_kb, a_kd = ak.next()
                c.dma("sp", a_q[:], aug[h, 0:6, :], a_qd, reads=[aug_b], writes=[a_qb])
                c.dma("sp", a_k[:], aug[h, 6:12, :], a_kd, reads=[aug_b], writes=[a_kb])
                parts = [(k_t, k_b, q_t, q_b, 128), (a_k, a_kb, a_q, a_qb, 6)]
            c.dma("sp", q_t[:], abf[qrow:qrow + 128, :], q_d, writes=[q_b])
            c.dma("sp", k_t[:], abf[krow:krow + 128, :], k_d, writes=[k_b])
            for i in range(4):
                c.dma("sp", v_t[:, 8 * i:8 * i + 8, :], vbf[1024 * i:1024 * (i + 1), vcol:vcol + 128].rearrange("(n p) d -> p n d", p=128),
                      v_d, writes=[v_b] if i == 0 else [], pwrites=[v_b] if i else [])
            c.dma("sp", v_t[:16, 32, :], vbf[4096:4112, vcol:vcol + 128], v_d, pwrites=[v_b])
            c.op("dve", lambda: nc.vector.memset(mx[:], 0.0), writes=[mx_b])
            for side in range(2):
                plist = [(p[2], p[3], p[4]) if side == 0 else (p[0], p[1], p[4]) for p in parts if p[4] > 6]
                for (q0, n) in QT:
                    for i, (t_, b_, kp) in enumerate(plist):
                        s, s_b, _ = sqr.next()
                        c.op("act", lambda: nc.scalar.activation(s[:kp, :n], t_[:kp, q0:q0 + n], AF.Square), reads=[b_], writes=[s_b])
                        c.op("pe", lambda: nc.tensor.matmul(P[6][:, :n], ones[:kp, :128], s[:kp, :n], start=(i == 0), stop=(i == len(plist) - 1)),
                             reads=[s_b, ones_b], writes=[Pb[6]])
                    c.op("dve", lambda: nc.vector.reduce_max(mx[:, 2:3], P[6][:, :n], axis=AX.X), reads=[Pb[6]], writes=[mx_b])
                    c.op("dve", lambda: nc.vector.tensor_tensor(mx[:, side:side + 1], mx[:, side:side + 1], mx[:, 2:3], ALU.max),
                         reads=[mx_b], writes=[mx_b])
            c.op("dve", lambda: nc.vector.tensor_tensor(mx[:, 3:4], mx[:, 0:1], mx[:, 1:2], ALU.mult), reads=[mx_b], writes=[mx_b])
            c.op("act", lambda: nc.scalar.activation(negm[:], mx[:, 3:4], AF.Sqrt), reads=[mx_b], writes=[negm_b])
            c.op("dve", lambda: nc.vector.tensor_scalar(negm[:], negm[:], -1.0, None, ALU.mult), reads=[negm_b], writes=[negm_b])
            for ti, (q0, n) in enumerate(QT):
                last_c = min(4 * ti + 3, NKC - 1)
                po, po_b = P[2 + ti % 2], Pb[2 + ti % 2]
                pl, pl_b = P[4 + ti % 2], Pb[4 + ti % 2]
                for kc in range(last_c + 1):
                    k0 = kc * 128
                    kn_ = min(128, NT - k0)
                    ps, ps_b = P[kc % 2], Pb[kc % 2]
                    for i, (kt_, kb_, qt_, qb_, kp) in enumerate(parts):
                        c.op("pe", lambda: nc.tensor.matmul(ps[:kn_, :n], kt_[:kp, k0:k0 + kn_], qt_[:kp, q0:q0 + n],
                                                            start=(i == 0), stop=(i == len(parts) - 1)),
                             reads=[kb_, qb_], writes=[ps_b], inc=(i == len(parts) - 1))
                    p_, p_b2, _ = pT.next()
                    r = kc - 4 * ti
                    if r >= 0 and kind == "fox":
                        x_, x_b, _ = exr.next()
                        c.op("dve", lambda: nc.vector.tensor_scalar(x_[:kn_, :n], ps[:kn_, :n], negm[:kn_, :], 0.0, ALU.add, ALU.min),
                             reads=[ps_b, negm_b], writes=[x_b])
                        c.op("act", lambda: nc.scalar.activation(p_[:kn_, :n], x_[:kn_, :n], AF.Exp), reads=[x_b], writes=[p_b2])
                    else:
                        c.op("act", lambda: nc.scalar.activation(p_[:kn_, :n], ps[:kn_, :n], AF.Exp, bias=negm[:kn_, :]),
                             reads=[ps_b, negm_b], writes=[p_b2])
                    if r >= 0:
                        c.op("dve", lambda: nc.vector.tensor_tensor(p_[:kn_, :n], p_[:kn_, :n], mk[:kn_, r, :n], ALU.mult),
                             reads=[p_b2, mk_b], writes=[p_b2])
                    c.op("pe", lambda: nc.tensor.matmul(po[:, :n], v_t[:kn_, kc, :], p_[:kn_, :n], start=(kc == 0), stop=(kc == last_c)),
                         reads=[v_b, p_b2], writes=[po_b], inc=(kc == last_c))
                    c.op("pe", lambda: nc.tensor.matmul(pl[:, :n], onesb[:kn_, :], p_[:kn_, :n], start=(kc == 0), stop=(kc == last_c)),
                         reads=[onesb_b, p_b2], writes=[pl_b], inc=True)
                r_, r_b, _ = rl.next()
                c.op("dve", lambda: nc.vector.reciprocal(r_[:, :n], pl[:, :n]), reads=[pl_b], writes=[r_b])
                o_, o_b, _ = osb.next()
                c.op("dve", lambda: nc.vector.tensor_tensor(o_[:, :n], po[:, :n], r_[:, :n], ALU.mult), reads=[po_b, r_b], writes=[o_b])
                head_norm_out([(o_[:, :n], o_b)], gain, gain_b, h, n, q0, orow)
    c.wait_all("sp", [out_b])


_PROG = {}


def _prog(name, builder):
    if name not in _PROG:
        _PROG[name] = builder()
    return _PROG[name]


def _fmaj(a):
    T = a.shape[0]
    return np.ascontiguousarray(a.T.reshape(-1, 128, T).transpose(1, 0, 2))


def _unfm(a):
    return a.transpose(1, 0, 2).reshape(-1, a.shape[2]).T


def _pcol(v, n):
    return np.ascontiguousarray(np.asarray(v).reshape(n, 128).T)


def _p2_consts():
    kk = np.arange(128)[:, None]
    qq = np.arange(512)[None, :]
    mask4 = np.stack([(qq >= r * 128 + kk) for r in range(4)], 1).astype(NPBF)
    s = np.arange(64)[:, None]
    t = np.arange(64)[None, :]
    m16 = np.float32(-1.0 / 16.0)
    tri01 = (s <= t).astype(np.float32)
    tris64 = np.concatenate([(s <= t) * m16, np.full((64, 1), m16)], 1).astype(np.float32)
    tris16 = np.ascontiguousarray(np.concatenate([tris64[:16, :16], tris64[:16, 64:65]], 1))
    sus = ((s > t) * m16).astype(np.float32)
    return dict(mask4=mask4, tri01=tri01, tris64=tris64, tris16=tris16, sus=sus)


def _core_cols(g4):
    s = NMETA + g4 * OWN
    return np.concatenate([np.arange(s - 2, s + OWN), np.array([0, 0]), np.arange(0, NMETA)])


def kernel_unfused(x, meta_tokens, attn_norm, w_in, mla_q_norm, mla_w_uq, mla_kv_norm, mla_w_ukv,
           gla_w_gate2, gla_b_gate, fox_b_f, out_norm_mla, out_norm_gla, out_norm_fox,
           w_out, ffn_norm, ffn_w_up, ffn_conv_w, ffn_conv_b, ffn_w_down, final_norm):
    f32 = np.float32
    x = np.asarray(x, f32)
    B = x.shape[0]
    cores = list(range(8))
    h = np.concatenate([np.broadcast_to(np.asarray(meta_tokens, f32)[None], (B, NMETA, D)), x], axis=1)
    pos = np.arange(NT, dtype=f32)
    inv = (f32(1.0) / (f32(10000.0) ** (np.arange(0, 64, 2, dtype=f32) / f32(64)))).astype(f32)
    ang = (pos[:, None] * inv[None, :]).astype(f32)
    cosT, sinT = np.cos(ang).astype(f32).T, np.sin(ang).astype(f32).T
    zero_cols = np.array([2 + OWN, 3 + OWN])
    p2c = _p2_consts()
    p1, p2, p3 = _prog("p1", build_p1), _prog("p2", build_p2), _prog("p3", build_p3)
    y_final = None
    for l in range(2):
        uq3 = np.asarray(mla_w_uq[l]).reshape(1536, 12, 192)
        w_uq_p = np.ascontiguousarray(np.concatenate(
            [uq3[:, :, :128].reshape(1536, -1), uq3[:, :, 128:160].reshape(1536, -1), uq3[:, :, 160:].reshape(1536, -1)], 1))
        kv3 = np.asarray(mla_w_ukv[l]).reshape(512, 12, 256)
        w_ukv_p = np.ascontiguousarray(np.concatenate([kv3[:, :, :128].reshape(512, -1), kv3[:, :, 128:].reshape(512, -1)], 1))
        hTs = []
        maps = []
        for core in cores:
            b, g4 = divmod(core, 4)
            cols = _core_cols(g4)
            Hc = h[b][cols]
            Hc[zero_cols] = 0
            hT = _fmaj(Hc)
            hTs.append(hT)
            cs = cosT[:, cols].copy(); sn = sinT[:, cols].copy()
            maps.append(dict(hT=hT, w_in=np.asarray(w_in[l]), w_uq=w_uq_p, w_ukv=w_ukv_p,
                             g_attn=_pcol(attn_norm[l], 32), g_q=_pcol(mla_q_norm[l], 12), g_kv=_pcol(mla_kv_norm[l], 4),
                             cos4=np.ascontiguousarray(np.tile(cs, (4, 1))), sin4=np.ascontiguousarray(np.tile(sn, (4, 1)))))
        r1 = run_bass_kernel_spmd(p1, maps, core_ids=cores).results
        del maps
        maps = []
        for b in range(B):
            def gather_fm(name):
                parts = [r1[b * 4][name][:, 4 + OWN:4 + OWN + NMETA]] + [r1[b * 4 + g][name][:, 2:2 + OWN] for g in range(4)]
                return np.concatenate(parts, axis=1)

            def gather_tm(name):
                parts = [r1[b * 4][name][4 + OWN:4 + OWN + NMETA]] + [r1[b * 4 + g][name][2:2 + OWN] for g in range(4)]
                return np.concatenate(parts, axis=0)
            obf, o32, otf, otb = gather_fm("obf"), gather_fm("o32"), gather_tm("otf"), gather_tm("otb")
            for g in range(4):
                abf = np.concatenate([obf[RB_Q + 3 * g * 192:RB_Q + 3 * (g + 1) * 192],
                                      obf[RB_KN + 3 * g * 128:RB_KN + 3 * (g + 1) * 128],
                                      obf[RB_KPE:RB_KPE + 64],
                                      obf[RB_FQ + 3 * g * 128:RB_FQ + 3 * (g + 1) * 128],
                                      obf[RB_FK + 3 * g * 128:RB_FK + 3 * (g + 1) * 128]], 0)
                vbf = np.concatenate([otb[:, CB_VM + 3 * g * 128:CB_VM + 3 * (g + 1) * 128],
                                      otb[:, CB_FV + 3 * g * 128:CB_FV + 3 * (g + 1) * 128],
                                      otb[:, CB_GV + g * 256:CB_GV + (g + 1) * 256]], 1)
                ff = np.concatenate([o32[R32_GQ + g * 128:R32_GQ + (g + 1) * 128],
                                     o32[R32_GK + g * 128:R32_GK + (g + 1) * 128],
                                     o32[R32_GZ:R32_GZ + 16], np.ones((1, NT), f32),
                                     o32[R32_GR + g * 256:R32_GR + (g + 1) * 256],
                                     o32[R32_FZ + 3 * g:R32_FZ + 3 * (g + 1)]], 0)
                wg2 = np.concatenate([np.asarray(gla_w_gate2[l])[:, g * 128:(g + 1) * 128],
                                      np.asarray(gla_b_gate[l])[None, g * 128:(g + 1) * 128]], 0).astype(f32)
                maps.append(dict(
                    abf=np.ascontiguousarray(abf), vbf=np.ascontiguousarray(vbf), f32=np.ascontiguousarray(ff),
                    gkm=np.ascontiguousarray(otf[:, g * 128:(g + 1) * 128]), wg2=np.ascontiguousarray(wg2),
                    fbf=np.ascontiguousarray(np.asarray(fox_b_f[l], f32)[3 * g:3 * g + 3, None]),
                    gmla=np.ascontiguousarray(np.asarray(out_norm_mla[l], f32).reshape(12, 128)[3 * g:3 * g + 3].T),
                    gfox=np.ascontiguousarray(np.asarray(out_norm_fox[l], f32).reshape(12, 128)[3 * g:3 * g + 3].T),
                    ggla=np.ascontiguousarray(np.asarray(out_norm_gla[l], f32).reshape(4, 2, 128)[g].T),
                    **p2c))
        del r1
        r2 = run_bass_kernel_spmd(p2, maps, core_ids=cores).results
        del maps
        maps = []
        for b in range(B):
            om = np.empty((D, NT), NPBF)
            for g in range(4):
                o = r2[b * 4 + g]["oT"]
                om[3 * g * 128:3 * (g + 1) * 128] = o[0:384]
                om[1536 + g * 256:1536 + (g + 1) * 256] = o[384:640]
                om[2560 + 3 * g * 128:2560 + 3 * (g + 1) * 128] = o[640:1024]
            for g4 in range(4):
                cols = _core_cols(g4)
                oc = om[:, cols]
                oc[:, zero_cols] = 0
                oTc = np.ascontiguousarray(oc.reshape(32, 128, TL).transpose(1, 0, 2))
                gn = final_norm if l == 1 else attn_norm[1]
                cw = np.asarray(ffn_conv_w[l], f32)
                maps.append(dict(oT=oTc, hT=hTs[b * 4 + g4], w_out=np.asarray(w_out[l]), w_up=np.asarray(ffn_w_up[l]),
                                 w_down=np.asarray(ffn_w_down[l]), g_ffn=_pcol(ffn_norm[l], 32), g_next=_pcol(gn, 32),
                                 conv_w=np.ascontiguousarray(cw.T.reshape(172, 128, 3).transpose(1, 0, 2)),
                                 conv_b=_pcol(ffn_conv_b[l], 172)))
        del r2
        r3 = run_bass_kernel_spmd(p3, maps, core_ids=cores).results
        del maps
        for core in cores:
            b, g4 = divmod(core, 4)
            s = NMETA + g4 * OWN
            ho = _unfm(r3[core]["h_out"])
            h[b, s:s + OWN] = ho[2:2 + OWN]
            if g4 == 0:
                h[b, 0:NMETA] = ho[4 + OWN:4 + OWN + NMETA]
        if l == 1:
            y_final = np.empty((B, SEQ, D), f32)
            for core in cores:
                b, g4 = divmod(core, 4)
                y_final[b, g4 * OWN:(g4 + 1) * OWN] = _unfm(r3[core]["y_out"])[2:2 + OWN]
        del r3
    return y_final


class Gath:
    def __init__(self, nc, name, R, C, dtype, esz):
        rp = (1 << 20) // (C * esz)
        if rp >= 64:
            rp = (rp // 64) * 64
        self.C = C
        self.pieces = [(r0, min(rp, R - r0)) for r0 in range(0, R, rp)]
        self.g = [nc.dram_tensor(f"{name}_g{i}", [4 * n, C], dtype, kind="Internal").ap() for i, (r0, n) in enumerate(self.pieces)]
        self.buf = Buf()

    def gather(self, c, X, cs):
        for (r0, n), g in zip(self.pieces, self.g):
            c.allgather(g, X[r0:r0 + n, :], cs, writes=[self.buf])

    def segs(self, row0, n):
        out = []
        for (r0, pn), g in zip(self.pieces, self.g):
            a, b = max(row0, r0), min(row0 + n, r0 + pn)
            if a < b:
                out.append((g.rearrange("(r n) c -> r n c", r=4)[:, a - r0:b - r0, :], a - row0, b - a))
        return out


def emit_select1(c, sel, G32, GBF, GTF, GTB, abf, vbf, f32, gkm, dst_b):
    nc, es = c.nc, c.es
    K = _consts(c, [("sel", sel, [128, 4])])
    sl, sl_b = K["sel"]
    for (G, dst, dtype, jobs, tag) in (
            (GBF, abf, BF16, [(A_Q, RB_Q, 576, 576), (A_KN, RB_KN, 384, 384), (A_KPE, RB_KPE, 64, 0),
                              (A_FQ, RB_FQ, 384, 384), (A_FK, RB_FK, 384, 384)], "sb"),
            (G32, f32, F32, [(F_GQ, R32_GQ, 128, 128), (F_GK, R32_GK, 128, 128), (F_GZ, R32_GZ, 16, 0),
                             (F_GR, R32_GR, 256, 256), (F_FZ, R32_FZ, 3, 3)], "sf")):
        cand = Ring(c, "cand" + tag, 4, [128, NT], dtype)
        accr = Ring(c, "acc" + tag, 2, [128, NT], dtype)
        for (d0, s0, nrows, stride) in jobs:
            for r0 in range(0, nrows, 128):
                n = min(128, nrows - r0)
                a, a_b, a_d = accr.next()
                for g in range(4):
                    t, b, d = cand.next()
                    srow = s0 + g * stride + r0
                    first = True
                    for (gv, p0, ln) in G.segs(srow, n):
                        c.dma("sp", t[p0:p0 + ln, NMETA:].rearrange("p (r t) -> p r t", r=4),
                              gv[:, :, 2:2 + OWN].rearrange("r p t -> p r t"), d, reads=[G.buf],
                              writes=[b] if first else [], pwrites=[] if first else [b])
                        first = False
                        c.dma("sp", t[p0:p0 + ln, :NMETA], gv[0, :, 4 + OWN:4 + OWN + NMETA], d, reads=[G.buf], pwrites=[b])
                    if g == 0:
                        c.op("dve", lambda: nc.vector.tensor_scalar(a[:n, :], t[:n, :], sl[:n, 0:1], None, ALU.mult),
                             reads=[b, sl_b], writes=[a_b])
                    else:
                        c.op("dve", lambda: nc.vector.scalar_tensor_tensor(a[:n, :], t[:n, :], sl[:n, g:g + 1], a[:n, :], ALU.mult, ALU.add),
                             reads=[b, sl_b, a_b], writes=[a_b])
                c.dma("sp", dst[d0 + r0:d0 + r0 + n, :], a[:n, :], a_d, reads=[a_b], pwrites=[dst_b])
    on = es.enter_context(nc.sbuf_tensor(c.nm("ones_row"), [1, NT], F32)); on_b = Buf()
    c.op("dve", lambda: nc.vector.memset(on[:], 1.0), writes=[on_b])
    c.dma("sp", f32[F_GZ + 16:F_GZ + 17, :], on[:], c.dsem("onr"), reads=[on_b], pwrites=[dst_b])
    candt = Ring(c, "candt", 4, [128, NV], BF16)
    acct = Ring(c, "acct", 2, [128, NV], BF16)
    candk = Ring(c, "candk", 4, [128, 128], F32)
    acck = Ring(c, "acck", 2, [128, 128], F32)
    chunks = [(0, 4 + OWN, NMETA, 0)] + [(r, 2 + 128 * i, 128, NMETA + OWN * r + 128 * i) for r in range(4) for i in range(8)]
    for (r, srow, n, drow) in chunks:
        a, a_b, a_d = acct.next()
        k, k_b, k_d = acck.next()
        for g in range(4):
            t, b, d = candt.next()
            first = True
            for (gv, p0, ln) in GTB.segs(srow, n):
                for (dc, sc, w) in ((V_VM, CB_VM + 384 * g, 384), (V_FV, CB_FV + 384 * g, 384), (V_GV, CB_GV + 256 * g, 256)):
                    c.dma("sp", t[p0:p0 + ln, dc:dc + w], gv[r, :, sc:sc + w], d, reads=[GTB.buf],
                          writes=[b] if first else [], pwrites=[] if first else [b])
                    first = False
            t2, b2, d2 = candk.next()
            first = True
            for (gv, p0, ln) in GTF.segs(srow, n):
                c.dma("sp", t2[p0:p0 + ln, :], gv[r, :, 128 * g:128 * (g + 1)], d2, reads=[GTF.buf],
                      writes=[b2] if first else [], pwrites=[] if first else [b2])
                first = False
            if g == 0:
                c.op("dve", lambda: nc.vector.tensor_scalar(a[:n, :], t[:n, :], sl[:n, 0:1], None, ALU.mult), reads=[b, sl_b], writes=[a_b])
                c.op("dve", lambda: nc.vector.tensor_scalar(k[:n, :], t2[:n, :], sl[:n, 0:1], None, ALU.mult), reads=[b2, sl_b], writes=[k_b])
            else:
                c.op("dve", lambda: nc.vector.scalar_tensor_tensor(a[:n, :], t[:n, :], sl[:n, g:g + 1], a[:n, :], ALU.mult, ALU.add),
                     reads=[b, sl_b, a_b], writes=[a_b])
                c.op("dve", lambda: nc.vector.scalar_tensor_tensor(k[:n, :], t2[:n, :], sl[:n, g:g + 1], k[:n, :], ALU.mult, ALU.add),
                     reads=[b2, sl_b, k_b], writes=[k_b])
        c.dma("sp", vbf[drow:drow + n, :], a[:n, :], a_d, reads=[a_b], pwrites=[dst_b])
        c.dma("sp", gkm[drow:drow + n, :], k[:n, :], k_d, reads=[k_b], pwrites=[dst_b])


def _omix_src(kc):
    if kc < 12:
        return kc // 3, (kc % 3) * 128
    if kc < 20:
        return (kc - 12) // 2, 384 + ((kc - 12) % 2) * 128
    return (kc - 20) // 3, 640 + ((kc - 20) % 3) * 128


def emit_select2(c, sel, GO, oT3, dst_b):
    nc, es = c.nc, c.es
    K = _consts(c, [("sel", sel, [128, 4])])
    sl, sl_b = K["sel"]
    cand = Ring(c, "cand2", 4, [128, 2 + OWN], BF16)
    accr = Ring(c, "acc2", 2, [128, TL], BF16)
    for i in range(2):
        c.op("dve", lambda: nc.vector.memset(accr.t[i][:], 0.0), writes=[accr.b[i]])
    for kc in range(32):
        rk, row0 = _omix_src(kc)
        a, a_b, a_d = accr.next()
        segs = GO.segs(row0, 128)
        for (gv, p0, ln) in segs:
            c.dma("sp", a[p0:p0 + ln, 4 + OWN:], gv[rk, :, 0:NMETA], a_d, reads=[GO.buf], pwrites=[a_b])
        for dd in range(4):
            t, b, d = cand.next()
            s0 = NMETA + OWN * dd - 2
            first = True
            for (gv, p0, ln) in segs:
                c.dma("sp", t[p0:p0 + ln, :], gv[rk, :, s0:s0 + 2 + OWN], d, reads=[GO.buf],
                      writes=[b] if first else [], pwrites=[] if first else [b])
                first = False
            if dd == 0:
                c.op("dve", lambda: nc.vector.tensor_scalar(a[:, :2 + OWN], t[:], sl[:, 0:1], None, ALU.mult), reads=[b, sl_b], pwrites=[a_b])
            else:
                c.op("dve", lambda: nc.vector.scalar_tensor_tensor(a[:, :2 + OWN], t[:], sl[:, dd:dd + 1], a[:, :2 + OWN], ALU.mult, ALU.add),
                     reads=[b, sl_b, a_b], pwrites=[a_b])
        c.dma("sp", oT3[:, kc, :], a[:], a_d, reads=[a_b], pwrites=[dst_b])


def emit_halo(c, selh, hbuf, h_b, tail, g_tail, cs):
    nc, es = c.nc, c.es
    K = _consts(c, [("selh", selh, [128, 5])])
    sh, sh_b = K["selh"]
    tail_b, gt_b = Buf(), Buf()
    d = c.dsem("halo")
    c.dma("sp", tail.rearrange("p (k t) -> p k t", t=2), hbuf[:, :, OWN:OWN + 2], d, reads=[h_b], writes=[tail_b])
    c.allgather(g_tail, tail, cs, reads=[tail_b], writes=[gt_b])
    cnd = es.enter_context(nc.sbuf_tensor(c.nm("hcand"), [128, 5, 64], F32)); cnd_b = Buf()
    d2 = c.dsem("halo2")
    c.dma("sp", cnd[:, 0:4, :], g_tail.rearrange("(r p) n -> p r n", r=4), d2, reads=[gt_b], writes=[cnd_b])
    c.dma("sp", cnd[:, 4, :].rearrange("p (k t) -> p k t", t=2), hbuf[:, :, TL - 2:TL], d2, reads=[h_b], pwrites=[cnd_b])
    Ctx.seal(d2, [cnd_b])
    acc = es.enter_context(nc.sbuf_tensor(c.nm("hacc"), [128, 64], F32)); acc_b = Buf()
    zz = es.enter_context(nc.sbuf_tensor(c.nm("hzero"), [128, 64], F32)); zz_b = Buf()
    c.op("dve", lambda: nc.vector.memset(zz[:], 0.0), writes=[zz_b])
    c.op("dve", lambda: nc.vector.tensor_scalar(acc[:], cnd[:, 0, :], sh[:, 0:1], None, ALU.mult), reads=[cnd_b, sh_b], writes=[acc_b])
    for i in range(1, 5):
        c.op("dve", lambda: nc.vector.scalar_tensor_tensor(acc[:], cnd[:, i, :], sh[:, i:i + 1], acc[:], ALU.mult, ALU.add),
             reads=[cnd_b, sh_b, acc_b], writes=[acc_b])
    d3 = c.dsem("halo3")
    c.dma("sp", hbuf[:, :, 0:2], acc[:].rearrange("p (k t) -> p k t", t=2), d3, reads=[acc_b, tail_b, cnd_b], pwrites=[h_b])
    c.dma("sp", hbuf[:, :, 2 + OWN:4 + OWN], zz[:].rearrange("p (k t) -> p k t", t=2), d3, reads=[zz_b], pwrites=[h_b])


def build_fused():
    nc = bass.Bass("TRN2", target_bir_lowering=False)
    dt = lambda n, s, d=F32, k="ExternalInput": nc.dram_tensor(n, s, d, kind=k).ap()
    I = lambda n, s, d=F32: nc.dram_tensor(n, s, d, kind="Internal").ap()
    hT0 = dt("hT0", [128, 32, TL])
    cos4, sin4 = dt("cos4", [128, TL]), dt("sin4", [128, TL])
    sel, selh = dt("sel", [128, 4]), dt("selh", [128, 5])
    w_in = dt("w_in", [2, D, DIN]); w_uq = dt("w_uq", [2, 1536, 2304]); w_ukv = dt("w_ukv", [2, 512, 3072])
    w_out = dt("w_out", [2, D, D]); w_up = dt("w_up", [2, D, 2 * DFF]); w_down = dt("w_down", [2, DFF, D])
    g_attn, g_q, g_kv = dt("g_attn", [2, 128, 32]), dt("g_q", [2, 128, 12]), dt("g_kv", [2, 128, 4])
    g_ffn, g_next = dt("g_ffn", [2, 128, 32]), dt("g_next", [2, 128, 32])
    conv_w, conv_b = dt("conv_w", [2, 128, 172, 3]), dt("conv_b", [2, 128, 172])
    wg2, fbf = dt("wg2", [2, 17, 128]), dt("fbf", [2, 3, 1])
    gmla, gfox, ggla = dt("gmla", [2, 128, 3]), dt("gfox", [2, 128, 3]), dt("ggla", [2, 128, 2])
    mask4 = dt("mask4", [128, 4, 512], BF16)
    tri01, tris64, tris16, sus = dt("tri01", [64, 64]), dt("tris64", [64, 65]), dt("tris16", [16, 17]), dt("sus", [64, 64])
    y_out = dt("y_out", [128, 32, TL], F32, "ExternalOutput")
    o32, obf, otf, otb = I("o32", [N32, TL]), I("obf", [NB, TL], BF16), I("otf", [TL, 512]), I("otb", [TL, NCB], BF16)
    G32, GBF = Gath(nc, "o32", N32, TL, F32, 4), Gath(nc, "obf", NB, TL, BF16, 2)
    GTF, GTB = Gath(nc, "otf", TL, 512, F32, 4), Gath(nc, "otb", TL, NCB, BF16, 2)
    GO = Gath(nc, "oT2", 1024, NT, BF16, 2)
    abf, vbf, f32, gkm = I("abf", [NA, NT], BF16), I("vbf", [NT, NV], BF16), I("f32s", [NF, NT]), I("gkm", [NT, 128])
    aug = I("aug_scr", [3, 12, NT], BF16)
    oT2, oT3 = I("oT2", [1024, NT], BF16), I("oT3", [128, 32, TL], BF16)
    h_mid, act = I("h_mid", [128, 32, TL]), I("act_scr", [86, 128, TL], BF16)
    hA, hB = I("hA", [128, 32, TL]), I("hB", [128, 32, TL])
    tail, g_tail = I("tail", [128, 64]), I("g_tail", [4 * 128, 64])
    with ExitStack() as es:
        c = Ctx(nc, es)
        cs = c.dsem("coll")
        c.phase_dsems.remove(cs)

        def phase(fn):
            with ExitStack() as pes:
                c.es = pes
                fn()
                c.es = c.sem_es
            c.end_phase()

        hcur = hT0
        for l in range(2):
            phase(lambda: emit_p1(c, hcur, w_in[l], w_uq[l], w_ukv[l], g_attn[l], g_q[l], g_kv[l], cos4, sin4, o32, obf, otf, otb))
            for (G_, x_) in ((G32, o32), (GBF, obf), (GTF, otf), (GTB, otb)):
                G_.gather(c, x_, cs)
            db = Buf()
            phase(lambda: emit_select1(c, sel, G32, GBF, GTF, GTB, abf, vbf, f32, gkm, db))
            phase(lambda: emit_p2(c, abf, vbf, f32, gkm, wg2[l], fbf[l], gmla[l], gfox[l], ggla[l], mask4, tri01, tris64, tris16, sus, oT2, aug))
            GO.gather(c, oT2, cs)
            db2 = Buf()
            phase(lambda: emit_select2(c, sel, GO, oT3, db2))
            hnext = y_out if False else (hA if l == 0 else hB)
            phase(lambda: emit_p3(c, oT3, hcur, w_out[l], w_up[l], w_down[l], g_ffn[l], g_next[l], conv_w[l], conv_b[l],
                                  hnext, y_out, h_mid, act))
            if l == 0:
                hb_ = Buf()
                phase(lambda: emit_halo(c, selh, hnext, hb_, tail, g_tail, cs))
            hcur = hnext
        c.barrier()
        print("fused program instructions:", c.n_inst)
    return nc


def kernel(x, meta_tokens, attn_norm, w_in, mla_q_norm, mla_w_uq, mla_kv_norm, mla_w_ukv,
           gla_w_gate2, gla_b_gate, fox_b_f, out_norm_mla, out_norm_gla, out_norm_fox,
           w_out, ffn_norm, ffn_w_up, ffn_conv_w, ffn_conv_b, ffn_w_down, final_norm):
    f32 = np.float32
    x = np.asarray(x, f32)
    B = x.shape[0]
    cores = list(range(8))
    h = np.concatenate([np.broadcast_to(np.asarray(meta_tokens, f32)[None], (B, NMETA, D)), x], axis=1)
    pos = np.arange(NT, dtype=f32)
    inv = (f32(1.0) / (f32(10000.0) ** (np.arange(0, 64, 2, dtype=f32) / f32(64)))).astype(f32)
    ang = (pos[:, None] * inv[None, :]).astype(f32)
    cosT, sinT = np.cos(ang).astype(f32).T, np.sin(ang).astype(f32).T
    zero_cols = np.array([2 + OWN, 3 + OWN])
    A = lambda v: np.asarray(v, f32)
    uq3 = A(mla_w_uq).reshape(2, 1536, 12, 192)
    w_uq_p = np.ascontiguousarray(np.concatenate(
        [uq3[..., :128].reshape(2, 1536, -1), uq3[..., 128:160].reshape(2, 1536, -1), uq3[..., 160:].reshape(2, 1536, -1)], 2))
    kv3 = A(mla_w_ukv).reshape(2, 512, 12, 256)
    w_ukv_p = np.ascontiguousarray(np.concatenate([kv3[..., :128].reshape(2, 512, -1), kv3[..., 128:].reshape(2, 512, -1)], 2))
    cw = A(ffn_conv_w)
    shared = dict(
        w_in=A(w_in), w_uq=w_uq_p, w_ukv=w_ukv_p, w_out=A(w_out), w_up=A(ffn_w_up), w_down=A(ffn_w_down),
        g_attn=np.stack([_pcol(attn_norm[l], 32) for l in range(2)]),
        g_q=np.stack([_pcol(mla_q_norm[l], 12) for l in range(2)]),
        g_kv=np.stack([_pcol(mla_kv_norm[l], 4) for l in range(2)]),
        g_ffn=np.stack([_pcol(ffn_norm[l], 32) for l in range(2)]),
        g_next=np.stack([_pcol(attn_norm[1], 32), _pcol(final_norm, 32)]),
        conv_w=np.stack([np.ascontiguousarray(cw[l].T.reshape(172, 128, 3).transpose(1, 0, 2)) for l in range(2)]),
        conv_b=np.stack([_pcol(ffn_conv_b[l], 172) for l in range(2)]),
        **_p2_consts())
    maps = []
    for core in cores:
        b, g = divmod(core, 4)
        cols = _core_cols(g)
        Hc = h[b][cols]
        Hc[zero_cols] = 0
        sel = np.zeros((128, 4), f32); sel[:, g] = 1
        selh = np.zeros((128, 5), f32); selh[:, 4 if g == 0 else g - 1] = 1
        m = dict(shared)
        m.update(
            hT0=_fmaj(Hc),
            cos4=np.ascontiguousarray(np.tile(cosT[:, cols], (4, 1))), sin4=np.ascontiguousarray(np.tile(sinT[:, cols], (4, 1))),
            sel=sel, selh=selh,
            wg2=np.stack([np.concatenate([A(gla_w_gate2[l])[:, g * 128:(g + 1) * 128], A(gla_b_gate[l])[None, g * 128:(g + 1) * 128]], 0)
                          for l in range(2)]),
            fbf=np.stack([A(fox_b_f[l])[3 * g:3 * g + 3, None] for l in range(2)]),
            gmla=np.stack([np.ascontiguousarray(A(out_norm_mla[l]).reshape(12, 128)[3 * g:3 * g + 3].T) for l in range(2)]),
            gfox=np.stack([np.ascontiguousarray(A(out_norm_fox[l]).reshape(12, 128)[3 * g:3 * g + 3].T) for l in range(2)]),
            ggla=np.stack([np.ascontiguousarray(A(out_norm_gla[l]).reshape(4, 2, 128)[g].T) for l in range(2)]))
        maps.append(m)
    res = run_bass_kernel_spmd(_prog("fused", build_fused), maps, core_ids=cores).results
    y = np.empty((B, SEQ, D), f32)
    for core in cores:
        b, g = divmod(core, 4)
        y[b, g * OWN:(g + 1) * OWN] = _unfm(res[core]["y_out"])[2:2 + OWN]
    return y
```

```python
import numpy as np
from contextlib import ExitStack
import ml_dtypes
import concourse.bass as bass
import concourse.mybir as mybir
from concourse.bass_utils import run_bass_kernel_spmd

F32 = mybir.dt.float32
BF16 = mybir.dt.bfloat16
AF = mybir.ActivationFunctionType
ALU = mybir.AluOpType
AX = mybir.AxisListType
NPBF = ml_dtypes.bfloat16

D = 4096
SEQ = 4096
NMETA = 16
OWN = 1024
TL = 2 + OWN + 2 + NMETA
TT = [(0, 512), (512, 512), (1024, TL - 1024)]
TM = [(i * 128, 128) for i in range(8)] + [(1024, TL - 1024)]
NT = NMETA + SEQ
EPS = 1e-6
DFF = 11008
GW = 256


def fm_groups(col0, nchunks, hf):
    per = GW // 128
    return [(col0 + g * GW, min(GW, (nchunks - g * per) * 128),
             [(j * 128, 128, hf(g * per + j)) for j in range(min(per, nchunks - g * per))])
            for g in range((nchunks + per - 1) // per)]


def tm_groups(col0, ncols, hf):
    return [(col0 + g * GW, min(GW, ncols - g * GW), hf(g * GW)) for g in range((ncols + GW - 1) // GW)]

O_CQ, O_CKV, O_KR, O_GQ, O_GK, O_GV, O_GZ, O_GR, O_FQ, O_FK, O_FV, O_FZ = (
    0, 1536, 2048, 2112, 2624, 3136, 4160, 4176, 5200, 6736, 8272, 9808)
DIN = 9820
R32_GQ, R32_GK, R32_GZ, R32_GR, R32_FZ, N32 = 0, 512, 1024, 1040, 2064, 2076
RB_Q, RB_KN, RB_KPE, RB_FQ, RB_FK, NB = 0, 2304, 3840, 3904, 5440, 6976
CB_GV, CB_FV, CB_VM, NCB = 0, 1024, 2560, 4096


class Buf:
    __slots__ = ("w", "r")

    def __init__(self):
        self.w = {}
        self.r = {}


def fresh(olds):
    b = Buf()
    for o in olds:
        for t in list(o.w.values()) + list(o.r.values()):
            Ctx._add(b.r, t)
    return b


class DSem:
    __slots__ = ("h", "v")

    def __init__(self, h):
        self.h = h
        self.v = 0


class Ctx:
    CE = ("pe", "act", "dve", "pool")

    def __init__(self, nc, es):
        self.nc = nc
        self.es = es
        self.sem_es = es
        self.eng = {"pe": nc.tensor, "act": nc.scalar, "dve": nc.vector,
                    "pool": nc.gpsimd, "sp": nc.sync}
        self.sem = {}
        self.cnt = {}
        self.nsem = 0
        self.latest = {}
        self.free_dsems = []
        self.phase_dsems = []
        self.phase = 0
        for e in self.CE:
            self._new_engine_sem(e)
        self.seen = {e: {} for e in self.eng}
        self.pe_pending = False
        self.n_inst = 0

    def _new_engine_sem(self, e):
        self.nsem += 1
        self.sem[e] = self.sem_es.enter_context(self.nc.semaphore(f"s_{e}_{self.nsem}"))
        self.cnt[e] = 0

    def nm(self, name):
        return f"{name}_p{self.phase}"

    def dsem(self, name):
        if self.free_dsems:
            d = self.free_dsems.pop()
        else:
            self.nsem += 1
            d = DSem(self.sem_es.enter_context(self.nc.semaphore(f"d_{name}_{self.nsem}")))
        self.phase_dsems.append(d)
        return d

    def barrier(self):
        assert not self.pe_pending
        for e in self.eng:
            own = id(self.sem[e]) if e in self.sem else None
            deps = {k: t for k, t in self.latest.items() if k != own}
            self._emit_waits(e, deps)

    def end_phase(self):
        self.barrier()
        self.free_dsems.extend(self.phase_dsems)
        self.phase_dsems = []
        self.phase += 1

    def allgather(self, out, in_, cs, reads=(), writes=()):
        deps = self._collect("pool", reads, writes)
        self._emit_waits("pool", deps)
        ins = self.nc.gpsimd.collective_compute("AllGather", ALU.bypass, replica_groups=[[0, 1, 2, 3], [4, 5, 6, 7]],
                                                ins=[in_], outs=[out])
        self.n_inst += 1
        cs.v += 1
        ins.then_inc(cs.h)
        t = (cs.h, cs.v)
        self.latest[id(cs.h)] = t
        self._record(t, reads, writes)
        return ins

    @staticmethod
    def _add(deps, t):
        k = id(t[0])
        if k not in deps or deps[k][1] < t[1]:
            deps[k] = t

    def _collect(self, e, reads, writes, pwrites=()):
        deps = {}
        own = id(self.sem[e]) if e in self.sem else None
        for b in pwrites:
            for t in b.r.values():
                self._add(deps, t)
        for b in reads:
            for t in b.w.values():
                if id(t[0]) == own and e == "pe":
                    continue
                self._add(deps, t)
        for b in writes:
            for t in b.w.values():
                if id(t[0]) == own:
                    continue
                self._add(deps, t)
            for t in b.r.values():
                if id(t[0]) == own:
                    continue
                self._add(deps, t)
        return deps

    def _emit_waits(self, e, deps):
        seen = self.seen[e]
        for k, (s, v) in deps.items():
            if seen.get(k, 0) >= v:
                continue
            self.eng[e].wait_ge(s, v)
            self.n_inst += 1
            seen[k] = v

    def _record(self, t, reads, writes, pwrites=()):
        k = id(t[0])
        for b in pwrites:
            if k not in b.w or b.w[k][1] < t[1]:
                b.w[k] = t
        for b in reads:
            if k not in b.r or b.r[k][1] < t[1]:
                b.r[k] = t
        for b in writes:
            b.w = {k: t}
            b.r = {}

    def op(self, e, fn, reads=(), writes=(), inc=True, pwrites=()):
        deps = self._collect(e, reads, writes, pwrites)
        self._emit_waits(e, deps)
        ins = fn()
        self.n_inst += 1
        if inc:
            if self.cnt[e] >= 30000 and not (e == "pe" and self.pe_pending):
                self._new_engine_sem(e)
            self.cnt[e] += 1
            ins.then_inc(self.sem[e], 1)
            t = (self.sem[e], self.cnt[e])
            self.latest[id(t[0])] = t
            if e == "pe":
                self.pe_pending = False
        else:
            assert e == "pe"
            t = (self.sem[e], self.cnt[e] + 1)
            self.pe_pending = True
        self._record(t, reads, writes, pwrites)
        return ins

    def dma(self, q, out, in_, ds, reads=(), writes=(), pwrites=(), **kw):
        deps = self._collect(q, reads, writes, pwrites)
        self._emit_waits(q, deps)
        ins = self.eng[q].dma_start(out=out, in_=in_, **kw)
        self.n_inst += 1
        ds.v += 16
        ins.then_inc(ds.h, 16)
        self.latest[id(ds.h)] = (ds.h, ds.v)
        self._record((ds.h, ds.v), reads, writes, pwrites)
        return ins

    @staticmethod
    def seal(ds, bufs):
        k = id(ds.h)
        for b in bufs:
            if k in b.w:
                b.w[k] = (ds.h, ds.v)

    def wait_all(self, e, bufs):
        deps = {}
        for b in bufs:
            for t in b.w.values():
                self._add(deps, t)
        self._emit_waits(e, deps)


class Ring:
    def __init__(self, c, name, n, shape, dtype, es=None):
        es = es or c.es
        self.t = [es.enter_context(c.nc.sbuf_tensor(c.nm(f"{name}{i}"), shape, dtype)) for i in range(n)]
        self.b = [Buf() for _ in range(n)]
        self.d = [c.dsem(f"{name}{i}") for i in range(n)]
        self.i = -1
        self.n = n

    def next(self):
        self.i = (self.i + 1) % self.n
        return self.t[self.i], self.b[self.i], self.d[self.i]


class Dense:
    def __init__(self, c, kcmax=32):
        nc, es = c.nc, c.es
        self.c = c
        self.ws = Ring(c, "ws", 2, [128, kcmax * GW], BF16)
        self.psA = [es.enter_context(nc.psum_tensor(c.nm(f"psA{i}"), [128, 512], F32)) for i in range(2)]
        self.psB = [es.enter_context(nc.psum_tensor(c.nm(f"psB{i}"), [128, 512], F32)) for i in range(2)]
        self.psC = es.enter_context(nc.psum_tensor(c.nm("psC"), [128, 512], F32))
        self.bA = [Buf(), Buf()]
        self.bB = [Buf(), Buf()]
        self.bC = Buf()
        self.psT = [es.enter_context(nc.psum_tensor(c.nm(f"psT{i}"), [128, 512], F32)) for i in range(2)]
        self.bT = [Buf(), Buf()]
        self.psS = es.enter_context(nc.psum_tensor(c.nm("psS"), [128, 512], F32))
        self.bS = Buf()
        self.it = 0
        self.itT = 0

    def load(self, W, KC, col0, ncols, row0=0):
        c = self.c
        t, b, d = self.ws.next()
        t = t[:, :KC * ncols].rearrange("p (k m) -> p k m", m=ncols)
        Wv = W[row0:row0 + KC * 128, col0:col0 + ncols].rearrange("(kc p) m -> p kc m", p=128)
        step = 8
        for i, k0 in enumerate(range(0, KC, step)):
            k1 = min(KC, k0 + step)
            c.dma("pool", t[:, k0:k1, :], Wv[:, k0:k1, :], d,
                  writes=[b] if i == 0 else [], pwrites=[b] if i else [])
        return t, b

    def fm(self, W, KC, act, act_b, groups, tts=TT, row0=0):
        c, nc = self.c, self.c.nc
        for (col0, ncols, chunks) in groups:
            wt, wb = self.load(W, KC, col0, ncols, row0)
            for (off, M, handler) in chunks:
                pb = self.it % 2
                self.it += 1
                pst = [(self.psA[pb], self.bA[pb]), (self.psB[pb], self.bB[pb]), (self.psC, self.bC)]
                for kc in range(KC):
                    for j, (t0, tn) in enumerate(tts):
                        last = kc == KC - 1
                        ps, pbuf = pst[j]
                        c.op("pe", lambda: nc.tensor.matmul(ps[:M, :tn], wt[:, kc, off:off + M], act[:, kc, t0:t0 + tn],
                                                            start=(kc == 0), stop=last),
                             reads=[wb, act_b[kc]], writes=[pbuf], inc=last)
                handler([(pst[j][0][:M, :tn], pst[j][1], t0, tn) for j, (t0, tn) in enumerate(tts)])

    def tm(self, W, KC, act, act_b, groups, tms=TM, row0=0):
        c, nc = self.c, self.c.nc
        for (col0, ncols, handler) in groups:
            wt, wb = self.load(W, KC, col0, ncols, row0)
            for ti, (t0, tn) in enumerate(tms):
                pb = self.itT % 2
                self.itT += 1
                ps, pbuf = self.psT[pb], self.bT[pb]
                for kc in range(KC):
                    last = kc == KC - 1
                    c.op("pe", lambda: nc.tensor.matmul(ps[:tn, :ncols], act[:, kc, t0:t0 + tn], wt[:, kc, :ncols],
                                                        start=(kc == 0), stop=last),
                         reads=[wb, act_b[kc]], writes=[pbuf], inc=last)
                handler(ps[:tn, :ncols], pbuf, ti, t0, tn)


def _consts(c, names_shapes, es=None, olds=()):
    out = {}
    es = es or c.es
    ds = c.dsem("consts")
    for name, ap, shape in names_shapes:
        t = es.enter_context(c.nc.sbuf_tensor(c.nm("k_" + name), shape, F32))
        b = fresh(olds)
        c.dma("sp", t[:], ap, ds, writes=[b])
        out[name] = (t, b)
    Ctx.seal(ds, [b for (_, b) in out.values()])
    return out


def col_stats(c, dn, acc, acc_b, rstd, rstd_b, n_feat, post_scale=1.0, tts=TT):
    nc = c.nc
    for (t0, tn) in tts:
        c.op("pe", lambda: nc.tensor.matmul(dn.psS[:, :tn], dn.ones[:], acc[:, t0:t0 + tn], start=True, stop=True),
             reads=[acc_b, dn.ones_b], writes=[dn.bS])
        c.op("dve", lambda: nc.vector.tensor_scalar(rstd[:, t0:t0 + tn], dn.psS[:, :tn], 1.0 / n_feat, EPS,
                                                    ALU.mult, ALU.add),
             reads=[dn.bS], pwrites=[rstd_b])
    c.op("act", lambda: nc.scalar.activation(rstd[:], rstd[:], AF.Sqrt, scale=float(1.0 / post_scale ** 2)),
         reads=[rstd_b], writes=[rstd_b])
    c.op("dve", lambda: nc.vector.reciprocal(rstd[:], rstd[:]), reads=[rstd_b], writes=[rstd_b])


def build_p1():
    nc = bass.Bass("TRN2", target_bir_lowering=False)
    dt = lambda n, s, d=F32, k="ExternalInput": nc.dram_tensor(n, s, d, kind=k).ap()
    hT = dt("hT", [128, 32, TL])
    w_in = dt("w_in", [D, DIN])
    w_uq = dt("w_uq", [1536, 2304])
    w_ukv = dt("w_ukv", [512, 3072])
    g_attn = dt("g_attn", [128, 32])
    g_q = dt("g_q", [128, 12])
    g_kv = dt("g_kv", [128, 4])
    cos4 = dt("cos4", [128, TL])
    sin4 = dt("sin4", [128, TL])
    o32 = dt("o32", [N32, TL], F32, "ExternalOutput")
    obf = dt("obf", [NB, TL], BF16, "ExternalOutput")
    otf = dt("otf", [TL, 512], F32, "ExternalOutput")
    otb = dt("otb", [TL, NCB], BF16, "ExternalOutput")
    with ExitStack() as es:
        c = Ctx(nc, es)
        emit_p1(c, hT, w_in, w_uq, w_ukv, g_attn, g_q, g_kv, cos4, sin4, o32, obf, otf, otb)
    return nc


def emit_p1(c, hT, w_in, w_uq, w_ukv, g_attn, g_q, g_kv, cos4, sin4, o32, obf, otf, otb):
    nc, es = c.nc, c.es
    sb = lambda n, s, d=F32, st=None: (st or es).enter_context(nc.sbuf_tensor(c.nm(n), s, d))
    dn = Dense(c)
    K = _consts(c, [("g_attn", g_attn, [128, 32]), ("g_q", g_q, [128, 12]), ("g_kv", g_kv, [128, 4])])
    dn.ones = sb("ones", [128, 128])
    dn.ones_b = Buf()
    c.op("dve", lambda: nc.vector.memset(dn.ones[:], 1.0), writes=[dn.ones_b])
    sq = Ring(c, "sq", 2, [128, TL], F32)
    st32 = Ring(c, "st32", 2, [128, TL], F32)
    stbf = Ring(c, "stbf", 3, [128, TL], BF16)
    ttf = Ring(c, "ttf", 2, [128, GW], F32)
    ttb = Ring(c, "ttb", 3, [128, GW], BF16)
    out_b = Buf()
    cqn = sb("cqn", [128, 12, TL], BF16); cqn_b = [Buf() for _ in range(12)]
    ckn = sb("ckn", [128, 4, TL], BF16); ckn_b = [Buf() for _ in range(4)]
    accq = sb("accq", [128, TL]); accq_b = Buf()
    acck = sb("acck", [128, TL]); acck_b = Buf()
    kr = [sb("kr1", [32, TL]), sb("kr2", [32, TL])]
    kr_b = [Buf(), Buf()]
    es_hn = ExitStack()
    hn = sb("hn", [128, 32, TL], BF16, es_hn)
    hn_b = [Buf() for _ in range(32)]
    with ExitStack() as es1:
        hp = Ring(c, "hp", 2, [128, 1, TL], F32, es1)
        acc = sb("acc", [128, TL], F32, es1); acc_b = Buf()
        rstd = sb("rstd", [128, TL], F32, es1); rstd_b = Buf()
        c.op("dve", lambda: nc.vector.memset(acc[:], 0.0), writes=[acc_b])
        for g in range(32):
            t, b, d = hp.next()
            c.dma("sp", t[:], hT[:, g:g + 1, :], d, writes=[b])
            for i in range(1):
                s, s_b, _ = sq.next()
                c.op("act", lambda: nc.scalar.activation(s[:], t[:, i, :], AF.Square), reads=[b], writes=[s_b])
                c.op("dve", lambda: nc.vector.tensor_tensor(acc[:], acc[:], s[:], ALU.add), reads=[acc_b, s_b], writes=[acc_b])
        col_stats(c, dn, acc, acc_b, rstd, rstd_b, D)
        ga, ga_b = K["g_attn"]
        for g in range(32):
            t, b, d = hp.next()
            c.dma("sp", t[:], hT[:, g:g + 1, :], d, writes=[b])
            for i in range(1):
                kc = g + i
                c.op("dve", lambda: nc.vector.scalar_tensor_tensor(hn[:, kc, :], t[:, i, :], ga[:, kc:kc + 1], rstd[:],
                                                                   ALU.mult, ALU.mult),
                     reads=[b, ga_b, rstd_b], writes=[hn_b[kc]])


    def out_fm(dst, row0, func=None, scale=1.0, dtype=F32):
        def h(tiles):
            t, b, d = (st32 if dtype == F32 else stbf).next()
            M = None
            for (ps, pb, t0, tn) in tiles:
                M = ps.shape[0]
                c.op("act", lambda: nc.scalar.activation(t[:M, t0:t0 + tn], ps, func or AF.Identity, scale=float(scale)),
                     reads=[pb], pwrites=[b])
            c.dma("sp", dst[row0:row0 + M, :], t[:M, :], d, reads=[b], pwrites=[out_b])
        return h

    c.op("dve", lambda: nc.vector.memset(accq[:], 0.0), writes=[accq_b])
    c.op("dve", lambda: nc.vector.memset(acck[:], 0.0), writes=[acck_b])

    def lat(dstt, dst_b, i, gain, gain_b, ac, ac_b):
        def h(tiles):
            s, s_b, _ = sq.next()
            for (ps, pb, t0, tn) in tiles:
                c.op("act", lambda: nc.scalar.activation(dstt[:, i, t0:t0 + tn], ps, AF.Identity, scale=gain[:, i:i + 1]),
                     reads=[pb, gain_b], pwrites=[dst_b[i]])
                c.op("act", lambda: nc.scalar.activation(s[:, t0:t0 + tn], ps, AF.Square), reads=[pb], pwrites=[s_b])
            c.op("dve", lambda: nc.vector.tensor_tensor(ac[:], ac[:], s[:], ALU.add), reads=[ac_b, s_b], writes=[ac_b])
        return h

    def krope(i):
        def h(tiles):
            for (ps, pb, t0, tn) in tiles:
                c.op("act", lambda: nc.scalar.copy(kr[i][:, t0:t0 + tn], ps), reads=[pb], pwrites=[kr_b[i]])
        return h

    gq, gq_b = K["g_q"]
    gk, gk_b = K["g_kv"]
    groups = []
    groups += fm_groups(O_CQ, 12, lambda i: lat(cqn, cqn_b, i, gq, gq_b, accq, accq_b))
    groups += fm_groups(O_CKV, 4, lambda i: lat(ckn, ckn_b, i, gk, gk_b, acck, acck_b))
    groups.append((O_KR, 64, [(0, 32, krope(0)), (32, 32, krope(1))]))
    groups += fm_groups(O_GQ, 4, lambda i: out_fm(o32, R32_GQ + i * 128, scale=128 ** -0.5))
    groups += fm_groups(O_GK, 4, lambda i: out_fm(o32, R32_GK + i * 128))
    groups.append((O_GZ, 16, [(0, 16, out_fm(o32, R32_GZ))]))
    groups += fm_groups(O_GR, 8, lambda i: out_fm(o32, R32_GR + i * 128, func=AF.Silu))
    groups += fm_groups(O_FQ, 12, lambda i: out_fm(obf, RB_FQ + i * 128, scale=128 ** -0.5, dtype=BF16))
    groups += fm_groups(O_FK, 12, lambda i: out_fm(obf, RB_FK + i * 128, dtype=BF16))
    groups.append((O_FZ, 12, [(0, 12, out_fm(o32, R32_FZ))]))
    dn.fm(w_in, 32, hn, hn_b, groups)


    def out_tm(dst, col0, ring):
        def h(ps, pb, ti, t0, tn):
            t, b, d = ring.next()
            n = ps.shape[1]
            c.op("act", lambda: nc.scalar.copy(t[:tn, :n], ps), reads=[pb], writes=[b])
            c.dma("sp", dst[t0:t0 + tn, col0:col0 + n], t[:tn, :n], d, reads=[b], pwrites=[out_b])
        return h

    tg = tm_groups(O_GK, 512, lambda o: out_tm(otf, o, ttf))
    tg += tm_groups(O_GV, 1024, lambda o: out_tm(otb, CB_GV + o, ttb))
    tg += tm_groups(O_FV, 1536, lambda o: out_tm(otb, CB_FV + o, ttb))
    dn.tm(w_in, 32, hn, hn_b, tg)

    es_hn.close()
    olds = hn_b + hp.b + [acc_b, rstd_b]
    K2 = _consts(c, [("cos4", cos4, [128, TL]), ("sin4", sin4, [128, TL])], olds=olds)
    cs, cs_b = K2["cos4"]
    sn, sn_b = K2["sin4"]
    tmp = [sb(f"rtmp{i}", [128, TL]) for i in range(4)]
    tmp_b = [fresh(olds) for _ in range(4)]

    def rope(x1, x1_b, x2, x2_b, P, dst_rows1, dst_rows2):
        c.op("dve", lambda: nc.vector.tensor_tensor(tmp[0][:P, :], x1, cs[:P, :], ALU.mult), reads=[x1_b, cs_b], writes=[tmp_b[0]])
        c.op("dve", lambda: nc.vector.tensor_tensor(tmp[1][:P, :], x2, sn[:P, :], ALU.mult), reads=[x2_b, sn_b], writes=[tmp_b[1]])
        c.op("dve", lambda: nc.vector.tensor_tensor(tmp[2][:P, :], x2, cs[:P, :], ALU.mult), reads=[x2_b, cs_b], writes=[tmp_b[2]])
        c.op("dve", lambda: nc.vector.tensor_tensor(tmp[3][:P, :], x1, sn[:P, :], ALU.mult), reads=[x1_b, sn_b], writes=[tmp_b[3]])
        t, b, d = stbf.next()
        c.op("dve", lambda: nc.vector.tensor_tensor(t[:P, :], tmp[0][:P, :], tmp[1][:P, :], ALU.subtract),
             reads=[tmp_b[0], tmp_b[1]], writes=[b])
        for (r0, p0, n) in dst_rows1:
            c.dma("sp", obf[r0:r0 + n, :], t[p0:p0 + n, :], d, reads=[b], pwrites=[out_b])
        t2, b2, d2 = stbf.next()
        c.op("dve", lambda: nc.vector.tensor_tensor(t2[:P, :], tmp[2][:P, :], tmp[3][:P, :], ALU.add),
             reads=[tmp_b[2], tmp_b[3]], writes=[b2])
        for (r0, p0, n) in dst_rows2:
            c.dma("sp", obf[r0:r0 + n, :], t2[p0:p0 + n, :], d2, reads=[b2], pwrites=[out_b])

    rope(kr[0][:], kr_b[0], kr[1][:], kr_b[1], 32, [(RB_KPE, 0, 32)], [(RB_KPE + 32, 0, 32)])

    rq = sb("rq", [128, TL]); rq_b = fresh(olds)
    rk = sb("rk", [128, TL]); rk_b = fresh(olds)
    col_stats(c, dn, accq, accq_b, rq, rq_b, 1536)
    col_stats(c, dn, acck, acck_b, rk, rk_b, 512)
    for i in range(12):
        c.op("dve", lambda: nc.vector.tensor_tensor(cqn[:, i, :], cqn[:, i, :], rq[:], ALU.mult),
             reads=[cqn_b[i], rq_b], writes=[cqn_b[i]])
    for i in range(4):
        c.op("dve", lambda: nc.vector.tensor_tensor(ckn[:, i, :], ckn[:, i, :], rk[:], ALU.mult),
             reads=[ckn_b[i], rk_b], writes=[ckn_b[i]])

    QS = 192 ** -0.5
    qpe = [sb(f"qpe{i}", [128, TL]) for i in range(2)]
    qpe_b = [fresh(olds), fresh(olds)]

    def qpe_h(i, j3):
        def h(tiles):
            for (ps, pb, t0, tn) in tiles:
                c.op("act", lambda: nc.scalar.activation(qpe[i][:, t0:t0 + tn], ps, AF.Identity, scale=QS), reads=[pb], pwrites=[qpe_b[i]])
            if i == 1:
                rows1 = [(RB_Q + (4 * j3 + hh) * 192 + 128, 32 * hh, 32) for hh in range(4)]
                rows2 = [(RB_Q + (4 * j3 + hh) * 192 + 160, 32 * hh, 32) for hh in range(4)]
                rope(qpe[0][:], qpe_b[0], qpe[1][:], qpe_b[1], 128, rows1, rows2)
        return h

    qg = fm_groups(0, 12, lambda i: out_fm(obf, RB_Q + i * 192, scale=QS, dtype=BF16))
    for j3 in range(3):
        qg.append((1536 + j3 * 128, 128, [(0, 128, qpe_h(0, j3))]))
        qg.append((1920 + j3 * 128, 128, [(0, 128, qpe_h(1, j3))]))
    dn.fm(w_uq, 12, cqn, cqn_b, qg)
    kg = fm_groups(0, 12, lambda i: out_fm(obf, RB_KN + i * 128, dtype=BF16))
    dn.fm(w_ukv, 4, ckn, ckn_b, kg)
    vg = tm_groups(1536, 1536, lambda o: out_tm(otb, CB_VM + o, ttb))
    dn.tm(w_ukv, 4, ckn, ckn_b, vg)
    c.wait_all("sp", [out_b])


TH = TL // 2
TTH = [(0, 512), (512, TH - 512)]


def build_p3():
    nc = bass.Bass("TRN2", target_bir_lowering=False)
    dt = lambda n, s, d=F32, k="ExternalInput": nc.dram_tensor(n, s, d, kind=k).ap()
    oT = dt("oT", [128, 32, TL], BF16)
    hT = dt("hT", [128, 32, TL])
    w_out = dt("w_out", [D, D])
    w_up = dt("w_up", [D, 2 * DFF])
    w_down = dt("w_down", [DFF, D])
    g_ffn = dt("g_ffn", [128, 32])
    g_next = dt("g_next", [128, 32])
    conv_w = dt("conv_w", [128, 172, 3])
    conv_b = dt("conv_b", [128, 172])
    h_out = dt("h_out", [128, 32, TL], F32, "ExternalOutput")
    y_out = dt("y_out", [128, 32, TL], F32, "ExternalOutput")
    h_mid = dt("h_mid", [128, 32, TL], F32, "Internal")
    act = dt("act_scr", [86, 128, TL], BF16, "Internal")
    with ExitStack() as es:
        c = Ctx(nc, es)
        emit_p3(c, oT, hT, w_out, w_up, w_down, g_ffn, g_next, conv_w, conv_b, h_out, y_out, h_mid, act)
    return nc


def emit_p3(c, oT, hT, w_out, w_up, w_down, g_ffn, g_next, conv_w, conv_b, h_out, y_out, h_mid, act):
    nc, es = c.nc, c.es
    sb = lambda n, s, d=F32, st=None: (st or es).enter_context(nc.sbuf_tensor(c.nm(n), s, d))
    dn = Dense(c)
    K = _consts(c, [("g_ffn", g_ffn, [128, 32]), ("g_next", g_next, [128, 32]),
                    ("cw", conv_w, [128, 172, 3]), ("cb", conv_b, [128, 172])])
    dn.ones = sb("ones", [128, 128]); dn.ones_b = Buf()
    c.op("dve", lambda: nc.vector.memset(dn.ones[:], 1.0), writes=[dn.ones_b])
    sq = Ring(c, "sq", 2, [128, TL], F32)
    hc = Ring(c, "hc", 2, [128, TL], F32)
    acc = sb("acc", [128, TL]); acc_b = Buf()
    rstd = sb("rstd", [128, TL]); rstd_b = Buf()
    hmid_b = [Buf() for _ in range(32)]
    act_b = [Buf() for _ in range(86)]
    hout_b = [[Buf(), Buf()] for _ in range(32)]
    y_b = Buf()
    gf, gf_b = K["g_ffn"]
    c.op("dve", lambda: nc.vector.memset(acc[:], 0.0), writes=[acc_b])

    es_hg = ExitStack()
    hg = sb("hg", [128, 32, TL], BF16, es_hg); hg_b = [Buf() for _ in range(32)]
    with ExitStack() as es1:
        osb = sb("osb", [128, 32, TL], BF16, es1); osb_b = [Buf() for _ in range(32)]
        d_o = c.dsem("oT")
        for g in range(8):
            c.dma("sp", osb[:, g * 4:(g + 1) * 4, :], oT[:, g * 4:(g + 1) * 4, :], d_o, writes=osb_b[g * 4:(g + 1) * 4])
        Ctx.seal(d_o, osb_b)

        def h3a(m):
            def h(tiles):
                t, b, d = hc.next()
                c.dma("sp", t[:], hT[:, m, :], d, writes=[b])
                for (ps, pb, t0, tn) in tiles:
                    c.op("dve", lambda: nc.vector.tensor_tensor(t[:, t0:t0 + tn], ps, t[:, t0:t0 + tn], ALU.add),
                         reads=[pb, b], pwrites=[b])
                c.dma("sp", h_mid[:, m, :], t[:], d, reads=[b], writes=[hmid_b[m]])
                s, s_b, _ = sq.next()
                c.op("act", lambda: nc.scalar.activation(s[:], t[:], AF.Square), reads=[b], writes=[s_b])
                c.op("dve", lambda: nc.vector.tensor_tensor(acc[:], acc[:], s[:], ALU.add), reads=[acc_b, s_b], writes=[acc_b])
                c.op("act", lambda: nc.scalar.activation(hg[:, m, :], t[:], AF.Identity, scale=gf[:, m:m + 1]),
                     reads=[b, gf_b], writes=[hg_b[m]])
            return h
        dn.fm(w_out, 32, osb, osb_b, fm_groups(0, 32, h3a))
    olds = list(osb_b)
    col_stats(c, dn, acc, acc_b, rstd, rstd_b, D)

    cw, cw_b = K["cw"]
    cb, cb_b = K["cb"]
    with ExitStack() as es2:
        ug = Ring(c, "ug", 2, [128, TL], F32, es2)
        uv = Ring(c, "uv", 2, [128, TL], F32, es2)
        cvg = Ring(c, "cvg", 2, [128, TL], F32, es2)
        cvv = Ring(c, "cvv", 2, [128, TL], F32, es2)
        ab = Ring(c, "ab", 2, [128, TL], BF16, es2)
        for r in (ug, uv, cvg, cvv, ab):
            r.b = [fresh(olds) for _ in r.b]
        for i in range(2):
            c.op("dve", lambda: nc.vector.memset(ab.t[i][:], 0.0), writes=[ab.b[i]])
        gate_cv = {}

        def conv(u, u_b, ch, ring):
            cv, cv_b, _ = ring.next()
            n = TL - 2
            c.op("act", lambda: nc.scalar.activation(cv[:, 2:], u[:, 2:], AF.Identity, bias=cb[:, ch:ch + 1], scale=cw[:, ch, 2:3]),
                 reads=[u_b, cb_b, cw_b], writes=[cv_b])
            c.op("dve", lambda: nc.vector.scalar_tensor_tensor(cv[:, 2:], u[:, 1:1 + n], cw[:, ch, 1:2], cv[:, 2:], ALU.mult, ALU.add),
                 reads=[u_b, cw_b, cv_b], writes=[cv_b])
            c.op("dve", lambda: nc.vector.scalar_tensor_tensor(cv[:, 2:], u[:, 0:n], cw[:, ch, 0:1], cv[:, 2:], ALU.mult, ALU.add),
                 reads=[u_b, cw_b, cv_b], writes=[cv_b])
            return cv, cv_b

        def hup(m, is_val):
            def h(tiles):
                u, u_b, _ = (uv if is_val else ug).next()
                for (ps, pb, t0, tn) in tiles:
                    c.op("dve", lambda: nc.vector.tensor_tensor(u[:, t0:t0 + tn], ps, rstd[:, t0:t0 + tn], ALU.mult),
                         reads=[pb, rstd_b], pwrites=[u_b])
                if not is_val:
                    cv, cv_b = conv(u, u_b, m, cvg)
                    c.op("act", lambda: nc.scalar.activation(cv[:, 2:], cv[:, 2:], AF.Silu), reads=[cv_b], writes=[cv_b])
                    gate_cv[m] = (cv, cv_b)
                else:
                    cv, cv_b = conv(u, u_b, 86 + m, cvv)
                    g, g_b = gate_cv.pop(m)
                    a, a_b, a_d = ab.next()
                    c.op("dve", lambda: nc.vector.tensor_tensor(a[:, 2:], g[:, 2:], cv[:, 2:], ALU.mult),
                         reads=[g_b, cv_b], pwrites=[a_b])
                    c.dma("sp", act[m], a[:], a_d, reads=[a_b], writes=[act_b[m]])
            return h

        groups = []
        for m0 in range(0, 86, 2):
            groups += [(m0 * 128, 256, [(0, 128, hup(m0, False)), (128, 128, hup(m0 + 1, False))]),
                       (DFF + m0 * 128, 256, [(0, 128, hup(m0, True)), (128, 128, hup(m0 + 1, True))])]
        dn.fm(w_up, 32, hg, hg_b, groups)
        olds = olds + hg_b + ug.b + uv.b + cvg.b + cvv.b + ab.b
    es_hg.close()

    accf = sb("accf", [128, TL]); accf_b = fresh(olds)
    c.op("dve", lambda: nc.vector.memset(accf[:], 0.0), writes=[accf_b])
    asb = sb("asb", [128, 86, TH], BF16); asb_b = [fresh(olds) for _ in range(86)]
    d_a = c.dsem("asb")
    actv = act.rearrange("m p t -> p m t")
    for half in range(2):
        c0 = half * TH
        for g in range(0, 86, 8):
            g1 = min(86, g + 8)
            c.dma("sp", asb[:, g:g1, :], actv[:, g:g1, c0:c0 + TH], d_a, reads=act_b[g:g1], writes=asb_b[g:g1])
        Ctx.seal(d_a, asb_b)
        for m in range(32):
            pb = dn.it % 2
            dn.it += 1
            pst = [(dn.psA[pb], dn.bA[pb]), (dn.psB[pb], dn.bB[pb])]
            for part in range(2):
                wt, wb = dn.load(w_down, 43, m * 128, 128, row0=part * 43 * 128)
                for k in range(43):
                    kc = part * 43 + k
                    last = kc == 85
                    for j, (t0, tn) in enumerate(TTH):
                        ps, pbuf = pst[j]
                        c.op("pe", lambda: nc.tensor.matmul(ps[:, :tn], wt[:, k, 0:128], asb[:, kc, t0:t0 + tn],
                                                            start=(kc == 0), stop=last),
                             reads=[wb, asb_b[kc]], writes=[pbuf], inc=(last or k == 42))
            t, b, d = hc.next()
            c.dma("sp", t[:, :TH], h_mid[:, m, c0:c0 + TH], d, reads=[hmid_b[m]], writes=[b])
            for j, (t0, tn) in enumerate(TTH):
                ps, pbuf = pst[j]
                c.op("dve", lambda: nc.vector.tensor_tensor(t[:, t0:t0 + tn], ps[:, :tn], t[:, t0:t0 + tn], ALU.add),
                     reads=[pbuf, b], pwrites=[b])
            c.dma("sp", h_out[:, m, c0:c0 + TH], t[:, :TH], d, reads=[b], writes=[hout_b[m][half]])
            s, s_b, _ = sq.next()
            c.op("act", lambda: nc.scalar.activation(s[:, :TH], t[:, :TH], AF.Square), reads=[b], writes=[s_b])
            c.op("dve", lambda: nc.vector.tensor_tensor(accf[:, c0:c0 + TH], accf[:, c0:c0 + TH], s[:, :TH], ALU.add),
                 reads=[accf_b, s_b], writes=[accf_b])
    col_stats(c, dn, accf, accf_b, rstd, rstd_b, D)
    gn, gn_b = K["g_next"]
    for m in range(32):
        t, b, d = hc.next()
        c.dma("sp", t[:], h_out[:, m, :], d, reads=hout_b[m], writes=[b])
        c.op("dve", lambda: nc.vector.scalar_tensor_tensor(t[:], t[:], gn[:, m:m + 1], rstd[:], ALU.mult, ALU.mult),
             reads=[b, gn_b, rstd_b], writes=[b])
        c.dma("sp", y_out[:, m, :], t[:], d, reads=[b], pwrites=[y_b])
    c.wait_all("sp", [y_b] + [x for hb in hout_b for x in hb])


QT = [(i * 512, 512) for i in range(8)] + [(4096, 16)]
NKC = 33
A_Q, A_KN, A_KPE, A_FQ, A_FK, NA = 0, 576, 960, 1024, 1408, 1792
V_VM, V_FV, V_GV, NV = 0, 384, 768, 1024
F_GQ, F_GK, F_GZ, F_GR, F_FZ, NF = 0, 128, 256, 273, 529, 532
GCH = [(0, 16)] + [(16 + 64 * i, 64) for i in range(64)]


def build_p2():
    nc = bass.Bass("TRN2", target_bir_lowering=False)
    dt = lambda n, s, d=F32, k="ExternalInput": nc.dram_tensor(n, s, d, kind=k).ap()
    a = dict(
        abf=dt("abf", [NA, NT], BF16), vbf=dt("vbf", [NT, NV], BF16), f32=dt("f32", [NF, NT]),
        gkm=dt("gkm", [NT, 128]), wg2=dt("wg2", [17, 128]), fbf=dt("fbf", [3, 1]),
        gmla=dt("gmla", [128, 3]), gfox=dt("gfox", [128, 3]), ggla=dt("ggla", [128, 2]),
        mask4=dt("mask4", [128, 4, 512], BF16), tri01=dt("tri01", [64, 64]),
        tris64=dt("tris64", [64, 65]), tris16=dt("tris16", [16, 17]), sus=dt("sus", [64, 64]),
        oT=dt("oT", [1024, NT], BF16, "ExternalOutput"),
        aug=dt("aug_scr", [3, 12, NT], BF16, "Internal"))
    with ExitStack() as es:
        c = Ctx(nc, es)
        emit_p2(c, **a)
    return nc


def emit_p2(c, abf, vbf, f32, gkm, wg2, fbf, gmla, gfox, ggla, mask4, tri01, tris64, tris16, sus, oT, aug):
    nc, es = c.nc, c.es
    sb = lambda n, s, d=F32, st=None: (st or es).enter_context(nc.sbuf_tensor(c.nm(n), s, d))
    K = _consts(c, [("wg2", wg2, [17, 128]), ("fbf", fbf, [3, 1]), ("gmla", gmla, [128, 3]), ("gfox", gfox, [128, 3]),
                    ("ggla", ggla, [128, 2]), ("tri01", tri01, [64, 64]), ("tris64", tris64, [64, 65]),
                    ("tris16", tris16, [16, 17]), ("sus", sus, [64, 64])])
    mk = sb("mk_sb", [128, 4, 512], BF16); mk_b = Buf()
    c.dma("sp", mk[:], mask4, c.dsem("mk"), writes=[mk_b])
    ones = sb("ones", [128, 512]); ones_b = Buf()
    c.op("dve", lambda: nc.vector.memset(ones[:], 1.0), writes=[ones_b])
    onesb = sb("onesb", [128, 128], BF16); onesb_b = Buf()
    c.op("dve", lambda: nc.vector.memset(onesb[:], 1.0), writes=[onesb_b])
    P = [es.enter_context(nc.psum_tensor(c.nm(f"pp{i}"), [128, 512], F32)) for i in range(7)]
    Pb = [Buf() for _ in range(7)]
    out_b = Buf()
    stb = Ring(c, "stb", 3, [128, 512], BF16)
    olds = []

    def head_norm_out(o_parts, gain, gain_b, gcol0, n, q0, row0, extra=None):
        nf = 128 * len(o_parts)
        for i, (o, o_b) in enumerate(o_parts):
            s, s_b, _ = sqr.next()
            c.op("act", lambda: nc.scalar.activation(s[:, :n], o, AF.Square), reads=[o_b], writes=[s_b])
            c.op("pe", lambda: nc.tensor.matmul(P[6][:, :n], ones[:, :128], s[:, :n], start=(i == 0), stop=(i == len(o_parts) - 1)),
                 reads=[s_b, ones_b], writes=[Pb[6]])
        r, r_b, _ = rsr.next()
        c.op("dve", lambda: nc.vector.tensor_scalar(r[:, :n], P[6][:, :n], 1.0 / nf, EPS, ALU.mult, ALU.add), reads=[Pb[6]], writes=[r_b])
        c.op("act", lambda: nc.scalar.activation(r[:, :n], r[:, :n], AF.Sqrt), reads=[r_b], writes=[r_b])
        c.op("dve", lambda: nc.vector.reciprocal(r[:, :n], r[:, :n]), reads=[r_b], writes=[r_b])
        for i, (o, o_b) in enumerate(o_parts):
            t, b, d = stb.next()
            if extra is None:
                c.op("dve", lambda: nc.vector.scalar_tensor_tensor(t[:, :n], o, gain[:, gcol0 + i:gcol0 + i + 1], r[:, :n], ALU.mult, ALU.mult),
                     reads=[o_b, gain_b, r_b], writes=[b])
            else:
                ex, ex_b = extra[i]
                c.op("dve", lambda: nc.vector.scalar_tensor_tensor(o, o, gain[:, gcol0 + i:gcol0 + i + 1], r[:, :n], ALU.mult, ALU.mult),
                     reads=[o_b, gain_b, r_b], writes=[o_b])
                c.op("dve", lambda: nc.vector.tensor_tensor(t[:, :n], o, ex, ALU.mult), reads=[o_b, ex_b], writes=[b])
            c.dma("sp", oT[row0 + i * 128:row0 + (i + 1) * 128, q0:q0 + n], t[:, :n], d, reads=[b], pwrites=[out_b])

    sqr = Ring(c, "sqr", 2, [128, 512], F32)
    rsr = Ring(c, "rsr", 2, [128, 512], F32)

    with ExitStack() as eg:
        wg, wg_b = K["wg2"]
        tri, tri_b = K["tri01"]
        ts64, ts64_b = K["tris64"]
        ts16, ts16_b = K["tris16"]
        su, su_b = K["sus"]
        gv = sb("gv", [64, 65, 256], BF16, eg); gv_b = Buf()
        d_gv = c.dsem("gv")
        c.dma("sp", gv[:16, 0, :], vbf[0:16, V_GV:V_GV + 256], d_gv, writes=[gv_b])
        for i in range(4):
            c.dma("sp", gv[:, 1 + 16 * i:17 + 16 * i, :],
                  vbf[16 + 1024 * i:16 + 1024 * (i + 1), V_GV:V_GV + 256].rearrange("(n p) d -> p n d", p=64), d_gv, pwrites=[gv_b])
        qdec = sb("qdec", [128, NT], BF16, eg); qdec_b = [Buf() for _ in range(65)]
        kst = sb("kst", [64, 65, 128], BF16, eg); kst_b = [Buf() for _ in range(65)]
        Aall = sb("Aall", [64, 65, 64], BF16, eg); A_b = [Buf() for _ in range(65)]
        dec = sb("dec", [128, 65], F32, eg); dec_b = [Buf() for _ in range(65)]
        oall = sb("oall_sb", [128, 2, NT], F32, eg); oall_b = [Buf() for _ in range(9)]
        S = sb("S", [128, 256], F32, eg); S_b = Buf()
        Sbf = sb("Sbf", [128, 256], BF16, eg); Sbf_b = Buf()
        gzr = Ring(c, "gzr", 2, [17, 512], F32, eg)
        gqr = Ring(c, "gqr", 2, [128, 512], F32, eg)
        gkr = Ring(c, "gkr", 2, [128, 512], F32, eg)
        gkmr = Ring(c, "gkmr", 2, [64, 8, 128], F32, eg)
        spr = Ring(c, "spr", 2, [64, 8, 128], F32, eg)
        e4 = Ring(c, "e4", 2, [128, 64], F32, eg)
        ek = Ring(c, "ek", 2, [64, 128], F32, eg)
        kdr = Ring(c, "kdr", 2, [128, 64], BF16, eg)
        supers = [(0, 16, [0])] + [(16 + 512 * j, 512, list(range(1 + 8 * j, 9 + 8 * j))) for j in range(8)]
        for (r0, rn, chunks) in supers:
            gz, gz_b, gz_d = gzr.next()
            c.dma("sp", gz[:, :rn], f32[F_GZ:F_GZ + 17, r0:r0 + rn], gz_d, writes=[gz_b])
            gq, gq_b, gq_d = gqr.next()
            c.dma("sp", gq[:, :rn], f32[F_GQ:F_GQ + 128, r0:r0 + rn], gq_d, writes=[gq_b])
            gk, gk_b, gk_d = gkr.next()
            c.dma("sp", gk[:, :rn], f32[F_GK:F_GK + 128, r0:r0 + rn], gk_d, writes=[gk_b])
            gm, gm_b, gm_d = gkmr.next()
            C = GCH[chunks[0]][1]
            nch = len(chunks)
            c.dma("sp", gm[:C, :nch, :], gkm[r0:r0 + rn, :].rearrange("(n p) d -> p n d", p=C), gm_d, writes=[gm_b])
            sp, sp_b, _ = spr.next()
            for g4 in range(0, nch, 4):
                n4 = min(4, nch - g4)
                for i in range(n4):
                    o0 = (g4 + i) * C
                    c.op("pe", lambda: nc.tensor.matmul(P[0][:C, i * 128:(i + 1) * 128], gz[:, o0:o0 + C], wg[:], start=True, stop=True),
                         reads=[gz_b, wg_b], writes=[Pb[0]])
                c.op("act", lambda: nc.scalar.activation(sp[:C, g4:g4 + n4, :], P[0][:C, :n4 * 128].rearrange("p (a b) -> p a b", b=128), AF.Exp, scale=-1.0),
                     reads=[Pb[0]], pwrites=[sp_b])
            c.op("act", lambda: nc.scalar.activation(sp[:C, :nch, :], sp[:C, :nch, :], AF.Ln, bias=1.0), reads=[sp_b], writes=[sp_b])
            for i, n in enumerate(chunks):
                s0, C = GCH[n]
                l0 = i * C
                tsx, tsx_b = (ts64, ts64_b) if C == 64 else (ts16, ts16_b)
                c.op("pe", lambda: nc.tensor.matmul(P[1][:, :C + 1], sp[:C, i, :], tsx[:C, :C + 1], start=True, stop=True),
                     reads=[sp_b, tsx_b], writes=[Pb[1]])
                c.op("pe", lambda: nc.tensor.matmul(P[2][:C, :128], su[:C, :C], sp[:C, i, :], start=True, stop=True),
                     reads=[sp_b, su_b], writes=[Pb[2]])
                eb, eb_b, _ = e4.next()
                c.op("act", lambda: nc.scalar.activation(eb[:, :C], P[1][:, :C], AF.Exp), reads=[Pb[1]], writes=[eb_b])
                c.op("dve", lambda: nc.vector.tensor_tensor(qdec[:, s0:s0 + C], gq[:, l0:l0 + C], eb[:, :C], ALU.mult),
                     reads=[gq_b, eb_b], writes=[qdec_b[n]])
                en, en_b, _ = e4.next()
                c.op("act", lambda: nc.scalar.activation(en[:, :C], P[1][:, :C], AF.Exp, scale=-1.0), reads=[Pb[1]], writes=[en_b])
                kd, kd_b, _ = kdr.next()
                c.op("dve", lambda: nc.vector.tensor_tensor(kd[:, :C], gk[:, l0:l0 + C], en[:, :C], ALU.mult),
                     reads=[gk_b, en_b], writes=[kd_b])
                c.op("act", lambda: nc.scalar.activation(dec[:, n:n + 1], P[1][:, C:C + 1], AF.Exp), reads=[Pb[1]], writes=[dec_b[n]])
                ekt, ek_b, _ = ek.next()
                c.op("act", lambda: nc.scalar.activation(ekt[:C, :], P[2][:C, :128], AF.Exp), reads=[Pb[2]], writes=[ek_b])
                c.op("dve", lambda: nc.vector.tensor_tensor(kst[:C, n, :], gm[:C, i, :], ekt[:C, :], ALU.mult),
                     reads=[gm_b, ek_b], writes=[kst_b[n]])
                c.op("pe", lambda: nc.tensor.matmul(P[3][:C, :C], kd[:, :C], qdec[:, s0:s0 + C], start=True, stop=True),
                     reads=[kd_b, qdec_b[n]], writes=[Pb[3]])
                c.op("dve", lambda: nc.vector.tensor_tensor(Aall[:C, n, :C], P[3][:C, :C], tri[:C, :C], ALU.mult),
                     reads=[Pb[3], tri_b], writes=[A_b[n]])
        c.op("dve", lambda: nc.vector.memset(S[:], 0.0), writes=[S_b])
        for n, (s0, C) in enumerate(GCH):
            po, po_b = P[4 + n % 2], Pb[4 + n % 2]
            for half in range(2):
                c.op("pe", lambda: nc.tensor.matmul(po[:, half * 64:half * 64 + C], gv[:C, n, half * 128:(half + 1) * 128], Aall[:C, n, :C],
                                                    start=True, stop=(n == 0)),
                     reads=[gv_b, A_b[n]], writes=[po_b])
                if n > 0:
                    c.op("pe", lambda: nc.tensor.matmul(po[:, half * 64:half * 64 + C], Sbf[:, half * 128:(half + 1) * 128], qdec[:, s0:s0 + C],
                                                        start=False, stop=True),
                         reads=[Sbf_b, qdec_b[n]], writes=[po_b])
            ti = 0 if n == 0 else 0 + (s0 // 512)
            for half in range(2):
                c.op("act", lambda: nc.scalar.copy(oall[:, half, s0:s0 + C], po[:, half * 64:half * 64 + C]),
                     reads=[po_b], pwrites=[oall_b[min(8, s0 // 512)], oall_b[min(8, (s0 + C - 1) // 512)]])
            c.op("pe", lambda: nc.tensor.matmul(P[0][:, :256], kst[:C, n, :], gv[:C, n, :], start=True, stop=True),
                 reads=[kst_b[n], gv_b], writes=[Pb[0]])
            c.op("dve", lambda: nc.vector.scalar_tensor_tensor(S[:], S[:], dec[:, n:n + 1], P[0][:, :256], ALU.mult, ALU.add),
                 reads=[S_b, dec_b[n], Pb[0]], writes=[S_b])
            c.op("act", lambda: nc.scalar.copy(Sbf[:], S[:]), reads=[S_b], writes=[Sbf_b])
        gg, gg_b = K["ggla"]
        grr = Ring(c, "grr", 2, [128, 2, 512], F32, eg)
        for ti, (q0, n) in enumerate(QT):
            gr, gr_b, gr_d = grr.next()
            c.dma("sp", gr[:, :, :n], f32[F_GR:F_GR + 256, q0:q0 + n].rearrange("(h p) t -> p h t", p=128), gr_d, writes=[gr_b])
            head_norm_out([(oall[:, hh, q0:q0 + n], oall_b[ti]) for hh in range(2)], gg, gg_b, 0, n, q0, 384,
                          extra=[(gr[:, hh, :n], gr_b) for hh in range(2)])
        olds = [gv_b, Sbf_b, S_b] + qdec_b + kst_b + A_b + dec_b + oall_b + gzr.b + gqr.b + gkr.b + gkmr.b + spr.b + e4.b + ek.b + kdr.b + grr.b

    with ExitStack() as ea:
        qn = Ring(c, "qn", 2, [128, NT], BF16, ea)
        qp = Ring(c, "qp", 2, [64, NT], BF16, ea)
        kn = Ring(c, "kn", 2, [128, NT], BF16, ea)
        vv = Ring(c, "vv", 2, [128, NKC, 128], BF16, ea)
        kpe = sb("kpe", [64, NT], BF16, ea); kpe_b = fresh(olds)
        pT = Ring(c, "pT", 3, [128, 512], BF16, ea)
        osb = Ring(c, "osb", 2, [128, 512], F32, ea)
        rl = Ring(c, "rl", 2, [128, 512], F32, ea)
        exr = Ring(c, "exr", 2, [128, 512], F32, ea)
        mx = sb("mx", [128, 4], F32, ea); mx_b = fresh(olds)
        negm = sb("negm", [128, 1], F32, ea); negm_b = fresh(olds)
        nfb = sb("nfb", [3, 1], F32, ea); nfb_b = fresh(olds)
        aq = Ring(c, "aq", 1, [6, NT], BF16, ea)
        ak = Ring(c, "ak", 1, [6, NT], BF16, ea)
        for r in (qn, qp, kn, vv, pT, osb, rl, exr, aq, ak):
            r.b = [fresh(olds) for _ in r.b]
        c.dma("sp", kpe[:], abf[A_KPE:A_KPE + 64, :], c.dsem("kpe"), writes=[kpe_b])
        aug_b = Buf()
        with ExitStack() as ef:
            fz = sb("fz", [3, NT], F32, ef); fz_b = fresh(olds)
            cs = sb("cs", [3, NT], F32, ef); cs_b = fresh(olds)
            spl = Ring(c, "spl", 2, [3, NT], BF16, ef)
            spl.b = [fresh(olds) for _ in spl.b]
            fb, fb_b = K["fbf"]
            c.dma("sp", fz[:], f32[F_FZ:F_FZ + 3, :], c.dsem("fz"), writes=[fz_b])
            t1, t1_b, t1_d = spl.next()
            c.op("dve", lambda: nc.vector.memset(t1[:], 1.0), writes=[t1_b])
            for r in range(3, 9):
                c.dma("sp", aug[:, r, :], t1[:], t1_d, reads=[t1_b], pwrites=[aug_b])
            c.op("dve", lambda: nc.vector.tensor_scalar(nfb[:], fb[:], -1.0, None, ALU.mult), reads=[fb_b], writes=[nfb_b])
            c.op("act", lambda: nc.scalar.activation(fz[:], fz[:], AF.Exp, bias=nfb[:], scale=-1.0), reads=[fz_b, nfb_b], writes=[fz_b])
            c.op("act", lambda: nc.scalar.activation(fz[:], fz[:], AF.Ln, bias=1.0), reads=[fz_b], writes=[fz_b])
            c.op("dve", lambda: nc.vector.tensor_scalar(fz[:], fz[:], -1.0, None, ALU.mult), reads=[fz_b], writes=[fz_b])
            for j, (q0, n) in enumerate(QT):
                init = 0.0 if j == 0 else cs[:, q0 - 1:q0]
                c.op("dve", lambda: nc.vector.tensor_tensor_scan(cs[:, q0:q0 + n], ones[:3, :n], fz[:, q0:q0 + n], init, ALU.mult, ALU.add),
                     reads=[ones_b, fz_b, cs_b], writes=[cs_b])
            for i in range(3):
                t1, t1_b, t1_d = spl.next()
                c.op("dve", lambda: nc.vector.tensor_copy(t1[:], cs[:]), reads=[cs_b], writes=[t1_b])
                c.dma("sp", aug[:, i, :], t1[:], t1_d, reads=[t1_b], pwrites=[aug_b])
                if i < 2:
                    c.op("dve", lambda: nc.vector.tensor_tensor(cs[:], cs[:], t1[:], ALU.subtract), reads=[cs_b, t1_b], writes=[cs_b])
                t2, t2_b, t2_d = spl.next()
                c.op("act", lambda: nc.scalar.mul(t2[:], t1[:], -1.0), reads=[t1_b], writes=[t2_b])
                c.dma("sp", aug[:, 9 + i, :], t2[:], t2_d, reads=[t2_b], pwrites=[aug_b])
            olds = olds + [fz_b, cs_b] + spl.b

        heads = [("mla", h) for h in range(3)] + [("fox", h) for h in range(3)]
        for (kind, h) in heads:
            q_t, q_b, q_d = qn.next()
            k_t, k_b, k_d = kn.next()
            v_t, v_b, v_d = vv.next()
            parts = []
            if kind == "mla":
                qrow, krow, vcol, orow = A_Q + h * 192, A_KN + h * 128, V_VM + h * 128, h * 128
                gain, gain_b = K["gmla"]
                p_t, p_b, p_d = qp.next()
                c.dma("sp", p_t[:], abf[qrow + 128:qrow + 192, :], p_d, writes=[p_b])
                parts = [(k_t, k_b, q_t, q_b, 128), (kpe, kpe_b, p_t, p_b, 64)]
            else:
                qrow, krow, vcol, orow = A_FQ + h * 128, A_FK + h * 128, V_FV + h * 128, 640 + h * 128
                gain, gain_b = K["gfox"]
                a_q, a_qb, a_qd = aq.next()
                a_k, a_kb, a_kd = ak.next()
                c.dma("sp", a_q[:], aug[h, 0:6, :], a_qd, reads=[aug_b], writes=[a_qb])
                c.dma("sp", a_k[:], aug[h, 6:12, :], a_kd, reads=[aug_b], writes=[a_kb])
                parts = [(k_t, k_b, q_t, q_b, 128), (a_k, a_kb, a_q, a_qb, 6)]
            c.dma("sp", q_t[:], abf[qrow:qrow + 128, :], q_d, writes=[q_b])
            c.dma("sp", k_t[:], abf[krow:krow + 128, :], k_d, writes=[k_b])
            for i in range(4):
                c.dma("sp", v_t[:, 8 * i:8 * i + 8, :], vbf[1024 * i:1024 * (i + 1), vcol:vcol + 128].rearrange("(n p) d -> p n d", p=128),
                      v_d, writes=[v_b] if i == 0 else [], pwrites=[v_b] if i else [])
            c.dma("sp", v_t[:16, 32, :], vbf[4096:4112, vcol:vcol + 128], v_d, pwrites=[v_b])
            c.op("dve", lambda: nc.vector.memset(mx[:], 0.0), writes=[mx_b])
            for side in range(2):
                plist = [(p[2], p[3], p[4]) if side == 0 else (p[0], p[1], p[4]) for p in parts if p[4] > 6]
                for (q0, n) in QT:
                    for i, (t_, b_, kp) in enumerate(plist):
                        s, s_b, _ = sqr.next()
                        c.op("act", lambda: nc.scalar.activation(s[:kp, :n], t_[:kp, q0:q0 + n], AF.Square), reads=[b_], writes=[s_b])
                        c.op("pe", lambda: nc.tensor.matmul(P[6][:, :n], ones[:kp, :128], s[:kp, :n], start=(i == 0), stop=(i == len(plist) - 1)),
                             reads=[s_b, ones_b], writes=[Pb[6]])
                    c.op("dve", lambda: nc.vector.reduce_max(mx[:, 2:3], P[6][:, :n], axis=AX.X), reads=[Pb[6]], writes=[mx_b])
                    c.op("dve", lambda: nc.vector.tensor_tensor(mx[:, side:side + 1], mx[:, side:side + 1], mx[:, 2:3], ALU.max),
                         reads=[mx_b], writes=[mx_b])
            c.op("dve", lambda: nc.vector.tensor_tensor(mx[:, 3:4], mx[:, 0:1], mx[:, 1:2], ALU.mult), reads=[mx_b], writes=[mx_b])
            c.op("act", lambda: nc.scalar.activation(negm[:], mx[:, 3:4], AF.Sqrt), reads=[mx_b], writes=[negm_b])
            c.op("dve", lambda: nc.vector.tensor_scalar(negm[:], negm[:], -1.0, None, ALU.mult), reads=[negm_b], writes=[negm_b])
            for ti, (q0, n) in enumerate(QT):
                last_c = min(4 * ti + 3, NKC - 1)
                po, po_b = P[2 + ti % 2], Pb[2 + ti % 2]
                pl, pl_b = P[4 + ti % 2], Pb[4 + ti % 2]
                pend = None

                def pv(kc_, kn2, p2, p2_b):
                    c.op("pe", lambda: nc.tensor.matmul(po[:, :n], v_t[:kn2, kc_, :], p2[:kn2, :n], start=(kc_ == 0), stop=(kc_ == last_c)),
                         reads=[v_b, p2_b], writes=[po_b], inc=(kc_ == last_c))
                    c.op("pe", lambda: nc.tensor.matmul(pl[:, :n], onesb[:kn2, :], p2[:kn2, :n], start=(kc_ == 0), stop=(kc_ == last_c)),
                         reads=[onesb_b, p2_b], writes=[pl_b], inc=True)

                for kc in range(last_c + 1):
                    k0 = kc * 128
                    kn_ = min(128, NT - k0)
                    ps, ps_b = P[kc % 2], Pb[kc % 2]
                    for i, (kt_, kb_, qt_, qb_, kp) in enumerate(parts):
                        c.op("pe", lambda: nc.tensor.matmul(ps[:kn_, :n], kt_[:kp, k0:k0 + kn_], qt_[:kp, q0:q0 + n],
                                                            start=(i == 0), stop=(i == len(parts) - 1)),
                             reads=[kb_, qb_], writes=[ps_b], inc=(i == len(parts) - 1))
                    if pend is not None:
                        pv(*pend)
                    p_, p_b2, _ = pT.next()
                    r = kc - 4 * ti
                    if r >= 0 and kind == "fox":
                        x_, x_b, _ = exr.next()
                        c.op("dve", lambda: nc.vector.tensor_scalar(x_[:kn_, :n], ps[:kn_, :n], negm[:kn_, :], 0.0, ALU.add, ALU.min),
                             reads=[ps_b, negm_b], writes=[x_b])
                        c.op("act", lambda: nc.scalar.activation(p_[:kn_, :n], x_[:kn_, :n], AF.Exp), reads=[x_b], writes=[p_b2])
                    else:
                        c.op("act", lambda: nc.scalar.activation(p_[:kn_, :n], ps[:kn_, :n], AF.Exp, bias=negm[:kn_, :]),
                             reads=[ps_b, negm_b], writes=[p_b2])
                    if r >= 0:
                        c.op("dve", lambda: nc.vector.tensor_tensor(p_[:kn_, :n], p_[:kn_, :n], mk[:kn_, r, :n], ALU.mult),
                             reads=[p_b2, mk_b], writes=[p_b2])
                    pend = (kc, kn_, p_, p_b2)
                pv(*pend)
                r_, r_b, _ = rl.next()
                c.op("dve", lambda: nc.vector.reciprocal(r_[:, :n], pl[:, :n]), reads=[pl_b], writes=[r_b])
                o_, o_b, _ = osb.next()
                c.op("dve", lambda: nc.vector.tensor_tensor(o_[:, :n], po[:, :n], r_[:, :n], ALU.mult), reads=[po_b, r_b], writes=[o_b])
                head_norm_out([(o_[:, :n], o_b)], gain, gain_b, h, n, q0, orow)
    c.wait_all("sp", [out_b])


_PROG = {}


def _prog(name, builder):
    if name not in _PROG:
        _PROG[name] = builder()
    return _PROG[name]


def _fmaj(a):
    T = a.shape[0]
    return np.ascontiguousarray(a.T.reshape(-1, 128, T).transpose(1, 0, 2))


def _unfm(a):
    return a.transpose(1, 0, 2).reshape(-1, a.shape[2]).T


def _pcol(v, n):
    return np.ascontiguousarray(np.asarray(v).reshape(n, 128).T)


def _p2_consts():
    kk = np.arange(128)[:, None]
    qq = np.arange(512)[None, :]
    mask4 = np.stack([(qq >= r * 128 + kk) for r in range(4)], 1).astype(NPBF)
    s = np.arange(64)[:, None]
    t = np.arange(64)[None, :]
    m16 = np.float32(-1.0 / 16.0)
    tri01 = (s <= t).astype(np.float32)
    tris64 = np.concatenate([(s <= t) * m16, np.full((64, 1), m16)], 1).astype(np.float32)
    tris16 = np.ascontiguousarray(np.concatenate([tris64[:16, :16], tris64[:16, 64:65]], 1))
    sus = ((s > t) * m16).astype(np.float32)
    return dict(mask4=mask4, tri01=tri01, tris64=tris64, tris16=tris16, sus=sus)


def _core_cols(g4):
    s = NMETA + g4 * OWN
    return np.concatenate([np.arange(s - 2, s + OWN), np.array([0, 0]), np.arange(0, NMETA)])


def kernel_unfused(x, meta_tokens, attn_norm, w_in, mla_q_norm, mla_w_uq, mla_kv_norm, mla_w_ukv,
           gla_w_gate2, gla_b_gate, fox_b_f, out_norm_mla, out_norm_gla, out_norm_fox,
           w_out, ffn_norm, ffn_w_up, ffn_conv_w, ffn_conv_b, ffn_w_down, final_norm):
    f32 = np.float32
    x = np.asarray(x, f32)
    B = x.shape[0]
    cores = list(range(8))
    h = np.concatenate([np.broadcast_to(np.asarray(meta_tokens, f32)[None], (B, NMETA, D)), x], axis=1)
    pos = np.arange(NT, dtype=f32)
    inv = (f32(1.0) / (f32(10000.0) ** (np.arange(0, 64, 2, dtype=f32) / f32(64)))).astype(f32)
    ang = (pos[:, None] * inv[None, :]).astype(f32)
    cosT, sinT = np.cos(ang).astype(f32).T, np.sin(ang).astype(f32).T
    zero_cols = np.array([2 + OWN, 3 + OWN])
    p2c = _p2_consts()
    p1, p2, p3 = _prog("p1", build_p1), _prog("p2", build_p2), _prog("p3", build_p3)
    y_final = None
    for l in range(2):
        uq3 = np.asarray(mla_w_uq[l]).reshape(1536, 12, 192)
        w_uq_p = np.ascontiguousarray(np.concatenate(
            [uq3[:, :, :128].reshape(1536, -1), uq3[:, :, 128:160].reshape(1536, -1), uq3[:, :, 160:].reshape(1536, -1)], 1))
        kv3 = np.asarray(mla_w_ukv[l]).reshape(512, 12, 256)
        w_ukv_p = np.ascontiguousarray(np.concatenate([kv3[:, :, :128].reshape(512, -1), kv3[:, :, 128:].reshape(512, -1)], 1))
        hTs = []
        maps = []
        for core in cores:
            b, g4 = divmod(core, 4)
            cols = _core_cols(g4)
            Hc = h[b][cols]
            Hc[zero_cols] = 0
            hT = _fmaj(Hc)
            hTs.append(hT)
            cs = cosT[:, cols].copy(); sn = sinT[:, cols].copy()
            maps.append(dict(hT=hT, w_in=np.asarray(w_in[l]), w_uq=w_uq_p, w_ukv=w_ukv_p,
                             g_attn=_pcol(attn_norm[l], 32), g_q=_pcol(mla_q_norm[l], 12), g_kv=_pcol(mla_kv_norm[l], 4),
                             cos4=np.ascontiguousarray(np.tile(cs, (4, 1))), sin4=np.ascontiguousarray(np.tile(sn, (4, 1)))))
        r1 = run_bass_kernel_spmd(p1, maps, core_ids=cores).results
        del maps
        maps = []
        for b in range(B):
            def gather_fm(name):
                parts = [r1[b * 4][name][:, 4 + OWN:4 + OWN + NMETA]] + [r1[b * 4 + g][name][:, 2:2 + OWN] for g in range(4)]
                return np.concatenate(parts, axis=1)

            def gather_tm(name):
                parts = [r1[b * 4][name][4 + OWN:4 + OWN + NMETA]] + [r1[b * 4 + g][name][2:2 + OWN] for g in range(4)]
                return np.concatenate(parts, axis=0)
            obf, o32, otf, otb = gather_fm("obf"), gather_fm("o32"), gather_tm("otf"), gather_tm("otb")
            for g in range(4):
                abf = np.concatenate([obf[RB_Q + 3 * g * 192:RB_Q + 3 * (g + 1) * 192],
                                      obf[RB_KN + 3 * g * 128:RB_KN + 3 * (g + 1) * 128],
                                      obf[RB_KPE:RB_KPE + 64],
                                      obf[RB_FQ + 3 * g * 128:RB_FQ + 3 * (g + 1) * 128],
                                      obf[RB_FK + 3 * g * 128:RB_FK + 3 * (g + 1) * 128]], 0)
                vbf = np.concatenate([otb[:, CB_VM + 3 * g * 128:CB_VM + 3 * (g + 1) * 128],
                                      otb[:, CB_FV + 3 * g * 128:CB_FV + 3 * (g + 1) * 128],
                                      otb[:, CB_GV + g * 256:CB_GV + (g + 1) * 256]], 1)
                ff = np.concatenate([o32[R32_GQ + g * 128:R32_GQ + (g + 1) * 128],
                                     o32[R32_GK + g * 128:R32_GK + (g + 1) * 128],
                                     o32[R32_GZ:R32_GZ + 16], np.ones((1, NT), f32),
                                     o32[R32_GR + g * 256:R32_GR + (g + 1) * 256],
                                     o32[R32_FZ + 3 * g:R32_FZ + 3 * (g + 1)]], 0)
                wg2 = np.concatenate([np.asarray(gla_w_gate2[l])[:, g * 128:(g + 1) * 128],
                                      np.asarray(gla_b_gate[l])[None, g * 128:(g + 1) * 128]], 0).astype(f32)
                maps.append(dict(
                    abf=np.ascontiguousarray(abf), vbf=np.ascontiguousarray(vbf), f32=np.ascontiguousarray(ff),
                    gkm=np.ascontiguousarray(otf[:, g * 128:(g + 1) * 128]), wg2=np.ascontiguousarray(wg2),
                    fbf=np.ascontiguousarray(np.asarray(fox_b_f[l], f32)[3 * g:3 * g + 3, None]),
                    gmla=np.ascontiguousarray(np.asarray(out_norm_mla[l], f32).reshape(12, 128)[3 * g:3 * g + 3].T),
                    gfox=np.ascontiguousarray(np.asarray(out_norm_fox[l], f32).reshape(12, 128)[3 * g:3 * g + 3].T),
                    ggla=np.ascontiguousarray(np.asarray(out_norm_gla[l], f32).reshape(4, 2, 128)[g].T),
                    **p2c))
        del r1
        r2 = run_bass_kernel_spmd(p2, maps, core_ids=cores).results
        del maps
        maps = []
        for b in range(B):
            om = np.empty((D, NT), NPBF)
            for g in range(4):
                o = r2[b * 4 + g]["oT"]
                om[3 * g * 128:3 * (g + 1) * 128] = o[0:384]
                om[1536 + g * 256:1536 + (g + 1) * 256] = o[384:640]
                om[2560 + 3 * g * 128:2560 + 3 * (g + 1) * 128] = o[640:1024]
            for g4 in range(4):
                cols = _core_cols(g4)
                oc = om[:, cols]
                oc[:, zero_cols] = 0
                oTc = np.ascontiguousarray(oc.reshape(32, 128, TL).transpose(1, 0, 2))
                gn = final_norm if l == 1 else attn_norm[1]
                cw = np.asarray(ffn_conv_w[l], f32)
                maps.append(dict(oT=oTc, hT=hTs[b * 4 + g4], w_out=np.asarray(w_out[l]), w_up=np.asarray(ffn_w_up[l]),
                                 w_down=np.asarray(ffn_w_down[l]), g_ffn=_pcol(ffn_norm[l], 32), g_next=_pcol(gn, 32),
                                 conv_w=np.ascontiguousarray(cw.T.reshape(172, 128, 3).transpose(1, 0, 2)),
                                 conv_b=_pcol(ffn_conv_b[l], 172)))
        del r2
        r3 = run_bass_kernel_spmd(p3, maps, core_ids=cores).results
        del maps
        for core in cores:
            b, g4 = divmod(core, 4)
            s = NMETA + g4 * OWN
            ho = _unfm(r3[core]["h_out"])
            h[b, s:s + OWN] = ho[2:2 + OWN]
            if g4 == 0:
                h[b, 0:NMETA] = ho[4 + OWN:4 + OWN + NMETA]
        if l == 1:
            y_final = np.empty((B, SEQ, D), f32)
            for core in cores:
                b, g4 = divmod(core, 4)
                y_final[b, g4 * OWN:(g4 + 1) * OWN] = _unfm(r3[core]["y_out"])[2:2 + OWN]
        del r3
    return y_final


class Gath:
    def __init__(self, nc, name, R, C, dtype, esz):
        rp = (1 << 20) // (C * esz)
        if rp >= 64:
            rp = (rp // 64) * 64
        self.C = C
        self.pieces = [(r0, min(rp, R - r0)) for r0 in range(0, R, rp)]
        self.g = [nc.dram_tensor(f"{name}_g{i}", [4 * n, C], dtype, kind="Internal").ap() for i, (r0, n) in enumerate(self.pieces)]
        self.buf = Buf()

    def gather(self, c, X, cs):
        for (r0, n), g in zip(self.pieces, self.g):
            c.allgather(g, X[r0:r0 + n, :], cs, writes=[self.buf])

    def segs(self, row0, n):
        out = []
        for (r0, pn), g in zip(self.pieces, self.g):
            a, b = max(row0, r0), min(row0 + n, r0 + pn)
            if a < b:
                out.append((g.rearrange("(r n) c -> r n c", r=4)[:, a - r0:b - r0, :], a - row0, b - a))
        return out


def emit_select1(c, sel, G32, GBF, GTF, GTB, abf, vbf, f32, gkm, dst_b):
    nc, es = c.nc, c.es
    K = _consts(c, [("sel", sel, [128, 4])])
    sl, sl_b = K["sel"]
    for (G, dst, dtype, jobs, tag) in (
            (GBF, abf, BF16, [(A_Q, RB_Q, 576, 576), (A_KN, RB_KN, 384, 384), (A_KPE, RB_KPE, 64, 0),
                              (A_FQ, RB_FQ, 384, 384), (A_FK, RB_FK, 384, 384)], "sb"),
            (G32, f32, F32, [(F_GQ, R32_GQ, 128, 128), (F_GK, R32_GK, 128, 128), (F_GZ, R32_GZ, 16, 0),
                             (F_GR, R32_GR, 256, 256), (F_FZ, R32_FZ, 3, 3)], "sf")):
        cand = Ring(c, "cand" + tag, 4, [128, NT], dtype)
        accr = Ring(c, "acc" + tag, 2, [128, NT], dtype)
        for (d0, s0, nrows, stride) in jobs:
            for r0 in range(0, nrows, 128):
                n = min(128, nrows - r0)
                a, a_b, a_d = accr.next()
                for g in range(4):
                    t, b, d = cand.next()
                    srow = s0 + g * stride + r0
                    first = True
                    for (gv, p0, ln) in G.segs(srow, n):
                        c.dma("sp", t[p0:p0 + ln, NMETA:].rearrange("p (r t) -> p r t", r=4),
                              gv[:, :, 2:2 + OWN].rearrange("r p t -> p r t"), d, reads=[G.buf],
                              writes=[b] if first else [], pwrites=[] if first else [b])
                        first = False
                        c.dma("sp", t[p0:p0 + ln, :NMETA], gv[0, :, 4 + OWN:4 + OWN + NMETA], d, reads=[G.buf], pwrites=[b])
                    if g == 0:
                        c.op("dve", lambda: nc.vector.tensor_scalar(a[:n, :], t[:n, :], sl[:n, 0:1], None, ALU.mult),
                             reads=[b, sl_b], writes=[a_b])
                    else:
                        c.op("dve", lambda: nc.vector.scalar_tensor_tensor(a[:n, :], t[:n, :], sl[:n, g:g + 1], a[:n, :], ALU.mult, ALU.add),
                             reads=[b, sl_b, a_b], writes=[a_b])
                c.dma("sp", dst[d0 + r0:d0 + r0 + n, :], a[:n, :], a_d, reads=[a_b], pwrites=[dst_b])
    on = es.enter_context(nc.sbuf_tensor(c.nm("ones_row"), [1, NT], F32)); on_b = Buf()
    c.op("dve", lambda: nc.vector.memset(on[:], 1.0), writes=[on_b])
    c.dma("sp", f32[F_GZ + 16:F_GZ + 17, :], on[:], c.dsem("onr"), reads=[on_b], pwrites=[dst_b])
    candt = Ring(c, "candt", 3, [128, NCB], BF16)
    acct = Ring(c, "acct", 2, [128, NV], BF16)
    candk = Ring(c, "candk", 3, [128, 512], F32)
    acck = Ring(c, "acck", 2, [128, 128], F32)
    chunks = [(0, 4 + OWN, NMETA, 0)] + [(r, 2 + 128 * i, 128, NMETA + OWN * r + 128 * i) for r in range(4) for i in range(8)]
    for (r, srow, n, drow) in chunks:
        a, a_b, a_d = acct.next()
        k, k_b, k_d = acck.next()
        t, b, d = candt.next()
        first = True
        for (gv, p0, ln) in GTB.segs(srow, n):
            c.dma("sp", t[p0:p0 + ln, :], gv[r, :, :], d, reads=[GTB.buf], writes=[b] if first else [], pwrites=[] if first else [b])
            first = False
        t2, b2, d2 = candk.next()
        first = True
        for (gv, p0, ln) in GTF.segs(srow, n):
            c.dma("sp", t2[p0:p0 + ln, :], gv[r, :, :], d2, reads=[GTF.buf], writes=[b2] if first else [], pwrites=[] if first else [b2])
            first = False
        for g in range(4):
            for (dc, sc, w) in ((V_VM, CB_VM + 384 * g, 384), (V_FV, CB_FV + 384 * g, 384), (V_GV, CB_GV + 256 * g, 256)):
                if g == 0:
                    c.op("dve", lambda: nc.vector.tensor_scalar(a[:n, dc:dc + w], t[:n, sc:sc + w], sl[:n, 0:1], None, ALU.mult),
                         reads=[b, sl_b], pwrites=[a_b])
                else:
                    c.op("dve", lambda: nc.vector.scalar_tensor_tensor(a[:n, dc:dc + w], t[:n, sc:sc + w], sl[:n, g:g + 1], a[:n, dc:dc + w], ALU.mult, ALU.add),
                         reads=[b, sl_b, a_b], pwrites=[a_b])
            if g == 0:
                c.op("dve", lambda: nc.vector.tensor_scalar(k[:n, :], t2[:n, 0:128], sl[:n, 0:1], None, ALU.mult), reads=[b2, sl_b], writes=[k_b])
            else:
                c.op("dve", lambda: nc.vector.scalar_tensor_tensor(k[:n, :], t2[:n, 128 * g:128 * (g + 1)], sl[:n, g:g + 1], k[:n, :], ALU.mult, ALU.add),
                     reads=[b2, sl_b, k_b], writes=[k_b])
        c.dma("sp", vbf[drow:drow + n, :], a[:n, :], a_d, reads=[a_b], pwrites=[dst_b])
        c.dma("sp", gkm[drow:drow + n, :], k[:n, :], k_d, reads=[k_b], pwrites=[dst_b])


def _omix_src(kc):
    if kc < 12:
        return kc // 3, (kc % 3) * 128
    if kc < 20:
        return (kc - 12) // 2, 384 + ((kc - 12) % 2) * 128
    return (kc - 20) // 3, 640 + ((kc - 20) % 3) * 128


def emit_select2(c, sel, GO, oT3, dst_b):
    nc, es = c.nc, c.es
    K = _consts(c, [("sel", sel, [128, 4])])
    sl, sl_b = K["sel"]
    cand = Ring(c, "cand2", 4, [128, 2 + OWN], BF16)
    accr = Ring(c, "acc2", 2, [128, TL], BF16)
    for i in range(2):
        c.op("dve", lambda: nc.vector.memset(accr.t[i][:], 0.0), writes=[accr.b[i]])
    for kc in range(32):
        rk, row0 = _omix_src(kc)
        a, a_b, a_d = accr.next()
        segs = GO.segs(row0, 128)
        for (gv, p0, ln) in segs:
            c.dma("sp", a[p0:p0 + ln, 4 + OWN:], gv[rk, :, 0:NMETA], a_d, reads=[GO.buf], pwrites=[a_b])
        for dd in range(4):
            t, b, d = cand.next()
            s0 = NMETA + OWN * dd - 2
            first = True
            for (gv, p0, ln) in segs:
                c.dma("sp", t[p0:p0 + ln, :], gv[rk, :, s0:s0 + 2 + OWN], d, reads=[GO.buf],
                      writes=[b] if first else [], pwrites=[] if first else [b])
                first = False
            if dd == 0:
                c.op("dve", lambda: nc.vector.tensor_scalar(a[:, :2 + OWN], t[:], sl[:, 0:1], None, ALU.mult), reads=[b, sl_b], pwrites=[a_b])
            else:
                c.op("dve", lambda: nc.vector.scalar_tensor_tensor(a[:, :2 + OWN], t[:], sl[:, dd:dd + 1], a[:, :2 + OWN], ALU.mult, ALU.add),
                     reads=[b, sl_b, a_b], pwrites=[a_b])
        c.dma("sp", oT3[:, kc, :], a[:], a_d, reads=[a_b], pwrites=[dst_b])


def emit_halo(c, selh, hbuf, h_b, tail, g_tail, cs):
    nc, es = c.nc, c.es
    K = _consts(c, [("selh", selh, [128, 5])])
    sh, sh_b = K["selh"]
    tail_b, gt_b = Buf(), Buf()
    d = c.dsem("halo")
    c.dma("sp", tail.rearrange("p (k t) -> p k t", t=2), hbuf[:, :, OWN:OWN + 2], d, reads=[h_b], writes=[tail_b])
    c.allgather(g_tail, tail, cs, reads=[tail_b], writes=[gt_b])
    cnd = es.enter_context(nc.sbuf_tensor(c.nm("hcand"), [128, 5, 64], F32)); cnd_b = Buf()
    d2 = c.dsem("halo2")
    c.dma("sp", cnd[:, 0:4, :], g_tail.rearrange("(r p) n -> p r n", r=4), d2, reads=[gt_b], writes=[cnd_b])
    c.dma("sp", cnd[:, 4, :].rearrange("p (k t) -> p k t", t=2), hbuf[:, :, TL - 2:TL], d2, reads=[h_b], pwrites=[cnd_b])
    Ctx.seal(d2, [cnd_b])
    acc = es.enter_context(nc.sbuf_tensor(c.nm("hacc"), [128, 64], F32)); acc_b = Buf()
    zz = es.enter_context(nc.sbuf_tensor(c.nm("hzero"), [128, 64], F32)); zz_b = Buf()
    c.op("dve", lambda: nc.vector.memset(zz[:], 0.0), writes=[zz_b])
    c.op("dve", lambda: nc.vector.tensor_scalar(acc[:], cnd[:, 0, :], sh[:, 0:1], None, ALU.mult), reads=[cnd_b, sh_b], writes=[acc_b])
    for i in range(1, 5):
        c.op("dve", lambda: nc.vector.scalar_tensor_tensor(acc[:], cnd[:, i, :], sh[:, i:i + 1], acc[:], ALU.mult, ALU.add),
             reads=[cnd_b, sh_b, acc_b], writes=[acc_b])
    d3 = c.dsem("halo3")
    c.dma("sp", hbuf[:, :, 0:2], acc[:].rearrange("p (k t) -> p k t", t=2), d3, reads=[acc_b, tail_b, cnd_b], pwrites=[h_b])
    c.dma("sp", hbuf[:, :, 2 + OWN:4 + OWN], zz[:].rearrange("p (k t) -> p k t", t=2), d3, reads=[zz_b], pwrites=[h_b])


def build_fused():
    nc = bass.Bass("TRN2", target_bir_lowering=False)
    dt = lambda n, s, d=F32, k="ExternalInput": nc.dram_tensor(n, s, d, kind=k).ap()
    I = lambda n, s, d=F32: nc.dram_tensor(n, s, d, kind="Internal").ap()
    hT0 = dt("hT0", [128, 32, TL])
    cos4, sin4 = dt("cos4", [128, TL]), dt("sin4", [128, TL])
    sel, selh = dt("sel", [128, 4]), dt("selh", [128, 5])
    w_in = dt("w_in", [2, D, DIN]); w_uq = dt("w_uq", [2, 1536, 2304]); w_ukv = dt("w_ukv", [2, 512, 3072])
    w_out = dt("w_out", [2, D, D]); w_up = dt("w_up", [2, D, 2 * DFF]); w_down = dt("w_down", [2, DFF, D])
    g_attn, g_q, g_kv = dt("g_attn", [2, 128, 32]), dt("g_q", [2, 128, 12]), dt("g_kv", [2, 128, 4])
    g_ffn, g_next = dt("g_ffn", [2, 128, 32]), dt("g_next", [2, 128, 32])
    conv_w, conv_b = dt("conv_w", [2, 128, 172, 3]), dt("conv_b", [2, 128, 172])
    wg2, fbf = dt("wg2", [2, 17, 128]), dt("fbf", [2, 3, 1])
    gmla, gfox, ggla = dt("gmla", [2, 128, 3]), dt("gfox", [2, 128, 3]), dt("ggla", [2, 128, 2])
    mask4 = dt("mask4", [128, 4, 512], BF16)
    tri01, tris64, tris16, sus = dt("tri01", [64, 64]), dt("tris64", [64, 65]), dt("tris16", [16, 17]), dt("sus", [64, 64])
    y_out = dt("y_out", [128, 32, TL], F32, "ExternalOutput")
    o32, obf, otf, otb = I("o32", [N32, TL]), I("obf", [NB, TL], BF16), I("otf", [TL, 512]), I("otb", [TL, NCB], BF16)
    G32, GBF = Gath(nc, "o32", N32, TL, F32, 4), Gath(nc, "obf", NB, TL, BF16, 2)
    GTF, GTB = Gath(nc, "otf", TL, 512, F32, 4), Gath(nc, "otb", TL, NCB, BF16, 2)
    GO = Gath(nc, "oT2", 1024, NT, BF16, 2)
    abf, vbf, f32, gkm = I("abf", [NA, NT], BF16), I("vbf", [NT, NV], BF16), I("f32s", [NF, NT]), I("gkm", [NT, 128])
    aug = I("aug_scr", [3, 12, NT], BF16)
    oT2, oT3 = I("oT2", [1024, NT], BF16), I("oT3", [128, 32, TL], BF16)
    h_mid, act = I("h_mid", [128, 32, TL]), I("act_scr", [86, 128, TL], BF16)
    hA, hB = I("hA", [128, 32, TL]), I("hB", [128, 32, TL])
    tail, g_tail = I("tail", [128, 64]), I("g_tail", [4 * 128, 64])
    with ExitStack() as es:
        c = Ctx(nc, es)
        cs = c.dsem("coll")
        c.phase_dsems.remove(cs)

        def phase(fn):
            with ExitStack() as pes:
                c.es = pes
                fn()
                c.es = c.sem_es
            c.end_phase()

        hcur = hT0
        for l in range(2):
            phase(lambda: emit_p1(c, hcur, w_in[l], w_uq[l], w_ukv[l], g_attn[l], g_q[l], g_kv[l], cos4, sin4, o32, obf, otf, otb))
            for (G_, x_) in ((G32, o32), (GBF, obf), (GTF, otf), (GTB, otb)):
                G_.gather(c, x_, cs)
            db = Buf()
            phase(lambda: emit_select1(c, sel, G32, GBF, GTF, GTB, abf, vbf, f32, gkm, db))
            phase(lambda: emit_p2(c, abf, vbf, f32, gkm, wg2[l], fbf[l], gmla[l], gfox[l], ggla[l], mask4, tri01, tris64, tris16, sus, oT2, aug))
            GO.gather(c, oT2, cs)
            db2 = Buf()
            phase(lambda: emit_select2(c, sel, GO, oT3, db2))
            hnext = y_out if False else (hA if l == 0 else hB)
            phase(lambda: emit_p3(c, oT3, hcur, w_out[l], w_up[l], w_down[l], g_ffn[l], g_next[l], conv_w[l], conv_b[l],
                                  hnext, y_out, h_mid, act))
            if l == 0:
                hb_ = Buf()
                phase(lambda: emit_halo(c, selh, hnext, hb_, tail, g_tail, cs))
            hcur = hnext
        c.barrier()
        print("fused program instructions:", c.n_inst)
    return nc


def kernel(x, meta_tokens, attn_norm, w_in, mla_q_norm, mla_w_uq, mla_kv_norm, mla_w_ukv,
           gla_w_gate2, gla_b_gate, fox_b_f, out_norm_mla, out_norm_gla, out_norm_fox,
           w_out, ffn_norm, ffn_w_up, ffn_conv_w, ffn_conv_b, ffn_w_down, final_norm):
    f32 = np.float32
    x = np.asarray(x, f32)
    B = x.shape[0]
    cores = list(range(8))
    h = np.concatenate([np.broadcast_to(np.asarray(meta_tokens, f32)[None], (B, NMETA, D)), x], axis=1)
    pos = np.arange(NT, dtype=f32)
    inv = (f32(1.0) / (f32(10000.0) ** (np.arange(0, 64, 2, dtype=f32) / f32(64)))).astype(f32)
    ang = (pos[:, None] * inv[None, :]).astype(f32)
    cosT, sinT = np.cos(ang).astype(f32).T, np.sin(ang).astype(f32).T
    zero_cols = np.array([2 + OWN, 3 + OWN])
    A = lambda v: np.asarray(v, f32)
    uq3 = A(mla_w_uq).reshape(2, 1536, 12, 192)
    w_uq_p = np.ascontiguousarray(np.concatenate(
        [uq3[..., :128].reshape(2, 1536, -1), uq3[..., 128:160].reshape(2, 1536, -1), uq3[..., 160:].reshape(2, 1536, -1)], 2))
    kv3 = A(mla_w_ukv).reshape(2, 512, 12, 256)
    w_ukv_p = np.ascontiguousarray(np.concatenate([kv3[..., :128].reshape(2, 512, -1), kv3[..., 128:].reshape(2, 512, -1)], 2))
    cw = A(ffn_conv_w)
    shared = dict(
        w_in=A(w_in), w_uq=w_uq_p, w_ukv=w_ukv_p, w_out=A(w_out), w_up=A(ffn_w_up), w_down=A(ffn_w_down),
        g_attn=np.stack([_pcol(attn_norm[l], 32) for l in range(2)]),
        g_q=np.stack([_pcol(mla_q_norm[l], 12) for l in range(2)]),
        g_kv=np.stack([_pcol(mla_kv_norm[l], 4) for l in range(2)]),
        g_ffn=np.stack([_pcol(ffn_norm[l], 32) for l in range(2)]),
        g_next=np.stack([_pcol(attn_norm[1], 32), _pcol(final_norm, 32)]),
        conv_w=np.stack([np.ascontiguousarray(cw[l].T.reshape(172, 128, 3).transpose(1, 0, 2)) for l in range(2)]),
        conv_b=np.stack([_pcol(ffn_conv_b[l], 172) for l in range(2)]),
        **_p2_consts())
    maps = []
    for core in cores:
        b, g = divmod(core, 4)
        cols = _core_cols(g)
        Hc = h[b][cols]
        Hc[zero_cols] = 0
        sel = np.zeros((128, 4), f32); sel[:, g] = 1
        selh = np.zeros((128, 5), f32); selh[:, 4 if g == 0 else g - 1] = 1
        m = dict(shared)
        m.update(
            hT0=_fmaj(Hc),
            cos4=np.ascontiguousarray(np.tile(cosT[:, cols], (4, 1))), sin4=np.ascontiguousarray(np.tile(sinT[:, cols], (4, 1))),
            sel=sel, selh=selh,
            wg2=np.stack([np.concatenate([A(gla_w_gate2[l])[:, g * 128:(g + 1) * 128], A(gla_b_gate[l])[None, g * 128:(g + 1) * 128]], 0)
                          for l in range(2)]),
            fbf=np.stack([A(fox_b_f[l])[3 * g:3 * g + 3, None] for l in range(2)]),
            gmla=np.stack([np.ascontiguousarray(A(out_norm_mla[l]).reshape(12, 128)[3 * g:3 * g + 3].T) for l in range(2)]),
            gfox=np.stack([np.ascontiguousarray(A(out_norm_fox[l]).reshape(12, 128)[3 * g:3 * g + 3].T) for l in range(2)]),
            ggla=np.stack([np.ascontiguousarray(A(out_norm_gla[l]).reshape(4, 2, 128)[g].T) for l in range(2)]))
        maps.append(m)
    res = run_bass_kernel_spmd(_prog("fused", build_fused), maps, core_ids=cores).results
    y = np.empty((B, SEQ, D), f32)
    for core in cores:
        b, g = divmod(core, 4)
        y[b, g * OWN:(g + 1) * OWN] = _unfm(res[core]["y_out"])[2:2 + OWN]
    return y
```

```python
import numpy as np
from contextlib import ExitStack
import ml_dtypes
import concourse.bass as bass
import concourse.mybir as mybir
from concourse.bass_utils import run_bass_kernel_spmd

F32 = mybir.dt.float32
BF16 = mybir.dt.bfloat16
AF = mybir.ActivationFunctionType
ALU = mybir.AluOpType
AX = mybir.AxisListType
NPBF = ml_dtypes.bfloat16

D = 4096
SEQ = 4096
NMETA = 16
OWN = 1024
TL = 2 + OWN + 2 + NMETA
TT = [(0, 512), (512, 512), (1024, TL - 1024)]
TM = [(i * 128, 128) for i in range(8)] + [(1024, TL - 1024)]
NT = NMETA + SEQ
EPS = 1e-6
DFF = 11008
GW = 256


def fm_groups(col0, nchunks, hf):
    per = GW // 128
    return [(col0 + g * GW, min(GW, (nchunks - g * per) * 128),
             [(j * 128, 128, hf(g * per + j)) for j in range(min(per, nchunks - g * per))])
            for g in range((nchunks + per - 1) // per)]


def tm_groups(col0, ncols, hf):
    return [(col0 + g * GW, min(GW, ncols - g * GW), hf(g * GW)) for g in range((ncols + GW - 1) // GW)]

O_CQ, O_CKV, O_KR, O_GQ, O_GK, O_GV, O_GZ, O_GR, O_FQ, O_FK, O_FV, O_FZ = (
    0, 1536, 2048, 2112, 2624, 3136, 4160, 4176, 5200, 6736, 8272, 9808)
DIN = 9820
R32_GQ, R32_GK, R32_GZ, R32_GR, R32_FZ, N32 = 0, 512, 1024, 1040, 2064, 2076
RB_Q, RB_KN, RB_KPE, RB_FQ, RB_FK, NB = 0, 2304, 3840, 3904, 5440, 6976
CB_GV, CB_FV, CB_VM, NCB = 0, 1024, 2560, 4096


class Buf:
    __slots__ = ("w", "r")

    def __init__(self):
        self.w = {}
        self.r = {}


def fresh(olds):
    b = Buf()
    for o in olds:
        for t in list(o.w.values()) + list(o.r.values()):
            Ctx._add(b.r, t)
    return b


class DSem:
    __slots__ = ("h", "v")

    def __init__(self, h):
        self.h = h
        self.v = 0


class Ctx:
    CE = ("pe", "act", "dve", "pool")

    def __init__(self, nc, es):
        self.nc = nc
        self.es = es
        self.sem_es = es
        self.eng = {"pe": nc.tensor, "act": nc.scalar, "dve": nc.vector,
                    "pool": nc.gpsimd, "sp": nc.sync}
        self.sem = {}
        self.cnt = {}
        self.nsem = 0
        self.latest = {}
        self.free_dsems = []
        self.phase_dsems = []
        self.phase = 0
        for e in self.CE:
            self._new_engine_sem(e)
        self.seen = {e: {} for e in self.eng}
        self.pe_pending = False
        self.n_inst = 0

    def _new_engine_sem(self, e):
        self.nsem += 1
        self.sem[e] = self.sem_es.enter_context(self.nc.semaphore(f"s_{e}_{self.nsem}"))
        self.cnt[e] = 0

    def nm(self, name):
        return f"{name}_p{self.phase}"

    def dsem(self, name):
        if self.free_dsems:
            d = self.free_dsems.pop()
        else:
            self.nsem += 1
            d = DSem(self.sem_es.enter_context(self.nc.semaphore(f"d_{name}_{self.nsem}")))
        self.phase_dsems.append(d)
        return d

    def barrier(self, exclude=()):
        assert not self.pe_pending
        for e in self.eng:
            own = id(self.sem[e]) if e in self.sem else None
            deps = {k: t for k, t in self.latest.items() if k != own and k not in exclude}
            self._emit_waits(e, deps)

    def end_phase(self, exclude=()):
        self.barrier(exclude)
        self.free_dsems.extend(self.phase_dsems)
        self.phase_dsems = []
        self.phase += 1

    def allgather(self, out, in_, cs, reads=(), writes=()):
        deps = self._collect("pool", reads, writes)
        self._emit_waits("pool", deps)
        ins = self.nc.gpsimd.collective_compute("AllGather", ALU.bypass, replica_groups=[[0, 1, 2, 3], [4, 5, 6, 7]],
                                                ins=[in_], outs=[out])
        self.n_inst += 1
        cs.v += 1
        ins.then_inc(cs.h)
        t = (cs.h, cs.v)
        self.latest[id(cs.h)] = t
        self._record(t, reads, writes)
        return ins

    @staticmethod
    def _add(deps, t):
        k = id(t[0])
        if k not in deps or deps[k][1] < t[1]:
            deps[k] = t

    def _collect(self, e, reads, writes, pwrites=()):
        deps = {}
        own = id(self.sem[e]) if e in self.sem else None
        for b in pwrites:
            for t in b.r.values():
                self._add(deps, t)
        for b in reads:
            for t in b.w.values():
                if id(t[0]) == own and e == "pe":
                    continue
                self._add(deps, t)
        for b in writes:
            for t in b.w.values():
                if id(t[0]) == own:
                    continue
                self._add(deps, t)
            for t in b.r.values():
                if id(t[0]) == own:
                    continue
                self._add(deps, t)
        return deps

    def _emit_waits(self, e, deps):
        seen = self.seen[e]
        for k, (s, v) in deps.items():
            if seen.get(k, 0) >= v:
                continue
            self.eng[e].wait_ge(s, v)
            self.n_inst += 1
            seen[k] = v

    def _record(self, t, reads, writes, pwrites=()):
        k = id(t[0])
        for b in pwrites:
            if k not in b.w or b.w[k][1] < t[1]:
                b.w[k] = t
        for b in reads:
            if k not in b.r or b.r[k][1] < t[1]:
                b.r[k] = t
        for b in writes:
            b.w = {k: t}
            b.r = {}

    def op(self, e, fn, reads=(), writes=(), inc=True, pwrites=()):
        deps = self._collect(e, reads, writes, pwrites)
        self._emit_waits(e, deps)
        ins = fn()
        self.n_inst += 1
        if inc:
            if self.cnt[e] >= 30000 and not (e == "pe" and self.pe_pending):
                self._new_engine_sem(e)
            self.cnt[e] += 1
            ins.then_inc(self.sem[e], 1)
            t = (self.sem[e], self.cnt[e])
            self.latest[id(t[0])] = t
            if e == "pe":
                self.pe_pending = False
        else:
            assert e == "pe"
            t = (self.sem[e], self.cnt[e] + 1)
            self.pe_pending = True
        self._record(t, reads, writes, pwrites)
        return ins

    def dma(self, q, out, in_, ds, reads=(), writes=(), pwrites=(), **kw):
        deps = self._collect(q, reads, writes, pwrites)
        self._emit_waits(q, deps)
        ins = self.eng[q].dma_start(out=out, in_=in_, **kw)
        self.n_inst += 1
        ds.v += 16
        ins.then_inc(ds.h, 16)
        self.latest[id(ds.h)] = (ds.h, ds.v)
        self._record((ds.h, ds.v), reads, writes, pwrites)
        return ins

    @staticmethod
    def seal(ds, bufs):
        k = id(ds.h)
        for b in bufs:
            if k in b.w:
                b.w[k] = (ds.h, ds.v)

    def wait_all(self, e, bufs):
        deps = {}
        for b in bufs:
            for t in b.w.values():
                self._add(deps, t)
        self._emit_waits(e, deps)


class Ring:
    def __init__(self, c, name, n, shape, dtype, es=None):
        es = es or c.es
        self.t = [es.enter_context(c.nc.sbuf_tensor(c.nm(f"{name}{i}"), shape, dtype)) for i in range(n)]
        self.b = [Buf() for _ in range(n)]
        self.d = [c.dsem(f"{name}{i}") for i in range(n)]
        self.i = -1
        self.n = n

    def next(self):
        self.i = (self.i + 1) % self.n
        return self.t[self.i], self.b[self.i], self.d[self.i]


class Dense:
    def __init__(self, c, kcmax=32):
        nc, es = c.nc, c.es
        self.c = c
        self.ws = Ring(c, "ws", 2, [128, kcmax * GW], BF16)
        self.psA = [es.enter_context(nc.psum_tensor(c.nm(f"psA{i}"), [128, 512], F32)) for i in range(2)]
        self.psB = [es.enter_context(nc.psum_tensor(c.nm(f"psB{i}"), [128, 512], F32)) for i in range(2)]
        self.psC = es.enter_context(nc.psum_tensor(c.nm("psC"), [128, 512], F32))
        self.bA = [Buf(), Buf()]
        self.bB = [Buf(), Buf()]
        self.bC = Buf()
        self.psT = [es.enter_context(nc.psum_tensor(c.nm(f"psT{i}"), [128, 512], F32)) for i in range(2)]
        self.bT = [Buf(), Buf()]
        self.psS = es.enter_context(nc.psum_tensor(c.nm("psS"), [128, 512], F32))
        self.bS = Buf()
        self.it = 0
        self.itT = 0

    def load(self, W, KC, col0, ncols, row0=0):
        c = self.c
        t, b, d = self.ws.next()
        t = t[:, :KC * ncols].rearrange("p (k m) -> p k m", m=ncols)
        Wv = W[row0:row0 + KC * 128, col0:col0 + ncols].rearrange("(kc p) m -> p kc m", p=128)
        step = 8
        for i, k0 in enumerate(range(0, KC, step)):
            k1 = min(KC, k0 + step)
            c.dma("pool", t[:, k0:k1, :], Wv[:, k0:k1, :], d,
                  writes=[b] if i == 0 else [], pwrites=[b] if i else [])
        return t, b

    def fm(self, W, KC, act, act_b, groups, tts=TT, row0=0):
        c, nc = self.c, self.c.nc
        for (col0, ncols, chunks) in groups:
            wt, wb = self.load(W, KC, col0, ncols, row0)
            for (off, M, handler) in chunks:
                pb = self.it % 2
                self.it += 1
                pst = [(self.psA[pb], self.bA[pb]), (self.psB[pb], self.bB[pb]), (self.psC, self.bC)]
                for kc in range(KC):
                    for j, (t0, tn) in enumerate(tts):
                        last = kc == KC - 1
                        ps, pbuf = pst[j]
                        c.op("pe", lambda: nc.tensor.matmul(ps[:M, :tn], wt[:, kc, off:off + M], act[:, kc, t0:t0 + tn],
                                                            start=(kc == 0), stop=last),
                             reads=[wb, act_b[kc]], writes=[pbuf], inc=last)
                handler([(pst[j][0][:M, :tn], pst[j][1], t0, tn) for j, (t0, tn) in enumerate(tts)])

    def tm(self, W, KC, act, act_b, groups, tms=TM, row0=0):
        c, nc = self.c, self.c.nc
        for (col0, ncols, handler) in groups:
            wt, wb = self.load(W, KC, col0, ncols, row0)
            for ti, (t0, tn) in enumerate(tms):
                pb = self.itT % 2
                self.itT += 1
                ps, pbuf = self.psT[pb], self.bT[pb]
                for kc in range(KC):
                    last = kc == KC - 1
                    c.op("pe", lambda: nc.tensor.matmul(ps[:tn, :ncols], act[:, kc, t0:t0 + tn], wt[:, kc, :ncols],
                                                        start=(kc == 0), stop=last),
                         reads=[wb, act_b[kc]], writes=[pbuf], inc=last)
                handler(ps[:tn, :ncols], pbuf, ti, t0, tn)


def _consts(c, names_shapes, es=None, olds=()):
    out = {}
    es = es or c.es
    ds = c.dsem("consts")
    for name, ap, shape in names_shapes:
        t = es.enter_context(c.nc.sbuf_tensor(c.nm("k_" + name), shape, F32))
        b = fresh(olds)
        c.dma("sp", t[:], ap, ds, writes=[b])
        out[name] = (t, b)
    Ctx.seal(ds, [b for (_, b) in out.values()])
    return out


def col_stats(c, dn, acc, acc_b, rstd, rstd_b, n_feat, post_scale=1.0, tts=TT):
    nc = c.nc
    for (t0, tn) in tts:
        c.op("pe", lambda: nc.tensor.matmul(dn.psS[:, :tn], dn.ones[:], acc[:, t0:t0 + tn], start=True, stop=True),
             reads=[acc_b, dn.ones_b], writes=[dn.bS])
        c.op("dve", lambda: nc.vector.tensor_scalar(rstd[:, t0:t0 + tn], dn.psS[:, :tn], 1.0 / n_feat, EPS,
                                                    ALU.mult, ALU.add),
             reads=[dn.bS], pwrites=[rstd_b])
    c.op("act", lambda: nc.scalar.activation(rstd[:], rstd[:], AF.Sqrt, scale=float(1.0 / post_scale ** 2)),
         reads=[rstd_b], writes=[rstd_b])
    c.op("dve", lambda: nc.vector.reciprocal(rstd[:], rstd[:]), reads=[rstd_b], writes=[rstd_b])


def build_p1():
    nc = bass.Bass("TRN2", target_bir_lowering=False)
    dt = lambda n, s, d=F32, k="ExternalInput": nc.dram_tensor(n, s, d, kind=k).ap()
    hT = dt("hT", [128, 32, TL])
    w_in = dt("w_in", [D, DIN])
    w_uq = dt("w_uq", [1536, 2304])
    w_ukv = dt("w_ukv", [512, 3072])
    g_attn = dt("g_attn", [128, 32])
    g_q = dt("g_q", [128, 12])
    g_kv = dt("g_kv", [128, 4])
    cos4 = dt("cos4", [128, TL])
    sin4 = dt("sin4", [128, TL])
    o32 = dt("o32", [N32, TL], F32, "ExternalOutput")
    obf = dt("obf", [NB, TL], BF16, "ExternalOutput")
    otf = dt("otf", [TL, 512], F32, "ExternalOutput")
    otb = dt("otb", [TL, NCB], BF16, "ExternalOutput")
    with ExitStack() as es:
        c = Ctx(nc, es)
        emit_p1(c, hT, w_in, w_uq, w_ukv, g_attn, g_q, g_kv, cos4, sin4, o32, obf, otf, otb)
    return nc


def emit_p1(c, hT, w_in, w_uq, w_ukv, g_attn, g_q, g_kv, cos4, sin4, o32, obf, otf, otb, otb_split=None):
    nc, es = c.nc, c.es
    sb = lambda n, s, d=F32, st=None: (st or es).enter_context(nc.sbuf_tensor(c.nm(n), s, d))
    dn = Dense(c)
    K = _consts(c, [("g_attn", g_attn, [128, 32]), ("g_q", g_q, [128, 12]), ("g_kv", g_kv, [128, 4])])
    dn.ones = sb("ones", [128, 128])
    dn.ones_b = Buf()
    c.op("dve", lambda: nc.vector.memset(dn.ones[:], 1.0), writes=[dn.ones_b])
    sq = Ring(c, "sq", 2, [128, TL], F32)
    st32 = Ring(c, "st32", 2, [128, TL], F32)
    stbf = Ring(c, "stbf", 3, [128, TL], BF16)
    ttf = Ring(c, "ttf", 2, [128, GW], F32)
    ttb = Ring(c, "ttb", 3, [128, GW], BF16)
    out_b = Buf()
    cqn = sb("cqn", [128, 12, TL], BF16); cqn_b = [Buf() for _ in range(12)]
    ckn = sb("ckn", [128, 4, TL], BF16); ckn_b = [Buf() for _ in range(4)]
    accq = sb("accq", [128, TL]); accq_b = Buf()
    acck = sb("acck", [128, TL]); acck_b = Buf()
    kr = [sb("kr1", [32, TL]), sb("kr2", [32, TL])]
    kr_b = [Buf(), Buf()]
    es_hn = ExitStack()
    hn = sb("hn", [128, 32, TL], BF16, es_hn)
    hn_b = [Buf() for _ in range(32)]
    with ExitStack() as es1:
        hp = Ring(c, "hp", 2, [128, 1, TL], F32, es1)
        acc = sb("acc", [128, TL], F32, es1); acc_b = Buf()
        rstd = sb("rstd", [128, TL], F32, es1); rstd_b = Buf()
        c.op("dve", lambda: nc.vector.memset(acc[:], 0.0), writes=[acc_b])
        for g in range(32):
            t, b, d = hp.next()
            c.dma("sp", t[:], hT[:, g:g + 1, :], d, writes=[b])
            for i in range(1):
                s, s_b, _ = sq.next()
                c.op("act", lambda: nc.scalar.activation(s[:], t[:, i, :], AF.Square), reads=[b], writes=[s_b])
                c.op("dve", lambda: nc.vector.tensor_tensor(acc[:], acc[:], s[:], ALU.add), reads=[acc_b, s_b], writes=[acc_b])
        col_stats(c, dn, acc, acc_b, rstd, rstd_b, D)
        ga, ga_b = K["g_attn"]
        for g in range(32):
            t, b, d = hp.next()
            c.dma("sp", t[:], hT[:, g:g + 1, :], d, writes=[b])
            for i in range(1):
                kc = g + i
                c.op("dve", lambda: nc.vector.scalar_tensor_tensor(hn[:, kc, :], t[:, i, :], ga[:, kc:kc + 1], rstd[:],
                                                                   ALU.mult, ALU.mult),
                     reads=[b, ga_b, rstd_b], writes=[hn_b[kc]])


    def out_fm(dst, row0, func=None, scale=1.0, dtype=F32):
        def h(tiles):
            t, b, d = (st32 if dtype == F32 else stbf).next()
            M = None
            for (ps, pb, t0, tn) in tiles:
                M = ps.shape[0]
                c.op("act", lambda: nc.scalar.activation(t[:M, t0:t0 + tn], ps, func or AF.Identity, scale=float(scale)),
                     reads=[pb], pwrites=[b])
            c.dma("sp", dst[row0:row0 + M, :], t[:M, :], d, reads=[b], pwrites=[out_b])
        return h

    c.op("dve", lambda: nc.vector.memset(accq[:], 0.0), writes=[accq_b])
    c.op("dve", lambda: nc.vector.memset(acck[:], 0.0), writes=[acck_b])

    def lat(dstt, dst_b, i, gain, gain_b, ac, ac_b):
        def h(tiles):
            s, s_b, _ = sq.next()
            for (ps, pb, t0, tn) in tiles:
                c.op("act", lambda: nc.scalar.activation(dstt[:, i, t0:t0 + tn], ps, AF.Identity, scale=gain[:, i:i + 1]),
                     reads=[pb, gain_b], pwrites=[dst_b[i]])
                c.op("act", lambda: nc.scalar.activation(s[:, t0:t0 + tn], ps, AF.Square), reads=[pb], pwrites=[s_b])
            c.op("dve", lambda: nc.vector.tensor_tensor(ac[:], ac[:], s[:], ALU.add), reads=[ac_b, s_b], writes=[ac_b])
        return h

    def krope(i):
        def h(tiles):
            for (ps, pb, t0, tn) in tiles:
                c.op("act", lambda: nc.scalar.copy(kr[i][:, t0:t0 + tn], ps), reads=[pb], pwrites=[kr_b[i]])
        return h

    gq, gq_b = K["g_q"]
    gk, gk_b = K["g_kv"]
    groups = []
    groups += fm_groups(O_CQ, 12, lambda i: lat(cqn, cqn_b, i, gq, gq_b, accq, accq_b))
    groups += fm_groups(O_CKV, 4, lambda i: lat(ckn, ckn_b, i, gk, gk_b, acck, acck_b))
    groups.append((O_KR, 64, [(0, 32, krope(0)), (32, 32, krope(1))]))
    groups += fm_groups(O_GQ, 4, lambda i: out_fm(o32, R32_GQ + i * 128, scale=128 ** -0.5))
    groups += fm_groups(O_GK, 4, lambda i: out_fm(o32, R32_GK + i * 128))
    groups.append((O_GZ, 16, [(0, 16, out_fm(o32, R32_GZ))]))
    groups += fm_groups(O_GR, 8, lambda i: out_fm(o32, R32_GR + i * 128, func=AF.Silu))
    groups += fm_groups(O_FQ, 12, lambda i: out_fm(obf, RB_FQ + i * 128, scale=128 ** -0.5, dtype=BF16))
    groups += fm_groups(O_FK, 12, lambda i: out_fm(obf, RB_FK + i * 128, dtype=BF16))
    groups.append((O_FZ, 12, [(0, 12, out_fm(o32, R32_FZ))]))
    dn.fm(w_in, 32, hn, hn_b, groups)


    def out_tm(dst, col0, ring):
        def h(ps, pb, ti, t0, tn):
            t, b, d = ring.next()
            n = ps.shape[1]
            c.op("act", lambda: nc.scalar.copy(t[:tn, :n], ps), reads=[pb], writes=[b])
            dd, cc = dst, col0
            if otb_split is not None and dst is otb:
                dd, cc = (otb_split[0], col0) if col0 < CB_FV else (otb_split[1], col0 - CB_FV)
            c.dma("sp", dd[t0:t0 + tn, cc:cc + n], t[:tn, :n], d, reads=[b], pwrites=[out_b])
        return h

    tg = tm_groups(O_GK, 512, lambda o: out_tm(otf, o, ttf))
    tg += tm_groups(O_GV, 1024, lambda o: out_tm(otb, CB_GV + o, ttb))
    tg += tm_groups(O_FV, 1536, lambda o: out_tm(otb, CB_FV + o, ttb))
    dn.tm(w_in, 32, hn, hn_b, tg)

    es_hn.close()
    olds = hn_b + hp.b + [acc_b, rstd_b]
    K2 = _consts(c, [("cos4", cos4, [128, TL]), ("sin4", sin4, [128, TL])], olds=olds)
    cs, cs_b = K2["cos4"]
    sn, sn_b = K2["sin4"]
    tmp = [sb(f"rtmp{i}", [128, TL]) for i in range(4)]
    tmp_b = [fresh(olds) for _ in range(4)]

    def rope(x1, x1_b, x2, x2_b, P, dst_rows1, dst_rows2):
        c.op("dve", lambda: nc.vector.tensor_tensor(tmp[0][:P, :], x1, cs[:P, :], ALU.mult), reads=[x1_b, cs_b], writes=[tmp_b[0]])
        c.op("dve", lambda: nc.vector.tensor_tensor(tmp[1][:P, :], x2, sn[:P, :], ALU.mult), reads=[x2_b, sn_b], writes=[tmp_b[1]])
        c.op("dve", lambda: nc.vector.tensor_tensor(tmp[2][:P, :], x2, cs[:P, :], ALU.mult), reads=[x2_b, cs_b], writes=[tmp_b[2]])
        c.op("dve", lambda: nc.vector.tensor_tensor(tmp[3][:P, :], x1, sn[:P, :], ALU.mult), reads=[x1_b, sn_b], writes=[tmp_b[3]])
        t, b, d = stbf.next()
        c.op("dve", lambda: nc.vector.tensor_tensor(t[:P, :], tmp[0][:P, :], tmp[1][:P, :], ALU.subtract),
             reads=[tmp_b[0], tmp_b[1]], writes=[b])
        for (r0, p0, n) in dst_rows1:
            c.dma("sp", obf[r0:r0 + n, :], t[p0:p0 + n, :], d, reads=[b], pwrites=[out_b])
        t2, b2, d2 = stbf.next()
        c.op("dve", lambda: nc.vector.tensor_tensor(t2[:P, :], tmp[2][:P, :], tmp[3][:P, :], ALU.add),
             reads=[tmp_b[2], tmp_b[3]], writes=[b2])
        for (r0, p0, n) in dst_rows2:
            c.dma("sp", obf[r0:r0 + n, :], t2[p0:p0 + n, :], d2, reads=[b2], pwrites=[out_b])

    rope(kr[0][:], kr_b[0], kr[1][:], kr_b[1], 32, [(RB_KPE, 0, 32)], [(RB_KPE + 32, 0, 32)])

    rq = sb("rq", [128, TL]); rq_b = fresh(olds)
    rk = sb("rk", [128, TL]); rk_b = fresh(olds)
    col_stats(c, dn, accq, accq_b, rq, rq_b, 1536)
    col_stats(c, dn, acck, acck_b, rk, rk_b, 512)
    for i in range(12):
        c.op("dve", lambda: nc.vector.tensor_tensor(cqn[:, i, :], cqn[:, i, :], rq[:], ALU.mult),
             reads=[cqn_b[i], rq_b], writes=[cqn_b[i]])
    for i in range(4):
        c.op("dve", lambda: nc.vector.tensor_tensor(ckn[:, i, :], ckn[:, i, :], rk[:], ALU.mult),
             reads=[ckn_b[i], rk_b], writes=[ckn_b[i]])

    QS = 192 ** -0.5
    qpe = [sb(f"qpe{i}", [128, TL]) for i in range(2)]
    qpe_b = [fresh(olds), fresh(olds)]

    def qpe_h(i, j3):
        def h(tiles):
            for (ps, pb, t0, tn) in tiles:
                c.op("act", lambda: nc.scalar.activation(qpe[i][:, t0:t0 + tn], ps, AF.Identity, scale=QS), reads=[pb], pwrites=[qpe_b[i]])
            if i == 1:
                rows1 = [(RB_Q + (4 * j3 + hh) * 192 + 128, 32 * hh, 32) for hh in range(4)]
                rows2 = [(RB_Q + (4 * j3 + hh) * 192 + 160, 32 * hh, 32) for hh in range(4)]
                rope(qpe[0][:], qpe_b[0], qpe[1][:], qpe_b[1], 128, rows1, rows2)
        return h

    qg = fm_groups(0, 12, lambda i: out_fm(obf, RB_Q + i * 192, scale=QS, dtype=BF16))
    for j3 in range(3):
        qg.append((1536 + j3 * 128, 128, [(0, 128, qpe_h(0, j3))]))
        qg.append((1920 + j3 * 128, 128, [(0, 128, qpe_h(1, j3))]))
    dn.fm(w_uq, 12, cqn, cqn_b, qg)
    kg = fm_groups(0, 12, lambda i: out_fm(obf, RB_KN + i * 128, dtype=BF16))
    dn.fm(w_ukv, 4, ckn, ckn_b, kg)
    vg = tm_groups(1536, 1536, lambda o: out_tm(otb, CB_VM + o, ttb))
    dn.tm(w_ukv, 4, ckn, ckn_b, vg)
    c.wait_all("sp", [out_b])


TH = TL // 2
TTH = [(0, 512), (512, TH - 512)]


def build_p3():
    nc = bass.Bass("TRN2", target_bir_lowering=False)
    dt = lambda n, s, d=F32, k="ExternalInput": nc.dram_tensor(n, s, d, kind=k).ap()
    oT = dt("oT", [128, 32, TL], BF16)
    hT = dt("hT", [128, 32, TL])
    w_out = dt("w_out", [D, D])
    w_up = dt("w_up", [D, 2 * DFF])
    w_down = dt("w_down", [DFF, D])
    g_ffn = dt("g_ffn", [128, 32])
    g_next = dt("g_next", [128, 32])
    conv_w = dt("conv_w", [128, 172, 3])
    conv_b = dt("conv_b", [128, 172])
    h_out = dt("h_out", [128, 32, TL], F32, "ExternalOutput")
    y_out = dt("y_out", [128, 32, TL], F32, "ExternalOutput")
    h_mid = dt("h_mid", [128, 32, TL], F32, "Internal")
    act = dt("act_scr", [86, 128, TL], BF16, "Internal")
    with ExitStack() as es:
        c = Ctx(nc, es)
        emit_p3(c, oT, hT, w_out, w_up, w_down, g_ffn, g_next, conv_w, conv_b, h_out, y_out, h_mid, act)
    return nc


def emit_p3(c, oT, hT, w_out, w_up, w_down, g_ffn, g_next, conv_w, conv_b, h_out, y_out, h_mid, act):
    nc, es = c.nc, c.es
    sb = lambda n, s, d=F32, st=None: (st or es).enter_context(nc.sbuf_tensor(c.nm(n), s, d))
    dn = Dense(c)
    K = _consts(c, [("g_ffn", g_ffn, [128, 32]), ("g_next", g_next, [128, 32]),
                    ("cw", conv_w, [128, 172, 3]), ("cb", conv_b, [128, 172])])
    dn.ones = sb("ones", [128, 128]); dn.ones_b = Buf()
    c.op("dve", lambda: nc.vector.memset(dn.ones[:], 1.0), writes=[dn.ones_b])
    sq = Ring(c, "sq", 2, [128, TL], F32)
    hc = Ring(c, "hc", 2, [128, TL], F32)
    acc = sb("acc", [128, TL]); acc_b = Buf()
    rstd = sb("rstd", [128, TL]); rstd_b = Buf()
    hmid_b = [Buf() for _ in range(32)]
    act_b = [Buf() for _ in range(86)]
    hout_b = [[Buf(), Buf()] for _ in range(32)]
    y_b = Buf()
    gf, gf_b = K["g_ffn"]
    c.op("dve", lambda: nc.vector.memset(acc[:], 0.0), writes=[acc_b])

    es_hg = ExitStack()
    hg = sb("hg", [128, 32, TL], BF16, es_hg); hg_b = [Buf() for _ in range(32)]
    with ExitStack() as es1:
        osb = sb("osb", [128, 32, TL], BF16, es1); osb_b = [Buf() for _ in range(32)]
        d_o = c.dsem("oT")
        for g in range(8):
            c.dma("sp", osb[:, g * 4:(g + 1) * 4, :], oT[:, g * 4:(g + 1) * 4, :], d_o, writes=osb_b[g * 4:(g + 1) * 4])
        Ctx.seal(d_o, osb_b)

        def h3a(m):
            def h(tiles):
                t, b, d = hc.next()
                c.dma("sp", t[:], hT[:, m, :], d, writes=[b])
                for (ps, pb, t0, tn) in tiles:
                    c.op("dve", lambda: nc.vector.tensor_tensor(t[:, t0:t0 + tn], ps, t[:, t0:t0 + tn], ALU.add),
                         reads=[pb, b], pwrites=[b])
                c.dma("sp", h_mid[:, m, :], t[:], d, reads=[b], writes=[hmid_b[m]])
                s, s_b, _ = sq.next()
                c.op("act", lambda: nc.scalar.activation(s[:], t[:], AF.Square), reads=[b], writes=[s_b])
                c.op("dve", lambda: nc.vector.tensor_tensor(acc[:], acc[:], s[:], ALU.add), reads=[acc_b, s_b], writes=[acc_b])
                c.op("act", lambda: nc.scalar.activation(hg[:, m, :], t[:], AF.Identity, scale=gf[:, m:m + 1]),
                     reads=[b, gf_b], writes=[hg_b[m]])
            return h
        dn.fm(w_out, 32, osb, osb_b, fm_groups(0, 32, h3a))
    olds = list(osb_b)
    col_stats(c, dn, acc, acc_b, rstd, rstd_b, D)

    cw, cw_b = K["cw"]
    cb, cb_b = K["cb"]
    with ExitStack() as es2:
        ug = Ring(c, "ug", 2, [128, TL], F32, es2)
        uv = Ring(c, "uv", 2, [128, TL], F32, es2)
        cvg = Ring(c, "cvg", 2, [128, TL], F32, es2)
        cvv = Ring(c, "cvv", 2, [128, TL], F32, es2)
        ab = Ring(c, "ab", 2, [128, TL], BF16, es2)
        for r in (ug, uv, cvg, cvv, ab):
            r.b = [fresh(olds) for _ in r.b]
        for i in range(2):
            c.op("dve", lambda: nc.vector.memset(ab.t[i][:], 0.0), writes=[ab.b[i]])
        gate_cv = {}

        def conv(u, u_b, ch, ring):
            cv, cv_b, _ = ring.next()
            n = TL - 2
            c.op("act", lambda: nc.scalar.activation(cv[:, 2:], u[:, 2:], AF.Identity, bias=cb[:, ch:ch + 1], scale=cw[:, ch, 2:3]),
                 reads=[u_b, cb_b, cw_b], writes=[cv_b])
            c.op("dve", lambda: nc.vector.scalar_tensor_tensor(cv[:, 2:], u[:, 1:1 + n], cw[:, ch, 1:2], cv[:, 2:], ALU.mult, ALU.add),
                 reads=[u_b, cw_b, cv_b], writes=[cv_b])
            c.op("dve", lambda: nc.vector.scalar_tensor_tensor(cv[:, 2:], u[:, 0:n], cw[:, ch, 0:1], cv[:, 2:], ALU.mult, ALU.add),
                 reads=[u_b, cw_b, cv_b], writes=[cv_b])
            return cv, cv_b

        def hup(m, is_val):
            def h(tiles):
                u, u_b, _ = (uv if is_val else ug).next()
                for (ps, pb, t0, tn) in tiles:
                    c.op("dve", lambda: nc.vector.tensor_tensor(u[:, t0:t0 + tn], ps, rstd[:, t0:t0 + tn], ALU.mult),
                         reads=[pb, rstd_b], pwrites=[u_b])
                if not is_val:
                    cv, cv_b = conv(u, u_b, m, cvg)
                    c.op("act", lambda: nc.scalar.activation(cv[:, 2:], cv[:, 2:], AF.Silu), reads=[cv_b], writes=[cv_b])
                    gate_cv[m] = (cv, cv_b)
                else:
                    cv, cv_b = conv(u, u_b, 86 + m, cvv)
                    g, g_b = gate_cv.pop(m)
                    a, a_b, a_d = ab.next()
                    c.op("dve", lambda: nc.vector.tensor_tensor(a[:, 2:], g[:, 2:], cv[:, 2:], ALU.mult),
                         reads=[g_b, cv_b], pwrites=[a_b])
                    c.dma("sp", act[m], a[:], a_d, reads=[a_b], writes=[act_b[m]])
            return h

        groups = []
        for m0 in range(0, 86, 2):
            groups += [(m0 * 128, 256, [(0, 128, hup(m0, False)), (128, 128, hup(m0 + 1, False))]),
                       (DFF + m0 * 128, 256, [(0, 128, hup(m0, True)), (128, 128, hup(m0 + 1, True))])]
        dn.fm(w_up, 32, hg, hg_b, groups)
        olds = olds + hg_b + ug.b + uv.b + cvg.b + cvv.b + ab.b
    es_hg.close()

    accf = sb("accf", [128, TL]); accf_b = fresh(olds)
    c.op("dve", lambda: nc.vector.memset(accf[:], 0.0), writes=[accf_b])
    asb = sb("asb", [128, 86, TH], BF16); asb_b = [fresh(olds) for _ in range(86)]
    d_a = c.dsem("asb")
    actv = act.rearrange("m p t -> p m t")
    for half in range(2):
        c0 = half * TH
        for g in range(0, 86, 8):
            g1 = min(86, g + 8)
            c.dma("sp", asb[:, g:g1, :], actv[:, g:g1, c0:c0 + TH], d_a, reads=act_b[g:g1], writes=asb_b[g:g1])
        Ctx.seal(d_a, asb_b)
        for m in range(32):
            pb = dn.it % 2
            dn.it += 1
            pst = [(dn.psA[pb], dn.bA[pb]), (dn.psB[pb], dn.bB[pb])]
            for part in range(2):
                wt, wb = dn.load(w_down, 43, m * 128, 128, row0=part * 43 * 128)
                for k in range(43):
                    kc = part * 43 + k
                    last = kc == 85
                    for j, (t0, tn) in enumerate(TTH):
                        ps, pbuf = pst[j]
                        c.op("pe", lambda: nc.tensor.matmul(ps[:, :tn], wt[:, k, 0:128], asb[:, kc, t0:t0 + tn],
                                                            start=(kc == 0), stop=last),
                             reads=[wb, asb_b[kc]], writes=[pbuf], inc=(last or k == 42))
            t, b, d = hc.next()
            c.dma("sp", t[:, :TH], h_mid[:, m, c0:c0 + TH], d, reads=[hmid_b[m]], writes=[b])
            for j, (t0, tn) in enumerate(TTH):
                ps, pbuf = pst[j]
                c.op("dve", lambda: nc.vector.tensor_tensor(t[:, t0:t0 + tn], ps[:, :tn], t[:, t0:t0 + tn], ALU.add),
                     reads=[pbuf, b], pwrites=[b])
            c.dma("sp", h_out[:, m, c0:c0 + TH], t[:, :TH], d, reads=[b], writes=[hout_b[m][half]])
            s, s_b, _ = sq.next()
            c.op("act", lambda: nc.scalar.activation(s[:, :TH], t[:, :TH], AF.Square), reads=[b], writes=[s_b])
            c.op("dve", lambda: nc.vector.tensor_tensor(accf[:, c0:c0 + TH], accf[:, c0:c0 + TH], s[:, :TH], ALU.add),
                 reads=[accf_b, s_b], writes=[accf_b])
    col_stats(c, dn, accf, accf_b, rstd, rstd_b, D)
    gn, gn_b = K["g_next"]
    for m in range(32):
        t, b, d = hc.next()
        c.dma("sp", t[:], h_out[:, m, :], d, reads=hout_b[m], writes=[b])
        c.op("dve", lambda: nc.vector.scalar_tensor_tensor(t[:], t[:], gn[:, m:m + 1], rstd[:], ALU.mult, ALU.mult),
             reads=[b, gn_b, rstd_b], writes=[b])
        c.dma("sp", y_out[:, m, :], t[:], d, reads=[b], pwrites=[y_b])
    c.wait_all("sp", [y_b] + [x for hb in hout_b for x in hb])


QT = [(i * 512, 512) for i in range(8)] + [(4096, 16)]
NKC = 33
A_Q, A_KN, A_KPE, A_FQ, A_FK, NA = 0, 576, 960, 1024, 1408, 1792
V_VM, V_FV, V_GV, NV = 0, 384, 768, 1024
F_GQ, F_GK, F_GZ, F_GR, F_FZ, NF = 0, 128, 256, 273, 529, 532
GCH = [(0, 16)] + [(16 + 64 * i, 64) for i in range(64)]


def build_p2():
    nc = bass.Bass("TRN2", target_bir_lowering=False)
    dt = lambda n, s, d=F32, k="ExternalInput": nc.dram_tensor(n, s, d, kind=k).ap()
    a = dict(
        abf=dt("abf", [NA, NT], BF16), vbf=dt("vbf", [NT, NV], BF16), f32=dt("f32", [NF, NT]),
        gkm=dt("gkm", [NT, 128]), wg2=dt("wg2", [17, 128]), fbf=dt("fbf", [3, 1]),
        gmla=dt("gmla", [128, 3]), gfox=dt("gfox", [128, 3]), ggla=dt("ggla", [128, 2]),
        mask4=dt("mask4", [128, 4, 512], BF16), tri01=dt("tri01", [64, 64]),
        tris64=dt("tris64", [64, 65]), tris16=dt("tris16", [16, 17]), sus=dt("sus", [64, 64]),
        oT=dt("oT", [1024, NT], BF16, "ExternalOutput"),
        aug=dt("aug_scr", [3, 12, NT], BF16, "Internal"))
    with ExitStack() as es:
        c = Ctx(nc, es)
        emit_p2(c, **a)
    return nc


def emit_p2(c, abf, vbf, f32, gkm, wg2, fbf, gmla, gfox, ggla, mask4, tri01, tris64, tris16, sus, oT, aug,
            mid_hook=None, rows_done=None):
    nc, es = c.nc, c.es
    sb = lambda n, s, d=F32, st=None: (st or es).enter_context(nc.sbuf_tensor(c.nm(n), s, d))
    K = _consts(c, [("wg2", wg2, [17, 128]), ("fbf", fbf, [3, 1]), ("gmla", gmla, [128, 3]), ("gfox", gfox, [128, 3]),
                    ("ggla", ggla, [128, 2]), ("tri01", tri01, [64, 64]), ("tris64", tris64, [64, 65]),
                    ("tris16", tris16, [16, 17]), ("sus", sus, [64, 64])])
    mk = sb("mk_sb", [128, 4, 512], BF16); mk_b = Buf()
    c.dma("sp", mk[:], mask4, c.dsem("mk"), writes=[mk_b])
    ones = sb("ones", [128, 512]); ones_b = Buf()
    c.op("dve", lambda: nc.vector.memset(ones[:], 1.0), writes=[ones_b])
    onesb = sb("onesb", [128, 128], BF16); onesb_b = Buf()
    c.op("dve", lambda: nc.vector.memset(onesb[:], 1.0), writes=[onesb_b])
    P = [es.enter_context(nc.psum_tensor(c.nm(f"pp{i}"), [128, 512], F32)) for i in range(7)]
    Pb = [Buf() for _ in range(7)]
    out_b = Buf()
    blk_b = [Buf() for _ in range(8)]
    stb = Ring(c, "stb", 3, [128, 512], BF16)
    olds = []

    def head_norm_out(o_parts, gain, gain_b, gcol0, n, q0, row0, extra=None):
        nf = 128 * len(o_parts)
        for i, (o, o_b) in enumerate(o_parts):
            s, s_b, _ = sqr.next()
            c.op("act", lambda: nc.scalar.activation(s[:, :n], o, AF.Square), reads=[o_b], writes=[s_b])
            c.op("pe", lambda: nc.tensor.matmul(P[6][:, :n], ones[:, :128], s[:, :n], start=(i == 0), stop=(i == len(o_parts) - 1)),
                 reads=[s_b, ones_b], writes=[Pb[6]])
        r, r_b, _ = rsr.next()
        c.op("dve", lambda: nc.vector.tensor_scalar(r[:, :n], P[6][:, :n], 1.0 / nf, EPS, ALU.mult, ALU.add), reads=[Pb[6]], writes=[r_b])
        c.op("act", lambda: nc.scalar.activation(r[:, :n], r[:, :n], AF.Sqrt), reads=[r_b], writes=[r_b])
        c.op("dve", lambda: nc.vector.reciprocal(r[:, :n], r[:, :n]), reads=[r_b], writes=[r_b])
        for i, (o, o_b) in enumerate(o_parts):
            t, b, d = stb.next()
            if extra is None:
                c.op("dve", lambda: nc.vector.scalar_tensor_tensor(t[:, :n], o, gain[:, gcol0 + i:gcol0 + i + 1], r[:, :n], ALU.mult, ALU.mult),
                     reads=[o_b, gain_b, r_b], writes=[b])
            else:
                ex, ex_b = extra[i]
                c.op("dve", lambda: nc.vector.scalar_tensor_tensor(o, o, gain[:, gcol0 + i:gcol0 + i + 1], r[:, :n], ALU.mult, ALU.mult),
                     reads=[o_b, gain_b, r_b], writes=[o_b])
                c.op("dve", lambda: nc.vector.tensor_tensor(t[:, :n], o, ex, ALU.mult), reads=[o_b, ex_b], writes=[b])
            c.dma("sp", oT[row0 + i * 128:row0 + (i + 1) * 128, q0:q0 + n], t[:, :n], d, reads=[b],
                  pwrites=[out_b, blk_b[row0 // 128 + i]])

    sqr = Ring(c, "sqr", 2, [128, 512], F32)
    rsr = Ring(c, "rsr", 2, [128, 512], F32)

    with ExitStack() as eg:
        wg, wg_b = K["wg2"]
        tri, tri_b = K["tri01"]
        ts64, ts64_b = K["tris64"]
        ts16, ts16_b = K["tris16"]
        su, su_b = K["sus"]
        gv = sb("gv", [64, 65, 256], BF16, eg); gv_b = Buf()
        d_gv = c.dsem("gv")
        c.dma("sp", gv[:16, 0, :], vbf[0:16, V_GV:V_GV + 256], d_gv, writes=[gv_b])
        for i in range(4):
            c.dma("sp", gv[:, 1 + 16 * i:17 + 16 * i, :],
                  vbf[16 + 1024 * i:16 + 1024 * (i + 1), V_GV:V_GV + 256].rearrange("(n p) d -> p n d", p=64), d_gv, pwrites=[gv_b])
        qdec = sb("qdec", [128, NT], BF16, eg); qdec_b = [Buf() for _ in range(65)]
        kst = sb("kst", [64, 65, 128], BF16, eg); kst_b = [Buf() for _ in range(65)]
        Aall = sb("Aall", [64, 65, 64], BF16, eg); A_b = [Buf() for _ in range(65)]
        dec = sb("dec", [128, 65], F32, eg); dec_b = [Buf() for _ in range(65)]
        oall = sb("oall_sb", [128, 2, NT], F32, eg); oall_b = [Buf() for _ in range(9)]
        S = sb("S", [128, 256], F32, eg); S_b = Buf()
        Sbf = sb("Sbf", [128, 256], BF16, eg); Sbf_b = Buf()
        gzr = Ring(c, "gzr", 2, [17, 512], F32, eg)
        gqr = Ring(c, "gqr", 2, [128, 512], F32, eg)
        gkr = Ring(c, "gkr", 2, [128, 512], F32, eg)
        gkmr = Ring(c, "gkmr", 2, [64, 8, 128], F32, eg)
        spr = Ring(c, "spr", 2, [64, 8, 128], F32, eg)
        e4 = Ring(c, "e4", 2, [128, 64], F32, eg)
        ek = Ring(c, "ek", 2, [64, 128], F32, eg)
        kdr = Ring(c, "kdr", 2, [128, 64], BF16, eg)
        supers = [(0, 16, [0])] + [(16 + 512 * j, 512, list(range(1 + 8 * j, 9 + 8 * j))) for j in range(8)]
        for (r0, rn, chunks) in supers:
            gz, gz_b, gz_d = gzr.next()
            c.dma("sp", gz[:, :rn], f32[F_GZ:F_GZ + 17, r0:r0 + rn], gz_d, writes=[gz_b])
            gq, gq_b, gq_d = gqr.next()
            c.dma("sp", gq[:, :rn], f32[F_GQ:F_GQ + 128, r0:r0 + rn], gq_d, writes=[gq_b])
            gk, gk_b, gk_d = gkr.next()
            c.dma("sp", gk[:, :rn], f32[F_GK:F_GK + 128, r0:r0 + rn], gk_d, writes=[gk_b])
            gm, gm_b, gm_d = gkmr.next()
            C = GCH[chunks[0]][1]
            nch = len(chunks)
            c.dma("sp", gm[:C, :nch, :], gkm[r0:r0 + rn, :].rearrange("(n p) d -> p n d", p=C), gm_d, writes=[gm_b])
            sp, sp_b, _ = spr.next()
            for g4 in range(0, nch, 4):
                n4 = min(4, nch - g4)
                for i in range(n4):
                    o0 = (g4 + i) * C
                    c.op("pe", lambda: nc.tensor.matmul(P[0][:C, i * 128:(i + 1) * 128], gz[:, o0:o0 + C], wg[:], start=True, stop=True),
                         reads=[gz_b, wg_b], writes=[Pb[0]])
                c.op("act", lambda: nc.scalar.activation(sp[:C, g4:g4 + n4, :], P[0][:C, :n4 * 128].rearrange("p (a b) -> p a b", b=128), AF.Exp, scale=-1.0),
                     reads=[Pb[0]], pwrites=[sp_b])
            c.op("act", lambda: nc.scalar.activation(sp[:C, :nch, :], sp[:C, :nch, :], AF.Ln, bias=1.0), reads=[sp_b], writes=[sp_b])
            for i, n in enumerate(chunks):
                s0, C = GCH[n]
                l0 = i * C
                tsx, tsx_b = (ts64, ts64_b) if C == 64 else (ts16, ts16_b)
                c.op("pe", lambda: nc.tensor.matmul(P[1][:, :C + 1], sp[:C, i, :], tsx[:C, :C + 1], start=True, stop=True),
                     reads=[sp_b, tsx_b], writes=[Pb[1]])
                c.op("pe", lambda: nc.tensor.matmul(P[2][:C, :128], su[:C, :C], sp[:C, i, :], start=True, stop=True),
                     reads=[sp_b, su_b], writes=[Pb[2]])
                eb, eb_b, _ = e4.next()
                c.op("act", lambda: nc.scalar.activation(eb[:, :C], P[1][:, :C], AF.Exp), reads=[Pb[1]], writes=[eb_b])
                c.op("dve", lambda: nc.vector.tensor_tensor(qdec[:, s0:s0 + C], gq[:, l0:l0 + C], eb[:, :C], ALU.mult),
                     reads=[gq_b, eb_b], writes=[qdec_b[n]])
                en, en_b, _ = e4.next()
                c.op("act", lambda: nc.scalar.activation(en[:, :C], P[1][:, :C], AF.Exp, scale=-1.0), reads=[Pb[1]], writes=[en_b])
                kd, kd_b, _ = kdr.next()
                c.op("dve", lambda: nc.vector.tensor_tensor(kd[:, :C], gk[:, l0:l0 + C], en[:, :C], ALU.mult),
                     reads=[gk_b, en_b], writes=[kd_b])
                c.op("act", lambda: nc.scalar.activation(dec[:, n:n + 1], P[1][:, C:C + 1], AF.Exp), reads=[Pb[1]], writes=[dec_b[n]])
                ekt, ek_b, _ = ek.next()
                c.op("act", lambda: nc.scalar.activation(ekt[:C, :], P[2][:C, :128], AF.Exp), reads=[Pb[2]], writes=[ek_b])
                c.op("dve", lambda: nc.vector.tensor_tensor(kst[:C, n, :], gm[:C, i, :], ekt[:C, :], ALU.mult),
                     reads=[gm_b, ek_b], writes=[kst_b[n]])
                c.op("pe", lambda: nc.tensor.matmul(P[3][:C, :C], kd[:, :C], qdec[:, s0:s0 + C], start=True, stop=True),
                     reads=[kd_b, qdec_b[n]], writes=[Pb[3]])
                c.op("dve", lambda: nc.vector.tensor_tensor(Aall[:C, n, :C], P[3][:C, :C], tri[:C, :C], ALU.mult),
                     reads=[Pb[3], tri_b], writes=[A_b[n]])
        c.op("dve", lambda: nc.vector.memset(S[:], 0.0), writes=[S_b])
        for n, (s0, C) in enumerate(GCH):
            po, po_b = P[4 + n % 2], Pb[4 + n % 2]
            for half in range(2):
                c.op("pe", lambda: nc.tensor.matmul(po[:, half * 64:half * 64 + C], gv[:C, n, half * 128:(half + 1) * 128], Aall[:C, n, :C],
                                                    start=True, stop=(n == 0)),
                     reads=[gv_b, A_b[n]], writes=[po_b])
                if n > 0:
                    c.op("pe", lambda: nc.tensor.matmul(po[:, half * 64:half * 64 + C], Sbf[:, half * 128:(half + 1) * 128], qdec[:, s0:s0 + C],
                                                        start=False, stop=True),
                         reads=[Sbf_b, qdec_b[n]], writes=[po_b])
            ti = 0 if n == 0 else 0 + (s0 // 512)
            for half in range(2):
                c.op("act", lambda: nc.scalar.copy(oall[:, half, s0:s0 + C], po[:, half * 64:half * 64 + C]),
                     reads=[po_b], pwrites=[oall_b[min(8, s0 // 512)], oall_b[min(8, (s0 + C - 1) // 512)]])
            c.op("pe", lambda: nc.tensor.matmul(P[0][:, :256], kst[:C, n, :], gv[:C, n, :], start=True, stop=True),
                 reads=[kst_b[n], gv_b], writes=[Pb[0]])
            c.op("dve", lambda: nc.vector.scalar_tensor_tensor(S[:], S[:], dec[:, n:n + 1], P[0][:, :256], ALU.mult, ALU.add),
                 reads=[S_b, dec_b[n], Pb[0]], writes=[S_b])
            c.op("act", lambda: nc.scalar.copy(Sbf[:], S[:]), reads=[S_b], writes=[Sbf_b])
        gg, gg_b = K["ggla"]
        grr = Ring(c, "grr", 2, [128, 2, 512], F32, eg)
        for ti, (q0, n) in enumerate(QT):
            gr, gr_b, gr_d = grr.next()
            c.dma("sp", gr[:, :, :n], f32[F_GR:F_GR + 256, q0:q0 + n].rearrange("(h p) t -> p h t", p=128), gr_d, writes=[gr_b])
            head_norm_out([(oall[:, hh, q0:q0 + n], oall_b[ti]) for hh in range(2)], gg, gg_b, 0, n, q0, 384,
                          extra=[(gr[:, hh, :n], gr_b) for hh in range(2)])
        olds = [gv_b, Sbf_b, S_b] + qdec_b + kst_b + A_b + dec_b + oall_b + gzr.b + gqr.b + gkr.b + gkmr.b + spr.b + e4.b + ek.b + kdr.b + grr.b
        if rows_done is not None:
            rows_done(384, 256, blk_b[3:5])
    if mid_hook is not None:
        c.barrier(exclude=getattr(c, "soft", ()))
        with ExitStack() as eh:
            keep = c.es
            c.es = eh
            mid_hook()
            c.es = keep
        c.barrier(exclude=getattr(c, "soft", ()))

    with ExitStack() as ea:
        qn = Ring(c, "qn", 2, [128, NT], BF16, ea)
        qp = Ring(c, "qp", 2, [64, NT], BF16, ea)
        kn = Ring(c, "kn", 2, [128, NT], BF16, ea)
        vv = Ring(c, "vv", 2, [128, NKC, 128], BF16, ea)
        kpe = sb("kpe", [64, NT], BF16, ea); kpe_b = fresh(olds)
        pT = Ring(c, "pT", 3, [128, 512], BF16, ea)
        osb = Ring(c, "osb", 2, [128, 512], F32, ea)
        rl = Ring(c, "rl", 2, [128, 512], F32, ea)
        exr = Ring(c, "exr", 2, [128, 512], F32, ea)
        mx = sb("mx", [128, 4], F32, ea); mx_b = fresh(olds)
        negm = sb("negm", [128, 1], F32, ea); negm_b = fresh(olds)
        nfb = sb("nfb", [3, 1], F32, ea); nfb_b = fresh(olds)
        aq = Ring(c, "aq", 1, [6, NT], BF16, ea)
        ak = Ring(c, "ak", 1, [6, NT], BF16, ea)
        for r in (qn, qp, kn, vv, pT, osb, rl, exr, aq, ak):
            r.b = [fresh(olds) for _ in r.b]
        c.dma("sp", kpe[:], abf[A_KPE:A_KPE + 64, :], c.dsem("kpe"), writes=[kpe_b])
        aug_b = Buf()
        with ExitStack() as ef:
            fz = sb("fz", [3, NT], F32, ef); fz_b = fresh(olds)
            cs = sb("cs", [3, NT], F32, ef); cs_b = fresh(olds)
            spl = Ring(c, "spl", 2, [3, NT], BF16, ef)
            spl.b = [fresh(olds) for _ in spl.b]
            fb, fb_b = K["fbf"]
            c.dma("sp", fz[:], f32[F_FZ:F_FZ + 3, :], c.dsem("fz"), writes=[fz_b])
            t1, t1_b, t1_d = spl.next()
            c.op("dve", lambda: nc.vector.memset(t1[:], 1.0), writes=[t1_b])
            for r in range(3, 9):
                c.dma("sp", aug[:, r, :], t1[:], t1_d, reads=[t1_b], pwrites=[aug_b])
            c.op("dve", lambda: nc.vector.tensor_scalar(nfb[:], fb[:], -1.0, None, ALU.mult), reads=[fb_b], writes=[nfb_b])
            c.op("act", lambda: nc.scalar.activation(fz[:], fz[:], AF.Exp, bias=nfb[:], scale=-1.0), reads=[fz_b, nfb_b], writes=[fz_b])
            c.op("act", lambda: nc.scalar.activation(fz[:], fz[:], AF.Ln, bias=1.0), reads=[fz_b], writes=[fz_b])
            c.op("dve", lambda: nc.vector.tensor_scalar(fz[:], fz[:], -1.0, None, ALU.mult), reads=[fz_b], writes=[fz_b])
            for j, (q0, n) in enumerate(QT):
                init = 0.0 if j == 0 else cs[:, q0 - 1:q0]
                c.op("dve", lambda: nc.vector.tensor_tensor_scan(cs[:, q0:q0 + n], ones[:3, :n], fz[:, q0:q0 + n], init, ALU.mult, ALU.add),
                     reads=[ones_b, fz_b, cs_b], writes=[cs_b])
            for i in range(3):
                t1, t1_b, t1_d = spl.next()
                c.op("dve", lambda: nc.vector.tensor_copy(t1[:], cs[:]), reads=[cs_b], writes=[t1_b])
                c.dma("sp", aug[:, i, :], t1[:], t1_d, reads=[t1_b], pwrites=[aug_b])
                if i < 2:
                    c.op("dve", lambda: nc.vector.tensor_tensor(cs[:], cs[:], t1[:], ALU.subtract), reads=[cs_b, t1_b], writes=[cs_b])
                t2, t2_b, t2_d = spl.next()
                c.op("act", lambda: nc.scalar.mul(t2[:], t1[:], -1.0), reads=[t1_b], writes=[t2_b])
                c.dma("sp", aug[:, 9 + i, :], t2[:], t2_d, reads=[t2_b], pwrites=[aug_b])
            olds = olds + [fz_b, cs_b] + spl.b

        heads = [("mla", h) for h in range(3)] + [("fox", h) for h in range(3)]
        for (kind, h) in heads:
            q_t, q_b, q_d = qn.next()
            k_t, k_b, k_d = kn.next()
            v_t, v_b, v_d = vv.next()
            parts = []
            if kind == "mla":
                qrow, krow, vcol, orow = A_Q + h * 192, A_KN + h * 128, V_VM + h * 128, h * 128
                gain, gain_b = K["gmla"]
                p_t, p_b, p_d = qp.next()
                c.dma("sp", p_t[:], abf[qrow + 128:qrow + 192, :], p_d, writes=[p_b])
                parts = [(k_t, k_b, q_t, q_b, 128), (kpe, kpe_b, p_t, p_b, 64)]
            else:
                qrow, krow, vcol, orow = A_FQ + h * 128, A_FK + h * 128, V_FV + h * 128, 640 + h * 128
                gain, gain_b = K["gfox"]
                a_q, a_qb, a_qd = aq.next()
                a_k, a_kb, a_kd = ak.next()
                c.dma("sp", a_q[:], aug[h, 0:6, :], a_qd, reads=[aug_b], writes=[a_qb])
                c.dma("sp", a_k[:], aug[h, 6:12, :], a_kd, reads=[aug_b], writes=[a_kb])
                parts = [(k_t, k_b, q_t, q_b, 128), (a_k, a_kb, a_q, a_qb, 6)]
            c.dma("sp", q_t[:], abf[qrow:qrow + 128, :], q_d, writes=[q_b])
            c.dma("sp", k_t[:], abf[krow:krow + 128, :], k_d, writes=[k_b])
            for i in range(4):
                c.dma("sp", v_t[:, 8 * i:8 * i + 8, :], vbf[1024 * i:1024 * (i + 1), vcol:vcol + 128].rearrange("(n p) d -> p n d", p=128),
                      v_d, writes=[v_b] if i == 0 else [], pwrites=[v_b] if i else [])
            c.dma("sp", v_t[:16, 32, :], vbf[4096:4112, vcol:vcol + 128], v_d, pwrites=[v_b])
            c.op("dve", lambda: nc.vector.memset(mx[:], 0.0), writes=[mx_b])
            for side in range(2):
                plist = [(p[2], p[3], p[4]) if side == 0 else (p[0], p[1], p[4]) for p in parts if p[4] > 6]
                for (q0, n) in QT:
                    for i, (t_, b_, kp) in enumerate(plist):
                        s, s_b, _ = sqr.next()
                        c.op("act", lambda: nc.scalar.activation(s[:kp, :n], t_[:kp, q0:q0 + n], AF.Square), reads=[b_], writes=[s_b])
                        c.op("pe", lambda: nc.tensor.matmul(P[6][:, :n], ones[:kp, :128], s[:kp, :n], start=(i == 0), stop=(i == len(plist) - 1)),
                             reads=[s_b, ones_b], writes=[Pb[6]])
                    c.op("dve", lambda: nc.vector.reduce_max(mx[:, 2:3], P[6][:, :n], axis=AX.X), reads=[Pb[6]], writes=[mx_b])
                    c.op("dve", lambda: nc.vector.tensor_tensor(mx[:, side:side + 1], mx[:, side:side + 1], mx[:, 2:3], ALU.max),
                         reads=[mx_b], writes=[mx_b])
            c.op("dve", lambda: nc.vector.tensor_tensor(mx[:, 3:4], mx[:, 0:1], mx[:, 1:2], ALU.mult), reads=[mx_b], writes=[mx_b])
            c.op("act", lambda: nc.scalar.activation(negm[:], mx[:, 3:4], AF.Sqrt), reads=[mx_b], writes=[negm_b])
            c.op("dve", lambda: nc.vector.tensor_scalar(negm[:], negm[:], -1.0, None, ALU.mult), reads=[negm_b], writes=[negm_b])
            for ti, (q0, n) in enumerate(QT):
                last_c = min(4 * ti + 3, NKC - 1)
                po, po_b = P[2 + ti % 2], Pb[2 + ti % 2]
                pl, pl_b = P[4 + ti % 2], Pb[4 + ti % 2]
                pend = None

                def pv(kc_, kn2, p2, p2_b):
                    c.op("pe", lambda: nc.tensor.matmul(po[:, :n], v_t[:kn2, kc_, :], p2[:kn2, :n], start=(kc_ == 0), stop=(kc_ == last_c)),
                         reads=[v_b, p2_b], writes=[po_b], inc=(kc_ == last_c))
                    c.op("pe", lambda: nc.tensor.matmul(pl[:, :n], onesb[:kn2, :], p2[:kn2, :n], start=(kc_ == 0), stop=(kc_ == last_c)),
                         reads=[onesb_b, p2_b], writes=[pl_b], inc=True)

                for kc in range(last_c + 1):
                    k0 = kc * 128
                    kn_ = min(128, NT - k0)
                    ps, ps_b = P[kc % 2], Pb[kc % 2]
                    for i, (kt_, kb_, qt_, qb_, kp) in enumerate(parts):
                        c.op("pe", lambda: nc.tensor.matmul(ps[:kn_, :n], kt_[:kp, k0:k0 + kn_], qt_[:kp, q0:q0 + n],
                                                            start=(i == 0), stop=(i == len(parts) - 1)),
                             reads=[kb_, qb_], writes=[ps_b], inc=(i == len(parts) - 1))
                    if pend is not None:
                        pv(*pend)
                    p_, p_b2, _ = pT.next()
                    r = kc - 4 * ti
                    if r >= 0 and kind == "fox":
                        x_, x_b, _ = exr.next()
                        c.op("dve", lambda: nc.vector.tensor_scalar(x_[:kn_, :n], ps[:kn_, :n], negm[:kn_, :], 0.0, ALU.add, ALU.min),
                             reads=[ps_b, negm_b], writes=[x_b])
                        c.op("act", lambda: nc.scalar.activation(p_[:kn_, :n], x_[:kn_, :n], AF.Exp), reads=[x_b], writes=[p_b2])
                    else:
                        c.op("act", lambda: nc.scalar.activation(p_[:kn_, :n], ps[:kn_, :n], AF.Exp, bias=negm[:kn_, :]),
                             reads=[ps_b, negm_b], writes=[p_b2])
                    if r >= 0:
                        c.op("dve", lambda: nc.vector.tensor_tensor(p_[:kn_, :n], p_[:kn_, :n], mk[:kn_, r, :n], ALU.mult),
                             reads=[p_b2, mk_b], writes=[p_b2])
                    pend = (kc, kn_, p_, p_b2)
                pv(*pend)
                r_, r_b, _ = rl.next()
                c.op("dve", lambda: nc.vector.reciprocal(r_[:, :n], pl[:, :n]), reads=[pl_b], writes=[r_b])
                o_, o_b, _ = osb.next()
                c.op("dve", lambda: nc.vector.tensor_tensor(o_[:, :n], po[:, :n], r_[:, :n], ALU.mult), reads=[po_b, r_b], writes=[o_b])
                head_norm_out([(o_[:, :n], o_b)], gain, gain_b, h, n, q0, orow)
            if rows_done is not None:
                rows_done(orow, 128, [blk_b[orow // 128]])
    c.wait_all("sp", [out_b])


_PROG = {}


def _prog(name, builder):
    if name not in _PROG:
        _PROG[name] = builder()
    return _PROG[name]


def _fmaj(a):
    T = a.shape[0]
    return np.ascontiguousarray(a.T.reshape(-1, 128, T).transpose(1, 0, 2))


def _unfm(a):
    return a.transpose(1, 0, 2).reshape(-1, a.shape[2]).T


def _pcol(v, n):
    return np.ascontiguousarray(np.asarray(v).reshape(n, 128).T)


def _p2_consts():
    kk = np.arange(128)[:, None]
    qq = np.arange(512)[None, :]
    mask4 = np.stack([(qq >= r * 128 + kk) for r in range(4)], 1).astype(NPBF)
    s = np.arange(64)[:, None]
    t = np.arange(64)[None, :]
    m16 = np.float32(-1.0 / 16.0)
    tri01 = (s <= t).astype(np.float32)
    tris64 = np.concatenate([(s <= t) * m16, np.full((64, 1), m16)], 1).astype(np.float32)
    tris16 = np.ascontiguousarray(np.concatenate([tris64[:16, :16], tris64[:16, 64:65]], 1))
    sus = ((s > t) * m16).astype(np.float32)
    return dict(mask4=mask4, tri01=tri01, tris64=tris64, tris16=tris16, sus=sus)


def _core_cols(g4):
    s = NMETA + g4 * OWN
    return np.concatenate([np.arange(s - 2, s + OWN), np.array([0, 0]), np.arange(0, NMETA)])


def kernel_unfused(x, meta_tokens, attn_norm, w_in, mla_q_norm, mla_w_uq, mla_kv_norm, mla_w_ukv,
           gla_w_gate2, gla_b_gate, fox_b_f, out_norm_mla, out_norm_gla, out_norm_fox,
           w_out, ffn_norm, ffn_w_up, ffn_conv_w, ffn_conv_b, ffn_w_down, final_norm):
    f32 = np.float32
    x = np.asarray(x, f32)
    B = x.shape[0]
    cores = list(range(8))
    h = np.concatenate([np.broadcast_to(np.asarray(meta_tokens, f32)[None], (B, NMETA, D)), x], axis=1)
    pos = np.arange(NT, dtype=f32)
    inv = (f32(1.0) / (f32(10000.0) ** (np.arange(0, 64, 2, dtype=f32) / f32(64)))).astype(f32)
    ang = (pos[:, None] * inv[None, :]).astype(f32)
    cosT, sinT = np.cos(ang).astype(f32).T, np.sin(ang).astype(f32).T
    zero_cols = np.array([2 + OWN, 3 + OWN])
    p2c = _p2_consts()
    p1, p2, p3 = _prog("p1", build_p1), _prog("p2", build_p2), _prog("p3", build_p3)
    y_final = None
    for l in range(2):
        uq3 = np.asarray(mla_w_uq[l]).reshape(1536, 12, 192)
        w_uq_p = np.ascontiguousarray(np.concatenate(
            [uq3[:, :, :128].reshape(1536, -1), uq3[:, :, 128:160].reshape(1536, -1), uq3[:, :, 160:].reshape(1536, -1)], 1))
        kv3 = np.asarray(mla_w_ukv[l]).reshape(512, 12, 256)
        w_ukv_p = np.ascontiguousarray(np.concatenate([kv3[:, :, :128].reshape(512, -1), kv3[:, :, 128:].reshape(512, -1)], 1))
        hTs = []
        maps = []
        for core in cores:
            b, g4 = divmod(core, 4)
            cols = _core_cols(g4)
            Hc = h[b][cols]
            Hc[zero_cols] = 0
            hT = _fmaj(Hc)
            hTs.append(hT)
            cs = cosT[:, cols].copy(); sn = sinT[:, cols].copy()
            maps.append(dict(hT=hT, w_in=np.asarray(w_in[l]), w_uq=w_uq_p, w_ukv=w_ukv_p,
                             g_attn=_pcol(attn_norm[l], 32), g_q=_pcol(mla_q_norm[l], 12), g_kv=_pcol(mla_kv_norm[l], 4),
                             cos4=np.ascontiguousarray(np.tile(cs, (4, 1))), sin4=np.ascontiguousarray(np.tile(sn, (4, 1)))))
        r1 = run_bass_kernel_spmd(p1, maps, core_ids=cores).results
        del maps
        maps = []
        for b in range(B):
            def gather_fm(name):
                parts = [r1[b * 4][name][:, 4 + OWN:4 + OWN + NMETA]] + [r1[b * 4 + g][name][:, 2:2 + OWN] for g in range(4)]
                return np.concatenate(parts, axis=1)

            def gather_tm(name):
                parts = [r1[b * 4][name][4 + OWN:4 + OWN + NMETA]] + [r1[b * 4 + g][name][2:2 + OWN] for g in range(4)]
                return np.concatenate(parts, axis=0)
            obf, o32, otf, otb = gather_fm("obf"), gather_fm("o32"), gather_tm("otf"), gather_tm("otb")
            for g in range(4):
                abf = np.concatenate([obf[RB_Q + 3 * g * 192:RB_Q + 3 * (g + 1) * 192],
                                      obf[RB_KN + 3 * g * 128:RB_KN + 3 * (g + 1) * 128],
                                      obf[RB_KPE:RB_KPE + 64],
                                      obf[RB_FQ + 3 * g * 128:RB_FQ + 3 * (g + 1) * 128],
                                      obf[RB_FK + 3 * g * 128:RB_FK + 3 * (g + 1) * 128]], 0)
                vbf = np.concatenate([otb[:, CB_VM + 3 * g * 128:CB_VM + 3 * (g + 1) * 128],
                                      otb[:, CB_FV + 3 * g * 128:CB_FV + 3 * (g + 1) * 128],
                                      otb[:, CB_GV + g * 256:CB_GV + (g + 1) * 256]], 1)
                ff = np.concatenate([o32[R32_GQ + g * 128:R32_GQ + (g + 1) * 128],
                                     o32[R32_GK + g * 128:R32_GK + (g + 1) * 128],
                                     o32[R32_GZ:R32_GZ + 16], np.ones((1, NT), f32),
                                     o32[R32_GR + g * 256:R32_GR + (g + 1) * 256],
                                     o32[R32_FZ + 3 * g:R32_FZ + 3 * (g + 1)]], 0)
                wg2 = np.concatenate([np.asarray(gla_w_gate2[l])[:, g * 128:(g + 1) * 128],
                                      np.asarray(gla_b_gate[l])[None, g * 128:(g + 1) * 128]], 0).astype(f32)
                maps.append(dict(
                    abf=np.ascontiguousarray(abf), vbf=np.ascontiguousarray(vbf), f32=np.ascontiguousarray(ff),
                    gkm=np.ascontiguousarray(otf[:, g * 128:(g + 1) * 128]), wg2=np.ascontiguousarray(wg2),
                    fbf=np.ascontiguousarray(np.asarray(fox_b_f[l], f32)[3 * g:3 * g + 3, None]),
                    gmla=np.ascontiguousarray(np.asarray(out_norm_mla[l], f32).reshape(12, 128)[3 * g:3 * g + 3].T),
                    gfox=np.ascontiguousarray(np.asarray(out_norm_fox[l], f32).reshape(12, 128)[3 * g:3 * g + 3].T),
                    ggla=np.ascontiguousarray(np.asarray(out_norm_gla[l], f32).reshape(4, 2, 128)[g].T),
                    **p2c))
        del r1
        r2 = run_bass_kernel_spmd(p2, maps, core_ids=cores).results
        del maps
        maps = []
        for b in range(B):
            om = np.empty((D, NT), NPBF)
            for g in range(4):
                o = r2[b * 4 + g]["oT"]
                om[3 * g * 128:3 * (g + 1) * 128] = o[0:384]
                om[1536 + g * 256:1536 + (g + 1) * 256] = o[384:640]
                om[2560 + 3 * g * 128:2560 + 3 * (g + 1) * 128] = o[640:1024]
            for g4 in range(4):
                cols = _core_cols(g4)
                oc = om[:, cols]
                oc[:, zero_cols] = 0
                oTc = np.ascontiguousarray(oc.reshape(32, 128, TL).transpose(1, 0, 2))
                gn = final_norm if l == 1 else attn_norm[1]
                cw = np.asarray(ffn_conv_w[l], f32)
                maps.append(dict(oT=oTc, hT=hTs[b * 4 + g4], w_out=np.asarray(w_out[l]), w_up=np.asarray(ffn_w_up[l]),
                                 w_down=np.asarray(ffn_w_down[l]), g_ffn=_pcol(ffn_norm[l], 32), g_next=_pcol(gn, 32),
                                 conv_w=np.ascontiguousarray(cw.T.reshape(172, 128, 3).transpose(1, 0, 2)),
                                 conv_b=_pcol(ffn_conv_b[l], 172)))
        del r2
        r3 = run_bass_kernel_spmd(p3, maps, core_ids=cores).results
        del maps
        for core in cores:
            b, g4 = divmod(core, 4)
            s = NMETA + g4 * OWN
            ho = _unfm(r3[core]["h_out"])
            h[b, s:s + OWN] = ho[2:2 + OWN]
            if g4 == 0:
                h[b, 0:NMETA] = ho[4 + OWN:4 + OWN + NMETA]
        if l == 1:
            y_final = np.empty((B, SEQ, D), f32)
            for core in cores:
                b, g4 = divmod(core, 4)
                y_final[b, g4 * OWN:(g4 + 1) * OWN] = _unfm(r3[core]["y_out"])[2:2 + OWN]
        del r3
    return y_final


class Gath:
    def __init__(self, nc, name, R, C, dtype, esz):
        rp = (1 << 20) // (C * esz)
        if rp >= 64:
            rp = (rp // 64) * 64
        self.C = C
        self.pieces = [(r0, min(rp, R - r0)) for r0 in range(0, R, rp)]
        self.g = [nc.dram_tensor(f"{name}_g{i}", [4 * n, C], dtype, kind="Internal").ap() for i, (r0, n) in enumerate(self.pieces)]
        self.buf = Buf()

    def gather(self, c, X, cs):
        for (r0, n), g in zip(self.pieces, self.g):
            c.allgather(g, X[r0:r0 + n, :], cs, writes=[self.buf])

    def gather_rows(self, c, X, cs, row0, n, reads):
        for (r0, pn), g in zip(self.pieces, self.g):
            if r0 >= row0 and r0 + pn <= row0 + n:
                c.allgather(g, X[r0:r0 + pn, :], cs, reads=reads, writes=[self.buf])

    def segs(self, row0, n):
        out = []
        for (r0, pn), g in zip(self.pieces, self.g):
            a, b = max(row0, r0), min(row0 + n, r0 + pn)
            if a < b:
                out.append((g.rearrange("(r n) c -> r n c", r=4)[:, a - r0:b - r0, :], a - row0, b - a))
        return out


def emit_select1(c, sel, G32, GBF, GTF, GTBg, GTBr, abf, vbf, f32, gkm, dst_b, part):
    nc, es = c.nc, c.es
    K = _consts(c, [("sel", sel, [128, 4])])
    sl, sl_b = K["sel"]
    fm_jobs = [(GBF, abf, BF16, [(A_Q, RB_Q, 576, 576), (A_KN, RB_KN, 384, 384), (A_KPE, RB_KPE, 64, 0),
                                 (A_FQ, RB_FQ, 384, 384), (A_FK, RB_FK, 384, 384)], "sb")] if part == "B" else \
              [(G32, f32, F32, [(F_GQ, R32_GQ, 128, 128), (F_GK, R32_GK, 128, 128), (F_GZ, R32_GZ, 16, 0),
                                (F_GR, R32_GR, 256, 256), (F_FZ, R32_FZ, 3, 3)], "sf")]
    for (G, dst, dtype, jobs, tag) in fm_jobs:
        cand = Ring(c, "cand" + tag, 4, [128, NT], dtype)
        accr = Ring(c, "acc" + tag, 2, [128, NT], dtype)
        for (d0, s0, nrows, stride) in jobs:
            for r0 in range(0, nrows, 128):
                n = min(128, nrows - r0)
                a, a_b, a_d = accr.next()
                for g in range(4):
                    t, b, d = cand.next()
                    srow = s0 + g * stride + r0
                    first = True
                    for (gv, p0, ln) in G.segs(srow, n):
                        c.dma("sp", t[p0:p0 + ln, NMETA:].rearrange("p (r t) -> p r t", r=4),
                              gv[:, :, 2:2 + OWN].rearrange("r p t -> p r t"), d, reads=[G.buf],
                              writes=[b] if first else [], pwrites=[] if first else [b])
                        first = False
                        c.dma("sp", t[p0:p0 + ln, :NMETA], gv[0, :, 4 + OWN:4 + OWN + NMETA], d, reads=[G.buf], pwrites=[b])
                    if g == 0:
                        c.op("dve", lambda: nc.vector.tensor_scalar(a[:n, :], t[:n, :], sl[:n, 0:1], None, ALU.mult),
                             reads=[b, sl_b], writes=[a_b])
                    else:
                        c.op("dve", lambda: nc.vector.scalar_tensor_tensor(a[:n, :], t[:n, :], sl[:n, g:g + 1], a[:n, :], ALU.mult, ALU.add),
                             reads=[b, sl_b, a_b], writes=[a_b])
                c.dma("sp", dst[d0 + r0:d0 + r0 + n, :], a[:n, :], a_d, reads=[a_b], pwrites=[dst_b])
    if part == "A":
        on = es.enter_context(nc.sbuf_tensor(c.nm("ones_row"), [1, NT], F32)); on_b = Buf()
        c.op("dve", lambda: nc.vector.memset(on[:], 1.0), writes=[on_b])
        c.dma("sp", f32[F_GZ + 16:F_GZ + 17, :], on[:], c.dsem("onr"), reads=[on_b], pwrites=[dst_b])
    chunks = [(0, 4 + OWN, NMETA, 0)] + [(r, 2 + 128 * i, 128, NMETA + OWN * r + 128 * i) for r in range(4) for i in range(8)]
    if part == "A":
        jobs = [(GTBg, 1024, BF16, vbf, [(V_GV, 0, 256, 256)], "tg"), (GTF, 512, F32, gkm, [(0, 0, 128, 128)], "tk")]
    else:
        jobs = [(GTBr, 3072, BF16, vbf, [(V_VM, CB_VM - CB_FV, 384, 384), (V_FV, 0, 384, 384)], "tr")]
    for (G, width, dtype, dst, blocks, tag) in jobs:
        candt = Ring(c, "cand" + tag, 3, [128, width], dtype)
        acct = Ring(c, "acc" + tag, 2, [128, 768], dtype)
        for (r, srow, n, drow) in chunks:
            a, a_b, a_d = acct.next()
            t, b, d = candt.next()
            first = True
            for (gv, p0, ln) in G.segs(srow, n):
                c.dma("sp", t[p0:p0 + ln, :], gv[r, :, :], d, reads=[G.buf], writes=[b] if first else [], pwrites=[] if first else [b])
                first = False
            for g in range(4):
                for bi, (dc, sc0, w, stride) in enumerate(blocks):
                    sc = sc0 + stride * g
                    ao = sum(bb[2] for bb in blocks[:bi])
                    if g == 0:
                        c.op("dve", lambda: nc.vector.tensor_scalar(a[:n, ao:ao + w], t[:n, sc:sc + w], sl[:n, 0:1], None, ALU.mult),
                             reads=[b, sl_b], pwrites=[a_b])
                    else:
                        c.op("dve", lambda: nc.vector.scalar_tensor_tensor(a[:n, ao:ao + w], t[:n, sc:sc + w], sl[:n, g:g + 1], a[:n, ao:ao + w], ALU.mult, ALU.add),
                             reads=[b, sl_b, a_b], pwrites=[a_b])
            for bi, (dc, sc0, w, stride) in enumerate(blocks):
                ao = sum(bb[2] for bb in blocks[:bi])
                c.dma("sp", dst[drow:drow + n, dc:dc + w], a[:n, ao:ao + w], a_d, reads=[a_b], pwrites=[dst_b])


def _omix_src(kc):
    if kc < 12:
        return kc // 3, (kc % 3) * 128
    if kc < 20:
        return (kc - 12) // 2, 384 + ((kc - 12) % 2) * 128
    return (kc - 20) // 3, 640 + ((kc - 20) % 3) * 128


def emit_select2(c, sel, GO, oT3, dst_b):
    nc, es = c.nc, c.es
    K = _consts(c, [("sel", sel, [128, 4])])
    sl, sl_b = K["sel"]
    cand = Ring(c, "cand2", 4, [128, 2 + OWN], BF16)
    accr = Ring(c, "acc2", 2, [128, TL], BF16)
    for i in range(2):
        c.op("dve", lambda: nc.vector.memset(accr.t[i][:], 0.0), writes=[accr.b[i]])
    for kc in range(32):
        rk, row0 = _omix_src(kc)
        a, a_b, a_d = accr.next()
        segs = GO.segs(row0, 128)
        for (gv, p0, ln) in segs:
            c.dma("sp", a[p0:p0 + ln, 4 + OWN:], gv[rk, :, 0:NMETA], a_d, reads=[GO.buf], pwrites=[a_b])
        for dd in range(4):
            t, b, d = cand.next()
            s0 = NMETA + OWN * dd - 2
            first = True
            for (gv, p0, ln) in segs:
                c.dma("sp", t[p0:p0 + ln, :], gv[rk, :, s0:s0 + 2 + OWN], d, reads=[GO.buf],
                      writes=[b] if first else [], pwrites=[] if first else [b])
                first = False
            if dd == 0:
                c.op("dve", lambda: nc.vector.tensor_scalar(a[:, :2 + OWN], t[:], sl[:, 0:1], None, ALU.mult), reads=[b, sl_b], pwrites=[a_b])
            else:
                c.op("dve", lambda: nc.vector.scalar_tensor_tensor(a[:, :2 + OWN], t[:], sl[:, dd:dd + 1], a[:, :2 + OWN], ALU.mult, ALU.add),
                     reads=[b, sl_b, a_b], pwrites=[a_b])
        c.dma("sp", oT3[:, kc, :], a[:], a_d, reads=[a_b], pwrites=[dst_b])


def emit_halo(c, selh, hbuf, h_b, tail, g_tail, cs):
    nc, es = c.nc, c.es
    K = _consts(c, [("selh", selh, [128, 5])])
    sh, sh_b = K["selh"]
    tail_b, gt_b = Buf(), Buf()
    d = c.dsem("halo")
    c.dma("sp", tail.rearrange("p (k t) -> p k t", t=2), hbuf[:, :, OWN:OWN + 2], d, reads=[h_b], writes=[tail_b])
    c.allgather(g_tail, tail, cs, reads=[tail_b], writes=[gt_b])
    cnd = es.enter_context(nc.sbuf_tensor(c.nm("hcand"), [128, 5, 64], F32)); cnd_b = Buf()
    d2 = c.dsem("halo2")
    c.dma("sp", cnd[:, 0:4, :], g_tail.rearrange("(r p) n -> p r n", r=4), d2, reads=[gt_b], writes=[cnd_b])
    c.dma("sp", cnd[:, 4, :].rearrange("p (k t) -> p k t", t=2), hbuf[:, :, TL - 2:TL], d2, reads=[h_b], pwrites=[cnd_b])
    Ctx.seal(d2, [cnd_b])
    acc = es.enter_context(nc.sbuf_tensor(c.nm("hacc"), [128, 64], F32)); acc_b = Buf()
    zz = es.enter_context(nc.sbuf_tensor(c.nm("hzero"), [128, 64], F32)); zz_b = Buf()
    c.op("dve", lambda: nc.vector.memset(zz[:], 0.0), writes=[zz_b])
    c.op("dve", lambda: nc.vector.tensor_scalar(acc[:], cnd[:, 0, :], sh[:, 0:1], None, ALU.mult), reads=[cnd_b, sh_b], writes=[acc_b])
    for i in range(1, 5):
        c.op("dve", lambda: nc.vector.scalar_tensor_tensor(acc[:], cnd[:, i, :], sh[:, i:i + 1], acc[:], ALU.mult, ALU.add),
             reads=[cnd_b, sh_b, acc_b], writes=[acc_b])
    d3 = c.dsem("halo3")
    c.dma("sp", hbuf[:, :, 0:2], acc[:].rearrange("p (k t) -> p k t", t=2), d3, reads=[acc_b, tail_b, cnd_b], pwrites=[h_b])
    c.dma("sp", hbuf[:, :, 2 + OWN:4 + OWN], zz[:].rearrange("p (k t) -> p k t", t=2), d3, reads=[zz_b], pwrites=[h_b])


def build_fused():
    nc = bass.Bass("TRN2", target_bir_lowering=False)
    dt = lambda n, s, d=F32, k="ExternalInput": nc.dram_tensor(n, s, d, kind=k).ap()
    I = lambda n, s, d=F32: nc.dram_tensor(n, s, d, kind="Internal").ap()
    hT0 = dt("hT0", [128, 32, TL])
    cos4, sin4 = dt("cos4", [128, TL]), dt("sin4", [128, TL])
    sel, selh = dt("sel", [128, 4]), dt("selh", [128, 5])
    w_in = dt("w_in", [2, D, DIN]); w_uq = dt("w_uq", [2, 1536, 2304]); w_ukv = dt("w_ukv", [2, 512, 3072])
    w_out = dt("w_out", [2, D, D]); w_up = dt("w_up", [2, D, 2 * DFF]); w_down = dt("w_down", [2, DFF, D])
    g_attn, g_q, g_kv = dt("g_attn", [2, 128, 32]), dt("g_q", [2, 128, 12]), dt("g_kv", [2, 128, 4])
    g_ffn, g_next = dt("g_ffn", [2, 128, 32]), dt("g_next", [2, 128, 32])
    conv_w, conv_b = dt("conv_w", [2, 128, 172, 3]), dt("conv_b", [2, 128, 172])
    wg2, fbf = dt("wg2", [2, 17, 128]), dt("fbf", [2, 3, 1])
    gmla, gfox, ggla = dt("gmla", [2, 128, 3]), dt("gfox", [2, 128, 3]), dt("ggla", [2, 128, 2])
    mask4 = dt("mask4", [128, 4, 512], BF16)
    tri01, tris64, tris16, sus = dt("tri01", [64, 64]), dt("tris64", [64, 65]), dt("tris16", [16, 17]), dt("sus", [64, 64])
    y_out = dt("y_out", [128, 32, TL], F32, "ExternalOutput")
    o32, obf, otf, otb = I("o32", [N32, TL]), I("obf", [NB, TL], BF16), I("otf", [TL, 512]), I("otb", [TL, NCB], BF16)
    G32, GBF = Gath(nc, "o32", N32, TL, F32, 4), Gath(nc, "obf", NB, TL, BF16, 2)
    GTF = Gath(nc, "otf", TL, 512, F32, 4)
    otbG, otbR = I("otbG", [TL, 1024], BF16), I("otbR", [TL, 3072], BF16)
    GTBg, GTBr = Gath(nc, "otbG", TL, 1024, BF16, 2), Gath(nc, "otbR", TL, 3072, BF16, 2)
    GO = Gath(nc, "oT2", 1024, NT, BF16, 2)
    abf, vbf, f32, gkm = I("abf", [NA, NT], BF16), I("vbf", [NT, NV], BF16), I("f32s", [NF, NT]), I("gkm", [NT, 128])
    aug = I("aug_scr", [3, 12, NT], BF16)
    oT2, oT3 = I("oT2", [1024, NT], BF16), I("oT3", [128, 32, TL], BF16)
    h_mid, act = I("h_mid", [128, 32, TL]), I("act_scr", [86, 128, TL], BF16)
    hA, hB = I("hA", [128, 32, TL]), I("hB", [128, 32, TL])
    tail, g_tail = I("tail", [128, 64]), I("g_tail", [4 * 128, 64])
    with ExitStack() as es:
        c = Ctx(nc, es)
        cs = c.dsem("coll")
        c.phase_dsems.remove(cs)

        def phase(fn, exclude=()):
            with ExitStack() as pes:
                c.es = pes
                fn()
                c.es = c.sem_es
            c.end_phase(exclude)

        csA, csB, cs2 = c.dsem("collA"), c.dsem("collB"), c.dsem("coll2")
        for x_ in (csA, csB, cs2):
            c.phase_dsems.remove(x_)
        c.soft = {id(cs2.h)}
        hcur = hT0
        for l in range(2):
            phase(lambda: emit_p1(c, hcur, w_in[l], w_uq[l], w_ukv[l], g_attn[l], g_q[l], g_kv[l], cos4, sin4, o32, obf, otf, otb,
                                  otb_split=(otbG, otbR)))
            for (G_, x_) in ((G32, o32), (GTF, otf), (GTBg, otbG)):
                G_.gather(c, x_, csA)
            for (G_, x_) in ((GBF, obf), (GTBr, otbR)):
                G_.gather(c, x_, csB)
            db = Buf()
            phase(lambda: emit_select1(c, sel, G32, GBF, GTF, GTBg, GTBr, abf, vbf, f32, gkm, db, "A"), exclude={id(csB.h)})
            phase(lambda: emit_p2(c, abf, vbf, f32, gkm, wg2[l], fbf[l], gmla[l], gfox[l], ggla[l], mask4, tri01, tris64, tris16, sus, oT2, aug,
                                  mid_hook=lambda: emit_select1(c, sel, G32, GBF, GTF, GTBg, GTBr, abf, vbf, f32, gkm, db, "B"),
                                  rows_done=lambda r0, n, bufs: GO.gather_rows(c, oT2, cs2, r0, n, bufs)))
            db2 = Buf()
            phase(lambda: emit_select2(c, sel, GO, oT3, db2))
            hnext = y_out if False else (hA if l == 0 else hB)
            phase(lambda: emit_p3(c, oT3, hcur, w_out[l], w_up[l], w_down[l], g_ffn[l], g_next[l], conv_w[l], conv_b[l],
                                  hnext, y_out, h_mid, act))
            if l == 0:
                hb_ = Buf()
                phase(lambda: emit_halo(c, selh, hnext, hb_, tail, g_tail, cs))
            hcur = hnext
        c.barrier()
        print("fused program instructions:", c.n_inst)
    return nc


def kernel(x, meta_tokens, attn_norm, w_in, mla_q_norm, mla_w_uq, mla_kv_norm, mla_w_ukv,
           gla_w_gate2, gla_b_gate, fox_b_f, out_norm_mla, out_norm_gla, out_norm_fox,
           w_out, ffn_norm, ffn_w_up, ffn_conv_w, ffn_conv_b, ffn_w_down, final_norm):
    f32 = np.float32
    x = np.asarray(x, f32)
    B = x.shape[0]
    cores = list(range(8))
    h = np.concatenate([np.broadcast_to(np.asarray(meta_tokens, f32)[None], (B, NMETA, D)), x], axis=1)
    pos = np.arange(NT, dtype=f32)
    inv = (f32(1.0) / (f32(10000.0) ** (np.arange(0, 64, 2, dtype=f32) / f32(64)))).astype(f32)
    ang = (pos[:, None] * inv[None, :]).astype(f32)
    cosT, sinT = np.cos(ang).astype(f32).T, np.sin(ang).astype(f32).T
    zero_cols = np.array([2 + OWN, 3 + OWN])
    A = lambda v: np.asarray(v, f32)
    uq3 = A(mla_w_uq).reshape(2, 1536, 12, 192)
    w_uq_p = np.ascontiguousarray(np.concatenate(
        [uq3[..., :128].reshape(2, 1536, -1), uq3[..., 128:160].reshape(2, 1536, -1), uq3[..., 160:].reshape(2, 1536, -1)], 2))
    kv3 = A(mla_w_ukv).reshape(2, 512, 12, 256)
    w_ukv_p = np.ascontiguousarray(np.concatenate([kv3[..., :128].reshape(2, 512, -1), kv3[..., 128:].reshape(2, 512, -1)], 2))
    cw = A(ffn_conv_w)
    shared = dict(
        w_in=A(w_in), w_uq=w_uq_p, w_ukv=w_ukv_p, w_out=A(w_out), w_up=A(ffn_w_up), w_down=A(ffn_w_down),
        g_attn=np.stack([_pcol(attn_norm[l], 32) for l in range(2)]),
        g_q=np.stack([_pcol(mla_q_norm[l], 12) for l in range(2)]),
        g_kv=np.stack([_pcol(mla_kv_norm[l], 4) for l in range(2)]),
        g_ffn=np.stack([_pcol(ffn_norm[l], 32) for l in range(2)]),
        g_next=np.stack([_pcol(attn_norm[1], 32), _pcol(final_norm, 32)]),
        conv_w=np.stack([np.ascontiguousarray(cw[l].T.reshape(172, 128, 3).transpose(1, 0, 2)) for l in range(2)]),
        conv_b=np.stack([_pcol(ffn_conv_b[l], 172) for l in range(2)]),
        **_p2_consts())
    maps = []
    for core in cores:
        b, g = divmod(core, 4)
        cols = _core_cols(g)
        Hc = h[b][cols]
        Hc[zero_cols] = 0
        sel = np.zeros((128, 4), f32); sel[:, g] = 1
        selh = np.zeros((128, 5), f32); selh[:, 4 if g == 0 else g - 1] = 1
        m = dict(shared)
        m.update(
            hT0=_fmaj(Hc),
            cos4=np.ascontiguousarray(np.tile(cosT[:, cols], (4, 1))), sin4=np.ascontiguousarray(np.tile(sinT[:, cols], (4, 1))),
            sel=sel, selh=selh,
            wg2=np.stack([np.concatenate([A(gla_w_gate2[l])[:, g * 128:(g + 1) * 128], A(gla_b_gate[l])[None, g * 128:(g + 1) * 128]], 0)
                          for l in range(2)]),
            fbf=np.stack([A(fox_b_f[l])[3 * g:3 * g + 3, None] for l in range(2)]),
            gmla=np.stack([np.ascontiguousarray(A(out_norm_mla[l]).reshape(12, 128)[3 * g:3 * g + 3].T) for l in range(2)]),
            gfox=np.stack([np.ascontiguousarray(A(out_norm_fox[l]).reshape(12, 128)[3 * g:3 * g + 3].T) for l in range(2)]),
            ggla=np.stack([np.ascontiguousarray(A(out_norm_gla[l]).reshape(4, 2, 128)[g].T) for l in range(2)]))
        maps.append(m)
    res = run_bass_kernel_spmd(_prog("fused", build_fused), maps, core_ids=cores).results
    y = np.empty((B, SEQ, D), f32)
    for core in cores:
        b, g = divmod(core, 4)
        y[b, g * OWN:(g + 1) * OWN] = _unfm(res[core]["y_out"])[2:2 + OWN]
    return y
```

```python
import numpy as np
from contextlib import ExitStack
import ml_dtypes
import concourse.bass as bass
import concourse.mybir as mybir
from concourse.bass_utils import run_bass_kernel_spmd

F32 = mybir.dt.float32
BF16 = mybir.dt.bfloat16
AF = mybir.ActivationFunctionType
ALU = mybir.AluOpType
AX = mybir.AxisListType
NPBF = ml_dtypes.bfloat16

D = 4096
SEQ = 4096
NMETA = 16
OWN = 1024
TL = 2 + OWN + 2 + NMETA
TT = [(0, 512), (512, 512), (1024, TL - 1024)]
TM = [(i * 128, 128) for i in range(8)] + [(1024, TL - 1024)]
NT = NMETA + SEQ
EPS = 1e-6
DFF = 11008
GW = 256


def fm_groups(col0, nchunks, hf):
    per = GW // 128
    return [(col0 + g * GW, min(GW, (nchunks - g * per) * 128),
             [(j * 128, 128, hf(g * per + j)) for j in range(min(per, nchunks - g * per))])
            for g in range((nchunks + per - 1) // per)]


def tm_groups(col0, ncols, hf):
    return [(col0 + g * GW, min(GW, ncols - g * GW), hf(g * GW)) for g in range((ncols + GW - 1) // GW)]

O_CQ, O_CKV, O_KR, O_GQ, O_GK, O_GV, O_GZ, O_GR, O_FQ, O_FK, O_FV, O_FZ = (
    0, 1536, 2048, 2112, 2624, 3136, 4160, 4176, 5200, 6736, 8272, 9808)
DIN = 9820
R32_GQ, R32_GK, R32_GZ, R32_GR, R32_FZ, N32 = 0, 512, 1024, 1040, 2064, 2076
RB_Q, RB_KN, RB_KPE, RB_FQ, RB_FK, NB = 0, 2304, 3840, 3904, 5440, 6976
CB_GV, CB_FV, CB_VM, NCB = 0, 1024, 2560, 4096


class Buf:
    __slots__ = ("w", "r")

    def __init__(self):
        self.w = {}
        self.r = {}


def fresh(olds):
    b = Buf()
    for o in olds:
        for t in list(o.w.values()) + list(o.r.values()):
            Ctx._add(b.r, t)
    return b


class DSem:
    __slots__ = ("h", "v")

    def __init__(self, h):
        self.h = h
        self.v = 0


class Ctx:
    CE = ("pe", "act", "dve", "pool")

    def __init__(self, nc, es):
        self.nc = nc
        self.es = es
        self.sem_es = es
        self.eng = {"pe": nc.tensor, "act": nc.scalar, "dve": nc.vector,
                    "pool": nc.gpsimd, "sp": nc.sync}
        self.sem = {}
        self.cnt = {}
        self.nsem = 0
        self.latest = {}
        self.free_dsems = []
        self.phase_dsems = []
        self.phase = 0
        for e in self.CE:
            self._new_engine_sem(e)
        self.seen = {e: {} for e in self.eng}
        self.pe_pending = False
        self.n_inst = 0

    def _new_engine_sem(self, e):
        self.nsem += 1
        self.sem[e] = self.sem_es.enter_context(self.nc.semaphore(f"s_{e}_{self.nsem}"))
        self.cnt[e] = 0

    def nm(self, name):
        return f"{name}_p{self.phase}"

    def dsem(self, name):
        if self.free_dsems:
            d = self.free_dsems.pop()
        else:
            self.nsem += 1
            d = DSem(self.sem_es.enter_context(self.nc.semaphore(f"d_{name}_{self.nsem}")))
        self.phase_dsems.append(d)
        return d

    def barrier(self, exclude=()):
        assert not self.pe_pending
        for e in self.eng:
            own = id(self.sem[e]) if e in self.sem else None
            deps = {k: t for k, t in self.latest.items() if k != own and k not in exclude}
            self._emit_waits(e, deps)

    def end_phase(self, exclude=()):
        self.barrier(exclude)
        self.free_dsems.extend(self.phase_dsems)
        self.phase_dsems = []
        self.phase += 1

    def allgather(self, out, in_, cs, reads=(), writes=()):
        deps = self._collect("pool", reads, writes)
        self._emit_waits("pool", deps)
        ins = self.nc.gpsimd.collective_compute("AllGather", ALU.bypass, replica_groups=[[0, 1, 2, 3], [4, 5, 6, 7]],
                                                ins=[in_], outs=[out])
        self.n_inst += 1
        cs.v += 1
        ins.then_inc(cs.h)
        t = (cs.h, cs.v)
        self.latest[id(cs.h)] = t
        self._record(t, reads, writes)
        return ins

    @staticmethod
    def _add(deps, t):
        k = id(t[0])
        if k not in deps or deps[k][1] < t[1]:
            deps[k] = t

    def _collect(self, e, reads, writes, pwrites=()):
        deps = {}
        own = id(self.sem[e]) if e in self.sem else None
        for b in pwrites:
            for t in b.r.values():
                self._add(deps, t)
        for b in reads:
            for t in b.w.values():
                if id(t[0]) == own and e == "pe":
                    continue
                self._add(deps, t)
        for b in writes:
            for t in b.w.values():
                if id(t[0]) == own:
                    continue
                self._add(deps, t)
            for t in b.r.values():
                if id(t[0]) == own:
                    continue
                self._add(deps, t)
        return deps

    def _emit_waits(self, e, deps):
        seen = self.seen[e]
        for k, (s, v) in deps.items():
            if seen.get(k, 0) >= v:
                continue
            self.eng[e].wait_ge(s, v)
            self.n_inst += 1
            seen[k] = v

    def _record(self, t, reads, writes, pwrites=()):
        k = id(t[0])
        for b in pwrites:
            if k not in b.w or b.w[k][1] < t[1]:
                b.w[k] = t
        for b in reads:
            if k not in b.r or b.r[k][1] < t[1]:
                b.r[k] = t
        for b in writes:
            b.w = {k: t}
            b.r = {}

    def op(self, e, fn, reads=(), writes=(), inc=True, pwrites=()):
        deps = self._collect(e, reads, writes, pwrites)
        self._emit_waits(e, deps)
        ins = fn()
        self.n_inst += 1
        if inc:
            if self.cnt[e] >= 30000 and not (e == "pe" and self.pe_pending):
                self._new_engine_sem(e)
            self.cnt[e] += 1
            ins.then_inc(self.sem[e], 1)
            t = (self.sem[e], self.cnt[e])
            self.latest[id(t[0])] = t
            if e == "pe":
                self.pe_pending = False
        else:
            assert e == "pe"
            t = (self.sem[e], self.cnt[e] + 1)
            self.pe_pending = True
        self._record(t, reads, writes, pwrites)
        return ins

    def dma(self, q, out, in_, ds, reads=(), writes=(), pwrites=(), **kw):
        deps = self._collect(q, reads, writes, pwrites)
        self._emit_waits(q, deps)
        ins = self.eng[q].dma_start(out=out, in_=in_, **kw)
        self.n_inst += 1
        ds.v += 16
        ins.then_inc(ds.h, 16)
        self.latest[id(ds.h)] = (ds.h, ds.v)
        self._record((ds.h, ds.v), reads, writes, pwrites)
        return ins

    @staticmethod
    def seal(ds, bufs):
        k = id(ds.h)
        for b in bufs:
            if k in b.w:
                b.w[k] = (ds.h, ds.v)

    def wait_all(self, e, bufs):
        deps = {}
        for b in bufs:
            for t in b.w.values():
                self._add(deps, t)
        self._emit_waits(e, deps)


class Ring:
    def __init__(self, c, name, n, shape, dtype, es=None):
        es = es or c.es
        self.t = [es.enter_context(c.nc.sbuf_tensor(c.nm(f"{name}{i}"), shape, dtype)) for i in range(n)]
        self.b = [Buf() for _ in range(n)]
        self.d = [c.dsem(f"{name}{i}") for i in range(n)]
        self.i = -1
        self.n = n

    def next(self):
        self.i = (self.i + 1) % self.n
        return self.t[self.i], self.b[self.i], self.d[self.i]


class Dense:
    def __init__(self, c, kcmax=32):
        nc, es = c.nc, c.es
        self.c = c
        self.ws = Ring(c, "ws", 2, [128, kcmax * GW], BF16)
        self.psA = [es.enter_context(nc.psum_tensor(c.nm(f"psA{i}"), [128, 512], F32)) for i in range(2)]
        self.psB = [es.enter_context(nc.psum_tensor(c.nm(f"psB{i}"), [128, 512], F32)) for i in range(2)]
        self.psC = es.enter_context(nc.psum_tensor(c.nm("psC"), [128, 512], F32))
        self.bA = [Buf(), Buf()]
        self.bB = [Buf(), Buf()]
        self.bC = Buf()
        self.psT = [es.enter_context(nc.psum_tensor(c.nm(f"psT{i}"), [128, 512], F32)) for i in range(2)]
        self.bT = [Buf(), Buf()]
        self.psS = es.enter_context(nc.psum_tensor(c.nm("psS"), [128, 512], F32))
        self.bS = Buf()
        self.it = 0
        self.itT = 0

    def load(self, W, KC, col0, ncols, row0=0):
        c = self.c
        t, b, d = self.ws.next()
        t = t[:, :KC * ncols].rearrange("p (k m) -> p k m", m=ncols)
        Wv = W[row0:row0 + KC * 128, col0:col0 + ncols].rearrange("(kc p) m -> p kc m", p=128)
        step = 8
        for i, k0 in enumerate(range(0, KC, step)):
            k1 = min(KC, k0 + step)
            c.dma("pool", t[:, k0:k1, :], Wv[:, k0:k1, :], d,
                  writes=[b] if i == 0 else [], pwrites=[b] if i else [])
        return t, b

    def fm(self, W, KC, act, act_b, groups, tts=TT, row0=0):
        c, nc = self.c, self.c.nc
        for (col0, ncols, chunks) in groups:
            wt, wb = self.load(W, KC, col0, ncols, row0)
            for (off, M, handler) in chunks:
                pb = self.it % 2
                self.it += 1
                pst = [(self.psA[pb], self.bA[pb]), (self.psB[pb], self.bB[pb]), (self.psC, self.bC)]
                for kc in range(KC):
                    for j, (t0, tn) in enumerate(tts):
                        last = kc == KC - 1
                        ps, pbuf = pst[j]
                        c.op("pe", lambda: nc.tensor.matmul(ps[:M, :tn], wt[:, kc, off:off + M], act[:, kc, t0:t0 + tn],
                                                            start=(kc == 0), stop=last),
                             reads=[wb, act_b[kc]], writes=[pbuf], inc=last)
                handler([(pst[j][0][:M, :tn], pst[j][1], t0, tn) for j, (t0, tn) in enumerate(tts)])

    def tm(self, W, KC, act, act_b, groups, tms=TM, row0=0):
        c, nc = self.c, self.c.nc
        for (col0, ncols, handler) in groups:
            wt, wb = self.load(W, KC, col0, ncols, row0)
            for ti, (t0, tn) in enumerate(tms):
                pb = self.itT % 2
                self.itT += 1
                ps, pbuf = self.psT[pb], self.bT[pb]
                for kc in range(KC):
                    last = kc == KC - 1
                    c.op("pe", lambda: nc.tensor.matmul(ps[:tn, :ncols], act[:, kc, t0:t0 + tn], wt[:, kc, :ncols],
                                                        start=(kc == 0), stop=last),
                         reads=[wb, act_b[kc]], writes=[pbuf], inc=last)
                handler(ps[:tn, :ncols], pbuf, ti, t0, tn)


def _consts(c, names_shapes, es=None, olds=()):
    out = {}
    es = es or c.es
    ds = c.dsem("consts")
    for name, ap, shape in names_shapes:
        t = es.enter_context(c.nc.sbuf_tensor(c.nm("k_" + name), shape, F32))
        b = fresh(olds)
        c.dma("sp", t[:], ap, ds, writes=[b])
        out[name] = (t, b)
    Ctx.seal(ds, [b for (_, b) in out.values()])
    return out


def col_stats(c, dn, acc, acc_b, rstd, rstd_b, n_feat, post_scale=1.0, tts=TT):
    nc = c.nc
    for (t0, tn) in tts:
        c.op("pe", lambda: nc.tensor.matmul(dn.psS[:, :tn], dn.ones[:], acc[:, t0:t0 + tn], start=True, stop=True),
             reads=[acc_b, dn.ones_b], writes=[dn.bS])
        c.op("dve", lambda: nc.vector.tensor_scalar(rstd[:, t0:t0 + tn], dn.psS[:, :tn], 1.0 / n_feat, EPS,
                                                    ALU.mult, ALU.add),
             reads=[dn.bS], pwrites=[rstd_b])
    c.op("act", lambda: nc.scalar.activation(rstd[:], rstd[:], AF.Sqrt, scale=float(1.0 / post_scale ** 2)),
         reads=[rstd_b], writes=[rstd_b])
    c.op("dve", lambda: nc.vector.reciprocal(rstd[:], rstd[:]), reads=[rstd_b], writes=[rstd_b])


def build_p1():
    nc = bass.Bass("TRN2", target_bir_lowering=False)
    dt = lambda n, s, d=F32, k="ExternalInput": nc.dram_tensor(n, s, d, kind=k).ap()
    hT = dt("hT", [128, 32, TL])
    w_in = dt("w_in", [D, DIN])
    w_uq = dt("w_uq", [1536, 2304])
    w_ukv = dt("w_ukv", [512, 3072])
    g_attn = dt("g_attn", [128, 32])
    g_q = dt("g_q", [128, 12])
    g_kv = dt("g_kv", [128, 4])
    cos4 = dt("cos4", [128, TL])
    sin4 = dt("sin4", [128, TL])
    o32 = dt("o32", [N32, TL], F32, "ExternalOutput")
    obf = dt("obf", [NB, TL], BF16, "ExternalOutput")
    otf = dt("otf", [TL, 512], F32, "ExternalOutput")
    otb = dt("otb", [TL, NCB], BF16, "ExternalOutput")
    with ExitStack() as es:
        c = Ctx(nc, es)
        emit_p1(c, hT, w_in, w_uq, w_ukv, g_attn, g_q, g_kv, cos4, sin4, o32, obf, otf, otb)
    return nc


def emit_p1(c, hT, w_in, w_uq, w_ukv, g_attn, g_q, g_kv, cos4, sin4, o32, obf, otf, otb, otb_split=None, after_win=None):
    nc, es = c.nc, c.es
    sb = lambda n, s, d=F32, st=None: (st or es).enter_context(nc.sbuf_tensor(c.nm(n), s, d))
    dn = Dense(c)
    K = _consts(c, [("g_attn", g_attn, [128, 32]), ("g_q", g_q, [128, 12]), ("g_kv", g_kv, [128, 4])])
    dn.ones = sb("ones", [128, 128])
    dn.ones_b = Buf()
    c.op("dve", lambda: nc.vector.memset(dn.ones[:], 1.0), writes=[dn.ones_b])
    sq = Ring(c, "sq", 2, [128, TL], F32)
    st32 = Ring(c, "st32", 2, [128, TL], F32)
    stbf = Ring(c, "stbf", 3, [128, TL], BF16)
    ttf = Ring(c, "ttf", 2, [128, GW], F32)
    ttb = Ring(c, "ttb", 3, [128, GW], BF16)
    out_b = Buf()
    cqn = sb("cqn", [128, 12, TL], BF16); cqn_b = [Buf() for _ in range(12)]
    ckn = sb("ckn", [128, 4, TL], BF16); ckn_b = [Buf() for _ in range(4)]
    accq = sb("accq", [128, TL]); accq_b = Buf()
    acck = sb("acck", [128, TL]); acck_b = Buf()
    kr = [sb("kr1", [32, TL]), sb("kr2", [32, TL])]
    kr_b = [Buf(), Buf()]
    es_hn = ExitStack()
    hn = sb("hn", [128, 32, TL], BF16, es_hn)
    hn_b = [Buf() for _ in range(32)]
    with ExitStack() as es1:
        hp = Ring(c, "hp", 2, [128, 1, TL], F32, es1)
        acc = sb("acc", [128, TL], F32, es1); acc_b = Buf()
        rstd = sb("rstd", [128, TL], F32, es1); rstd_b = Buf()
        c.op("dve", lambda: nc.vector.memset(acc[:], 0.0), writes=[acc_b])
        for g in range(32):
            t, b, d = hp.next()
            c.dma("sp", t[:], hT[:, g:g + 1, :], d, writes=[b])
            for i in range(1):
                s, s_b, _ = sq.next()
                c.op("act", lambda: nc.scalar.activation(s[:], t[:, i, :], AF.Square), reads=[b], writes=[s_b])
                c.op("dve", lambda: nc.vector.tensor_tensor(acc[:], acc[:], s[:], ALU.add), reads=[acc_b, s_b], writes=[acc_b])
        col_stats(c, dn, acc, acc_b, rstd, rstd_b, D)
        ga, ga_b = K["g_attn"]
        for g in range(32):
            t, b, d = hp.next()
            c.dma("sp", t[:], hT[:, g:g + 1, :], d, writes=[b])
            for i in range(1):
                kc = g + i
                c.op("dve", lambda: nc.vector.scalar_tensor_tensor(hn[:, kc, :], t[:, i, :], ga[:, kc:kc + 1], rstd[:],
                                                                   ALU.mult, ALU.mult),
                     reads=[b, ga_b, rstd_b], writes=[hn_b[kc]])


    def out_fm(dst, row0, func=None, scale=1.0, dtype=F32):
        def h(tiles):
            t, b, d = (st32 if dtype == F32 else stbf).next()
            M = None
            for (ps, pb, t0, tn) in tiles:
                M = ps.shape[0]
                c.op("act", lambda: nc.scalar.activation(t[:M, t0:t0 + tn], ps, func or AF.Identity, scale=float(scale)),
                     reads=[pb], pwrites=[b])
            c.dma("sp", dst[row0:row0 + M, :], t[:M, :], d, reads=[b], pwrites=[out_b])
        return h

    c.op("dve", lambda: nc.vector.memset(accq[:], 0.0), writes=[accq_b])
    c.op("dve", lambda: nc.vector.memset(acck[:], 0.0), writes=[acck_b])

    def lat(dstt, dst_b, i, gain, gain_b, ac, ac_b):
        def h(tiles):
            s, s_b, _ = sq.next()
            for (ps, pb, t0, tn) in tiles:
                c.op("act", lambda: nc.scalar.activation(dstt[:, i, t0:t0 + tn], ps, AF.Identity, scale=gain[:, i:i + 1]),
                     reads=[pb, gain_b], pwrites=[dst_b[i]])
                c.op("act", lambda: nc.scalar.activation(s[:, t0:t0 + tn], ps, AF.Square), reads=[pb], pwrites=[s_b])
            c.op("dve", lambda: nc.vector.tensor_tensor(ac[:], ac[:], s[:], ALU.add), reads=[ac_b, s_b], writes=[ac_b])
        return h

    def krope(i):
        def h(tiles):
            for (ps, pb, t0, tn) in tiles:
                c.op("act", lambda: nc.scalar.copy(kr[i][:, t0:t0 + tn], ps), reads=[pb], pwrites=[kr_b[i]])
        return h

    gq, gq_b = K["g_q"]
    gk, gk_b = K["g_kv"]
    groups = []
    groups += fm_groups(O_CQ, 12, lambda i: lat(cqn, cqn_b, i, gq, gq_b, accq, accq_b))
    groups += fm_groups(O_CKV, 4, lambda i: lat(ckn, ckn_b, i, gk, gk_b, acck, acck_b))
    groups.append((O_KR, 64, [(0, 32, krope(0)), (32, 32, krope(1))]))
    groups += fm_groups(O_GQ, 4, lambda i: out_fm(o32, R32_GQ + i * 128, scale=128 ** -0.5))
    groups += fm_groups(O_GK, 4, lambda i: out_fm(o32, R32_GK + i * 128))
    groups.append((O_GZ, 16, [(0, 16, out_fm(o32, R32_GZ))]))
    groups += fm_groups(O_GR, 8, lambda i: out_fm(o32, R32_GR + i * 128, func=AF.Silu))
    groups += fm_groups(O_FQ, 12, lambda i: out_fm(obf, RB_FQ + i * 128, scale=128 ** -0.5, dtype=BF16))
    groups += fm_groups(O_FK, 12, lambda i: out_fm(obf, RB_FK + i * 128, dtype=BF16))
    groups.append((O_FZ, 12, [(0, 12, out_fm(o32, R32_FZ))]))
    dn.fm(w_in, 32, hn, hn_b, groups)


    def out_tm(dst, col0, ring):
        def h(ps, pb, ti, t0, tn):
            t, b, d = ring.next()
            n = ps.shape[1]
            c.op("act", lambda: nc.scalar.copy(t[:tn, :n], ps), reads=[pb], writes=[b])
            dd, cc = dst, col0
            if otb_split is not None and dst is otb:
                dd, cc = (otb_split[0], col0) if col0 < CB_FV else (otb_split[1], col0 - CB_FV)
            c.dma("sp", dd[t0:t0 + tn, cc:cc + n], t[:tn, :n], d, reads=[b], pwrites=[out_b])
        return h

    tg = tm_groups(O_GK, 512, lambda o: out_tm(otf, o, ttf))
    tg += tm_groups(O_GV, 1024, lambda o: out_tm(otb, CB_GV + o, ttb))
    tg += tm_groups(O_FV, 1536, lambda o: out_tm(otb, CB_FV + o, ttb))
    dn.tm(w_in, 32, hn, hn_b, tg)
    if after_win is not None:
        after_win(out_b)

    es_hn.close()
    olds = hn_b + hp.b + [acc_b, rstd_b]
    K2 = _consts(c, [("cos4", cos4, [128, TL]), ("sin4", sin4, [128, TL])], olds=olds)
    cs, cs_b = K2["cos4"]
    sn, sn_b = K2["sin4"]
    tmp = [sb(f"rtmp{i}", [128, TL]) for i in range(4)]
    tmp_b = [fresh(olds) for _ in range(4)]

    def rope(x1, x1_b, x2, x2_b, P, dst_rows1, dst_rows2):
        c.op("dve", lambda: nc.vector.tensor_tensor(tmp[0][:P, :], x1, cs[:P, :], ALU.mult), reads=[x1_b, cs_b], writes=[tmp_b[0]])
        c.op("dve", lambda: nc.vector.tensor_tensor(tmp[1][:P, :], x2, sn[:P, :], ALU.mult), reads=[x2_b, sn_b], writes=[tmp_b[1]])
        c.op("dve", lambda: nc.vector.tensor_tensor(tmp[2][:P, :], x2, cs[:P, :], ALU.mult), reads=[x2_b, cs_b], writes=[tmp_b[2]])
        c.op("dve", lambda: nc.vector.tensor_tensor(tmp[3][:P, :], x1, sn[:P, :], ALU.mult), reads=[x1_b, sn_b], writes=[tmp_b[3]])
        t, b, d = stbf.next()
        c.op("dve", lambda: nc.vector.tensor_tensor(t[:P, :], tmp[0][:P, :], tmp[1][:P, :], ALU.subtract),
             reads=[tmp_b[0], tmp_b[1]], writes=[b])
        for (r0, p0, n) in dst_rows1:
            c.dma("sp", obf[r0:r0 + n, :], t[p0:p0 + n, :], d, reads=[b], pwrites=[out_b])
        t2, b2, d2 = stbf.next()
        c.op("dve", lambda: nc.vector.tensor_tensor(t2[:P, :], tmp[2][:P, :], tmp[3][:P, :], ALU.add),
             reads=[tmp_b[2], tmp_b[3]], writes=[b2])
        for (r0, p0, n) in dst_rows2:
            c.dma("sp", obf[r0:r0 + n, :], t2[p0:p0 + n, :], d2, reads=[b2], pwrites=[out_b])

    rope(kr[0][:], kr_b[0], kr[1][:], kr_b[1], 32, [(RB_KPE, 0, 32)], [(RB_KPE + 32, 0, 32)])

    rq = sb("rq", [128, TL]); rq_b = fresh(olds)
    rk = sb("rk", [128, TL]); rk_b = fresh(olds)
    col_stats(c, dn, accq, accq_b, rq, rq_b, 1536)
    col_stats(c, dn, acck, acck_b, rk, rk_b, 512)
    for i in range(12):
        c.op("dve", lambda: nc.vector.tensor_tensor(cqn[:, i, :], cqn[:, i, :], rq[:], ALU.mult),
             reads=[cqn_b[i], rq_b], writes=[cqn_b[i]])
    for i in range(4):
        c.op("dve", lambda: nc.vector.tensor_tensor(ckn[:, i, :], ckn[:, i, :], rk[:], ALU.mult),
             reads=[ckn_b[i], rk_b], writes=[ckn_b[i]])

    QS = 192 ** -0.5
    qpe = [sb(f"qpe{i}", [128, TL]) for i in range(2)]
    qpe_b = [fresh(olds), fresh(olds)]

    def qpe_h(i, j3):
        def h(tiles):
            for (ps, pb, t0, tn) in tiles:
                c.op("act", lambda: nc.scalar.activation(qpe[i][:, t0:t0 + tn], ps, AF.Identity, scale=QS), reads=[pb], pwrites=[qpe_b[i]])
            if i == 1:
                rows1 = [(RB_Q + (4 * j3 + hh) * 192 + 128, 32 * hh, 32) for hh in range(4)]
                rows2 = [(RB_Q + (4 * j3 + hh) * 192 + 160, 32 * hh, 32) for hh in range(4)]
                rope(qpe[0][:], qpe_b[0], qpe[1][:], qpe_b[1], 128, rows1, rows2)
        return h

    qg = fm_groups(0, 12, lambda i: out_fm(obf, RB_Q + i * 192, scale=QS, dtype=BF16))
    for j3 in range(3):
        qg.append((1536 + j3 * 128, 128, [(0, 128, qpe_h(0, j3))]))
        qg.append((1920 + j3 * 128, 128, [(0, 128, qpe_h(1, j3))]))
    dn.fm(w_uq, 12, cqn, cqn_b, qg)
    kg = fm_groups(0, 12, lambda i: out_fm(obf, RB_KN + i * 128, dtype=BF16))
    dn.fm(w_ukv, 4, ckn, ckn_b, kg)
    vg = tm_groups(1536, 1536, lambda o: out_tm(otb, CB_VM + o, ttb))
    dn.tm(w_ukv, 4, ckn, ckn_b, vg)
    c.wait_all("sp", [out_b])


TH = TL // 2
TTH = [(0, 512), (512, TH - 512)]


def build_p3():
    nc = bass.Bass("TRN2", target_bir_lowering=False)
    dt = lambda n, s, d=F32, k="ExternalInput": nc.dram_tensor(n, s, d, kind=k).ap()
    oT = dt("oT", [128, 32, TL], BF16)
    hT = dt("hT", [128, 32, TL])
    w_out = dt("w_out", [D, D])
    w_up = dt("w_up", [D, 2 * DFF])
    w_down = dt("w_down", [DFF, D])
    g_ffn = dt("g_ffn", [128, 32])
    g_next = dt("g_next", [128, 32])
    conv_w = dt("conv_w", [128, 172, 3])
    conv_b = dt("conv_b", [128, 172])
    h_out = dt("h_out", [128, 32, TL], F32, "ExternalOutput")
    y_out = dt("y_out", [128, 32, TL], F32, "ExternalOutput")
    h_mid = dt("h_mid", [128, 32, TL], F32, "Internal")
    act = dt("act_scr", [86, 128, TL], BF16, "Internal")
    with ExitStack() as es:
        c = Ctx(nc, es)
        emit_p3(c, oT, hT, w_out, w_up, w_down, g_ffn, g_next, conv_w, conv_b, h_out, y_out, h_mid, act)
    return nc


def emit_p3(c, oT, hT, w_out, w_up, w_down, g_ffn, g_next, conv_w, conv_b, h_out, y_out, h_mid, act):
    nc, es = c.nc, c.es
    sb = lambda n, s, d=F32, st=None: (st or es).enter_context(nc.sbuf_tensor(c.nm(n), s, d))
    dn = Dense(c)
    K = _consts(c, [("g_ffn", g_ffn, [128, 32]), ("g_next", g_next, [128, 32]),
                    ("cw", conv_w, [128, 172, 3]), ("cb", conv_b, [128, 172])])
    dn.ones = sb("ones", [128, 128]); dn.ones_b = Buf()
    c.op("dve", lambda: nc.vector.memset(dn.ones[:], 1.0), writes=[dn.ones_b])
    sq = Ring(c, "sq", 2, [128, TL], F32)
    hc = Ring(c, "hc", 2, [128, TL], F32)
    acc = sb("acc", [128, TL]); acc_b = Buf()
    rstd = sb("rstd", [128, TL]); rstd_b = Buf()
    hmid_b = [Buf() for _ in range(32)]
    act_b = [Buf() for _ in range(86)]
    hout_b = [Buf() for _ in range(32)]
    y_b = Buf()
    gf, gf_b = K["g_ffn"]
    c.op("dve", lambda: nc.vector.memset(acc[:], 0.0), writes=[acc_b])

    es_hg = ExitStack()
    hg = sb("hg", [128, 32, TL], BF16, es_hg); hg_b = [Buf() for _ in range(32)]
    with ExitStack() as es1:
        osb = sb("osb", [128, 32, TL], BF16, es1); osb_b = [Buf() for _ in range(32)]
        d_o = c.dsem("oT")
        for g in range(8):
            c.dma("sp", osb[:, g * 4:(g + 1) * 4, :], oT[:, g * 4:(g + 1) * 4, :], d_o, writes=osb_b[g * 4:(g + 1) * 4])
        Ctx.seal(d_o, osb_b)

        def h3a(m):
            def h(tiles):
                t, b, d = hc.next()
                c.dma("sp", t[:], hT[:, m, :], d, writes=[b])
                for (ps, pb, t0, tn) in tiles:
                    c.op("dve", lambda: nc.vector.tensor_tensor(t[:, t0:t0 + tn], ps, t[:, t0:t0 + tn], ALU.add),
                         reads=[pb, b], pwrites=[b])
                c.dma("sp", h_mid[:, m, :], t[:], d, reads=[b], writes=[hmid_b[m]])
                s, s_b, _ = sq.next()
                c.op("act", lambda: nc.scalar.activation(s[:], t[:], AF.Square), reads=[b], writes=[s_b])
                c.op("dve", lambda: nc.vector.tensor_tensor(acc[:], acc[:], s[:], ALU.add), reads=[acc_b, s_b], writes=[acc_b])
                c.op("act", lambda: nc.scalar.activation(hg[:, m, :], t[:], AF.Identity, scale=gf[:, m:m + 1]),
                     reads=[b, gf_b], writes=[hg_b[m]])
            return h
        dn.fm(w_out, 32, osb, osb_b, fm_groups(0, 32, h3a))
    olds = list(osb_b)
    col_stats(c, dn, acc, acc_b, rstd, rstd_b, D)

    cw, cw_b = K["cw"]
    cb, cb_b = K["cb"]
    with ExitStack() as es2:
        ug = Ring(c, "ug", 2, [128, TL], F32, es2)
        uv = Ring(c, "uv", 2, [128, TL], F32, es2)
        cvg = Ring(c, "cvg", 2, [128, TL], F32, es2)
        cvv = Ring(c, "cvv", 2, [128, TL], F32, es2)
        ab = Ring(c, "ab", 2, [128, TL], BF16, es2)
        for r in (ug, uv, cvg, cvv, ab):
            r.b = [fresh(olds) for _ in r.b]
        for i in range(2):
            c.op("dve", lambda: nc.vector.memset(ab.t[i][:], 0.0), writes=[ab.b[i]])
        gate_cv = {}

        def conv(u, u_b, ch, ring):
            cv, cv_b, _ = ring.next()
            n = TL - 2
            c.op("act", lambda: nc.scalar.activation(cv[:, 2:], u[:, 2:], AF.Identity, bias=cb[:, ch:ch + 1], scale=cw[:, ch, 2:3]),
                 reads=[u_b, cb_b, cw_b], writes=[cv_b])
            c.op("dve", lambda: nc.vector.scalar_tensor_tensor(cv[:, 2:], u[:, 1:1 + n], cw[:, ch, 1:2], cv[:, 2:], ALU.mult, ALU.add),
                 reads=[u_b, cw_b, cv_b], writes=[cv_b])
            c.op("dve", lambda: nc.vector.scalar_tensor_tensor(cv[:, 2:], u[:, 0:n], cw[:, ch, 0:1], cv[:, 2:], ALU.mult, ALU.add),
                 reads=[u_b, cw_b, cv_b], writes=[cv_b])
            return cv, cv_b

        def hup(m, is_val):
            def h(tiles):
                u, u_b, _ = (uv if is_val else ug).next()
                for (ps, pb, t0, tn) in tiles:
                    c.op("dve", lambda: nc.vector.tensor_tensor(u[:, t0:t0 + tn], ps, rstd[:, t0:t0 + tn], ALU.mult),
                         reads=[pb, rstd_b], pwrites=[u_b])
                if not is_val:
                    cv, cv_b = conv(u, u_b, m, cvg)
                    c.op("act", lambda: nc.scalar.activation(cv[:, 2:], cv[:, 2:], AF.Silu), reads=[cv_b], writes=[cv_b])
                    gate_cv[m] = (cv, cv_b)
                else:
                    cv, cv_b = conv(u, u_b, 86 + m, cvv)
                    g, g_b = gate_cv.pop(m)
                    a, a_b, a_d = ab.next()
                    c.op("dve", lambda: nc.vector.tensor_tensor(a[:, 2:], g[:, 2:], cv[:, 2:], ALU.mult),
                         reads=[g_b, cv_b], pwrites=[a_b])
                    c.dma("sp", act[m], a[:], a_d, reads=[a_b], writes=[act_b[m]])
            return h

        groups = []
        for m0 in range(0, 86, 2):
            groups += [(m0 * 128, 256, [(0, 128, hup(m0, False)), (128, 128, hup(m0 + 1, False))]),
                       (DFF + m0 * 128, 256, [(0, 128, hup(m0, True)), (128, 128, hup(m0 + 1, True))])]
        dn.fm(w_up, 32, hg, hg_b, groups)
        olds = olds + hg_b + ug.b + uv.b + cvg.b + cvv.b + ab.b
    es_hg.close()

    accf = sb("accf", [128, TL]); accf_b = fresh(olds)
    c.op("dve", lambda: nc.vector.memset(accf[:], 0.0), writes=[accf_b])
    asb = sb("asb", [128, 43, TL], BF16); asb_b = [fresh(olds) for _ in range(43)]
    d_a = c.dsem("asb")
    actv = act.rearrange("m p t -> p m t")
    for part in range(2):
        for g in range(0, 43, 8):
            g1 = min(43, g + 8)
            c.dma("sp", asb[:, g:g1, :], actv[:, part * 43 + g:part * 43 + g1, :], d_a,
                  reads=act_b[part * 43 + g:part * 43 + g1], writes=asb_b[g:g1])
        Ctx.seal(d_a, asb_b)
        for m in range(32):
            pb = dn.it % 2
            dn.it += 1
            pst = [(dn.psA[pb], dn.bA[pb]), (dn.psB[pb], dn.bB[pb]), (dn.psC, dn.bC)]
            wt, wb = dn.load(w_down, 43, m * 128, 128, row0=part * 43 * 128)
            for k in range(43):
                last = k == 42
                for j, (t0, tn) in enumerate(TT):
                    ps, pbuf = pst[j]
                    c.op("pe", lambda: nc.tensor.matmul(ps[:, :tn], wt[:, k, 0:128], asb[:, k, t0:t0 + tn], start=(k == 0), stop=last),
                         reads=[wb, asb_b[k]], writes=[pbuf], inc=last)
            t, b, d = hc.next()
            if part == 0:
                c.dma("sp", t[:], h_mid[:, m, :], d, reads=[hmid_b[m]], writes=[b])
            else:
                c.dma("sp", t[:], h_out[:, m, :], d, reads=[hout_b[m]], writes=[b])
            for j, (t0, tn) in enumerate(TT):
                ps, pbuf = pst[j]
                c.op("dve", lambda: nc.vector.tensor_tensor(t[:, t0:t0 + tn], ps[:, :tn], t[:, t0:t0 + tn], ALU.add),
                     reads=[pbuf, b], pwrites=[b])
            c.dma("sp", h_out[:, m, :], t[:], d, reads=[b], writes=[hout_b[m]])
            if part == 1:
                s_, s_b, _ = sq.next()
                c.op("act", lambda: nc.scalar.activation(s_[:], t[:], AF.Square), reads=[b], writes=[s_b])
                c.op("dve", lambda: nc.vector.tensor_tensor(accf[:], accf[:], s_[:], ALU.add), reads=[accf_b, s_b], writes=[accf_b])
    col_stats(c, dn, accf, accf_b, rstd, rstd_b, D)
    gn, gn_b = K["g_next"]
    for m in range(32):
        t, b, d = hc.next()
        c.dma("sp", t[:], h_out[:, m, :], d, reads=[hout_b[m]], writes=[b])
        c.op("dve", lambda: nc.vector.scalar_tensor_tensor(t[:], t[:], gn[:, m:m + 1], rstd[:], ALU.mult, ALU.mult),
             reads=[b, gn_b, rstd_b], writes=[b])
        c.dma("sp", y_out[:, m, :], t[:], d, reads=[b], pwrites=[y_b])
    c.wait_all("sp", [y_b] + hout_b)


QT = [(i * 512, 512) for i in range(8)] + [(4096, 16)]
NKC = 33
A_Q, A_KN, A_KPE, A_FQ, A_FK, NA = 0, 576, 960, 1024, 1408, 1792
V_VM, V_FV, V_GV, NV = 0, 384, 768, 1024
F_GQ, F_GK, F_GZ, F_GR, F_FZ, NF = 0, 128, 256, 273, 529, 532
GCH = [(0, 16)] + [(16 + 64 * i, 64) for i in range(64)]


def build_p2():
    nc = bass.Bass("TRN2", target_bir_lowering=False)
    dt = lambda n, s, d=F32, k="ExternalInput": nc.dram_tensor(n, s, d, kind=k).ap()
    a = dict(
        abf=dt("abf", [NA, NT], BF16), vbf=dt("vbf", [NT, NV], BF16), f32=dt("f32", [NF, NT]),
        gkm=dt("gkm", [NT, 128]), wg2=dt("wg2", [17, 128]), fbf=dt("fbf", [3, 1]),
        gmla=dt("gmla", [128, 3]), gfox=dt("gfox", [128, 3]), ggla=dt("ggla", [128, 2]),
        mask4=dt("mask4", [128, 4, 512], BF16), tri01=dt("tri01", [64, 64]),
        tris64=dt("tris64", [64, 65]), tris16=dt("tris16", [16, 17]), sus=dt("sus", [64, 64]),
        oT=dt("oT", [1024, NT], BF16, "ExternalOutput"),
        aug=dt("aug_scr", [3, 12, NT], BF16, "Internal"))
    with ExitStack() as es:
        c = Ctx(nc, es)
        emit_p2(c, **a)
    return nc


def emit_p2(c, abf, vbf, f32, gkm, wg2, fbf, gmla, gfox, ggla, mask4, tri01, tris64, tris16, sus, oT, aug,
            mid_hook=None, rows_done=None):
    nc, es = c.nc, c.es
    sb = lambda n, s, d=F32, st=None: (st or es).enter_context(nc.sbuf_tensor(c.nm(n), s, d))
    K = _consts(c, [("wg2", wg2, [17, 128]), ("fbf", fbf, [3, 1]), ("gmla", gmla, [128, 3]), ("gfox", gfox, [128, 3]),
                    ("ggla", ggla, [128, 2]), ("tri01", tri01, [64, 64]), ("tris64", tris64, [64, 65]),
                    ("tris16", tris16, [16, 17]), ("sus", sus, [64, 64])])
    mk = sb("mk_sb", [128, 4, 512], BF16); mk_b = Buf()
    c.dma("sp", mk[:], mask4, c.dsem("mk"), writes=[mk_b])
    ones = sb("ones", [128, 512]); ones_b = Buf()
    c.op("dve", lambda: nc.vector.memset(ones[:], 1.0), writes=[ones_b])
    onesb = sb("onesb", [128, 128], BF16); onesb_b = Buf()
    c.op("dve", lambda: nc.vector.memset(onesb[:], 1.0), writes=[onesb_b])
    P = [es.enter_context(nc.psum_tensor(c.nm(f"pp{i}"), [128, 512], F32)) for i in range(7)]
    Pb = [Buf() for _ in range(7)]
    out_b = Buf()
    blk_b = [Buf() for _ in range(8)]
    stb = Ring(c, "stb", 3, [128, 512], BF16)
    olds = []

    def head_norm_out(o_parts, gain, gain_b, gcol0, n, q0, row0, extra=None):
        nf = 128 * len(o_parts)
        for i, (o, o_b) in enumerate(o_parts):
            s, s_b, _ = sqr.next()
            c.op("act", lambda: nc.scalar.activation(s[:, :n], o, AF.Square), reads=[o_b], writes=[s_b])
            c.op("pe", lambda: nc.tensor.matmul(P[6][:, :n], ones[:, :128], s[:, :n], start=(i == 0), stop=(i == len(o_parts) - 1)),
                 reads=[s_b, ones_b], writes=[Pb[6]])
        r, r_b, _ = rsr.next()
        c.op("dve", lambda: nc.vector.tensor_scalar(r[:, :n], P[6][:, :n], 1.0 / nf, EPS, ALU.mult, ALU.add), reads=[Pb[6]], writes=[r_b])
        c.op("act", lambda: nc.scalar.activation(r[:, :n], r[:, :n], AF.Sqrt), reads=[r_b], writes=[r_b])
        c.op("dve", lambda: nc.vector.reciprocal(r[:, :n], r[:, :n]), reads=[r_b], writes=[r_b])
        for i, (o, o_b) in enumerate(o_parts):
            t, b, d = stb.next()
            if extra is None:
                c.op("dve", lambda: nc.vector.scalar_tensor_tensor(t[:, :n], o, gain[:, gcol0 + i:gcol0 + i + 1], r[:, :n], ALU.mult, ALU.mult),
                     reads=[o_b, gain_b, r_b], writes=[b])
            else:
                ex, ex_b = extra[i]
                c.op("dve", lambda: nc.vector.scalar_tensor_tensor(o, o, gain[:, gcol0 + i:gcol0 + i + 1], r[:, :n], ALU.mult, ALU.mult),
                     reads=[o_b, gain_b, r_b], writes=[o_b])
                c.op("dve", lambda: nc.vector.tensor_tensor(t[:, :n], o, ex, ALU.mult), reads=[o_b, ex_b], writes=[b])
            c.dma("sp", oT[row0 + i * 128:row0 + (i + 1) * 128, q0:q0 + n], t[:, :n], d, reads=[b],
                  pwrites=[out_b, blk_b[row0 // 128 + i]])

    sqr = Ring(c, "sqr", 2, [128, 512], F32)
    rsr = Ring(c, "rsr", 2, [128, 512], F32)

    with ExitStack() as eg:
        wg, wg_b = K["wg2"]
        tri, tri_b = K["tri01"]
        ts64, ts64_b = K["tris64"]
        ts16, ts16_b = K["tris16"]
        su, su_b = K["sus"]
        gv = sb("gv", [64, 65, 256], BF16, eg); gv_b = Buf()
        d_gv = c.dsem("gv")
        c.dma("sp", gv[:16, 0, :], vbf[0:16, V_GV:V_GV + 256], d_gv, writes=[gv_b])
        for i in range(4):
            c.dma("sp", gv[:, 1 + 16 * i:17 + 16 * i, :],
                  vbf[16 + 1024 * i:16 + 1024 * (i + 1), V_GV:V_GV + 256].rearrange("(n p) d -> p n d", p=64), d_gv, pwrites=[gv_b])
        qdec = sb("qdec", [128, NT], BF16, eg); qdec_b = [Buf() for _ in range(65)]
        kst = sb("kst", [64, 65, 128], BF16, eg); kst_b = [Buf() for _ in range(65)]
        Aall = sb("Aall", [64, 65, 64], BF16, eg); A_b = [Buf() for _ in range(65)]
        dec = sb("dec", [128, 65], F32, eg); dec_b = [Buf() for _ in range(65)]
        oall = sb("oall_sb", [128, 2, NT], F32, eg); oall_b = [Buf() for _ in range(9)]
        S = sb("S", [128, 256], F32, eg); S_b = Buf()
        Sbf = sb("Sbf", [128, 256], BF16, eg); Sbf_b = Buf()
        gzr = Ring(c, "gzr", 2, [17, 512], F32, eg)
        gqr = Ring(c, "gqr", 2, [128, 512], F32, eg)
        gkr = Ring(c, "gkr", 2, [128, 512], F32, eg)
        gkmr = Ring(c, "gkmr", 2, [64, 8, 128], F32, eg)
        spr = Ring(c, "spr", 2, [64, 8, 128], F32, eg)
        e4 = Ring(c, "e4", 2, [128, 64], F32, eg)
        ek = Ring(c, "ek", 2, [64, 128], F32, eg)
        kdr = Ring(c, "kdr", 2, [128, 64], BF16, eg)
        supers = [(0, 16, [0])] + [(16 + 512 * j, 512, list(range(1 + 8 * j, 9 + 8 * j))) for j in range(8)]
        for (r0, rn, chunks) in supers:
            gz, gz_b, gz_d = gzr.next()
            c.dma("sp", gz[:, :rn], f32[F_GZ:F_GZ + 17, r0:r0 + rn], gz_d, writes=[gz_b])
            gq, gq_b, gq_d = gqr.next()
            c.dma("sp", gq[:, :rn], f32[F_GQ:F_GQ + 128, r0:r0 + rn], gq_d, writes=[gq_b])
            gk, gk_b, gk_d = gkr.next()
            c.dma("sp", gk[:, :rn], f32[F_GK:F_GK + 128, r0:r0 + rn], gk_d, writes=[gk_b])
            gm, gm_b, gm_d = gkmr.next()
            C = GCH[chunks[0]][1]
            nch = len(chunks)
            c.dma("sp", gm[:C, :nch, :], gkm[r0:r0 + rn, :].rearrange("(n p) d -> p n d", p=C), gm_d, writes=[gm_b])
            sp, sp_b, _ = spr.next()
            for g4 in range(0, nch, 4):
                n4 = min(4, nch - g4)
                for i in range(n4):
                    o0 = (g4 + i) * C
                    c.op("pe", lambda: nc.tensor.matmul(P[0][:C, i * 128:(i + 1) * 128], gz[:, o0:o0 + C], wg[:], start=True, stop=True),
                         reads=[gz_b, wg_b], writes=[Pb[0]])
                c.op("act", lambda: nc.scalar.activation(sp[:C, g4:g4 + n4, :], P[0][:C, :n4 * 128].rearrange("p (a b) -> p a b", b=128), AF.Exp, scale=-1.0),
                     reads=[Pb[0]], pwrites=[sp_b])
            c.op("act", lambda: nc.scalar.activation(sp[:C, :nch, :], sp[:C, :nch, :], AF.Ln, bias=1.0), reads=[sp_b], writes=[sp_b])
            for i, n in enumerate(chunks):
                s0, C = GCH[n]
                l0 = i * C
                tsx, tsx_b = (ts64, ts64_b) if C == 64 else (ts16, ts16_b)
                c.op("pe", lambda: nc.tensor.matmul(P[1][:, :C + 1], sp[:C, i, :], tsx[:C, :C + 1], start=True, stop=True),
                     reads=[sp_b, tsx_b], writes=[Pb[1]])
                c.op("pe", lambda: nc.tensor.matmul(P[2][:C, :128], su[:C, :C], sp[:C, i, :], start=True, stop=True),
                     reads=[sp_b, su_b], writes=[Pb[2]])
                eb, eb_b, _ = e4.next()
                c.op("act", lambda: nc.scalar.activation(eb[:, :C], P[1][:, :C], AF.Exp), reads=[Pb[1]], writes=[eb_b])
                c.op("dve", lambda: nc.vector.tensor_tensor(qdec[:, s0:s0 + C], gq[:, l0:l0 + C], eb[:, :C], ALU.mult),
                     reads=[gq_b, eb_b], writes=[qdec_b[n]])
                en, en_b, _ = e4.next()
                c.op("act", lambda: nc.scalar.activation(en[:, :C], P[1][:, :C], AF.Exp, scale=-1.0), reads=[Pb[1]], writes=[en_b])
                kd, kd_b, _ = kdr.next()
                c.op("dve", lambda: nc.vector.tensor_tensor(kd[:, :C], gk[:, l0:l0 + C], en[:, :C], ALU.mult),
                     reads=[gk_b, en_b], writes=[kd_b])
                c.op("act", lambda: nc.scalar.activation(dec[:, n:n + 1], P[1][:, C:C + 1], AF.Exp), reads=[Pb[1]], writes=[dec_b[n]])
                ekt, ek_b, _ = ek.next()
                c.op("act", lambda: nc.scalar.activation(ekt[:C, :], P[2][:C, :128], AF.Exp), reads=[Pb[2]], writes=[ek_b])
                c.op("dve", lambda: nc.vector.tensor_tensor(kst[:C, n, :], gm[:C, i, :], ekt[:C, :], ALU.mult),
                     reads=[gm_b, ek_b], writes=[kst_b[n]])
                c.op("pe", lambda: nc.tensor.matmul(P[3][:C, :C], kd[:, :C], qdec[:, s0:s0 + C], start=True, stop=True),
                     reads=[kd_b, qdec_b[n]], writes=[Pb[3]])
                c.op("dve", lambda: nc.vector.tensor_tensor(Aall[:C, n, :C], P[3][:C, :C], tri[:C, :C], ALU.mult),
                     reads=[Pb[3], tri_b], writes=[A_b[n]])
        c.op("dve", lambda: nc.vector.memset(S[:], 0.0), writes=[S_b])
        for n, (s0, C) in enumerate(GCH):
            po, po_b = P[4 + n % 2], Pb[4 + n % 2]
            for half in range(2):
                c.op("pe", lambda: nc.tensor.matmul(po[:, half * 64:half * 64 + C], gv[:C, n, half * 128:(half + 1) * 128], Aall[:C, n, :C],
                                                    start=True, stop=(n == 0)),
                     reads=[gv_b, A_b[n]], writes=[po_b])
                if n > 0:
                    c.op("pe", lambda: nc.tensor.matmul(po[:, half * 64:half * 64 + C], Sbf[:, half * 128:(half + 1) * 128], qdec[:, s0:s0 + C],
                                                        start=False, stop=True),
                         reads=[Sbf_b, qdec_b[n]], writes=[po_b])
            ti = 0 if n == 0 else 0 + (s0 // 512)
            for half in range(2):
                c.op("act", lambda: nc.scalar.copy(oall[:, half, s0:s0 + C], po[:, half * 64:half * 64 + C]),
                     reads=[po_b], pwrites=[oall_b[min(8, s0 // 512)], oall_b[min(8, (s0 + C - 1) // 512)]])
            c.op("pe", lambda: nc.tensor.matmul(P[0][:, :256], kst[:C, n, :], gv[:C, n, :], start=True, stop=True),
                 reads=[kst_b[n], gv_b], writes=[Pb[0]])
            c.op("dve", lambda: nc.vector.scalar_tensor_tensor(S[:], S[:], dec[:, n:n + 1], P[0][:, :256], ALU.mult, ALU.add),
                 reads=[S_b, dec_b[n], Pb[0]], writes=[S_b])
            c.op("act", lambda: nc.scalar.copy(Sbf[:], S[:]), reads=[S_b], writes=[Sbf_b])
        gg, gg_b = K["ggla"]
        grr = Ring(c, "grr", 2, [128, 2, 512], F32, eg)
        for ti, (q0, n) in enumerate(QT):
            gr, gr_b, gr_d = grr.next()
            c.dma("sp", gr[:, :, :n], f32[F_GR:F_GR + 256, q0:q0 + n].rearrange("(h p) t -> p h t", p=128), gr_d, writes=[gr_b])
            head_norm_out([(oall[:, hh, q0:q0 + n], oall_b[ti]) for hh in range(2)], gg, gg_b, 0, n, q0, 384,
                          extra=[(gr[:, hh, :n], gr_b) for hh in range(2)])
        olds = [gv_b, Sbf_b, S_b] + qdec_b + kst_b + A_b + dec_b + oall_b + gzr.b + gqr.b + gkr.b + gkmr.b + spr.b + e4.b + ek.b + kdr.b + grr.b
        if rows_done is not None:
            rows_done(384, 256, blk_b[3:5])
    if mid_hook is not None:
        c.barrier(exclude=getattr(c, "soft", ()))
        with ExitStack() as eh:
            keep = c.es
            c.es = eh
            mid_hook()
            c.es = keep
        c.barrier(exclude=getattr(c, "soft", ()))

    with ExitStack() as ea:
        qn = Ring(c, "qn", 2, [128, NT], BF16, ea)
        qp = Ring(c, "qp", 2, [64, NT], BF16, ea)
        kn = Ring(c, "kn", 2, [128, NT], BF16, ea)
        vv = Ring(c, "vv", 2, [128, NKC, 128], BF16, ea)
        kpe = sb("kpe", [64, NT], BF16, ea); kpe_b = fresh(olds)
        pT = Ring(c, "pT", 3, [128, 512], BF16, ea)
        osb = Ring(c, "osb", 2, [128, 512], F32, ea)
        rl = Ring(c, "rl", 2, [128, 512], F32, ea)
        exr = Ring(c, "exr", 2, [128, 512], F32, ea)
        mx = sb("mx", [128, 4], F32, ea); mx_b = fresh(olds)
        negm = sb("negm", [128, 1], F32, ea); negm_b = fresh(olds)
        nfb = sb("nfb", [3, 1], F32, ea); nfb_b = fresh(olds)
        aq = Ring(c, "aq", 1, [6, NT], BF16, ea)
        ak = Ring(c, "ak", 1, [6, NT], BF16, ea)
        for r in (qn, qp, kn, vv, pT, osb, rl, exr, aq, ak):
            r.b = [fresh(olds) for _ in r.b]
        c.dma("sp", kpe[:], abf[A_KPE:A_KPE + 64, :], c.dsem("kpe"), writes=[kpe_b])
        aug_b = Buf()
        with ExitStack() as ef:
            fz = sb("fz", [3, NT], F32, ef); fz_b = fresh(olds)
            cs = sb("cs", [3, NT], F32, ef); cs_b = fresh(olds)
            spl = Ring(c, "spl", 2, [3, NT], BF16, ef)
            spl.b = [fresh(olds) for _ in spl.b]
            fb, fb_b = K["fbf"]
            c.dma("sp", fz[:], f32[F_FZ:F_FZ + 3, :], c.dsem("fz"), writes=[fz_b])
            t1, t1_b, t1_d = spl.next()
            c.op("dve", lambda: nc.vector.memset(t1[:], 1.0), writes=[t1_b])
            for r in range(3, 9):
                c.dma("sp", aug[:, r, :], t1[:], t1_d, reads=[t1_b], pwrites=[aug_b])
            c.op("dve", lambda: nc.vector.tensor_scalar(nfb[:], fb[:], -1.0, None, ALU.mult), reads=[fb_b], writes=[nfb_b])
            c.op("act", lambda: nc.scalar.activation(fz[:], fz[:], AF.Exp, bias=nfb[:], scale=-1.0), reads=[fz_b, nfb_b], writes=[fz_b])
            c.op("act", lambda: nc.scalar.activation(fz[:], fz[:], AF.Ln, bias=1.0), reads=[fz_b], writes=[fz_b])
            c.op("dve", lambda: nc.vector.tensor_scalar(fz[:], fz[:], -1.0, None, ALU.mult), reads=[fz_b], writes=[fz_b])
            for j, (q0, n) in enumerate(QT):
                init = 0.0 if j == 0 else cs[:, q0 - 1:q0]
                c.op("dve", lambda: nc.vector.tensor_tensor_scan(cs[:, q0:q0 + n], ones[:3, :n], fz[:, q0:q0 + n], init, ALU.mult, ALU.add),
                     reads=[ones_b, fz_b, cs_b], writes=[cs_b])
            for i in range(3):
                t1, t1_b, t1_d = spl.next()
                c.op("dve", lambda: nc.vector.tensor_copy(t1[:], cs[:]), reads=[cs_b], writes=[t1_b])
                c.dma("sp", aug[:, i, :], t1[:], t1_d, reads=[t1_b], pwrites=[aug_b])
                if i < 2:
                    c.op("dve", lambda: nc.vector.tensor_tensor(cs[:], cs[:], t1[:], ALU.subtract), reads=[cs_b, t1_b], writes=[cs_b])
                t2, t2_b, t2_d = spl.next()
                c.op("act", lambda: nc.scalar.mul(t2[:], t1[:], -1.0), reads=[t1_b], writes=[t2_b])
                c.dma("sp", aug[:, 9 + i, :], t2[:], t2_d, reads=[t2_b], pwrites=[aug_b])
            olds = olds + [fz_b, cs_b] + spl.b

        heads = [("mla", h) for h in range(3)] + [("fox", h) for h in range(3)]
        for (kind, h) in heads:
            q_t, q_b, q_d = qn.next()
            k_t, k_b, k_d = kn.next()
            v_t, v_b, v_d = vv.next()
            parts = []
            if kind == "mla":
                qrow, krow, vcol, orow = A_Q + h * 192, A_KN + h * 128, V_VM + h * 128, h * 128
                gain, gain_b = K["gmla"]
                p_t, p_b, p_d = qp.next()
                c.dma("sp", p_t[:], abf[qrow + 128:qrow + 192, :], p_d, writes=[p_b])
                parts = [(k_t, k_b, q_t, q_b, 128), (kpe, kpe_b, p_t, p_b, 64)]
            else:
                qrow, krow, vcol, orow = A_FQ + h * 128, A_FK + h * 128, V_FV + h * 128, 640 + h * 128
                gain, gain_b = K["gfox"]
                a_q, a_qb, a_qd = aq.next()
                a_k, a_kb, a_kd = ak.next()
                c.dma("sp", a_q[:], aug[h, 0:6, :], a_qd, reads=[aug_b], writes=[a_qb])
                c.dma("sp", a_k[:], aug[h, 6:12, :], a_kd, reads=[aug_b], writes=[a_kb])
                parts = [(k_t, k_b, q_t, q_b, 128), (a_k, a_kb, a_q, a_qb, 6)]
            c.dma("sp", q_t[:], abf[qrow:qrow + 128, :], q_d, writes=[q_b])
            c.dma("sp", k_t[:], abf[krow:krow + 128, :], k_d, writes=[k_b])
            for i in range(4):
                c.dma("sp", v_t[:, 8 * i:8 * i + 8, :], vbf[1024 * i:1024 * (i + 1), vcol:vcol + 128].rearrange("(n p) d -> p n d", p=128),
                      v_d, writes=[v_b] if i == 0 else [], pwrites=[v_b] if i else [])
            c.dma("sp", v_t[:16, 32, :], vbf[4096:4112, vcol:vcol + 128], v_d, pwrites=[v_b])
            c.op("dve", lambda: nc.vector.memset(mx[:], 0.0), writes=[mx_b])
            for side in range(2):
                plist = [(p[2], p[3], p[4]) if side == 0 else (p[0], p[1], p[4]) for p in parts if p[4] > 6]
                for (q0, n) in QT:
                    for i, (t_, b_, kp) in enumerate(plist):
                        s, s_b, _ = sqr.next()
                        c.op("act", lambda: nc.scalar.activation(s[:kp, :n], t_[:kp, q0:q0 + n], AF.Square), reads=[b_], writes=[s_b])
                        c.op("pe", lambda: nc.tensor.matmul(P[6][:, :n], ones[:kp, :128], s[:kp, :n], start=(i == 0), stop=(i == len(plist) - 1)),
                             reads=[s_b, ones_b], writes=[Pb[6]])
                    c.op("dve", lambda: nc.vector.reduce_max(mx[:, 2:3], P[6][:, :n], axis=AX.X), reads=[Pb[6]], writes=[mx_b])
                    c.op("dve", lambda: nc.vector.tensor_tensor(mx[:, side:side + 1], mx[:, side:side + 1], mx[:, 2:3], ALU.max),
                         reads=[mx_b], writes=[mx_b])
            c.op("dve", lambda: nc.vector.tensor_tensor(mx[:, 3:4], mx[:, 0:1], mx[:, 1:2], ALU.mult), reads=[mx_b], writes=[mx_b])
            c.op("act", lambda: nc.scalar.activation(negm[:], mx[:, 3:4], AF.Sqrt), reads=[mx_b], writes=[negm_b])
            c.op("dve", lambda: nc.vector.tensor_scalar(negm[:], negm[:], -1.0, None, ALU.mult), reads=[negm_b], writes=[negm_b])
            for ti, (q0, n) in enumerate(QT):
                last_c = min(4 * ti + 3, NKC - 1)
                po, po_b = P[2 + ti % 2], Pb[2 + ti % 2]
                pl, pl_b = P[4 + ti % 2], Pb[4 + ti % 2]
                pend = None

                def pv(kc_, kn2, p2, p2_b):
                    c.op("pe", lambda: nc.tensor.matmul(po[:, :n], v_t[:kn2, kc_, :], p2[:kn2, :n], start=(kc_ == 0), stop=(kc_ == last_c)),
                         reads=[v_b, p2_b], writes=[po_b], inc=(kc_ == last_c))
                    c.op("pe", lambda: nc.tensor.matmul(pl[:, :n], onesb[:kn2, :], p2[:kn2, :n], start=(kc_ == 0), stop=(kc_ == last_c)),
                         reads=[onesb_b, p2_b], writes=[pl_b], inc=True)

                for kc in range(last_c + 1):
                    k0 = kc * 128
                    kn_ = min(128, NT - k0)
                    ps, ps_b = P[kc % 2], Pb[kc % 2]
                    for i, (kt_, kb_, qt_, qb_, kp) in enumerate(parts):
                        c.op("pe", lambda: nc.tensor.matmul(ps[:kn_, :n], kt_[:kp, k0:k0 + kn_], qt_[:kp, q0:q0 + n],
                                                            start=(i == 0), stop=(i == len(parts) - 1)),
                             reads=[kb_, qb_], writes=[ps_b], inc=(i == len(parts) - 1))
                    if pend is not None:
                        pv(*pend)
                    p_, p_b2, _ = pT.next()
                    r = kc - 4 * ti
                    if r >= 0 and kind == "fox":
                        x_, x_b, _ = exr.next()
                        c.op("dve", lambda: nc.vector.tensor_scalar(x_[:kn_, :n], ps[:kn_, :n], negm[:kn_, :], 0.0, ALU.add, ALU.min),
                             reads=[ps_b, negm_b], writes=[x_b])
                        c.op("act", lambda: nc.scalar.activation(p_[:kn_, :n], x_[:kn_, :n], AF.Exp), reads=[x_b], writes=[p_b2])
                    else:
                        c.op("act", lambda: nc.scalar.activation(p_[:kn_, :n], ps[:kn_, :n], AF.Exp, bias=negm[:kn_, :]),
                             reads=[ps_b, negm_b], writes=[p_b2])
                    if r >= 0:
                        c.op("dve", lambda: nc.vector.tensor_tensor(p_[:kn_, :n], p_[:kn_, :n], mk[:kn_, r, :n], ALU.mult),
                             reads=[p_b2, mk_b], writes=[p_b2])
                    pend = (kc, kn_, p_, p_b2)
                pv(*pend)
                r_, r_b, _ = rl.next()
                c.op("dve", lambda: nc.vector.reciprocal(r_[:, :n], pl[:, :n]), reads=[pl_b], writes=[r_b])
                o_, o_b, _ = osb.next()
                c.op("dve", lambda: nc.vector.tensor_tensor(o_[:, :n], po[:, :n], r_[:, :n], ALU.mult), reads=[po_b, r_b], writes=[o_b])
                head_norm_out([(o_[:, :n], o_b)], gain, gain_b, h, n, q0, orow)
            if rows_done is not None:
                rows_done(orow, 128, [blk_b[orow // 128]])
    c.wait_all("sp", [out_b])


_PROG = {}


def _prog(name, builder):
    if name not in _PROG:
        _PROG[name] = builder()
    return _PROG[name]


def _fmaj(a):
    T = a.shape[0]
    return np.ascontiguousarray(a.T.reshape(-1, 128, T).transpose(1, 0, 2))


def _unfm(a):
    return a.transpose(1, 0, 2).reshape(-1, a.shape[2]).T


def _pcol(v, n):
    return np.ascontiguousarray(np.asarray(v).reshape(n, 128).T)


def _p2_consts():
    kk = np.arange(128)[:, None]
    qq = np.arange(512)[None, :]
    mask4 = np.stack([(qq >= r * 128 + kk) for r in range(4)], 1).astype(NPBF)
    s = np.arange(64)[:, None]
    t = np.arange(64)[None, :]
    m16 = np.float32(-1.0 / 16.0)
    tri01 = (s <= t).astype(np.float32)
    tris64 = np.concatenate([(s <= t) * m16, np.full((64, 1), m16)], 1).astype(np.float32)
    tris16 = np.ascontiguousarray(np.concatenate([tris64[:16, :16], tris64[:16, 64:65]], 1))
    sus = ((s > t) * m16).astype(np.float32)
    return dict(mask4=mask4, tri01=tri01, tris64=tris64, tris16=tris16, sus=sus)


def _core_cols(g4):
    s = NMETA + g4 * OWN
    return np.concatenate([np.arange(s - 2, s + OWN), np.array([0, 0]), np.arange(0, NMETA)])


def kernel_unfused(x, meta_tokens, attn_norm, w_in, mla_q_norm, mla_w_uq, mla_kv_norm, mla_w_ukv,
           gla_w_gate2, gla_b_gate, fox_b_f, out_norm_mla, out_norm_gla, out_norm_fox,
           w_out, ffn_norm, ffn_w_up, ffn_conv_w, ffn_conv_b, ffn_w_down, final_norm):
    f32 = np.float32
    x = np.asarray(x, f32)
    B = x.shape[0]
    cores = list(range(8))
    h = np.concatenate([np.broadcast_to(np.asarray(meta_tokens, f32)[None], (B, NMETA, D)), x], axis=1)
    pos = np.arange(NT, dtype=f32)
    inv = (f32(1.0) / (f32(10000.0) ** (np.arange(0, 64, 2, dtype=f32) / f32(64)))).astype(f32)
    ang = (pos[:, None] * inv[None, :]).astype(f32)
    cosT, sinT = np.cos(ang).astype(f32).T, np.sin(ang).astype(f32).T
    zero_cols = np.array([2 + OWN, 3 + OWN])
    p2c = _p2_consts()
    p1, p2, p3 = _prog("p1", build_p1), _prog("p2", build_p2), _prog("p3", build_p3)
    y_final = None
    for l in range(2):
        uq3 = np.asarray(mla_w_uq[l]).reshape(1536, 12, 192)
        w_uq_p = np.ascontiguousarray(np.concatenate(
            [uq3[:, :, :128].reshape(1536, -1), uq3[:, :, 128:160].reshape(1536, -1), uq3[:, :, 160:].reshape(1536, -1)], 1))
        kv3 = np.asarray(mla_w_ukv[l]).reshape(512, 12, 256)
        w_ukv_p = np.ascontiguousarray(np.concatenate([kv3[:, :, :128].reshape(512, -1), kv3[:, :, 128:].reshape(512, -1)], 1))
        hTs = []
        maps = []
        for core in cores:
            b, g4 = divmod(core, 4)
            cols = _core_cols(g4)
            Hc = h[b][cols]
            Hc[zero_cols] = 0
            hT = _fmaj(Hc)
            hTs.append(hT)
            cs = cosT[:, cols].copy(); sn = sinT[:, cols].copy()
            maps.append(dict(hT=hT, w_in=np.asarray(w_in[l]), w_uq=w_uq_p, w_ukv=w_ukv_p,
                             g_attn=_pcol(attn_norm[l], 32), g_q=_pcol(mla_q_norm[l], 12), g_kv=_pcol(mla_kv_norm[l], 4),
                             cos4=np.ascontiguousarray(np.tile(cs, (4, 1))), sin4=np.ascontiguousarray(np.tile(sn, (4, 1)))))
        r1 = run_bass_kernel_spmd(p1, maps, core_ids=cores).results
        del maps
        maps = []
        for b in range(B):
            def gather_fm(name):
                parts = [r1[b * 4][name][:, 4 + OWN:4 + OWN + NMETA]] + [r1[b * 4 + g][name][:, 2:2 + OWN] for g in range(4)]
                return np.concatenate(parts, axis=1)

            def gather_tm(name):
                parts = [r1[b * 4][name][4 + OWN:4 + OWN + NMETA]] + [r1[b * 4 + g][name][2:2 + OWN] for g in range(4)]
                return np.concatenate(parts, axis=0)
            obf, o32, otf, otb = gather_fm("obf"), gather_fm("o32"), gather_tm("otf"), gather_tm("otb")
            for g in range(4):
                abf = np.concatenate([obf[RB_Q + 3 * g * 192:RB_Q + 3 * (g + 1) * 192],
                                      obf[RB_KN + 3 * g * 128:RB_KN + 3 * (g + 1) * 128],
                                      obf[RB_KPE:RB_KPE + 64],
                                      obf[RB_FQ + 3 * g * 128:RB_FQ + 3 * (g + 1) * 128],
                                      obf[RB_FK + 3 * g * 128:RB_FK + 3 * (g + 1) * 128]], 0)
                vbf = np.concatenate([otb[:, CB_VM + 3 * g * 128:CB_VM + 3 * (g + 1) * 128],
                                      otb[:, CB_FV + 3 * g * 128:CB_FV + 3 * (g + 1) * 128],
                                      otb[:, CB_GV + g * 256:CB_GV + (g + 1) * 256]], 1)
                ff = np.concatenate([o32[R32_GQ + g * 128:R32_GQ + (g + 1) * 128],
                                     o32[R32_GK + g * 128:R32_GK + (g + 1) * 128],
                                     o32[R32_GZ:R32_GZ + 16], np.ones((1, NT), f32),
                                     o32[R32_GR + g * 256:R32_GR + (g + 1) * 256],
                                     o32[R32_FZ + 3 * g:R32_FZ + 3 * (g + 1)]], 0)
                wg2 = np.concatenate([np.asarray(gla_w_gate2[l])[:, g * 128:(g + 1) * 128],
                                      np.asarray(gla_b_gate[l])[None, g * 128:(g + 1) * 128]], 0).astype(f32)
                maps.append(dict(
                    abf=np.ascontiguousarray(abf), vbf=np.ascontiguousarray(vbf), f32=np.ascontiguousarray(ff),
                    gkm=np.ascontiguousarray(otf[:, g * 128:(g + 1) * 128]), wg2=np.ascontiguousarray(wg2),
                    fbf=np.ascontiguousarray(np.asarray(fox_b_f[l], f32)[3 * g:3 * g + 3, None]),
                    gmla=np.ascontiguousarray(np.asarray(out_norm_mla[l], f32).reshape(12, 128)[3 * g:3 * g + 3].T),
                    gfox=np.ascontiguousarray(np.asarray(out_norm_fox[l], f32).reshape(12, 128)[3 * g:3 * g + 3].T),
                    ggla=np.ascontiguousarray(np.asarray(out_norm_gla[l], f32).reshape(4, 2, 128)[g].T),
                    **p2c))
        del r1
        r2 = run_bass_kernel_spmd(p2, maps, core_ids=cores).results
        del maps
        maps = []
        for b in range(B):
            om = np.empty((D, NT), NPBF)
            for g in range(4):
                o = r2[b * 4 + g]["oT"]
                om[3 * g * 128:3 * (g + 1) * 128] = o[0:384]
                om[1536 + g * 256:1536 + (g + 1) * 256] = o[384:640]
                om[2560 + 3 * g * 128:2560 + 3 * (g + 1) * 128] = o[640:1024]
            for g4 in range(4):
                cols = _core_cols(g4)
                oc = om[:, cols]
                oc[:, zero_cols] = 0
                oTc = np.ascontiguousarray(oc.reshape(32, 128, TL).transpose(1, 0, 2))
                gn = final_norm if l == 1 else attn_norm[1]
                cw = np.asarray(ffn_conv_w[l], f32)
                maps.append(dict(oT=oTc, hT=hTs[b * 4 + g4], w_out=np.asarray(w_out[l]), w_up=np.asarray(ffn_w_up[l]),
                                 w_down=np.asarray(ffn_w_down[l]), g_ffn=_pcol(ffn_norm[l], 32), g_next=_pcol(gn, 32),
                                 conv_w=np.ascontiguousarray(cw.T.reshape(172, 128, 3).transpose(1, 0, 2)),
                                 conv_b=_pcol(ffn_conv_b[l], 172)))
        del r2
        r3 = run_bass_kernel_spmd(p3, maps, core_ids=cores).results
        del maps
        for core in cores:
            b, g4 = divmod(core, 4)
            s = NMETA + g4 * OWN
            ho = _unfm(r3[core]["h_out"])
            h[b, s:s + OWN] = ho[2:2 + OWN]
            if g4 == 0:
                h[b, 0:NMETA] = ho[4 + OWN:4 + OWN + NMETA]
        if l == 1:
            y_final = np.empty((B, SEQ, D), f32)
            for core in cores:
                b, g4 = divmod(core, 4)
                y_final[b, g4 * OWN:(g4 + 1) * OWN] = _unfm(r3[core]["y_out"])[2:2 + OWN]
        del r3
    return y_final


class Gath:
    def __init__(self, nc, name, R, C, dtype, esz):
        rp = (1 << 20) // (C * esz)
        if rp >= 64:
            rp = (rp // 64) * 64
        self.C = C
        self.pieces = [(r0, min(rp, R - r0)) for r0 in range(0, R, rp)]
        self.g = [nc.dram_tensor(f"{name}_g{i}", [4 * n, C], dtype, kind="Internal").ap() for i, (r0, n) in enumerate(self.pieces)]
        self.buf = Buf()

    def gather(self, c, X, cs, reads=()):
        for (r0, n), g in zip(self.pieces, self.g):
            c.allgather(g, X[r0:r0 + n, :], cs, reads=reads, writes=[self.buf])

    def gather_where(self, c, X, cs, pred, reads):
        for (r0, pn), g in zip(self.pieces, self.g):
            if pred(r0):
                c.allgather(g, X[r0:r0 + pn, :], cs, reads=reads, writes=[self.buf])

    def gather_rows(self, c, X, cs, row0, n, reads):
        for (r0, pn), g in zip(self.pieces, self.g):
            if r0 >= row0 and r0 + pn <= row0 + n:
                c.allgather(g, X[r0:r0 + pn, :], cs, reads=reads, writes=[self.buf])

    def segs(self, row0, n):
        out = []
        for (r0, pn), g in zip(self.pieces, self.g):
            a, b = max(row0, r0), min(row0 + n, r0 + pn)
            if a < b:
                out.append((g.rearrange("(r n) c -> r n c", r=4)[:, a - r0:b - r0, :], a - row0, b - a))
        return out


def emit_select1(c, sel, G32, GBF, GTF, GTBg, GTBr, abf, vbf, f32, gkm, dst_b, part):
    nc, es = c.nc, c.es
    K = _consts(c, [("sel", sel, [128, 4])])
    sl, sl_b = K["sel"]
    fm_jobs = [(GBF, abf, BF16, [(A_Q, RB_Q, 576, 576), (A_KN, RB_KN, 384, 384), (A_KPE, RB_KPE, 64, 0),
                                 (A_FQ, RB_FQ, 384, 384), (A_FK, RB_FK, 384, 384)], "sb")] if part == "B" else \
              [(G32, f32, F32, [(F_GQ, R32_GQ, 128, 128), (F_GK, R32_GK, 128, 128), (F_GZ, R32_GZ, 16, 0),
                                (F_GR, R32_GR, 256, 256), (F_FZ, R32_FZ, 3, 3)], "sf")]
    for (G, dst, dtype, jobs, tag) in fm_jobs:
        cand = Ring(c, "cand" + tag, 4, [128, NT], dtype)
        accr = Ring(c, "acc" + tag, 2, [128, NT], dtype)
        for (d0, s0, nrows, stride) in jobs:
            for r0 in range(0, nrows, 128):
                n = min(128, nrows - r0)
                a, a_b, a_d = accr.next()
                for g in range(4):
                    t, b, d = cand.next()
                    srow = s0 + g * stride + r0
                    first = True
                    for (gv, p0, ln) in G.segs(srow, n):
                        c.dma("sp", t[p0:p0 + ln, NMETA:].rearrange("p (r t) -> p r t", r=4),
                              gv[:, :, 2:2 + OWN].rearrange("r p t -> p r t"), d, reads=[G.buf],
                              writes=[b] if first else [], pwrites=[] if first else [b])
                        first = False
                        c.dma("sp", t[p0:p0 + ln, :NMETA], gv[0, :, 4 + OWN:4 + OWN + NMETA], d, reads=[G.buf], pwrites=[b])
                    if g == 0:
                        c.op("dve", lambda: nc.vector.tensor_scalar(a[:n, :], t[:n, :], sl[:n, 0:1], None, ALU.mult),
                             reads=[b, sl_b], writes=[a_b])
                    else:
                        c.op("dve", lambda: nc.vector.scalar_tensor_tensor(a[:n, :], t[:n, :], sl[:n, g:g + 1], a[:n, :], ALU.mult, ALU.add),
                             reads=[b, sl_b, a_b], writes=[a_b])
                c.dma("sp", dst[d0 + r0:d0 + r0 + n, :], a[:n, :], a_d, reads=[a_b], pwrites=[dst_b])
    if part == "A":
        on = es.enter_context(nc.sbuf_tensor(c.nm("ones_row"), [1, NT], F32)); on_b = Buf()
        c.op("dve", lambda: nc.vector.memset(on[:], 1.0), writes=[on_b])
        c.dma("sp", f32[F_GZ + 16:F_GZ + 17, :], on[:], c.dsem("onr"), reads=[on_b], pwrites=[dst_b])
    chunks = [(0, 4 + OWN, NMETA, 0)] + [(r, 2 + 128 * i, 128, NMETA + OWN * r + 128 * i) for r in range(4) for i in range(8)]
    if part == "A":
        jobs = [(GTBg, 1024, BF16, vbf, [(V_GV, 0, 256, 256)], "tg"), (GTF, 512, F32, gkm, [(0, 0, 128, 128)], "tk")]
    else:
        jobs = [(GTBr, 3072, BF16, vbf, [(V_VM, CB_VM - CB_FV, 384, 384), (V_FV, 0, 384, 384)], "tr")]
    for (G, width, dtype, dst, blocks, tag) in jobs:
        candt = Ring(c, "cand" + tag, 3, [128, width], dtype)
        acct = Ring(c, "acc" + tag, 2, [128, 768], dtype)
        for (r, srow, n, drow) in chunks:
            a, a_b, a_d = acct.next()
            t, b, d = candt.next()
            first = True
            for (gv, p0, ln) in G.segs(srow, n):
                c.dma("sp", t[p0:p0 + ln, :], gv[r, :, :], d, reads=[G.buf], writes=[b] if first else [], pwrites=[] if first else [b])
                first = False
            for g in range(4):
                for bi, (dc, sc0, w, stride) in enumerate(blocks):
                    sc = sc0 + stride * g
                    ao = sum(bb[2] for bb in blocks[:bi])
                    if g == 0:
                        c.op("dve", lambda: nc.vector.tensor_scalar(a[:n, ao:ao + w], t[:n, sc:sc + w], sl[:n, 0:1], None, ALU.mult),
                             reads=[b, sl_b], pwrites=[a_b])
                    else:
                        c.op("dve", lambda: nc.vector.scalar_tensor_tensor(a[:n, ao:ao + w], t[:n, sc:sc + w], sl[:n, g:g + 1], a[:n, ao:ao + w], ALU.mult, ALU.add),
                             reads=[b, sl_b, a_b], pwrites=[a_b])
            for bi, (dc, sc0, w, stride) in enumerate(blocks):
                ao = sum(bb[2] for bb in blocks[:bi])
                c.dma("sp", dst[drow:drow + n, dc:dc + w], a[:n, ao:ao + w], a_d, reads=[a_b], pwrites=[dst_b])


def _omix_src(kc):
    if kc < 12:
        return kc // 3, (kc % 3) * 128
    if kc < 20:
        return (kc - 12) // 2, 384 + ((kc - 12) % 2) * 128
    return (kc - 20) // 3, 640 + ((kc - 20) % 3) * 128


def emit_select2(c, sel, GO, oT3, dst_b):
    nc, es = c.nc, c.es
    K = _consts(c, [("sel", sel, [128, 4])])
    sl, sl_b = K["sel"]
    cand = Ring(c, "cand2", 4, [128, 2 + OWN], BF16)
    accr = Ring(c, "acc2", 2, [128, TL], BF16)
    for i in range(2):
        c.op("dve", lambda: nc.vector.memset(accr.t[i][:], 0.0), writes=[accr.b[i]])
    for kc in range(32):
        rk, row0 = _omix_src(kc)
        a, a_b, a_d = accr.next()
        segs = GO.segs(row0, 128)
        for (gv, p0, ln) in segs:
            c.dma("sp", a[p0:p0 + ln, 4 + OWN:], gv[rk, :, 0:NMETA], a_d, reads=[GO.buf], pwrites=[a_b])
        for dd in range(4):
            t, b, d = cand.next()
            s0 = NMETA + OWN * dd - 2
            first = True
            for (gv, p0, ln) in segs:
                c.dma("sp", t[p0:p0 + ln, :], gv[rk, :, s0:s0 + 2 + OWN], d, reads=[GO.buf],
                      writes=[b] if first else [], pwrites=[] if first else [b])
                first = False
            if dd == 0:
                c.op("dve", lambda: nc.vector.tensor_scalar(a[:, :2 + OWN], t[:], sl[:, 0:1], None, ALU.mult), reads=[b, sl_b], pwrites=[a_b])
            else:
                c.op("dve", lambda: nc.vector.scalar_tensor_tensor(a[:, :2 + OWN], t[:], sl[:, dd:dd + 1], a[:, :2 + OWN], ALU.mult, ALU.add),
                     reads=[b, sl_b, a_b], pwrites=[a_b])
        c.dma("sp", oT3[:, kc, :], a[:], a_d, reads=[a_b], pwrites=[dst_b])


def emit_halo(c, selh, hbuf, h_b, tail, g_tail, cs):
    nc, es = c.nc, c.es
    K = _consts(c, [("selh", selh, [128, 5])])
    sh, sh_b = K["selh"]
    tail_b, gt_b = Buf(), Buf()
    d = c.dsem("halo")
    c.dma("sp", tail.rearrange("p (k t) -> p k t", t=2), hbuf[:, :, OWN:OWN + 2], d, reads=[h_b], writes=[tail_b])
    c.allgather(g_tail, tail, cs, reads=[tail_b], writes=[gt_b])
    cnd = es.enter_context(nc.sbuf_tensor(c.nm("hcand"), [128, 5, 64], F32)); cnd_b = Buf()
    d2 = c.dsem("halo2")
    c.dma("sp", cnd[:, 0:4, :], g_tail.rearrange("(r p) n -> p r n", r=4), d2, reads=[gt_b], writes=[cnd_b])
    c.dma("sp", cnd[:, 4, :].rearrange("p (k t) -> p k t", t=2), hbuf[:, :, TL - 2:TL], d2, reads=[h_b], pwrites=[cnd_b])
    Ctx.seal(d2, [cnd_b])
    acc = es.enter_context(nc.sbuf_tensor(c.nm("hacc"), [128, 64], F32)); acc_b = Buf()
    zz = es.enter_context(nc.sbuf_tensor(c.nm("hzero"), [128, 64], F32)); zz_b = Buf()
    c.op("dve", lambda: nc.vector.memset(zz[:], 0.0), writes=[zz_b])
    c.op("dve", lambda: nc.vector.tensor_scalar(acc[:], cnd[:, 0, :], sh[:, 0:1], None, ALU.mult), reads=[cnd_b, sh_b], writes=[acc_b])
    for i in range(1, 5):
        c.op("dve", lambda: nc.vector.scalar_tensor_tensor(acc[:], cnd[:, i, :], sh[:, i:i + 1], acc[:], ALU.mult, ALU.add),
             reads=[cnd_b, sh_b, acc_b], writes=[acc_b])
    d3 = c.dsem("halo3")
    c.dma("sp", hbuf[:, :, 0:2], acc[:].rearrange("p (k t) -> p k t", t=2), d3, reads=[acc_b, tail_b, cnd_b], pwrites=[h_b])
    c.dma("sp", hbuf[:, :, 2 + OWN:4 + OWN], zz[:].rearrange("p (k t) -> p k t", t=2), d3, reads=[zz_b], pwrites=[h_b])


def build_fused():
    nc = bass.Bass("TRN2", target_bir_lowering=False)
    dt = lambda n, s, d=F32, k="ExternalInput": nc.dram_tensor(n, s, d, kind=k).ap()
    I = lambda n, s, d=F32: nc.dram_tensor(n, s, d, kind="Internal").ap()
    hT0 = dt("hT0", [128, 32, TL])
    cos4, sin4 = dt("cos4", [128, TL]), dt("sin4", [128, TL])
    sel, selh = dt("sel", [128, 4]), dt("selh", [128, 5])
    w_in = dt("w_in", [2, D, DIN]); w_uq = dt("w_uq", [2, 1536, 2304]); w_ukv = dt("w_ukv", [2, 512, 3072])
    w_out = dt("w_out", [2, D, D]); w_up = dt("w_up", [2, D, 2 * DFF]); w_down = dt("w_down", [2, DFF, D])
    g_attn, g_q, g_kv = dt("g_attn", [2, 128, 32]), dt("g_q", [2, 128, 12]), dt("g_kv", [2, 128, 4])
    g_ffn, g_next = dt("g_ffn", [2, 128, 32]), dt("g_next", [2, 128, 32])
    conv_w, conv_b = dt("conv_w", [2, 128, 172, 3]), dt("conv_b", [2, 128, 172])
    wg2, fbf = dt("wg2", [2, 17, 128]), dt("fbf", [2, 3, 1])
    gmla, gfox, ggla = dt("gmla", [2, 128, 3]), dt("gfox", [2, 128, 3]), dt("ggla", [2, 128, 2])
    mask4 = dt("mask4", [128, 4, 512], BF16)
    tri01, tris64, tris16, sus = dt("tri01", [64, 64]), dt("tris64", [64, 65]), dt("tris16", [16, 17]), dt("sus", [64, 64])
    y_out = dt("y_out", [128, 32, TL], F32, "ExternalOutput")
    o32, obf, otf, otb = I("o32", [N32, TL]), I("obf", [NB, TL], BF16), I("otf", [TL, 512]), I("otb", [TL, NCB], BF16)
    G32, GBF = Gath(nc, "o32", N32, TL, F32, 4), Gath(nc, "obf", NB, TL, BF16, 2)
    GTF = Gath(nc, "otf", TL, 512, F32, 4)
    otbG, otbR = I("otbG", [TL, 1024], BF16), I("otbR", [TL, 3072], BF16)
    GTBg, GTBr = Gath(nc, "otbG", TL, 1024, BF16, 2), Gath(nc, "otbR", TL, 3072, BF16, 2)
    GO = Gath(nc, "oT2", 1024, NT, BF16, 2)
    abf, vbf, f32, gkm = I("abf", [NA, NT], BF16), I("vbf", [NT, NV], BF16), I("f32s", [NF, NT]), I("gkm", [NT, 128])
    aug = I("aug_scr", [3, 12, NT], BF16)
    oT2, oT3 = I("oT2", [1024, NT], BF16), I("oT3", [128, 32, TL], BF16)
    h_mid, act = I("h_mid", [128, 32, TL]), I("act_scr", [86, 128, TL], BF16)
    hA, hB = I("hA", [128, 32, TL]), I("hB", [128, 32, TL])
    tail, g_tail = I("tail", [128, 64]), I("g_tail", [4 * 128, 64])
    with ExitStack() as es:
        c = Ctx(nc, es)
        cs = c.dsem("coll")
        c.phase_dsems.remove(cs)

        def phase(fn, exclude=()):
            with ExitStack() as pes:
                c.es = pes
                fn()
                c.es = c.sem_es
            c.end_phase(exclude)

        csA, csB, cs2 = c.dsem("collA"), c.dsem("collB"), c.dsem("coll2")
        for x_ in (csA, csB, cs2):
            c.phase_dsems.remove(x_)
        c.soft = {id(cs2.h)}
        hcur = hT0
        for l in range(2):
            def early(ob):
                for (G_, x_) in ((G32, o32), (GTF, otf), (GTBg, otbG)):
                    G_.gather(c, x_, csA, reads=[ob])
                GBF.gather_where(c, obf, csB, lambda r0: r0 >= RB_KPE, [ob])
            phase(lambda: emit_p1(c, hcur, w_in[l], w_uq[l], w_ukv[l], g_attn[l], g_q[l], g_kv[l], cos4, sin4, o32, obf, otf, otb,
                                  otb_split=(otbG, otbR), after_win=early), exclude={id(csA.h), id(csB.h)})
            GBF.gather_where(c, obf, csB, lambda r0: r0 < RB_KPE, [])
            GTBr.gather(c, otbR, csB)
            db = Buf()
            phase(lambda: emit_select1(c, sel, G32, GBF, GTF, GTBg, GTBr, abf, vbf, f32, gkm, db, "A"), exclude={id(csB.h)})
            phase(lambda: emit_p2(c, abf, vbf, f32, gkm, wg2[l], fbf[l], gmla[l], gfox[l], ggla[l], mask4, tri01, tris64, tris16, sus, oT2, aug,
                                  mid_hook=lambda: emit_select1(c, sel, G32, GBF, GTF, GTBg, GTBr, abf, vbf, f32, gkm, db, "B"),
                                  rows_done=lambda r0, n, bufs: GO.gather_rows(c, oT2, cs2, r0, n, bufs)))
            db2 = Buf()
            phase(lambda: emit_select2(c, sel, GO, oT3, db2))
            hnext = y_out if False else (hA if l == 0 else hB)
            phase(lambda: emit_p3(c, oT3, hcur, w_out[l], w_up[l], w_down[l], g_ffn[l], g_next[l], conv_w[l], conv_b[l],
                                  hnext, y_out, h_mid, act))
            if l == 0:
                hb_ = Buf()
                phase(lambda: emit_halo(c, selh, hnext, hb_, tail, g_tail, cs))
            hcur = hnext
        c.barrier()
        print("fused program instructions:", c.n_inst)
    return nc


def kernel(x, meta_tokens, attn_norm, w_in, mla_q_norm, mla_w_uq, mla_kv_norm, mla_w_ukv,
           gla_w_gate2, gla_b_gate, fox_b_f, out_norm_mla, out_norm_gla, out_norm_fox,
           w_out, ffn_norm, ffn_w_up, ffn_conv_w, ffn_conv_b, ffn_w_down, final_norm):
    f32 = np.float32
    x = np.asarray(x, f32)
    B = x.shape[0]
    cores = list(range(8))
    h = np.concatenate([np.broadcast_to(np.asarray(meta_tokens, f32)[None], (B, NMETA, D)), x], axis=1)
    pos = np.arange(NT, dtype=f32)
    inv = (f32(1.0) / (f32(10000.0) ** (np.arange(0, 64, 2, dtype=f32) / f32(64)))).astype(f32)
    ang = (pos[:, None] * inv[None, :]).astype(f32)
    cosT, sinT = np.cos(ang).astype(f32).T, np.sin(ang).astype(f32).T
    zero_cols = np.array([2 + OWN, 3 + OWN])
    A = lambda v: np.asarray(v, f32)
    uq3 = A(mla_w_uq).reshape(2, 1536, 12, 192)
    w_uq_p = np.ascontiguousarray(np.concatenate(
        [uq3[..., :128].reshape(2, 1536, -1), uq3[..., 128:160].reshape(2, 1536, -1), uq3[..., 160:].reshape(2, 1536, -1)], 2))
    kv3 = A(mla_w_ukv).reshape(2, 512, 12, 256)
    w_ukv_p = np.ascontiguousarray(np.concatenate([kv3[..., :128].reshape(2, 512, -1), kv3[..., 128:].reshape(2, 512, -1)], 2))
    cw = A(ffn_conv_w)
    shared = dict(
        w_in=A(w_in), w_uq=w_uq_p, w_ukv=w_ukv_p, w_out=A(w_out), w_up=A(ffn_w_up), w_down=A(ffn_w_down),
        g_attn=np.stack([_pcol(attn_norm[l], 32) for l in range(2)]),
        g_q=np.stack([_pcol(mla_q_norm[l], 12) for l in range(2)]),
        g_kv=np.stack([_pcol(mla_kv_norm[l], 4) for l in range(2)]),
        g_ffn=np.stack([_pcol(ffn_norm[l], 32) for l in range(2)]),
        g_next=np.stack([_pcol(attn_norm[1], 32), _pcol(final_norm, 32)]),
        conv_w=np.stack([np.ascontiguousarray(cw[l].T.reshape(172, 128, 3).transpose(1, 0, 2)) for l in range(2)]),
        conv_b=np.stack([_pcol(ffn_conv_b[l], 172) for l in range(2)]),
        **_p2_consts())
    maps = []
    for core in cores:
        b, g = divmod(core, 4)
        cols = _core_cols(g)
        Hc = h[b][cols]
        Hc[zero_cols] = 0
        sel = np.zeros((128, 4), f32); sel[:, g] = 1
        selh = np.zeros((128, 5), f32); selh[:, 4 if g == 0 else g - 1] = 1
        m = dict(shared)
        m.update(
            hT0=_fmaj(Hc),
            cos4=np.ascontiguousarray(np.tile(cosT[:, cols], (4, 1))), sin4=np.ascontiguousarray(np.tile(sinT[:, cols], (4, 1))),
            sel=sel, selh=selh,
            wg2=np.stack([np.concatenate([A(gla_w_gate2[l])[:, g * 128:(g + 1) * 128], A(gla_b_gate[l])[None, g * 128:(g + 1) * 128]], 0)
                          for l in range(2)]),
            fbf=np.stack([A(fox_b_f[l])[3 * g:3 * g + 3, None] for l in range(2)]),
            gmla=np.stack([np.ascontiguousarray(A(out_norm_mla[l]).reshape(12, 128)[3 * g:3 * g + 3].T) for l in range(2)]),
            gfox=np.stack([np.ascontiguousarray(A(out_norm_fox[l]).reshape(12, 128)[3 * g:3 * g + 3].T) for l in range(2)]),
            ggla=np.stack([np.ascontiguousarray(A(out_norm_gla[l]).reshape(4, 2, 128)[g].T) for l in range(2)]))
        maps.append(m)
    res = run_bass_kernel_spmd(_prog("fused", build_fused), maps, core_ids=cores).results
    y = np.empty((B, SEQ, D), f32)
    for core in cores:
        b, g = divmod(core, 4)
        y[b, g * OWN:(g + 1) * OWN] = _unfm(res[core]["y_out"])[2:2 + OWN]
    return y
```

```python
import numpy as np
from contextlib import ExitStack
import ml_dtypes
import concourse.bass as bass
import concourse.mybir as mybir
from concourse.bass_utils import run_bass_kernel_spmd

F32 = mybir.dt.float32
BF16 = mybir.dt.bfloat16
AF = mybir.ActivationFunctionType
ALU = mybir.AluOpType
AX = mybir.AxisListType
NPBF = ml_dtypes.bfloat16

D = 4096
SEQ = 4096
NMETA = 16
OWN = 1024
TL = 2 + OWN + 2 + NMETA
TT = [(0, 512), (512, 512), (1024, TL - 1024)]
TM = [(i * 128, 128) for i in range(8)] + [(1024, TL - 1024)]
NT = NMETA + SEQ
EPS = 1e-6
DFF = 11008
GW = 256


def fm_groups(col0, nchunks, hf):
    per = GW // 128
    return [(col0 + g * GW, min(GW, (nchunks - g * per) * 128),
             [(j * 128, 128, hf(g * per + j)) for j in range(min(per, nchunks - g * per))])
            for g in range((nchunks + per - 1) // per)]


def tm_groups(col0, ncols, hf):
    return [(col0 + g * GW, min(GW, ncols - g * GW), hf(g * GW)) for g in range((ncols + GW - 1) // GW)]

O_CQ, O_CKV, O_KR, O_GQ, O_GK, O_GV, O_GZ, O_GR, O_FQ, O_FK, O_FV, O_FZ = (
    0, 1536, 2048, 2112, 2624, 3136, 4160, 4176, 5200, 6736, 8272, 9808)
DIN = 9820
R32_GQ, R32_GK, R32_GZ, R32_GR, R32_FZ, N32 = 0, 512, 1024, 1040, 2064, 2076
RB_Q, RB_KN, RB_KPE, RB_FQ, RB_FK, NB = 0, 2304, 3840, 3904, 5440, 6976
CB_GV, CB_FV, CB_VM, NCB = 0, 1024, 2560, 4096


class Buf:
    __slots__ = ("w", "r")

    def __init__(self):
        self.w = {}
        self.r = {}


def fresh(olds):
    b = Buf()
    for o in olds:
        for t in list(o.w.values()) + list(o.r.values()):
            Ctx._add(b.r, t)
    return b


class DSem:
    __slots__ = ("h", "v")

    def __init__(self, h):
        self.h = h
        self.v = 0


class Ctx:
    CE = ("pe", "act", "dve", "pool")

    def __init__(self, nc, es):
        self.nc = nc
        self.es = es
        self.sem_es = es
        self.eng = {"pe": nc.tensor, "act": nc.scalar, "dve": nc.vector,
                    "pool": nc.gpsimd, "sp": nc.sync}
        self.sem = {}
        self.cnt = {}
        self.nsem = 0
        self.latest = {}
        self.free_dsems = []
        self.phase_dsems = []
        self.phase = 0
        for e in self.CE:
            self._new_engine_sem(e)
        self.seen = {e: {} for e in self.eng}
        self.pe_pending = False
        self.n_inst = 0

    def _new_engine_sem(self, e):
        self.nsem += 1
        self.sem[e] = self.sem_es.enter_context(self.nc.semaphore(f"s_{e}_{self.nsem}"))
        self.cnt[e] = 0

    def nm(self, name):
        return f"{name}_p{self.phase}"

    def dsem(self, name):
        if self.free_dsems:
            d = self.free_dsems.pop()
        else:
            self.nsem += 1
            d = DSem(self.sem_es.enter_context(self.nc.semaphore(f"d_{name}_{self.nsem}")))
        self.phase_dsems.append(d)
        return d

    def barrier(self, exclude=()):
        assert not self.pe_pending
        for e in self.eng:
            own = id(self.sem[e]) if e in self.sem else None
            deps = {k: t for k, t in self.latest.items() if k != own and k not in exclude}
            self._emit_waits(e, deps)

    def end_phase(self, exclude=()):
        self.barrier(exclude)
        self.free_dsems.extend(self.phase_dsems)
        self.phase_dsems = []
        self.phase += 1

    def allgather(self, out, in_, cs, reads=(), writes=()):
        deps = self._collect("pool", reads, writes)
        self._emit_waits("pool", deps)
        ins = self.nc.gpsimd.collective_compute("AllGather", ALU.bypass, replica_groups=[[0, 1, 2, 3], [4, 5, 6, 7]],
                                                ins=[in_], outs=[out])
        self.n_inst += 1
        cs.v += 1
        ins.then_inc(cs.h)
        t = (cs.h, cs.v)
        self.latest[id(cs.h)] = t
        self._record(t, reads, writes)
        return ins

    @staticmethod
    def _add(deps, t):
        k = id(t[0])
        if k not in deps or deps[k][1] < t[1]:
            deps[k] = t

    def _collect(self, e, reads, writes, pwrites=()):
        deps = {}
        own = id(self.sem[e]) if e in self.sem else None
        for b in pwrites:
            for t in b.r.values():
                self._add(deps, t)
        for b in reads:
            for t in b.w.values():
                if id(t[0]) == own and e == "pe":
                    continue
                self._add(deps, t)
        for b in writes:
            for t in b.w.values():
                if id(t[0]) == own:
                    continue
                self._add(deps, t)
            for t in b.r.values():
                if id(t[0]) == own:
                    continue
                self._add(deps, t)
        return deps

    def _emit_waits(self, e, deps):
        seen = self.seen[e]
        for k, (s, v) in deps.items():
            if seen.get(k, 0) >= v:
                continue
            self.eng[e].wait_ge(s, v)
            self.n_inst += 1
            seen[k] = v

    def _record(self, t, reads, writes, pwrites=()):
        k = id(t[0])
        for b in pwrites:
            if k not in b.w or b.w[k][1] < t[1]:
                b.w[k] = t
        for b in reads:
            if k not in b.r or b.r[k][1] < t[1]:
                b.r[k] = t
        for b in writes:
            b.w = {k: t}
            b.r = {}

    def op(self, e, fn, reads=(), writes=(), inc=True, pwrites=()):
        deps = self._collect(e, reads, writes, pwrites)
        self._emit_waits(e, deps)
        ins = fn()
        self.n_inst += 1
        if inc:
            if self.cnt[e] >= 30000 and not (e == "pe" and self.pe_pending):
                self._new_engine_sem(e)
            self.cnt[e] += 1
            ins.then_inc(self.sem[e], 1)
            t = (self.sem[e], self.cnt[e])
            self.latest[id(t[0])] = t
            if e == "pe":
                self.pe_pending = False
        else:
            assert e == "pe"
            t = (self.sem[e], self.cnt[e] + 1)
            self.pe_pending = True
        self._record(t, reads, writes, pwrites)
        return ins

    def dma(self, q, out, in_, ds, reads=(), writes=(), pwrites=(), **kw):
        deps = self._collect(q, reads, writes, pwrites)
        self._emit_waits(q, deps)
        ins = self.eng[q].dma_start(out=out, in_=in_, **kw)
        self.n_inst += 1
        ds.v += 16
        ins.then_inc(ds.h, 16)
        self.latest[id(ds.h)] = (ds.h, ds.v)
        self._record((ds.h, ds.v), reads, writes, pwrites)
        return ins

    @staticmethod
    def seal(ds, bufs):
        k = id(ds.h)
        for b in bufs:
            if k in b.w:
                b.w[k] = (ds.h, ds.v)

    def wait_all(self, e, bufs):
        deps = {}
        for b in bufs:
            for t in b.w.values():
                self._add(deps, t)
        self._emit_waits(e, deps)


class Ring:
    def __init__(self, c, name, n, shape, dtype, es=None):
        es = es or c.es
        self.t = [es.enter_context(c.nc.sbuf_tensor(c.nm(f"{name}{i}"), shape, dtype)) for i in range(n)]
        self.b = [Buf() for _ in range(n)]
        self.d = [c.dsem(f"{name}{i}") for i in range(n)]
        self.i = -1
        self.n = n

    def next(self):
        self.i = (self.i + 1) % self.n
        return self.t[self.i], self.b[self.i], self.d[self.i]


class Dense:
    def __init__(self, c, kcmax=32):
        nc, es = c.nc, c.es
        self.c = c
        self.ws = Ring(c, "ws", 2, [128, kcmax * GW], BF16)
        self.psA = [es.enter_context(nc.psum_tensor(c.nm(f"psA{i}"), [128, 512], F32)) for i in range(2)]
        self.psB = [es.enter_context(nc.psum_tensor(c.nm(f"psB{i}"), [128, 512], F32)) for i in range(2)]
        self.psC = es.enter_context(nc.psum_tensor(c.nm("psC"), [128, 512], F32))
        self.bA = [Buf(), Buf()]
        self.bB = [Buf(), Buf()]
        self.bC = Buf()
        self.psT = [es.enter_context(nc.psum_tensor(c.nm(f"psT{i}"), [128, 512], F32)) for i in range(2)]
        self.bT = [Buf(), Buf()]
        self.psS = es.enter_context(nc.psum_tensor(c.nm("psS"), [128, 512], F32))
        self.bS = Buf()
        self.it = 0
        self.itT = 0

    def load(self, W, KC, col0, ncols, row0=0):
        c = self.c
        t, b, d = self.ws.next()
        t = t[:, :KC * ncols].rearrange("p (k m) -> p k m", m=ncols)
        Wv = W[row0:row0 + KC * 128, col0:col0 + ncols].rearrange("(kc p) m -> p kc m", p=128)
        step = 8
        for i, k0 in enumerate(range(0, KC, step)):
            k1 = min(KC, k0 + step)
            c.dma("pool", t[:, k0:k1, :], Wv[:, k0:k1, :], d,
                  writes=[b] if i == 0 else [], pwrites=[b] if i else [])
        return t, b

    def fm(self, W, KC, act, act_b, groups, tts=TT, row0=0):
        c, nc = self.c, self.c.nc
        for (col0, ncols, chunks) in groups:
            wt, wb = self.load(W, KC, col0, ncols, row0)
            for (off, M, handler) in chunks:
                pb = self.it % 2
                self.it += 1
                pst = [(self.psA[pb], self.bA[pb]), (self.psB[pb], self.bB[pb]), (self.psC, self.bC)]
                for kc in range(KC):
                    for j, (t0, tn) in enumerate(tts):
                        last = kc == KC - 1
                        ps, pbuf = pst[j]
                        c.op("pe", lambda: nc.tensor.matmul(ps[:M, :tn], wt[:, kc, off:off + M], act[:, kc, t0:t0 + tn],
                                                            start=(kc == 0), stop=last),
                             reads=[wb, act_b[kc]], writes=[pbuf], inc=last)
                handler([(pst[j][0][:M, :tn], pst[j][1], t0, tn) for j, (t0, tn) in enumerate(tts)])

    def tm(self, W, KC, act, act_b, groups, tms=TM, row0=0):
        c, nc = self.c, self.c.nc
        for (col0, ncols, handler) in groups:
            wt, wb = self.load(W, KC, col0, ncols, row0)
            for ti, (t0, tn) in enumerate(tms):
                pb = self.itT % 2
                self.itT += 1
                ps, pbuf = self.psT[pb], self.bT[pb]
                for kc in range(KC):
                    last = kc == KC - 1
                    c.op("pe", lambda: nc.tensor.matmul(ps[:tn, :ncols], act[:, kc, t0:t0 + tn], wt[:, kc, :ncols],
                                                        start=(kc == 0), stop=last),
                         reads=[wb, act_b[kc]], writes=[pbuf], inc=last)
                handler(ps[:tn, :ncols], pbuf, ti, t0, tn)


def _consts(c, names_shapes, es=None, olds=()):
    out = {}
    es = es or c.es
    ds = c.dsem("consts")
    for name, ap, shape in names_shapes:
        t = es.enter_context(c.nc.sbuf_tensor(c.nm("k_" + name), shape, F32))
        b = fresh(olds)
        c.dma("sp", t[:], ap, ds, writes=[b])
        out[name] = (t, b)
    Ctx.seal(ds, [b for (_, b) in out.values()])
    return out


def col_stats(c, dn, acc, acc_b, rstd, rstd_b, n_feat, post_scale=1.0, tts=TT):
    nc = c.nc
    for (t0, tn) in tts:
        c.op("pe", lambda: nc.tensor.matmul(dn.psS[:, :tn], dn.ones[:], acc[:, t0:t0 + tn], start=True, stop=True),
             reads=[acc_b, dn.ones_b], writes=[dn.bS])
        c.op("dve", lambda: nc.vector.tensor_scalar(rstd[:, t0:t0 + tn], dn.psS[:, :tn], 1.0 / n_feat, EPS,
                                                    ALU.mult, ALU.add),
             reads=[dn.bS], pwrites=[rstd_b])
    c.op("act", lambda: nc.scalar.activation(rstd[:], rstd[:], AF.Sqrt, scale=float(1.0 / post_scale ** 2)),
         reads=[rstd_b], writes=[rstd_b])
    c.op("dve", lambda: nc.vector.reciprocal(rstd[:], rstd[:]), reads=[rstd_b], writes=[rstd_b])


def build_p1():
    nc = bass.Bass("TRN2", target_bir_lowering=False)
    dt = lambda n, s, d=F32, k="ExternalInput": nc.dram_tensor(n, s, d, kind=k).ap()
    hT = dt("hT", [128, 32, TL])
    w_in = dt("w_in", [D, DIN])
    w_uq = dt("w_uq", [1536, 2304])
    w_ukv = dt("w_ukv", [512, 3072])
    g_attn = dt("g_attn", [128, 32])
    g_q = dt("g_q", [128, 12])
    g_kv = dt("g_kv", [128, 4])
    cos4 = dt("cos4", [128, TL])
    sin4 = dt("sin4", [128, TL])
    o32 = dt("o32", [N32, TL], F32, "ExternalOutput")
    obf = dt("obf", [NB, TL], BF16, "ExternalOutput")
    otf = dt("otf", [TL, 512], F32, "ExternalOutput")
    otb = dt("otb", [TL, NCB], BF16, "ExternalOutput")
    with ExitStack() as es:
        c = Ctx(nc, es)
        emit_p1(c, hT, w_in, w_uq, w_ukv, g_attn, g_q, g_kv, cos4, sin4, o32, obf, otf, otb)
    return nc


def emit_p1(c, hT, w_in, w_uq, w_ukv, g_attn, g_q, g_kv, cos4, sin4, o32, obf, otf, otb, otb_split=None, after_win=None):
    nc, es = c.nc, c.es
    sb = lambda n, s, d=F32, st=None: (st or es).enter_context(nc.sbuf_tensor(c.nm(n), s, d))
    dn = Dense(c)
    K = _consts(c, [("g_attn", g_attn, [128, 32]), ("g_q", g_q, [128, 12]), ("g_kv", g_kv, [128, 4])])
    dn.ones = sb("ones", [128, 128])
    dn.ones_b = Buf()
    c.op("dve", lambda: nc.vector.memset(dn.ones[:], 1.0), writes=[dn.ones_b])
    sq = Ring(c, "sq", 2, [128, TL], F32)
    st32 = Ring(c, "st32", 2, [128, TL], F32)
    stbf = Ring(c, "stbf", 3, [128, TL], BF16)
    ttf = Ring(c, "ttf", 2, [128, GW], F32)
    ttb = Ring(c, "ttb", 3, [128, GW], BF16)
    out_b = Buf()
    cqn = sb("cqn", [128, 12, TL], BF16); cqn_b = [Buf() for _ in range(12)]
    ckn = sb("ckn", [128, 4, TL], BF16); ckn_b = [Buf() for _ in range(4)]
    accq = sb("accq", [128, TL]); accq_b = Buf()
    acck = sb("acck", [128, TL]); acck_b = Buf()
    kr = [sb("kr1", [32, TL]), sb("kr2", [32, TL])]
    kr_b = [Buf(), Buf()]
    es_hn = ExitStack()
    hn = sb("hn", [128, 32, TL], BF16, es_hn)
    hn_b = [Buf() for _ in range(32)]
    with ExitStack() as es1:
        hp = Ring(c, "hp", 2, [128, 1, TL], F32, es1)
        acc = sb("acc", [128, TL], F32, es1); acc_b = Buf()
        rstd = sb("rstd", [128, TL], F32, es1); rstd_b = Buf()
        c.op("dve", lambda: nc.vector.memset(acc[:], 0.0), writes=[acc_b])
        for g in range(32):
            t, b, d = hp.next()
            c.dma("sp", t[:], hT[:, g:g + 1, :], d, writes=[b])
            for i in range(1):
                s, s_b, _ = sq.next()
                c.op("act", lambda: nc.scalar.activation(s[:], t[:, i, :], AF.Square), reads=[b], writes=[s_b])
                c.op("dve", lambda: nc.vector.tensor_tensor(acc[:], acc[:], s[:], ALU.add), reads=[acc_b, s_b], writes=[acc_b])
        col_stats(c, dn, acc, acc_b, rstd, rstd_b, D)
        ga, ga_b = K["g_attn"]
        for g in range(32):
            t, b, d = hp.next()
            c.dma("sp", t[:], hT[:, g:g + 1, :], d, writes=[b])
            for i in range(1):
                kc = g + i
                c.op("dve", lambda: nc.vector.scalar_tensor_tensor(hn[:, kc, :], t[:, i, :], ga[:, kc:kc + 1], rstd[:],
                                                                   ALU.mult, ALU.mult),
                     reads=[b, ga_b, rstd_b], writes=[hn_b[kc]])


    def out_fm(dst, row0, func=None, scale=1.0, dtype=F32):
        def h(tiles):
            t, b, d = (st32 if dtype == F32 else stbf).next()
            M = None
            for (ps, pb, t0, tn) in tiles:
                M = ps.shape[0]
                c.op("act", lambda: nc.scalar.activation(t[:M, t0:t0 + tn], ps, func or AF.Identity, scale=float(scale)),
                     reads=[pb], pwrites=[b])
            c.dma("sp", dst[row0:row0 + M, :], t[:M, :], d, reads=[b], pwrites=[out_b])
        return h

    c.op("dve", lambda: nc.vector.memset(accq[:], 0.0), writes=[accq_b])
    c.op("dve", lambda: nc.vector.memset(acck[:], 0.0), writes=[acck_b])

    def lat(dstt, dst_b, i, gain, gain_b, ac, ac_b):
        def h(tiles):
            s, s_b, _ = sq.next()
            for (ps, pb, t0, tn) in tiles:
                c.op("act", lambda: nc.scalar.activation(dstt[:, i, t0:t0 + tn], ps, AF.Identity, scale=gain[:, i:i + 1]),
                     reads=[pb, gain_b], pwrites=[dst_b[i]])
                c.op("act", lambda: nc.scalar.activation(s[:, t0:t0 + tn], ps, AF.Square), reads=[pb], pwrites=[s_b])
            c.op("dve", lambda: nc.vector.tensor_tensor(ac[:], ac[:], s[:], ALU.add), reads=[ac_b, s_b], writes=[ac_b])
        return h

    def krope(i):
        def h(tiles):
            for (ps, pb, t0, tn) in tiles:
                c.op("act", lambda: nc.scalar.copy(kr[i][:, t0:t0 + tn], ps), reads=[pb], pwrites=[kr_b[i]])
        return h

    gq, gq_b = K["g_q"]
    gk, gk_b = K["g_kv"]
    groups = []
    groups += fm_groups(O_CQ, 12, lambda i: lat(cqn, cqn_b, i, gq, gq_b, accq, accq_b))
    groups += fm_groups(O_CKV, 4, lambda i: lat(ckn, ckn_b, i, gk, gk_b, acck, acck_b))
    groups.append((O_KR, 64, [(0, 32, krope(0)), (32, 32, krope(1))]))
    groups += fm_groups(O_GQ, 4, lambda i: out_fm(o32, R32_GQ + i * 128, scale=128 ** -0.5))
    groups += fm_groups(O_GK, 4, lambda i: out_fm(o32, R32_GK + i * 128))
    groups.append((O_GZ, 16, [(0, 16, out_fm(o32, R32_GZ))]))
    groups += fm_groups(O_GR, 8, lambda i: out_fm(o32, R32_GR + i * 128, func=AF.Silu))
    groups += fm_groups(O_FQ, 12, lambda i: out_fm(obf, RB_FQ + i * 128, scale=128 ** -0.5, dtype=BF16))
    groups += fm_groups(O_FK, 12, lambda i: out_fm(obf, RB_FK + i * 128, dtype=BF16))
    groups.append((O_FZ, 12, [(0, 12, out_fm(o32, R32_FZ))]))
    dn.fm(w_in, 32, hn, hn_b, groups)


    def out_tm(dst, col0, ring):
        def h(ps, pb, ti, t0, tn):
            t, b, d = ring.next()
            n = ps.shape[1]
            c.op("act", lambda: nc.scalar.copy(t[:tn, :n], ps), reads=[pb], writes=[b])
            dd, cc = dst, col0
            if otb_split is not None and dst is otb:
                dd, cc = (otb_split[0], col0) if col0 < CB_FV else (otb_split[1], col0 - CB_FV)
            c.dma("sp", dd[t0:t0 + tn, cc:cc + n], t[:tn, :n], d, reads=[b], pwrites=[out_b])
        return h

    tg = tm_groups(O_GK, 512, lambda o: out_tm(otf, o, ttf))
    tg += tm_groups(O_GV, 1024, lambda o: out_tm(otb, CB_GV + o, ttb))
    tg += tm_groups(O_FV, 1536, lambda o: out_tm(otb, CB_FV + o, ttb))
    dn.tm(w_in, 32, hn, hn_b, tg)
    if after_win is not None:
        after_win(out_b)

    es_hn.close()
    olds = hn_b + hp.b + [acc_b, rstd_b]
    K2 = _consts(c, [("cos4", cos4, [128, TL]), ("sin4", sin4, [128, TL])], olds=olds)
    cs, cs_b = K2["cos4"]
    sn, sn_b = K2["sin4"]
    tmp = [sb(f"rtmp{i}", [128, TL]) for i in range(4)]
    tmp_b = [fresh(olds) for _ in range(4)]

    def rope(x1, x1_b, x2, x2_b, P, dst_rows1, dst_rows2):
        c.op("dve", lambda: nc.vector.tensor_tensor(tmp[0][:P, :], x1, cs[:P, :], ALU.mult), reads=[x1_b, cs_b], writes=[tmp_b[0]])
        c.op("dve", lambda: nc.vector.tensor_tensor(tmp[1][:P, :], x2, sn[:P, :], ALU.mult), reads=[x2_b, sn_b], writes=[tmp_b[1]])
        c.op("dve", lambda: nc.vector.tensor_tensor(tmp[2][:P, :], x2, cs[:P, :], ALU.mult), reads=[x2_b, cs_b], writes=[tmp_b[2]])
        c.op("dve", lambda: nc.vector.tensor_tensor(tmp[3][:P, :], x1, sn[:P, :], ALU.mult), reads=[x1_b, sn_b], writes=[tmp_b[3]])
        t, b, d = stbf.next()
        c.op("dve", lambda: nc.vector.tensor_tensor(t[:P, :], tmp[0][:P, :], tmp[1][:P, :], ALU.subtract),
             reads=[tmp_b[0], tmp_b[1]], writes=[b])
        for (r0, p0, n) in dst_rows1:
            c.dma("sp", obf[r0:r0 + n, :], t[p0:p0 + n, :], d, reads=[b], pwrites=[out_b])
        t2, b2, d2 = stbf.next()
        c.op("dve", lambda: nc.vector.tensor_tensor(t2[:P, :], tmp[2][:P, :], tmp[3][:P, :], ALU.add),
             reads=[tmp_b[2], tmp_b[3]], writes=[b2])
        for (r0, p0, n) in dst_rows2:
            c.dma("sp", obf[r0:r0 + n, :], t2[p0:p0 + n, :], d2, reads=[b2], pwrites=[out_b])

    rope(kr[0][:], kr_b[0], kr[1][:], kr_b[1], 32, [(RB_KPE, 0, 32)], [(RB_KPE + 32, 0, 32)])

    rq = sb("rq", [128, TL]); rq_b = fresh(olds)
    rk = sb("rk", [128, TL]); rk_b = fresh(olds)
    col_stats(c, dn, accq, accq_b, rq, rq_b, 1536)
    col_stats(c, dn, acck, acck_b, rk, rk_b, 512)
    for i in range(12):
        c.op("dve", lambda: nc.vector.tensor_tensor(cqn[:, i, :], cqn[:, i, :], rq[:], ALU.mult),
             reads=[cqn_b[i], rq_b], writes=[cqn_b[i]])
    for i in range(4):
        c.op("dve", lambda: nc.vector.tensor_tensor(ckn[:, i, :], ckn[:, i, :], rk[:], ALU.mult),
             reads=[ckn_b[i], rk_b], writes=[ckn_b[i]])

    QS = 192 ** -0.5
    qpe = [sb(f"qpe{i}", [128, TL]) for i in range(2)]
    qpe_b = [fresh(olds), fresh(olds)]

    def qpe_h(i, j3):
        def h(tiles):
            for (ps, pb, t0, tn) in tiles:
                c.op("act", lambda: nc.scalar.activation(qpe[i][:, t0:t0 + tn], ps, AF.Identity, scale=QS), reads=[pb], pwrites=[qpe_b[i]])
            if i == 1:
                rows1 = [(RB_Q + (4 * j3 + hh) * 192 + 128, 32 * hh, 32) for hh in range(4)]
                rows2 = [(RB_Q + (4 * j3 + hh) * 192 + 160, 32 * hh, 32) for hh in range(4)]
                rope(qpe[0][:], qpe_b[0], qpe[1][:], qpe_b[1], 128, rows1, rows2)
        return h

    qg = fm_groups(0, 12, lambda i: out_fm(obf, RB_Q + i * 192, scale=QS, dtype=BF16))
    for j3 in range(3):
        qg.append((1536 + j3 * 128, 128, [(0, 128, qpe_h(0, j3))]))
        qg.append((1920 + j3 * 128, 128, [(0, 128, qpe_h(1, j3))]))
    dn.fm(w_uq, 12, cqn, cqn_b, qg)
    kg = fm_groups(0, 12, lambda i: out_fm(obf, RB_KN + i * 128, dtype=BF16))
    dn.fm(w_ukv, 4, ckn, ckn_b, kg)
    vg = tm_groups(1536, 1536, lambda o: out_tm(otb, CB_VM + o, ttb))
    dn.tm(w_ukv, 4, ckn, ckn_b, vg)
    c.wait_all("sp", [out_b])


TH = TL // 2
TTH = [(0, 512), (512, TH - 512)]


def build_p3():
    nc = bass.Bass("TRN2", target_bir_lowering=False)
    dt = lambda n, s, d=F32, k="ExternalInput": nc.dram_tensor(n, s, d, kind=k).ap()
    oT = dt("oT", [128, 32, TL], BF16)
    hT = dt("hT", [128, 32, TL])
    w_out = dt("w_out", [D, D])
    w_up = dt("w_up", [D, 2 * DFF])
    w_down = dt("w_down", [DFF, D])
    g_ffn = dt("g_ffn", [128, 32])
    g_next = dt("g_next", [128, 32])
    conv_w = dt("conv_w", [128, 172, 3])
    conv_b = dt("conv_b", [128, 172])
    h_out = dt("h_out", [128, 32, TL], F32, "ExternalOutput")
    y_out = dt("y_out", [128, 32, TL], F32, "ExternalOutput")
    h_mid = dt("h_mid", [128, 32, TL], F32, "Internal")
    act = dt("act_scr", [86, 128, TL], BF16, "Internal")
    with ExitStack() as es:
        c = Ctx(nc, es)
        emit_p3(c, oT, hT, w_out, w_up, w_down, g_ffn, g_next, conv_w, conv_b, h_out, y_out, h_mid, act)
    return nc


def emit_p3(c, oT, hT, w_out, w_up, w_down, g_ffn, g_next, conv_w, conv_b, h_out, y_out, h_mid, act, final=True):
    nc, es = c.nc, c.es
    sb = lambda n, s, d=F32, st=None: (st or es).enter_context(nc.sbuf_tensor(c.nm(n), s, d))
    dn = Dense(c)
    K = _consts(c, [("g_ffn", g_ffn, [128, 32]), ("g_next", g_next, [128, 32]),
                    ("cw", conv_w, [128, 172, 3]), ("cb", conv_b, [128, 172])])
    dn.ones = sb("ones", [128, 128]); dn.ones_b = Buf()
    c.op("dve", lambda: nc.vector.memset(dn.ones[:], 1.0), writes=[dn.ones_b])
    sq = Ring(c, "sq", 2, [128, TL], F32)
    hc = Ring(c, "hc", 2, [128, TL], F32)
    acc = sb("acc", [128, TL]); acc_b = Buf()
    rstd = sb("rstd", [128, TL]); rstd_b = Buf()
    hmid_b = [Buf() for _ in range(32)]
    act_b = [Buf() for _ in range(86)]
    hout_b = [Buf() for _ in range(32)]
    y_b = Buf()
    gf, gf_b = K["g_ffn"]
    c.op("dve", lambda: nc.vector.memset(acc[:], 0.0), writes=[acc_b])

    es_hg = ExitStack()
    hg = sb("hg", [128, 32, TL], BF16, es_hg); hg_b = [Buf() for _ in range(32)]
    with ExitStack() as es1:
        osb = sb("osb", [128, 32, TL], BF16, es1); osb_b = [Buf() for _ in range(32)]
        d_o = c.dsem("oT")
        for g in range(8):
            c.dma("sp", osb[:, g * 4:(g + 1) * 4, :], oT[:, g * 4:(g + 1) * 4, :], d_o, writes=osb_b[g * 4:(g + 1) * 4])
        Ctx.seal(d_o, osb_b)

        def h3a(m):
            def h(tiles):
                t, b, d = hc.next()
                c.dma("sp", t[:], hT[:, m, :], d, writes=[b])
                for (ps, pb, t0, tn) in tiles:
                    c.op("dve", lambda: nc.vector.tensor_tensor(t[:, t0:t0 + tn], ps, t[:, t0:t0 + tn], ALU.add),
                         reads=[pb, b], pwrites=[b])
                c.dma("sp", h_mid[:, m, :], t[:], d, reads=[b], writes=[hmid_b[m]])
                s, s_b, _ = sq.next()
                c.op("act", lambda: nc.scalar.activation(s[:], t[:], AF.Square), reads=[b], writes=[s_b])
                c.op("dve", lambda: nc.vector.tensor_tensor(acc[:], acc[:], s[:], ALU.add), reads=[acc_b, s_b], writes=[acc_b])
                c.op("act", lambda: nc.scalar.activation(hg[:, m, :], t[:], AF.Identity, scale=gf[:, m:m + 1]),
                     reads=[b, gf_b], writes=[hg_b[m]])
            return h
        dn.fm(w_out, 32, osb, osb_b, fm_groups(0, 32, h3a))
    olds = list(osb_b)
    col_stats(c, dn, acc, acc_b, rstd, rstd_b, D)

    cw, cw_b = K["cw"]
    cb, cb_b = K["cb"]
    with ExitStack() as es2:
        ug = Ring(c, "ug", 2, [128, TL], F32, es2)
        uv = Ring(c, "uv", 2, [128, TL], F32, es2)
        cvg = Ring(c, "cvg", 2, [128, TL], F32, es2)
        cvv = Ring(c, "cvv", 2, [128, TL], F32, es2)
        ab = Ring(c, "ab", 2, [128, TL], BF16, es2)
        for r in (ug, uv, cvg, cvv, ab):
            r.b = [fresh(olds) for _ in r.b]
        for i in range(2):
            c.op("dve", lambda: nc.vector.memset(ab.t[i][:], 0.0), writes=[ab.b[i]])
        gate_cv = {}

        def conv(u, u_b, ch, ring):
            cv, cv_b, _ = ring.next()
            n = TL - 2
            c.op("act", lambda: nc.scalar.activation(cv[:, 2:], u[:, 2:], AF.Identity, bias=cb[:, ch:ch + 1], scale=cw[:, ch, 2:3]),
                 reads=[u_b, cb_b, cw_b], writes=[cv_b])
            c.op("dve", lambda: nc.vector.scalar_tensor_tensor(cv[:, 2:], u[:, 1:1 + n], cw[:, ch, 1:2], cv[:, 2:], ALU.mult, ALU.add),
                 reads=[u_b, cw_b, cv_b], writes=[cv_b])
            c.op("dve", lambda: nc.vector.scalar_tensor_tensor(cv[:, 2:], u[:, 0:n], cw[:, ch, 0:1], cv[:, 2:], ALU.mult, ALU.add),
                 reads=[u_b, cw_b, cv_b], writes=[cv_b])
            return cv, cv_b

        def hup(m, is_val):
            def h(tiles):
                u, u_b, _ = (uv if is_val else ug).next()
                for (ps, pb, t0, tn) in tiles:
                    c.op("dve", lambda: nc.vector.tensor_tensor(u[:, t0:t0 + tn], ps, rstd[:, t0:t0 + tn], ALU.mult),
                         reads=[pb, rstd_b], pwrites=[u_b])
                if not is_val:
                    cv, cv_b = conv(u, u_b, m, cvg)
                    c.op("act", lambda: nc.scalar.activation(cv[:, 2:], cv[:, 2:], AF.Silu), reads=[cv_b], writes=[cv_b])
                    gate_cv[m] = (cv, cv_b)
                else:
                    cv, cv_b = conv(u, u_b, 86 + m, cvv)
                    g, g_b = gate_cv.pop(m)
                    a, a_b, a_d = ab.next()
                    c.op("dve", lambda: nc.vector.tensor_tensor(a[:, 2:], g[:, 2:], cv[:, 2:], ALU.mult),
                         reads=[g_b, cv_b], pwrites=[a_b])
                    c.dma("sp", act[m], a[:], a_d, reads=[a_b], writes=[act_b[m]])
            return h

        groups = []
        for m0 in range(0, 86, 2):
            groups += [(m0 * 128, 256, [(0, 128, hup(m0, False)), (128, 128, hup(m0 + 1, False))]),
                       (DFF + m0 * 128, 256, [(0, 128, hup(m0, True)), (128, 128, hup(m0 + 1, True))])]
        dn.fm(w_up, 32, hg, hg_b, groups)
        olds = olds + hg_b + ug.b + uv.b + cvg.b + cvv.b + ab.b
    es_hg.close()

    accf = sb("accf", [128, TL]); accf_b = fresh(olds)
    c.op("dve", lambda: nc.vector.memset(accf[:], 0.0), writes=[accf_b])
    asb = sb("asb", [128, 43, TL], BF16); asb_b = [fresh(olds) for _ in range(43)]
    d_a = c.dsem("asb")
    actv = act.rearrange("m p t -> p m t")
    for part in range(2):
        for g in range(0, 43, 8):
            g1 = min(43, g + 8)
            c.dma("sp", asb[:, g:g1, :], actv[:, part * 43 + g:part * 43 + g1, :], d_a,
                  reads=act_b[part * 43 + g:part * 43 + g1], writes=asb_b[g:g1])
        Ctx.seal(d_a, asb_b)
        for m in range(32):
            pb = dn.it % 2
            dn.it += 1
            pst = [(dn.psA[pb], dn.bA[pb]), (dn.psB[pb], dn.bB[pb]), (dn.psC, dn.bC)]
            wt, wb = dn.load(w_down, 43, m * 128, 128, row0=part * 43 * 128)
            for k in range(43):
                last = k == 42
                for j, (t0, tn) in enumerate(TT):
                    ps, pbuf = pst[j]
                    c.op("pe", lambda: nc.tensor.matmul(ps[:, :tn], wt[:, k, 0:128], asb[:, k, t0:t0 + tn], start=(k == 0), stop=last),
                         reads=[wb, asb_b[k]], writes=[pbuf], inc=last)
            t, b, d = hc.next()
            if part == 0:
                c.dma("sp", t[:], h_mid[:, m, :], d, reads=[hmid_b[m]], writes=[b])
            else:
                c.dma("sp", t[:], h_out[:, m, :], d, reads=[hout_b[m]], writes=[b])
            for j, (t0, tn) in enumerate(TT):
                ps, pbuf = pst[j]
                c.op("dve", lambda: nc.vector.tensor_tensor(t[:, t0:t0 + tn], ps[:, :tn], t[:, t0:t0 + tn], ALU.add),
                     reads=[pbuf, b], pwrites=[b])
            c.dma("sp", h_out[:, m, :], t[:], d, reads=[b], writes=[hout_b[m]])
            if part == 1 and final:
                s_, s_b, _ = sq.next()
                c.op("act", lambda: nc.scalar.activation(s_[:], t[:], AF.Square), reads=[b], writes=[s_b])
                c.op("dve", lambda: nc.vector.tensor_tensor(accf[:], accf[:], s_[:], ALU.add), reads=[accf_b, s_b], writes=[accf_b])
    if not final:
        c.wait_all("sp", hout_b)
        return
    col_stats(c, dn, accf, accf_b, rstd, rstd_b, D)
    gn, gn_b = K["g_next"]
    for m in range(32):
        t, b, d = hc.next()
        c.dma("sp", t[:], h_out[:, m, :], d, reads=[hout_b[m]], writes=[b])
        c.op("dve", lambda: nc.vector.scalar_tensor_tensor(t[:], t[:], gn[:, m:m + 1], rstd[:], ALU.mult, ALU.mult),
             reads=[b, gn_b, rstd_b], writes=[b])
        c.dma("sp", y_out[:, m, :], t[:], d, reads=[b], pwrites=[y_b])
    c.wait_all("sp", [y_b] + hout_b)


QT = [(i * 512, 512) for i in range(8)] + [(4096, 16)]
NKC = 33
A_Q, A_KN, A_KPE, A_FQ, A_FK, NA = 0, 576, 960, 1024, 1408, 1792
V_VM, V_FV, V_GV, NV = 0, 384, 768, 1024
F_GQ, F_GK, F_GZ, F_GR, F_FZ, NF = 0, 128, 256, 273, 529, 532
GCH = [(0, 16)] + [(16 + 64 * i, 64) for i in range(64)]


def build_p2():
    nc = bass.Bass("TRN2", target_bir_lowering=False)
    dt = lambda n, s, d=F32, k="ExternalInput": nc.dram_tensor(n, s, d, kind=k).ap()
    a = dict(
        abf=dt("abf", [NA, NT], BF16), vbf=dt("vbf", [NT, NV], BF16), f32=dt("f32", [NF, NT]),
        gkm=dt("gkm", [NT, 128]), wg2=dt("wg2", [17, 128]), fbf=dt("fbf", [3, 1]),
        gmla=dt("gmla", [128, 3]), gfox=dt("gfox", [128, 3]), ggla=dt("ggla", [128, 2]),
        mask4=dt("mask4", [128, 4, 512], BF16), tri01=dt("tri01", [64, 64]),
        tris64=dt("tris64", [64, 65]), tris16=dt("tris16", [16, 17]), sus=dt("sus", [64, 64]),
        oT=dt("oT", [1024, NT], BF16, "ExternalOutput"),
        aug=dt("aug_scr", [3, 12, NT], BF16, "Internal"))
    with ExitStack() as es:
        c = Ctx(nc, es)
        emit_p2(c, **a)
    return nc


def emit_p2(c, abf, vbf, f32, gkm, wg2, fbf, gmla, gfox, ggla, mask4, tri01, tris64, tris16, sus, oT, aug,
            mid_hook=None, rows_done=None):
    nc, es = c.nc, c.es
    sb = lambda n, s, d=F32, st=None: (st or es).enter_context(nc.sbuf_tensor(c.nm(n), s, d))
    K = _consts(c, [("wg2", wg2, [17, 128]), ("fbf", fbf, [3, 1]), ("gmla", gmla, [128, 3]), ("gfox", gfox, [128, 3]),
                    ("ggla", ggla, [128, 2]), ("tri01", tri01, [64, 64]), ("tris64", tris64, [64, 65]),
                    ("tris16", tris16, [16, 17]), ("sus", sus, [64, 64])])
    mk = sb("mk_sb", [128, 4, 512], BF16); mk_b = Buf()
    c.dma("sp", mk[:], mask4, c.dsem("mk"), writes=[mk_b])
    ones = sb("ones", [128, 512]); ones_b = Buf()
    c.op("dve", lambda: nc.vector.memset(ones[:], 1.0), writes=[ones_b])
    onesb = sb("onesb", [128, 128], BF16); onesb_b = Buf()
    c.op("dve", lambda: nc.vector.memset(onesb[:], 1.0), writes=[onesb_b])
    P = [es.enter_context(nc.psum_tensor(c.nm(f"pp{i}"), [128, 512], F32)) for i in range(7)]
    Pb = [Buf() for _ in range(7)]
    out_b = Buf()
    blk_b = [Buf() for _ in range(8)]
    stb = Ring(c, "stb", 3, [128, 512], BF16)
    olds = []

    def head_norm_out(o_parts, gain, gain_b, gcol0, n, q0, row0, extra=None):
        nf = 128 * len(o_parts)
        for i, (o, o_b) in enumerate(o_parts):
            s, s_b, _ = sqr.next()
            c.op("act", lambda: nc.scalar.activation(s[:, :n], o, AF.Square), reads=[o_b], writes=[s_b])
            c.op("pe", lambda: nc.tensor.matmul(P[6][:, :n], ones[:, :128], s[:, :n], start=(i == 0), stop=(i == len(o_parts) - 1)),
                 reads=[s_b, ones_b], writes=[Pb[6]])
        r, r_b, _ = rsr.next()
        c.op("dve", lambda: nc.vector.tensor_scalar(r[:, :n], P[6][:, :n], 1.0 / nf, EPS, ALU.mult, ALU.add), reads=[Pb[6]], writes=[r_b])
        c.op("act", lambda: nc.scalar.activation(r[:, :n], r[:, :n], AF.Sqrt), reads=[r_b], writes=[r_b])
        c.op("dve", lambda: nc.vector.reciprocal(r[:, :n], r[:, :n]), reads=[r_b], writes=[r_b])
        for i, (o, o_b) in enumerate(o_parts):
            t, b, d = stb.next()
            if extra is None:
                c.op("dve", lambda: nc.vector.scalar_tensor_tensor(t[:, :n], o, gain[:, gcol0 + i:gcol0 + i + 1], r[:, :n], ALU.mult, ALU.mult),
                     reads=[o_b, gain_b, r_b], writes=[b])
            else:
                ex, ex_b = extra[i]
                c.op("dve", lambda: nc.vector.scalar_tensor_tensor(o, o, gain[:, gcol0 + i:gcol0 + i + 1], r[:, :n], ALU.mult, ALU.mult),
                     reads=[o_b, gain_b, r_b], writes=[o_b])
                c.op("dve", lambda: nc.vector.tensor_tensor(t[:, :n], o, ex, ALU.mult), reads=[o_b, ex_b], writes=[b])
            c.dma("sp", oT[row0 + i * 128:row0 + (i + 1) * 128, q0:q0 + n], t[:, :n], d, reads=[b],
                  pwrites=[out_b, blk_b[row0 // 128 + i]])

    sqr = Ring(c, "sqr", 2, [128, 512], F32)
    rsr = Ring(c, "rsr", 2, [128, 512], F32)

    with ExitStack() as eg:
        wg, wg_b = K["wg2"]
        tri, tri_b = K["tri01"]
        ts64, ts64_b = K["tris64"]
        ts16, ts16_b = K["tris16"]
        su, su_b = K["sus"]
        gv = sb("gv", [64, 65, 256], BF16, eg); gv_b = Buf()
        d_gv = c.dsem("gv")
        c.dma("sp", gv[:16, 0, :], vbf[0:16, V_GV:V_GV + 256], d_gv, writes=[gv_b])
        for i in range(4):
            c.dma("sp", gv[:, 1 + 16 * i:17 + 16 * i, :],
                  vbf[16 + 1024 * i:16 + 1024 * (i + 1), V_GV:V_GV + 256].rearrange("(n p) d -> p n d", p=64), d_gv, pwrites=[gv_b])
        qdec = sb("qdec", [128, NT], BF16, eg); qdec_b = [Buf() for _ in range(65)]
        kst = sb("kst", [64, 65, 128], BF16, eg); kst_b = [Buf() for _ in range(65)]
        Aall = sb("Aall", [64, 65, 64], BF16, eg); A_b = [Buf() for _ in range(65)]
        dec = sb("dec", [128, 65], F32, eg); dec_b = [Buf() for _ in range(65)]
        oall = sb("oall_sb", [128, 2, NT], F32, eg); oall_b = [Buf() for _ in range(9)]
        S = sb("S", [128, 256], F32, eg); S_b = Buf()
        Sbf = sb("Sbf", [128, 256], BF16, eg); Sbf_b = Buf()
        gzr = Ring(c, "gzr", 2, [17, 512], F32, eg)
        gqr = Ring(c, "gqr", 2, [128, 512], F32, eg)
        gkr = Ring(c, "gkr", 2, [128, 512], F32, eg)
        gkmr = Ring(c, "gkmr", 2, [64, 8, 128], F32, eg)
        spr = Ring(c, "spr", 2, [64, 8, 128], F32, eg)
        e4 = Ring(c, "e4", 2, [128, 64], F32, eg)
        ek = Ring(c, "ek", 2, [64, 128], F32, eg)
        kdr = Ring(c, "kdr", 2, [128, 64], BF16, eg)
        supers = [(0, 16, [0])] + [(16 + 512 * j, 512, list(range(1 + 8 * j, 9 + 8 * j))) for j in range(8)]
        for (r0, rn, chunks) in supers:
            gz, gz_b, gz_d = gzr.next()
            c.dma("sp", gz[:, :rn], f32[F_GZ:F_GZ + 17, r0:r0 + rn], gz_d, writes=[gz_b])
            gq, gq_b, gq_d = gqr.next()
            c.dma("sp", gq[:, :rn], f32[F_GQ:F_GQ + 128, r0:r0 + rn], gq_d, writes=[gq_b])
            gk, gk_b, gk_d = gkr.next()
            c.dma("sp", gk[:, :rn], f32[F_GK:F_GK + 128, r0:r0 + rn], gk_d, writes=[gk_b])
            gm, gm_b, gm_d = gkmr.next()
            C = GCH[chunks[0]][1]
            nch = len(chunks)
            c.dma("sp", gm[:C, :nch, :], gkm[r0:r0 + rn, :].rearrange("(n p) d -> p n d", p=C), gm_d, writes=[gm_b])
            sp, sp_b, _ = spr.next()
            for g4 in range(0, nch, 4):
                n4 = min(4, nch - g4)
                for i in range(n4):
                    o0 = (g4 + i) * C
                    c.op("pe", lambda: nc.tensor.matmul(P[0][:C, i * 128:(i + 1) * 128], gz[:, o0:o0 + C], wg[:], start=True, stop=True),
                         reads=[gz_b, wg_b], writes=[Pb[0]])
                c.op("act", lambda: nc.scalar.activation(sp[:C, g4:g4 + n4, :], P[0][:C, :n4 * 128].rearrange("p (a b) -> p a b", b=128), AF.Exp, scale=-1.0),
                     reads=[Pb[0]], pwrites=[sp_b])
            c.op("act", lambda: nc.scalar.activation(sp[:C, :nch, :], sp[:C, :nch, :], AF.Ln, bias=1.0), reads=[sp_b], writes=[sp_b])
            for i, n in enumerate(chunks):
                s0, C = GCH[n]
                l0 = i * C
                tsx, tsx_b = (ts64, ts64_b) if C == 64 else (ts16, ts16_b)
                c.op("pe", lambda: nc.tensor.matmul(P[1][:, :C + 1], sp[:C, i, :], tsx[:C, :C + 1], start=True, stop=True),
                     reads=[sp_b, tsx_b], writes=[Pb[1]])
                c.op("pe", lambda: nc.tensor.matmul(P[2][:C, :128], su[:C, :C], sp[:C, i, :], start=True, stop=True),
                     reads=[sp_b, su_b], writes=[Pb[2]])
                eb, eb_b, _ = e4.next()
                c.op("act", lambda: nc.scalar.activation(eb[:, :C], P[1][:, :C], AF.Exp), reads=[Pb[1]], writes=[eb_b])
                c.op("dve", lambda: nc.vector.tensor_tensor(qdec[:, s0:s0 + C], gq[:, l0:l0 + C], eb[:, :C], ALU.mult),
                     reads=[gq_b, eb_b], writes=[qdec_b[n]])
                en, en_b, _ = e4.next()
                c.op("act", lambda: nc.scalar.activation(en[:, :C], P[1][:, :C], AF.Exp, scale=-1.0), reads=[Pb[1]], writes=[en_b])
                kd, kd_b, _ = kdr.next()
                c.op("dve", lambda: nc.vector.tensor_tensor(kd[:, :C], gk[:, l0:l0 + C], en[:, :C], ALU.mult),
                     reads=[gk_b, en_b], writes=[kd_b])
                c.op("act", lambda: nc.scalar.activation(dec[:, n:n + 1], P[1][:, C:C + 1], AF.Exp), reads=[Pb[1]], writes=[dec_b[n]])
                ekt, ek_b, _ = ek.next()
                c.op("act", lambda: nc.scalar.activation(ekt[:C, :], P[2][:C, :128], AF.Exp), reads=[Pb[2]], writes=[ek_b])
                c.op("dve", lambda: nc.vector.tensor_tensor(kst[:C, n, :], gm[:C, i, :], ekt[:C, :], ALU.mult),
                     reads=[gm_b, ek_b], writes=[kst_b[n]])
                c.op("pe", lambda: nc.tensor.matmul(P[3][:C, :C], kd[:, :C], qdec[:, s0:s0 + C], start=True, stop=True),
                     reads=[kd_b, qdec_b[n]], writes=[Pb[3]])
                c.op("dve", lambda: nc.vector.tensor_tensor(Aall[:C, n, :C], P[3][:C, :C], tri[:C, :C], ALU.mult),
                     reads=[Pb[3], tri_b], writes=[A_b[n]])
        c.op("dve", lambda: nc.vector.memset(S[:], 0.0), writes=[S_b])
        for n, (s0, C) in enumerate(GCH):
            po, po_b = P[4 + n % 2], Pb[4 + n % 2]
            for half in range(2):
                c.op("pe", lambda: nc.tensor.matmul(po[:, half * 64:half * 64 + C], gv[:C, n, half * 128:(half + 1) * 128], Aall[:C, n, :C],
                                                    start=True, stop=(n == 0)),
                     reads=[gv_b, A_b[n]], writes=[po_b])
                if n > 0:
                    c.op("pe", lambda: nc.tensor.matmul(po[:, half * 64:half * 64 + C], Sbf[:, half * 128:(half + 1) * 128], qdec[:, s0:s0 + C],
                                                        start=False, stop=True),
                         reads=[Sbf_b, qdec_b[n]], writes=[po_b])
            ti = 0 if n == 0 else 0 + (s0 // 512)
            for half in range(2):
                c.op("act", lambda: nc.scalar.copy(oall[:, half, s0:s0 + C], po[:, half * 64:half * 64 + C]),
                     reads=[po_b], pwrites=[oall_b[min(8, s0 // 512)], oall_b[min(8, (s0 + C - 1) // 512)]])
            c.op("pe", lambda: nc.tensor.matmul(P[0][:, :256], kst[:C, n, :], gv[:C, n, :], start=True, stop=True),
                 reads=[kst_b[n], gv_b], writes=[Pb[0]])
            c.op("dve", lambda: nc.vector.scalar_tensor_tensor(S[:], S[:], dec[:, n:n + 1], P[0][:, :256], ALU.mult, ALU.add),
                 reads=[S_b, dec_b[n], Pb[0]], writes=[S_b])
            c.op("act", lambda: nc.scalar.copy(Sbf[:], S[:]), reads=[S_b], writes=[Sbf_b])
        gg, gg_b = K["ggla"]
        grr = Ring(c, "grr", 2, [128, 2, 512], F32, eg)
        for ti, (q0, n) in enumerate(QT):
            gr, gr_b, gr_d = grr.next()
            c.dma("sp", gr[:, :, :n], f32[F_GR:F_GR + 256, q0:q0 + n].rearrange("(h p) t -> p h t", p=128), gr_d, writes=[gr_b])
            head_norm_out([(oall[:, hh, q0:q0 + n], oall_b[ti]) for hh in range(2)], gg, gg_b, 0, n, q0, 384,
                          extra=[(gr[:, hh, :n], gr_b) for hh in range(2)])
        olds = [gv_b, Sbf_b, S_b] + qdec_b + kst_b + A_b + dec_b + oall_b + gzr.b + gqr.b + gkr.b + gkmr.b + spr.b + e4.b + ek.b + kdr.b + grr.b
        if rows_done is not None:
            rows_done(384, 256, blk_b[3:5])
    if mid_hook is not None:
        c.barrier(exclude=getattr(c, "soft", ()))
        with ExitStack() as eh:
            keep = c.es
            c.es = eh
            mid_hook()
            c.es = keep
        c.barrier(exclude=getattr(c, "soft", ()))

    with ExitStack() as ea:
        qn = Ring(c, "qn", 2, [128, NT], BF16, ea)
        qp = Ring(c, "qp", 2, [64, NT], BF16, ea)
        kn = Ring(c, "kn", 2, [128, NT], BF16, ea)
        vv = Ring(c, "vv", 2, [128, NKC, 128], BF16, ea)
        kpe = sb("kpe", [64, NT], BF16, ea); kpe_b = fresh(olds)
        pT = Ring(c, "pT", 3, [128, 512], BF16, ea)
        osb = Ring(c, "osb", 2, [128, 512], F32, ea)
        rl = Ring(c, "rl", 2, [128, 512], F32, ea)
        exr = Ring(c, "exr", 2, [128, 512], F32, ea)
        mx = sb("mx", [128, 4], F32, ea); mx_b = fresh(olds)
        negm = sb("negm", [128, 1], F32, ea); negm_b = fresh(olds)
        nfb = sb("nfb", [3, 1], F32, ea); nfb_b = fresh(olds)
        aq = Ring(c, "aq", 1, [6, NT], BF16, ea)
        ak = Ring(c, "ak", 1, [6, NT], BF16, ea)
        for r in (qn, qp, kn, vv, pT, osb, rl, exr, aq, ak):
            r.b = [fresh(olds) for _ in r.b]
        c.dma("sp", kpe[:], abf[A_KPE:A_KPE + 64, :], c.dsem("kpe"), writes=[kpe_b])
        aug_b = Buf()
        with ExitStack() as ef:
            fz = sb("fz", [3, NT], F32, ef); fz_b = fresh(olds)
            cs = sb("cs", [3, NT], F32, ef); cs_b = fresh(olds)
            spl = Ring(c, "spl", 2, [3, NT], BF16, ef)
            spl.b = [fresh(olds) for _ in spl.b]
            fb, fb_b = K["fbf"]
            c.dma("sp", fz[:], f32[F_FZ:F_FZ + 3, :], c.dsem("fz"), writes=[fz_b])
            t1, t1_b, t1_d = spl.next()
            c.op("dve", lambda: nc.vector.memset(t1[:], 1.0), writes=[t1_b])
            for r in range(3, 9):
                c.dma("sp", aug[:, r, :], t1[:], t1_d, reads=[t1_b], pwrites=[aug_b])
            c.op("dve", lambda: nc.vector.tensor_scalar(nfb[:], fb[:], -1.0, None, ALU.mult), reads=[fb_b], writes=[nfb_b])
            c.op("act", lambda: nc.scalar.activation(fz[:], fz[:], AF.Exp, bias=nfb[:], scale=-1.0), reads=[fz_b, nfb_b], writes=[fz_b])
            c.op("act", lambda: nc.scalar.activation(fz[:], fz[:], AF.Ln, bias=1.0), reads=[fz_b], writes=[fz_b])
            c.op("dve", lambda: nc.vector.tensor_scalar(fz[:], fz[:], -1.0, None, ALU.mult), reads=[fz_b], writes=[fz_b])
            for j, (q0, n) in enumerate(QT):
                init = 0.0 if j == 0 else cs[:, q0 - 1:q0]
                c.op("dve", lambda: nc.vector.tensor_tensor_scan(cs[:, q0:q0 + n], ones[:3, :n], fz[:, q0:q0 + n], init, ALU.mult, ALU.add),
                     reads=[ones_b, fz_b, cs_b], writes=[cs_b])
            for i in range(3):
                t1, t1_b, t1_d = spl.next()
                c.op("dve", lambda: nc.vector.tensor_copy(t1[:], cs[:]), reads=[cs_b], writes=[t1_b])
                c.dma("sp", aug[:, i, :], t1[:], t1_d, reads=[t1_b], pwrites=[aug_b])
                if i < 2:
                    c.op("dve", lambda: nc.vector.tensor_tensor(cs[:], cs[:], t1[:], ALU.subtract), reads=[cs_b, t1_b], writes=[cs_b])
                t2, t2_b, t2_d = spl.next()
                c.op("act", lambda: nc.scalar.mul(t2[:], t1[:], -1.0), reads=[t1_b], writes=[t2_b])
                c.dma("sp", aug[:, 9 + i, :], t2[:], t2_d, reads=[t2_b], pwrites=[aug_b])
            olds = olds + [fz_b, cs_b] + spl.b

        heads = [("mla", h) for h in range(3)] + [("fox", h) for h in range(3)]
        for (kind, h) in heads:
            q_t, q_b, q_d = qn.next()
            k_t, k_b, k_d = kn.next()
            v_t, v_b, v_d = vv.next()
            parts = []
            if kind == "mla":
                qrow, krow, vcol, orow = A_Q + h * 192, A_KN + h * 128, V_VM + h * 128, h * 128
                gain, gain_b = K["gmla"]
                p_t, p_b, p_d = qp.next()
                c.dma("sp", p_t[:], abf[qrow + 128:qrow + 192, :], p_d, writes=[p_b])
                parts = [(k_t, k_b, q_t, q_b, 128), (kpe, kpe_b, p_t, p_b, 64)]
            else:
                qrow, krow, vcol, orow = A_FQ + h * 128, A_FK + h * 128, V_FV + h * 128, 640 + h * 128
                gain, gain_b = K["gfox"]
                a_q, a_qb, a_qd = aq.next()
                a_k, a_kb, a_kd = ak.next()
                c.dma("sp", a_q[:], aug[h, 0:6, :], a_qd, reads=[aug_b], writes=[a_qb])
                c.dma("sp", a_k[:], aug[h, 6:12, :], a_kd, reads=[aug_b], writes=[a_kb])
                parts = [(k_t, k_b, q_t, q_b, 128), (a_k, a_kb, a_q, a_qb, 6)]
            c.dma("sp", q_t[:], abf[qrow:qrow + 128, :], q_d, writes=[q_b])
            c.dma("sp", k_t[:], abf[krow:krow + 128, :], k_d, writes=[k_b])
            for i in range(4):
                c.dma("sp", v_t[:, 8 * i:8 * i + 8, :], vbf[1024 * i:1024 * (i + 1), vcol:vcol + 128].rearrange("(n p) d -> p n d", p=128),
                      v_d, writes=[v_b] if i == 0 else [], pwrites=[v_b] if i else [])
            c.dma("sp", v_t[:16, 32, :], vbf[4096:4112, vcol:vcol + 128], v_d, pwrites=[v_b])
            c.op("dve", lambda: nc.vector.memset(mx[:], 0.0), writes=[mx_b])
            for side in range(2):
                plist = [(p[2], p[3], p[4]) if side == 0 else (p[0], p[1], p[4]) for p in parts if p[4] > 6]
                for (q0, n) in QT:
                    for i, (t_, b_, kp) in enumerate(plist):
                        s, s_b, _ = sqr.next()
                        c.op("act", lambda: nc.scalar.activation(s[:kp, :n], t_[:kp, q0:q0 + n], AF.Square), reads=[b_], writes=[s_b])
                        c.op("pe", lambda: nc.tensor.matmul(P[6][:, :n], ones[:kp, :128], s[:kp, :n], start=(i == 0), stop=(i == len(plist) - 1)),
                             reads=[s_b, ones_b], writes=[Pb[6]])
                    c.op("dve", lambda: nc.vector.reduce_max(mx[:, 2:3], P[6][:, :n], axis=AX.X), reads=[Pb[6]], writes=[mx_b])
                    c.op("dve", lambda: nc.vector.tensor_tensor(mx[:, side:side + 1], mx[:, side:side + 1], mx[:, 2:3], ALU.max),
                         reads=[mx_b], writes=[mx_b])
            c.op("dve", lambda: nc.vector.tensor_tensor(mx[:, 3:4], mx[:, 0:1], mx[:, 1:2], ALU.mult), reads=[mx_b], writes=[mx_b])
            c.op("act", lambda: nc.scalar.activation(negm[:], mx[:, 3:4], AF.Sqrt), reads=[mx_b], writes=[negm_b])
            c.op("dve", lambda: nc.vector.tensor_scalar(negm[:], negm[:], -1.0, None, ALU.mult), reads=[negm_b], writes=[negm_b])
            for ti, (q0, n) in enumerate(QT):
                last_c = min(4 * ti + 3, NKC - 1)
                po, po_b = P[2 + ti % 2], Pb[2 + ti % 2]
                pl, pl_b = P[4 + ti % 2], Pb[4 + ti % 2]
                pend = None

                def pv(kc_, kn2, p2, p2_b):
                    c.op("pe", lambda: nc.tensor.matmul(po[:, :n], v_t[:kn2, kc_, :], p2[:kn2, :n], start=(kc_ == 0), stop=(kc_ == last_c)),
                         reads=[v_b, p2_b], writes=[po_b], inc=(kc_ == last_c))
                    c.op("pe", lambda: nc.tensor.matmul(pl[:, :n], onesb[:kn2, :], p2[:kn2, :n], start=(kc_ == 0), stop=(kc_ == last_c)),
                         reads=[onesb_b, p2_b], writes=[pl_b], inc=True)

                for kc in range(last_c + 1):
                    k0 = kc * 128
                    kn_ = min(128, NT - k0)
                    ps, ps_b = P[kc % 2], Pb[kc % 2]
                    for i, (kt_, kb_, qt_, qb_, kp) in enumerate(parts):
                        c.op("pe", lambda: nc.tensor.matmul(ps[:kn_, :n], kt_[:kp, k0:k0 + kn_], qt_[:kp, q0:q0 + n],
                                                            start=(i == 0), stop=(i == len(parts) - 1)),
                             reads=[kb_, qb_], writes=[ps_b], inc=(i == len(parts) - 1))
                    if pend is not None:
                        pv(*pend)
                    p_, p_b2, _ = pT.next()
                    r = kc - 4 * ti
                    if r >= 0 and kind == "fox":
                        x_, x_b, _ = exr.next()
                        c.op("dve", lambda: nc.vector.tensor_scalar(x_[:kn_, :n], ps[:kn_, :n], negm[:kn_, :], 0.0, ALU.add, ALU.min),
                             reads=[ps_b, negm_b], writes=[x_b])
                        c.op("act", lambda: nc.scalar.activation(p_[:kn_, :n], x_[:kn_, :n], AF.Exp), reads=[x_b], writes=[p_b2])
                    else:
                        c.op("act", lambda: nc.scalar.activation(p_[:kn_, :n], ps[:kn_, :n], AF.Exp, bias=negm[:kn_, :]),
                             reads=[ps_b, negm_b], writes=[p_b2])
                    if r >= 0:
                        c.op("dve", lambda: nc.vector.tensor_tensor(p_[:kn_, :n], p_[:kn_, :n], mk[:kn_, r, :n], ALU.mult),
                             reads=[p_b2, mk_b], writes=[p_b2])
                    pend = (kc, kn_, p_, p_b2)
                pv(*pend)
                r_, r_b, _ = rl.next()
                c.op("dve", lambda: nc.vector.reciprocal(r_[:, :n], pl[:, :n]), reads=[pl_b], writes=[r_b])
                o_, o_b, _ = osb.next()
                c.op("dve", lambda: nc.vector.tensor_tensor(o_[:, :n], po[:, :n], r_[:, :n], ALU.mult), reads=[po_b, r_b], writes=[o_b])
                head_norm_out([(o_[:, :n], o_b)], gain, gain_b, h, n, q0, orow)
            if rows_done is not None:
                rows_done(orow, 128, [blk_b[orow // 128]])
    c.wait_all("sp", [out_b])


_PROG = {}


def _prog(name, builder):
    if name not in _PROG:
        _PROG[name] = builder()
    return _PROG[name]


def _fmaj(a):
    T = a.shape[0]
    return np.ascontiguousarray(a.T.reshape(-1, 128, T).transpose(1, 0, 2))


def _unfm(a):
    return a.transpose(1, 0, 2).reshape(-1, a.shape[2]).T


def _pcol(v, n):
    return np.ascontiguousarray(np.asarray(v).reshape(n, 128).T)


def _p2_consts():
    kk = np.arange(128)[:, None]
    qq = np.arange(512)[None, :]
    mask4 = np.stack([(qq >= r * 128 + kk) for r in range(4)], 1).astype(NPBF)
    s = np.arange(64)[:, None]
    t = np.arange(64)[None, :]
    m16 = np.float32(-1.0 / 16.0)
    tri01 = (s <= t).astype(np.float32)
    tris64 = np.concatenate([(s <= t) * m16, np.full((64, 1), m16)], 1).astype(np.float32)
    tris16 = np.ascontiguousarray(np.concatenate([tris64[:16, :16], tris64[:16, 64:65]], 1))
    sus = ((s > t) * m16).astype(np.float32)
    return dict(mask4=mask4, tri01=tri01, tris64=tris64, tris16=tris16, sus=sus)


def _core_cols(g4):
    s = NMETA + g4 * OWN
    return np.concatenate([np.arange(s - 2, s + OWN), np.array([0, 0]), np.arange(0, NMETA)])


def kernel_unfused(x, meta_tokens, attn_norm, w_in, mla_q_norm, mla_w_uq, mla_kv_norm, mla_w_ukv,
           gla_w_gate2, gla_b_gate, fox_b_f, out_norm_mla, out_norm_gla, out_norm_fox,
           w_out, ffn_norm, ffn_w_up, ffn_conv_w, ffn_conv_b, ffn_w_down, final_norm):
    f32 = np.float32
    x = np.asarray(x, f32)
    B = x.shape[0]
    cores = list(range(8))
    h = np.concatenate([np.broadcast_to(np.asarray(meta_tokens, f32)[None], (B, NMETA, D)), x], axis=1)
    pos = np.arange(NT, dtype=f32)
    inv = (f32(1.0) / (f32(10000.0) ** (np.arange(0, 64, 2, dtype=f32) / f32(64)))).astype(f32)
    ang = (pos[:, None] * inv[None, :]).astype(f32)
    cosT, sinT = np.cos(ang).astype(f32).T, np.sin(ang).astype(f32).T
    zero_cols = np.array([2 + OWN, 3 + OWN])
    p2c = _p2_consts()
    p1, p2, p3 = _prog("p1", build_p1), _prog("p2", build_p2), _prog("p3", build_p3)
    y_final = None
    for l in range(2):
        uq3 = np.asarray(mla_w_uq[l]).reshape(1536, 12, 192)
        w_uq_p = np.ascontiguousarray(np.concatenate(
            [uq3[:, :, :128].reshape(1536, -1), uq3[:, :, 128:160].reshape(1536, -1), uq3[:, :, 160:].reshape(1536, -1)], 1))
        kv3 = np.asarray(mla_w_ukv[l]).reshape(512, 12, 256)
        w_ukv_p = np.ascontiguousarray(np.concatenate([kv3[:, :, :128].reshape(512, -1), kv3[:, :, 128:].reshape(512, -1)], 1))
        hTs = []
        maps = []
        for core in cores:
            b, g4 = divmod(core, 4)
            cols = _core_cols(g4)
            Hc = h[b][cols]
            Hc[zero_cols] = 0
            hT = _fmaj(Hc)
            hTs.append(hT)
            cs = cosT[:, cols].copy(); sn = sinT[:, cols].copy()
            maps.append(dict(hT=hT, w_in=np.asarray(w_in[l]), w_uq=w_uq_p, w_ukv=w_ukv_p,
                             g_attn=_pcol(attn_norm[l], 32), g_q=_pcol(mla_q_norm[l], 12), g_kv=_pcol(mla_kv_norm[l], 4),
                             cos4=np.ascontiguousarray(np.tile(cs, (4, 1))), sin4=np.ascontiguousarray(np.tile(sn, (4, 1)))))
        r1 = run_bass_kernel_spmd(p1, maps, core_ids=cores).results
        del maps
        maps = []
        for b in range(B):
            def gather_fm(name):
                parts = [r1[b * 4][name][:, 4 + OWN:4 + OWN + NMETA]] + [r1[b * 4 + g][name][:, 2:2 + OWN] for g in range(4)]
                return np.concatenate(parts, axis=1)

            def gather_tm(name):
                parts = [r1[b * 4][name][4 + OWN:4 + OWN + NMETA]] + [r1[b * 4 + g][name][2:2 + OWN] for g in range(4)]
                return np.concatenate(parts, axis=0)
            obf, o32, otf, otb = gather_fm("obf"), gather_fm("o32"), gather_tm("otf"), gather_tm("otb")
            for g in range(4):
                abf = np.concatenate([obf[RB_Q + 3 * g * 192:RB_Q + 3 * (g + 1) * 192],
                                      obf[RB_KN + 3 * g * 128:RB_KN + 3 * (g + 1) * 128],
                                      obf[RB_KPE:RB_KPE + 64],
                                      obf[RB_FQ + 3 * g * 128:RB_FQ + 3 * (g + 1) * 128],
                                      obf[RB_FK + 3 * g * 128:RB_FK + 3 * (g + 1) * 128]], 0)
                vbf = np.concatenate([otb[:, CB_VM + 3 * g * 128:CB_VM + 3 * (g + 1) * 128],
                                      otb[:, CB_FV + 3 * g * 128:CB_FV + 3 * (g + 1) * 128],
                                      otb[:, CB_GV + g * 256:CB_GV + (g + 1) * 256]], 1)
                ff = np.concatenate([o32[R32_GQ + g * 128:R32_GQ + (g + 1) * 128],
                                     o32[R32_GK + g * 128:R32_GK + (g + 1) * 128],
                                     o32[R32_GZ:R32_GZ + 16], np.ones((1, NT), f32),
                                     o32[R32_GR + g * 256:R32_GR + (g + 1) * 256],
                                     o32[R32_FZ + 3 * g:R32_FZ + 3 * (g + 1)]], 0)
                wg2 = np.concatenate([np.asarray(gla_w_gate2[l])[:, g * 128:(g + 1) * 128],
                                      np.asarray(gla_b_gate[l])[None, g * 128:(g + 1) * 128]], 0).astype(f32)
                maps.append(dict(
                    abf=np.ascontiguousarray(abf), vbf=np.ascontiguousarray(vbf), f32=np.ascontiguousarray(ff),
                    gkm=np.ascontiguousarray(otf[:, g * 128:(g + 1) * 128]), wg2=np.ascontiguousarray(wg2),
                    fbf=np.ascontiguousarray(np.asarray(fox_b_f[l], f32)[3 * g:3 * g + 3, None]),
                    gmla=np.ascontiguousarray(np.asarray(out_norm_mla[l], f32).reshape(12, 128)[3 * g:3 * g + 3].T),
                    gfox=np.ascontiguousarray(np.asarray(out_norm_fox[l], f32).reshape(12, 128)[3 * g:3 * g + 3].T),
                    ggla=np.ascontiguousarray(np.asarray(out_norm_gla[l], f32).reshape(4, 2, 128)[g].T),
                    **p2c))
        del r1
        r2 = run_bass_kernel_spmd(p2, maps, core_ids=cores).results
        del maps
        maps = []
        for b in range(B):
            om = np.empty((D, NT), NPBF)
            for g in range(4):
                o = r2[b * 4 + g]["oT"]
                om[3 * g * 128:3 * (g + 1) * 128] = o[0:384]
                om[1536 + g * 256:1536 + (g + 1) * 256] = o[384:640]
                om[2560 + 3 * g * 128:2560 + 3 * (g + 1) * 128] = o[640:1024]
            for g4 in range(4):
                cols = _core_cols(g4)
                oc = om[:, cols]
                oc[:, zero_cols] = 0
                oTc = np.ascontiguousarray(oc.reshape(32, 128, TL).transpose(1, 0, 2))
                gn = final_norm if l == 1 else attn_norm[1]
                cw = np.asarray(ffn_conv_w[l], f32)
                maps.append(dict(oT=oTc, hT=hTs[b * 4 + g4], w_out=np.asarray(w_out[l]), w_up=np.asarray(ffn_w_up[l]),
                                 w_down=np.asarray(ffn_w_down[l]), g_ffn=_pcol(ffn_norm[l], 32), g_next=_pcol(gn, 32),
                                 conv_w=np.ascontiguousarray(cw.T.reshape(172, 128, 3).transpose(1, 0, 2)),
                                 conv_b=_pcol(ffn_conv_b[l], 172)))
        del r2
        r3 = run_bass_kernel_spmd(p3, maps, core_ids=cores).results
        del maps
        for core in cores:
            b, g4 = divmod(core, 4)
            s = NMETA + g4 * OWN
            ho = _unfm(r3[core]["h_out"])
            h[b, s:s + OWN] = ho[2:2 + OWN]
            if g4 == 0:
                h[b, 0:NMETA] = ho[4 + OWN:4 + OWN + NMETA]
        if l == 1:
            y_final = np.empty((B, SEQ, D), f32)
            for core in cores:
                b, g4 = divmod(core, 4)
                y_final[b, g4 * OWN:(g4 + 1) * OWN] = _unfm(r3[core]["y_out"])[2:2 + OWN]
        del r3
    return y_final


class Gath:
    def __init__(self, nc, name, R, C, dtype, esz):
        rp = (1 << 20) // (C * esz)
        if rp >= 64:
            rp = (rp // 64) * 64
        self.C = C
        self.pieces = [(r0, min(rp, R - r0)) for r0 in range(0, R, rp)]
        self.g = [nc.dram_tensor(f"{name}_g{i}", [4 * n, C], dtype, kind="Internal").ap() for i, (r0, n) in enumerate(self.pieces)]
        self.buf = Buf()

    def gather(self, c, X, cs, reads=()):
        for (r0, n), g in zip(self.pieces, self.g):
            c.allgather(g, X[r0:r0 + n, :], cs, reads=reads, writes=[self.buf])

    def gather_where(self, c, X, cs, pred, reads):
        for (r0, pn), g in zip(self.pieces, self.g):
            if pred(r0):
                c.allgather(g, X[r0:r0 + pn, :], cs, reads=reads, writes=[self.buf])

    def gather_rows(self, c, X, cs, row0, n, reads):
        for (r0, pn), g in zip(self.pieces, self.g):
            if r0 >= row0 and r0 + pn <= row0 + n:
                c.allgather(g, X[r0:r0 + pn, :], cs, reads=reads, writes=[self.buf])

    def segs(self, row0, n):
        out = []
        for (r0, pn), g in zip(self.pieces, self.g):
            a, b = max(row0, r0), min(row0 + n, r0 + pn)
            if a < b:
                out.append((g.rearrange("(r n) c -> r n c", r=4)[:, a - r0:b - r0, :], a - row0, b - a))
        return out


def emit_select1(c, sel, G32, GBF, GTF, GTBg, GTBr, abf, vbf, f32, gkm, dst_b, part):
    nc, es = c.nc, c.es
    K = _consts(c, [("sel", sel, [128, 4])])
    sl, sl_b = K["sel"]
    fm_jobs = [(GBF, abf, BF16, [(A_Q, RB_Q, 576, 576), (A_KN, RB_KN, 384, 384), (A_KPE, RB_KPE, 64, 0),
                                 (A_FQ, RB_FQ, 384, 384), (A_FK, RB_FK, 384, 384)], "sb")] if part == "B" else \
              [(G32, f32, F32, [(F_GQ, R32_GQ, 128, 128), (F_GK, R32_GK, 128, 128), (F_GZ, R32_GZ, 16, 0),
                                (F_GR, R32_GR, 256, 256), (F_FZ, R32_FZ, 3, 3)], "sf")]
    for (G, dst, dtype, jobs, tag) in fm_jobs:
        cand = Ring(c, "cand" + tag, 4, [128, NT], dtype)
        accr = Ring(c, "acc" + tag, 2, [128, NT], dtype)
        for (d0, s0, nrows, stride) in jobs:
            for r0 in range(0, nrows, 128):
                n = min(128, nrows - r0)
                a, a_b, a_d = accr.next()
                for g in range(4):
                    t, b, d = cand.next()
                    srow = s0 + g * stride + r0
                    first = True
                    for (gv, p0, ln) in G.segs(srow, n):
                        c.dma("sp", t[p0:p0 + ln, NMETA:].rearrange("p (r t) -> p r t", r=4),
                              gv[:, :, 2:2 + OWN].rearrange("r p t -> p r t"), d, reads=[G.buf],
                              writes=[b] if first else [], pwrites=[] if first else [b])
                        first = False
                        c.dma("sp", t[p0:p0 + ln, :NMETA], gv[0, :, 4 + OWN:4 + OWN + NMETA], d, reads=[G.buf], pwrites=[b])
                    if g == 0:
                        c.op("dve", lambda: nc.vector.tensor_scalar(a[:n, :], t[:n, :], sl[:n, 0:1], None, ALU.mult),
                             reads=[b, sl_b], writes=[a_b])
                    else:
                        c.op("dve", lambda: nc.vector.scalar_tensor_tensor(a[:n, :], t[:n, :], sl[:n, g:g + 1], a[:n, :], ALU.mult, ALU.add),
                             reads=[b, sl_b, a_b], writes=[a_b])
                c.dma("sp", dst[d0 + r0:d0 + r0 + n, :], a[:n, :], a_d, reads=[a_b], pwrites=[dst_b])
    if part == "A":
        on = es.enter_context(nc.sbuf_tensor(c.nm("ones_row"), [1, NT], F32)); on_b = Buf()
        c.op("dve", lambda: nc.vector.memset(on[:], 1.0), writes=[on_b])
        c.dma("sp", f32[F_GZ + 16:F_GZ + 17, :], on[:], c.dsem("onr"), reads=[on_b], pwrites=[dst_b])
    chunks = [(0, 4 + OWN, NMETA, 0)] + [(r, 2 + 128 * i, 128, NMETA + OWN * r + 128 * i) for r in range(4) for i in range(8)]
    if part == "A":
        jobs = [(GTBg, 1024, BF16, vbf, [(V_GV, 0, 256, 256)], "tg"), (GTF, 512, F32, gkm, [(0, 0, 128, 128)], "tk")]
    else:
        jobs = [(GTBr, 3072, BF16, vbf, [(V_VM, CB_VM - CB_FV, 384, 384), (V_FV, 0, 384, 384)], "tr")]
    for (G, width, dtype, dst, blocks, tag) in jobs:
        candt = Ring(c, "cand" + tag, 3, [128, width], dtype)
        acct = Ring(c, "acc" + tag, 2, [128, 768], dtype)
        for (r, srow, n, drow) in chunks:
            a, a_b, a_d = acct.next()
            t, b, d = candt.next()
            first = True
            for (gv, p0, ln) in G.segs(srow, n):
                c.dma("sp", t[p0:p0 + ln, :], gv[r, :, :], d, reads=[G.buf], writes=[b] if first else [], pwrites=[] if first else [b])
                first = False
            for g in range(4):
                for bi, (dc, sc0, w, stride) in enumerate(blocks):
                    sc = sc0 + stride * g
                    ao = sum(bb[2] for bb in blocks[:bi])
                    if g == 0:
                        c.op("dve", lambda: nc.vector.tensor_scalar(a[:n, ao:ao + w], t[:n, sc:sc + w], sl[:n, 0:1], None, ALU.mult),
                             reads=[b, sl_b], pwrites=[a_b])
                    else:
                        c.op("dve", lambda: nc.vector.scalar_tensor_tensor(a[:n, ao:ao + w], t[:n, sc:sc + w], sl[:n, g:g + 1], a[:n, ao:ao + w], ALU.mult, ALU.add),
                             reads=[b, sl_b, a_b], pwrites=[a_b])
            for bi, (dc, sc0, w, stride) in enumerate(blocks):
                ao = sum(bb[2] for bb in blocks[:bi])
                c.dma("sp", dst[drow:drow + n, dc:dc + w], a[:n, ao:ao + w], a_d, reads=[a_b], pwrites=[dst_b])


def _omix_src(kc):
    if kc < 12:
        return kc // 3, (kc % 3) * 128
    if kc < 20:
        return (kc - 12) // 2, 384 + ((kc - 12) % 2) * 128
    return (kc - 20) // 3, 640 + ((kc - 20) % 3) * 128


def emit_select2(c, sel, GO, oT3, dst_b):
    nc, es = c.nc, c.es
    K = _consts(c, [("sel", sel, [128, 4])])
    sl, sl_b = K["sel"]
    cand = Ring(c, "cand2", 4, [128, 2 + OWN], BF16)
    accr = Ring(c, "acc2", 2, [128, TL], BF16)
    for i in range(2):
        c.op("dve", lambda: nc.vector.memset(accr.t[i][:], 0.0), writes=[accr.b[i]])
    for kc in range(32):
        rk, row0 = _omix_src(kc)
        a, a_b, a_d = accr.next()
        segs = GO.segs(row0, 128)
        for (gv, p0, ln) in segs:
            c.dma("sp", a[p0:p0 + ln, 4 + OWN:], gv[rk, :, 0:NMETA], a_d, reads=[GO.buf], pwrites=[a_b])
        for dd in range(4):
            t, b, d = cand.next()
            s0 = NMETA + OWN * dd - 2
            first = True
            for (gv, p0, ln) in segs:
                c.dma("sp", t[p0:p0 + ln, :], gv[rk, :, s0:s0 + 2 + OWN], d, reads=[GO.buf],
                      writes=[b] if first else [], pwrites=[] if first else [b])
                first = False
            if dd == 0:
                c.op("dve", lambda: nc.vector.tensor_scalar(a[:, :2 + OWN], t[:], sl[:, 0:1], None, ALU.mult), reads=[b, sl_b], pwrites=[a_b])
            else:
                c.op("dve", lambda: nc.vector.scalar_tensor_tensor(a[:, :2 + OWN], t[:], sl[:, dd:dd + 1], a[:, :2 + OWN], ALU.mult, ALU.add),
                     reads=[b, sl_b, a_b], pwrites=[a_b])
        c.dma("sp", oT3[:, kc, :], a[:], a_d, reads=[a_b], pwrites=[dst_b])


def emit_halo(c, selh, hbuf, h_b, tail, g_tail, cs):
    nc, es = c.nc, c.es
    K = _consts(c, [("selh", selh, [128, 5])])
    sh, sh_b = K["selh"]
    tail_b, gt_b = Buf(), Buf()
    d = c.dsem("halo")
    c.dma("sp", tail.rearrange("p (k t) -> p k t", t=2), hbuf[:, :, OWN:OWN + 2], d, reads=[h_b], writes=[tail_b])
    c.allgather(g_tail, tail, cs, reads=[tail_b], writes=[gt_b])
    cnd = es.enter_context(nc.sbuf_tensor(c.nm("hcand"), [128, 5, 64], F32)); cnd_b = Buf()
    d2 = c.dsem("halo2")
    c.dma("sp", cnd[:, 0:4, :], g_tail.rearrange("(r p) n -> p r n", r=4), d2, reads=[gt_b], writes=[cnd_b])
    c.dma("sp", cnd[:, 4, :].rearrange("p (k t) -> p k t", t=2), hbuf[:, :, TL - 2:TL], d2, reads=[h_b], pwrites=[cnd_b])
    Ctx.seal(d2, [cnd_b])
    acc = es.enter_context(nc.sbuf_tensor(c.nm("hacc"), [128, 64], F32)); acc_b = Buf()
    zz = es.enter_context(nc.sbuf_tensor(c.nm("hzero"), [128, 64], F32)); zz_b = Buf()
    c.op("dve", lambda: nc.vector.memset(zz[:], 0.0), writes=[zz_b])
    c.op("dve", lambda: nc.vector.tensor_scalar(acc[:], cnd[:, 0, :], sh[:, 0:1], None, ALU.mult), reads=[cnd_b, sh_b], writes=[acc_b])
    for i in range(1, 5):
        c.op("dve", lambda: nc.vector.scalar_tensor_tensor(acc[:], cnd[:, i, :], sh[:, i:i + 1], acc[:], ALU.mult, ALU.add),
             reads=[cnd_b, sh_b, acc_b], writes=[acc_b])
    d3 = c.dsem("halo3")
    c.dma("sp", hbuf[:, :, 0:2], acc[:].rearrange("p (k t) -> p k t", t=2), d3, reads=[acc_b, tail_b, cnd_b], pwrites=[h_b])
    c.dma("sp", hbuf[:, :, 2 + OWN:4 + OWN], zz[:].rearrange("p (k t) -> p k t", t=2), d3, reads=[zz_b], pwrites=[h_b])


def build_fused():
    nc = bass.Bass("TRN2", target_bir_lowering=False)
    dt = lambda n, s, d=F32, k="ExternalInput": nc.dram_tensor(n, s, d, kind=k).ap()
    I = lambda n, s, d=F32: nc.dram_tensor(n, s, d, kind="Internal").ap()
    hT0 = dt("hT0", [128, 32, TL])
    cos4, sin4 = dt("cos4", [128, TL]), dt("sin4", [128, TL])
    sel, selh = dt("sel", [128, 4]), dt("selh", [128, 5])
    w_in = dt("w_in", [2, D, DIN]); w_uq = dt("w_uq", [2, 1536, 2304]); w_ukv = dt("w_ukv", [2, 512, 3072])
    w_out = dt("w_out", [2, D, D]); w_up = dt("w_up", [2, D, 2 * DFF]); w_down = dt("w_down", [2, DFF, D])
    g_attn, g_q, g_kv = dt("g_attn", [2, 128, 32]), dt("g_q", [2, 128, 12]), dt("g_kv", [2, 128, 4])
    g_ffn, g_next = dt("g_ffn", [2, 128, 32]), dt("g_next", [2, 128, 32])
    conv_w, conv_b = dt("conv_w", [2, 128, 172, 3]), dt("conv_b", [2, 128, 172])
    wg2, fbf = dt("wg2", [2, 17, 128]), dt("fbf", [2, 3, 1])
    gmla, gfox, ggla = dt("gmla", [2, 128, 3]), dt("gfox", [2, 128, 3]), dt("ggla", [2, 128, 2])
    mask4 = dt("mask4", [128, 4, 512], BF16)
    tri01, tris64, tris16, sus = dt("tri01", [64, 64]), dt("tris64", [64, 65]), dt("tris16", [16, 17]), dt("sus", [64, 64])
    y_out = dt("y_out", [128, 32, TL], F32, "ExternalOutput")
    o32, obf, otf, otb = I("o32", [N32, TL]), I("obf", [NB, TL], BF16), I("otf", [TL, 512]), I("otb", [TL, NCB], BF16)
    G32, GBF = Gath(nc, "o32", N32, TL, F32, 4), Gath(nc, "obf", NB, TL, BF16, 2)
    GTF = Gath(nc, "otf", TL, 512, F32, 4)
    otbG, otbR = I("otbG", [TL, 1024], BF16), I("otbR", [TL, 3072], BF16)
    GTBg, GTBr = Gath(nc, "otbG", TL, 1024, BF16, 2), Gath(nc, "otbR", TL, 3072, BF16, 2)
    GO = Gath(nc, "oT2", 1024, NT, BF16, 2)
    abf, vbf, f32, gkm = I("abf", [NA, NT], BF16), I("vbf", [NT, NV], BF16), I("f32s", [NF, NT]), I("gkm", [NT, 128])
    aug = I("aug_scr", [3, 12, NT], BF16)
    oT2, oT3 = I("oT2", [1024, NT], BF16), I("oT3", [128, 32, TL], BF16)
    h_mid, act = I("h_mid", [128, 32, TL]), I("act_scr", [86, 128, TL], BF16)
    hA, hB = I("hA", [128, 32, TL]), I("hB", [128, 32, TL])
    tail, g_tail = I("tail", [128, 64]), I("g_tail", [4 * 128, 64])
    with ExitStack() as es:
        c = Ctx(nc, es)
        cs = c.dsem("coll")
        c.phase_dsems.remove(cs)

        def phase(fn, exclude=()):
            with ExitStack() as pes:
                c.es = pes
                fn()
                c.es = c.sem_es
            c.end_phase(exclude)

        csA, csB, cs2 = c.dsem("collA"), c.dsem("collB"), c.dsem("coll2")
        for x_ in (csA, csB, cs2):
            c.phase_dsems.remove(x_)
        c.soft = {id(cs2.h)}
        hcur = hT0
        for l in range(2):
            def early(ob):
                for (G_, x_) in ((G32, o32), (GTF, otf), (GTBg, otbG)):
                    G_.gather(c, x_, csA, reads=[ob])
                GBF.gather_where(c, obf, csB, lambda r0: r0 >= RB_KPE, [ob])
            phase(lambda: emit_p1(c, hcur, w_in[l], w_uq[l], w_ukv[l], g_attn[l], g_q[l], g_kv[l], cos4, sin4, o32, obf, otf, otb,
                                  otb_split=(otbG, otbR), after_win=early), exclude={id(csA.h), id(csB.h)})
            GBF.gather_where(c, obf, csB, lambda r0: r0 < RB_KPE, [])
            GTBr.gather(c, otbR, csB)
            db = Buf()
            phase(lambda: emit_select1(c, sel, G32, GBF, GTF, GTBg, GTBr, abf, vbf, f32, gkm, db, "A"), exclude={id(csB.h)})
            phase(lambda: emit_p2(c, abf, vbf, f32, gkm, wg2[l], fbf[l], gmla[l], gfox[l], ggla[l], mask4, tri01, tris64, tris16, sus, oT2, aug,
                                  mid_hook=lambda: emit_select1(c, sel, G32, GBF, GTF, GTBg, GTBr, abf, vbf, f32, gkm, db, "B"),
                                  rows_done=lambda r0, n, bufs: GO.gather_rows(c, oT2, cs2, r0, n, bufs)))
            db2 = Buf()
            phase(lambda: emit_select2(c, sel, GO, oT3, db2))
            hnext = y_out if False else (hA if l == 0 else hB)
            phase(lambda: emit_p3(c, oT3, hcur, w_out[l], w_up[l], w_down[l], g_ffn[l], g_next[l], conv_w[l], conv_b[l],
                                  hnext, y_out, h_mid, act, final=(l == 1)))
            if l == 0:
                hb_ = Buf()
                phase(lambda: emit_halo(c, selh, hnext, hb_, tail, g_tail, cs))
            hcur = hnext
        c.barrier()
        print("fused program instructions:", c.n_inst)
    return nc


def kernel(x, meta_tokens, attn_norm, w_in, mla_q_norm, mla_w_uq, mla_kv_norm, mla_w_ukv,
           gla_w_gate2, gla_b_gate, fox_b_f, out_norm_mla, out_norm_gla, out_norm_fox,
           w_out, ffn_norm, ffn_w_up, ffn_conv_w, ffn_conv_b, ffn_w_down, final_norm):
    f32 = np.float32
    x = np.asarray(x, f32)
    B = x.shape[0]
    cores = list(range(8))
    h = np.concatenate([np.broadcast_to(np.asarray(meta_tokens, f32)[None], (B, NMETA, D)), x], axis=1)
    pos = np.arange(NT, dtype=f32)
    inv = (f32(1.0) / (f32(10000.0) ** (np.arange(0, 64, 2, dtype=f32) / f32(64)))).astype(f32)
    ang = (pos[:, None] * inv[None, :]).astype(f32)
    cosT, sinT = np.cos(ang).astype(f32).T, np.sin(ang).astype(f32).T
    zero_cols = np.array([2 + OWN, 3 + OWN])
    A = lambda v: np.asarray(v, f32)
    uq3 = A(mla_w_uq).reshape(2, 1536, 12, 192)
    w_uq_p = np.ascontiguousarray(np.concatenate(
        [uq3[..., :128].reshape(2, 1536, -1), uq3[..., 128:160].reshape(2, 1536, -1), uq3[..., 160:].reshape(2, 1536, -1)], 2))
    kv3 = A(mla_w_ukv).reshape(2, 512, 12, 256)
    w_ukv_p = np.ascontiguousarray(np.concatenate([kv3[..., :128].reshape(2, 512, -1), kv3[..., 128:].reshape(2, 512, -1)], 2))
    cw = A(ffn_conv_w)
    shared = dict(
        w_in=A(w_in), w_uq=w_uq_p, w_ukv=w_ukv_p, w_out=A(w_out), w_up=A(ffn_w_up), w_down=A(ffn_w_down),
        g_attn=np.stack([_pcol(attn_norm[l], 32) for l in range(2)]),
        g_q=np.stack([_pcol(mla_q_norm[l], 12) for l in range(2)]),
        g_kv=np.stack([_pcol(mla_kv_norm[l], 4) for l in range(2)]),
        g_ffn=np.stack([_pcol(ffn_norm[l], 32) for l in range(2)]),
        g_next=np.stack([_pcol(attn_norm[1], 32), _pcol(final_norm, 32)]),
        conv_w=np.stack([np.ascontiguousarray(cw[l].T.reshape(172, 128, 3).transpose(1, 0, 2)) for l in range(2)]),
        conv_b=np.stack([_pcol(ffn_conv_b[l], 172) for l in range(2)]),
        **_p2_consts())
    maps = []
    for core in cores:
        b, g = divmod(core, 4)
        cols = _core_cols(g)
        Hc = h[b][cols]
        Hc[zero_cols] = 0
        sel = np.zeros((128, 4), f32); sel[:, g] = 1
        selh = np.zeros((128, 5), f32); selh[:, 4 if g == 0 else g - 1] = 1
        m = dict(shared)
        m.update(
            hT0=_fmaj(Hc),
            cos4=np.ascontiguousarray(np.tile(cosT[:, cols], (4, 1))), sin4=np.ascontiguousarray(np.tile(sinT[:, cols], (4, 1))),
            sel=sel, selh=selh,
            wg2=np.stack([np.concatenate([A(gla_w_gate2[l])[:, g * 128:(g + 1) * 128], A(gla_b_gate[l])[None, g * 128:(g + 1) * 128]], 0)
                          for l in range(2)]),
            fbf=np.stack([A(fox_b_f[l])[3 * g:3 * g + 3, None] for l in range(2)]),
            gmla=np.stack([np.ascontiguousarray(A(out_norm_mla[l]).reshape(12, 128)[3 * g:3 * g + 3].T) for l in range(2)]),
            gfox=np.stack([np.ascontiguousarray(A(out_norm_fox[l]).reshape(12, 128)[3 * g:3 * g + 3].T) for l in range(2)]),
            ggla=np.stack([np.ascontiguousarray(A(out_norm_gla[l]).reshape(4, 2, 128)[g].T) for l in range(2)]))
        maps.append(m)
    res = run_bass_kernel_spmd(_prog("fused", build_fused), maps, core_ids=cores).results
    y = np.empty((B, SEQ, D), f32)
    for core in cores:
        b, g = divmod(core, 4)
        y[b, g * OWN:(g + 1) * OWN] = _unfm(res[core]["y_out"])[2:2 + OWN]
    return y
```

```python
import numpy as np
from contextlib import ExitStack
import ml_dtypes
import concourse.bass as bass
import concourse.mybir as mybir
from concourse.bass_utils import run_bass_kernel_spmd

F32 = mybir.dt.float32
BF16 = mybir.dt.bfloat16
AF = mybir.ActivationFunctionType
ALU = mybir.AluOpType
AX = mybir.AxisListType
NPBF = ml_dtypes.bfloat16

D = 4096
SEQ = 4096
NMETA = 16
OWN = 1024
TL = 2 + OWN + 2 + NMETA
TT = [(0, 512), (512, 512), (1024, TL - 1024)]
TM = [(i * 128, 128) for i in range(8)] + [(1024, TL - 1024)]
NT = NMETA + SEQ
EPS = 1e-6
DFF = 11008
GW = 256


def fm_groups(col0, nchunks, hf):
    per = GW // 128
    return [(col0 + g * GW, min(GW, (nchunks - g * per) * 128),
             [(j * 128, 128, hf(g * per + j)) for j in range(min(per, nchunks - g * per))])
            for g in range((nchunks + per - 1) // per)]


def tm_groups(col0, ncols, hf):
    return [(col0 + g * GW, min(GW, ncols - g * GW), hf(g * GW)) for g in range((ncols + GW - 1) // GW)]

O_CQ, O_CKV, O_KR, O_GQ, O_GK, O_GV, O_GZ, O_GR, O_FQ, O_FK, O_FV, O_FZ = (
    0, 1536, 2048, 2112, 2624, 3136, 4160, 4176, 5200, 6736, 8272, 9808)
DIN = 9820
R32_GQ, R32_GK, R32_GZ, R32_GR, R32_FZ, N32 = 0, 512, 1024, 1040, 2064, 2076
RB_Q, RB_KN, RB_KPE, RB_FQ, RB_FK, NB = 0, 2304, 3840, 3904, 5440, 6976
CB_GV, CB_FV, CB_VM, NCB = 0, 1024, 2560, 4096


class Buf:
    __slots__ = ("w", "r")

    def __init__(self):
        self.w = {}
        self.r = {}


def fresh(olds):
    b = Buf()
    for o in olds:
        for t in list(o.w.values()) + list(o.r.values()):
            Ctx._add(b.r, t)
    return b


class DSem:
    __slots__ = ("h", "v")

    def __init__(self, h):
        self.h = h
        self.v = 0


class Ctx:
    CE = ("pe", "act", "dve", "pool")

    def __init__(self, nc, es):
        self.nc = nc
        self.es = es
        self.sem_es = es
        self.eng = {"pe": nc.tensor, "act": nc.scalar, "dve": nc.vector,
                    "pool": nc.gpsimd, "sp": nc.sync}
        self.sem = {}
        self.cnt = {}
        self.nsem = 0
        self.latest = {}
        self.free_dsems = []
        self.phase_dsems = []
        self.phase = 0
        for e in self.CE:
            self._new_engine_sem(e)
        self.seen = {e: {} for e in self.eng}
        self.pe_pending = False
        self.n_inst = 0

    def _new_engine_sem(self, e):
        self.nsem += 1
        self.sem[e] = self.sem_es.enter_context(self.nc.semaphore(f"s_{e}_{self.nsem}"))
        self.cnt[e] = 0

    def nm(self, name):
        return f"{name}_p{self.phase}"

    def dsem(self, name):
        if self.free_dsems:
            d = self.free_dsems.pop()
        else:
            self.nsem += 1
            d = DSem(self.sem_es.enter_context(self.nc.semaphore(f"d_{name}_{self.nsem}")))
        self.phase_dsems.append(d)
        return d

    def barrier(self, exclude=()):
        assert not self.pe_pending
        for e in self.eng:
            own = id(self.sem[e]) if e in self.sem else None
            deps = {k: t for k, t in self.latest.items() if k != own and k not in exclude}
            self._emit_waits(e, deps)

    def end_phase(self, exclude=()):
        self.barrier(exclude)
        self.free_dsems.extend(self.phase_dsems)
        self.phase_dsems = []
        self.phase += 1

    def allgather(self, out, in_, cs, reads=(), writes=()):
        deps = self._collect("pool", reads, writes)
        self._emit_waits("pool", deps)
        ins = self.nc.gpsimd.collective_compute("AllGather", ALU.bypass, replica_groups=[[0, 1, 2, 3], [4, 5, 6, 7]],
                                                ins=[in_], outs=[out])
        self.n_inst += 1
        cs.v += 1
        ins.then_inc(cs.h)
        t = (cs.h, cs.v)
        self.latest[id(cs.h)] = t
        self._record(t, reads, writes)
        return ins

    @staticmethod
    def _add(deps, t):
        k = id(t[0])
        if k not in deps or deps[k][1] < t[1]:
            deps[k] = t

    def _collect(self, e, reads, writes, pwrites=()):
        deps = {}
        own = id(self.sem[e]) if e in self.sem else None
        for b in pwrites:
            for t in b.r.values():
                self._add(deps, t)
        for b in reads:
            for t in b.w.values():
                if id(t[0]) == own and e == "pe":
                    continue
                self._add(deps, t)
        for b in writes:
            for t in b.w.values():
                if id(t[0]) == own:
                    continue
                self._add(deps, t)
            for t in b.r.values():
                if id(t[0]) == own:
                    continue
                self._add(deps, t)
        return deps

    def _emit_waits(self, e, deps):
        seen = self.seen[e]
        for k, (s, v) in deps.items():
            if seen.get(k, 0) >= v:
                continue
            self.eng[e].wait_ge(s, v)
            self.n_inst += 1
            seen[k] = v

    def _record(self, t, reads, writes, pwrites=()):
        k = id(t[0])
        for b in pwrites:
            if k not in b.w or b.w[k][1] < t[1]:
                b.w[k] = t
        for b in reads:
            if k not in b.r or b.r[k][1] < t[1]:
                b.r[k] = t
        for b in writes:
            b.w = {k: t}
            b.r = {}

    def op(self, e, fn, reads=(), writes=(), inc=True, pwrites=()):
        deps = self._collect(e, reads, writes, pwrites)
        self._emit_waits(e, deps)
        ins = fn()
        self.n_inst += 1
        if inc:
            if self.cnt[e] >= 30000 and not (e == "pe" and self.pe_pending):
                self._new_engine_sem(e)
            self.cnt[e] += 1
            ins.then_inc(self.sem[e], 1)
            t = (self.sem[e], self.cnt[e])
            self.latest[id(t[0])] = t
            if e == "pe":
                self.pe_pending = False
        else:
            assert e == "pe"
            t = (self.sem[e], self.cnt[e] + 1)
            self.pe_pending = True
        self._record(t, reads, writes, pwrites)
        return ins

    def dma(self, q, out, in_, ds, reads=(), writes=(), pwrites=(), **kw):
        deps = self._collect(q, reads, writes, pwrites)
        self._emit_waits(q, deps)
        ins = self.eng[q].dma_start(out=out, in_=in_, **kw)
        self.n_inst += 1
        ds.v += 16
        ins.then_inc(ds.h, 16)
        self.latest[id(ds.h)] = (ds.h, ds.v)
        self._record((ds.h, ds.v), reads, writes, pwrites)
        return ins

    @staticmethod
    def seal(ds, bufs):
        k = id(ds.h)
        for b in bufs:
            if k in b.w:
                b.w[k] = (ds.h, ds.v)

    def wait_all(self, e, bufs):
        deps = {}
        for b in bufs:
            for t in b.w.values():
                self._add(deps, t)
        self._emit_waits(e, deps)


class Ring:
    def __init__(self, c, name, n, shape, dtype, es=None):
        es = es or c.es
        self.t = [es.enter_context(c.nc.sbuf_tensor(c.nm(f"{name}{i}"), shape, dtype)) for i in range(n)]
        self.b = [Buf() for _ in range(n)]
        self.d = [c.dsem(f"{name}{i}") for i in range(n)]
        self.i = -1
        self.n = n

    def next(self):
        self.i = (self.i + 1) % self.n
        return self.t[self.i], self.b[self.i], self.d[self.i]


class Dense:
    def __init__(self, c, kcmax=32):
        nc, es = c.nc, c.es
        self.c = c
        self.ws = Ring(c, "ws", 2, [128, kcmax * GW], BF16)
        self.psA = [es.enter_context(nc.psum_tensor(c.nm(f"psA{i}"), [128, 512], F32)) for i in range(2)]
        self.psB = [es.enter_context(nc.psum_tensor(c.nm(f"psB{i}"), [128, 512], F32)) for i in range(2)]
        self.psC = es.enter_context(nc.psum_tensor(c.nm("psC"), [128, 512], F32))
        self.bA = [Buf(), Buf()]
        self.bB = [Buf(), Buf()]
        self.bC = Buf()
        self.psT = [es.enter_context(nc.psum_tensor(c.nm(f"psT{i}"), [128, 512], F32)) for i in range(2)]
        self.bT = [Buf(), Buf()]
        self.psS = es.enter_context(nc.psum_tensor(c.nm("psS"), [128, 512], F32))
        self.bS = Buf()
        self.it = 0
        self.itT = 0

    def load(self, W, KC, col0, ncols, row0=0):
        c = self.c
        t, b, d = self.ws.next()
        t = t[:, :KC * ncols].rearrange("p (k m) -> p k m", m=ncols)
        Wv = W[row0:row0 + KC * 128, col0:col0 + ncols].rearrange("(kc p) m -> p kc m", p=128)
        step = 8
        for i, k0 in enumerate(range(0, KC, step)):
            k1 = min(KC, k0 + step)
            c.dma("pool", t[:, k0:k1, :], Wv[:, k0:k1, :], d,
                  writes=[b] if i == 0 else [], pwrites=[b] if i else [])
        return t, b

    def fm(self, W, KC, act, act_b, groups, tts=TT, row0=0):
        c, nc = self.c, self.c.nc
        for (col0, ncols, chunks) in groups:
            wt, wb = self.load(W, KC, col0, ncols, row0)
            for (off, M, handler) in chunks:
                pb = self.it % 2
                self.it += 1
                pst = [(self.psA[pb], self.bA[pb]), (self.psB[pb], self.bB[pb]), (self.psC, self.bC)]
                for kc in range(KC):
                    for j, (t0, tn) in enumerate(tts):
                        last = kc == KC - 1
                        ps, pbuf = pst[j]
                        c.op("pe", lambda: nc.tensor.matmul(ps[:M, :tn], wt[:, kc, off:off + M], act[:, kc, t0:t0 + tn],
                                                            start=(kc == 0), stop=last),
                             reads=[wb, act_b[kc]], writes=[pbuf], inc=last)
                handler([(pst[j][0][:M, :tn], pst[j][1], t0, tn) for j, (t0, tn) in enumerate(tts)])

    def tm(self, W, KC, act, act_b, groups, tms=TM, row0=0):
        c, nc = self.c, self.c.nc
        for (col0, ncols, handler) in groups:
            wt, wb = self.load(W, KC, col0, ncols, row0)
            for ti, (t0, tn) in enumerate(tms):
                pb = self.itT % 2
                self.itT += 1
                ps, pbuf = self.psT[pb], self.bT[pb]
                for kc in range(KC):
                    last = kc == KC - 1
                    c.op("pe", lambda: nc.tensor.matmul(ps[:tn, :ncols], act[:, kc, t0:t0 + tn], wt[:, kc, :ncols],
                                                        start=(kc == 0), stop=last),
                         reads=[wb, act_b[kc]], writes=[pbuf], inc=last)
                handler(ps[:tn, :ncols], pbuf, ti, t0, tn)


def _consts(c, names_shapes, es=None, olds=()):
    out = {}
    es = es or c.es
    ds = c.dsem("consts")
    for name, ap, shape in names_shapes:
        t = es.enter_context(c.nc.sbuf_tensor(c.nm("k_" + name), shape, F32))
        b = fresh(olds)
        c.dma("sp", t[:], ap, ds, writes=[b])
        out[name] = (t, b)
    Ctx.seal(ds, [b for (_, b) in out.values()])
    return out


def col_stats(c, dn, acc, acc_b, rstd, rstd_b, n_feat, post_scale=1.0, tts=TT):
    nc = c.nc
    for (t0, tn) in tts:
        c.op("pe", lambda: nc.tensor.matmul(dn.psS[:, :tn], dn.ones[:], acc[:, t0:t0 + tn], start=True, stop=True),
             reads=[acc_b, dn.ones_b], writes=[dn.bS])
        c.op("dve", lambda: nc.vector.tensor_scalar(rstd[:, t0:t0 + tn], dn.psS[:, :tn], 1.0 / n_feat, EPS,
                                                    ALU.mult, ALU.add),
             reads=[dn.bS], pwrites=[rstd_b])
    c.op("act", lambda: nc.scalar.activation(rstd[:], rstd[:], AF.Sqrt, scale=float(1.0 / post_scale ** 2)),
         reads=[rstd_b], writes=[rstd_b])
    c.op("dve", lambda: nc.vector.reciprocal(rstd[:], rstd[:]), reads=[rstd_b], writes=[rstd_b])


def build_p1():
    nc = bass.Bass("TRN2", target_bir_lowering=False)
    dt = lambda n, s, d=F32, k="ExternalInput": nc.dram_tensor(n, s, d, kind=k).ap()
    hT = dt("hT", [128, 32, TL])
    w_in = dt("w_in", [D, DIN])
    w_uq = dt("w_uq", [1536, 2304])
    w_ukv = dt("w_ukv", [512, 3072])
    g_attn = dt("g_attn", [128, 32])
    g_q = dt("g_q", [128, 12])
    g_kv = dt("g_kv", [128, 4])
    cos4 = dt("cos4", [128, TL])
    sin4 = dt("sin4", [128, TL])
    o32 = dt("o32", [N32, TL], F32, "ExternalOutput")
    obf = dt("obf", [NB, TL], BF16, "ExternalOutput")
    otf = dt("otf", [TL, 512], F32, "ExternalOutput")
    otb = dt("otb", [TL, NCB], BF16, "ExternalOutput")
    with ExitStack() as es:
        c = Ctx(nc, es)
        emit_p1(c, hT, w_in, w_uq, w_ukv, g_attn, g_q, g_kv, cos4, sin4, o32, obf, otf, otb)
    return nc


def emit_p1(c, hT, w_in, w_uq, w_ukv, g_attn, g_q, g_kv, cos4, sin4, o32, obf, otf, otb, otb_split=None, after_win=None):
    nc, es = c.nc, c.es
    sb = lambda n, s, d=F32, st=None: (st or es).enter_context(nc.sbuf_tensor(c.nm(n), s, d))
    dn = Dense(c)
    K = _consts(c, [("g_attn", g_attn, [128, 32]), ("g_q", g_q, [128, 12]), ("g_kv", g_kv, [128, 4])])
    dn.ones = sb("ones", [128, 128])
    dn.ones_b = Buf()
    c.op("dve", lambda: nc.vector.memset(dn.ones[:], 1.0), writes=[dn.ones_b])
    sq = Ring(c, "sq", 2, [128, TL], F32)
    st32 = Ring(c, "st32", 2, [128, TL], F32)
    stbf = Ring(c, "stbf", 3, [128, TL], BF16)
    ttf = Ring(c, "ttf", 2, [128, GW], F32)
    ttb = Ring(c, "ttb", 3, [128, GW], BF16)
    out_b = Buf()
    cqn = sb("cqn", [128, 12, TL], BF16); cqn_b = [Buf() for _ in range(12)]
    ckn = sb("ckn", [128, 4, TL], BF16); ckn_b = [Buf() for _ in range(4)]
    accq = sb("accq", [128, TL]); accq_b = Buf()
    acck = sb("acck", [128, TL]); acck_b = Buf()
    kr = [sb("kr1", [32, TL]), sb("kr2", [32, TL])]
    kr_b = [Buf(), Buf()]
    es_hn = ExitStack()
    hn = sb("hn", [128, 32, TL], BF16, es_hn)
    hn_b = [Buf() for _ in range(32)]
    with ExitStack() as es1:
        hp = Ring(c, "hp", 2, [128, 1, TL], F32, es1)
        acc = sb("acc", [128, TL], F32, es1); acc_b = Buf()
        rstd = sb("rstd", [128, TL], F32, es1); rstd_b = Buf()
        c.op("dve", lambda: nc.vector.memset(acc[:], 0.0), writes=[acc_b])
        for g in range(32):
            t, b, d = hp.next()
            c.dma("sp", t[:], hT[:, g:g + 1, :], d, writes=[b])
            for i in range(1):
                s, s_b, _ = sq.next()
                c.op("act", lambda: nc.scalar.activation(s[:], t[:, i, :], AF.Square), reads=[b], writes=[s_b])
                c.op("dve", lambda: nc.vector.tensor_tensor(acc[:], acc[:], s[:], ALU.add), reads=[acc_b, s_b], writes=[acc_b])
        col_stats(c, dn, acc, acc_b, rstd, rstd_b, D)
        ga, ga_b = K["g_attn"]
        for g in range(32):
            t, b, d = hp.next()
            c.dma("sp", t[:], hT[:, g:g + 1, :], d, writes=[b])
            for i in range(1):
                kc = g + i
                c.op("dve", lambda: nc.vector.scalar_tensor_tensor(hn[:, kc, :], t[:, i, :], ga[:, kc:kc + 1], rstd[:],
                                                                   ALU.mult, ALU.mult),
                     reads=[b, ga_b, rstd_b], writes=[hn_b[kc]])


    def out_fm(dst, row0, func=None, scale=1.0, dtype=F32):
        def h(tiles):
            t, b, d = (st32 if dtype == F32 else stbf).next()
            M = None
            for (ps, pb, t0, tn) in tiles:
                M = ps.shape[0]
                c.op("act", lambda: nc.scalar.activation(t[:M, t0:t0 + tn], ps, func or AF.Identity, scale=float(scale)),
                     reads=[pb], pwrites=[b])
            c.dma("sp", dst[row0:row0 + M, :], t[:M, :], d, reads=[b], pwrites=[out_b])
        return h

    c.op("dve", lambda: nc.vector.memset(accq[:], 0.0), writes=[accq_b])
    c.op("dve", lambda: nc.vector.memset(acck[:], 0.0), writes=[acck_b])

    def lat(dstt, dst_b, i, gain, gain_b, ac, ac_b):
        def h(tiles):
            s, s_b, _ = sq.next()
            for (ps, pb, t0, tn) in tiles:
                c.op("act", lambda: nc.scalar.activation(dstt[:, i, t0:t0 + tn], ps, AF.Identity, scale=gain[:, i:i + 1]),
                     reads=[pb, gain_b], pwrites=[dst_b[i]])
                c.op("act", lambda: nc.scalar.activation(s[:, t0:t0 + tn], ps, AF.Square), reads=[pb], pwrites=[s_b])
            c.op("dve", lambda: nc.vector.tensor_tensor(ac[:], ac[:], s[:], ALU.add), reads=[ac_b, s_b], writes=[ac_b])
        return h

    def krope(i):
        def h(tiles):
            for (ps, pb, t0, tn) in tiles:
                c.op("act", lambda: nc.scalar.copy(kr[i][:, t0:t0 + tn], ps), reads=[pb], pwrites=[kr_b[i]])
        return h

    gq, gq_b = K["g_q"]
    gk, gk_b = K["g_kv"]
    groups = []
    groups += fm_groups(O_CQ, 12, lambda i: lat(cqn, cqn_b, i, gq, gq_b, accq, accq_b))
    groups += fm_groups(O_CKV, 4, lambda i: lat(ckn, ckn_b, i, gk, gk_b, acck, acck_b))
    groups.append((O_KR, 64, [(0, 32, krope(0)), (32, 32, krope(1))]))
    groups += fm_groups(O_GQ, 4, lambda i: out_fm(o32, R32_GQ + i * 128, scale=128 ** -0.5))
    groups += fm_groups(O_GK, 4, lambda i: out_fm(o32, R32_GK + i * 128))
    groups.append((O_GZ, 16, [(0, 16, out_fm(o32, R32_GZ))]))
    groups += fm_groups(O_GR, 8, lambda i: out_fm(o32, R32_GR + i * 128, func=AF.Silu))
    groups += fm_groups(O_FQ, 12, lambda i: out_fm(obf, RB_FQ + i * 128, scale=128 ** -0.5, dtype=BF16))
    groups += fm_groups(O_FK, 12, lambda i: out_fm(obf, RB_FK + i * 128, dtype=BF16))
    groups.append((O_FZ, 12, [(0, 12, out_fm(o32, R32_FZ))]))
    dn.fm(w_in, 32, hn, hn_b, groups)


    def out_tm(dst, col0, ring):
        def h(ps, pb, ti, t0, tn):
            t, b, d = ring.next()
            n = ps.shape[1]
            c.op("act", lambda: nc.scalar.copy(t[:tn, :n], ps), reads=[pb], writes=[b])
            dd, cc = dst, col0
            if otb_split is not None and dst is otb:
                dd, cc = (otb_split[0], col0) if col0 < CB_FV else (otb_split[1], col0 - CB_FV)
            c.dma("sp", dd[t0:t0 + tn, cc:cc + n], t[:tn, :n], d, reads=[b], pwrites=[out_b])
        return h

    tg = tm_groups(O_GK, 512, lambda o: out_tm(otf, o, ttf))
    tg += tm_groups(O_GV, 1024, lambda o: out_tm(otb, CB_GV + o, ttb))
    tg += tm_groups(O_FV, 1536, lambda o: out_tm(otb, CB_FV + o, ttb))
    dn.tm(w_in, 32, hn, hn_b, tg)
    if after_win is not None:
        after_win(out_b)

    es_hn.close()
    olds = hn_b + hp.b + [acc_b, rstd_b]
    K2 = _consts(c, [("cos4", cos4, [128, TL]), ("sin4", sin4, [128, TL])], olds=olds)
    cs, cs_b = K2["cos4"]
    sn, sn_b = K2["sin4"]
    tmp = [sb(f"rtmp{i}", [128, TL]) for i in range(4)]
    tmp_b = [fresh(olds) for _ in range(4)]

    def rope(x1, x1_b, x2, x2_b, P, dst_rows1, dst_rows2):
        c.op("dve", lambda: nc.vector.tensor_tensor(tmp[0][:P, :], x1, cs[:P, :], ALU.mult), reads=[x1_b, cs_b], writes=[tmp_b[0]])
        c.op("dve", lambda: nc.vector.tensor_tensor(tmp[1][:P, :], x2, sn[:P, :], ALU.mult), reads=[x2_b, sn_b], writes=[tmp_b[1]])
        c.op("dve", lambda: nc.vector.tensor_tensor(tmp[2][:P, :], x2, cs[:P, :], ALU.mult), reads=[x2_b, cs_b], writes=[tmp_b[2]])
        c.op("dve", lambda: nc.vector.tensor_tensor(tmp[3][:P, :], x1, sn[:P, :], ALU.mult), reads=[x1_b, sn_b], writes=[tmp_b[3]])
        t, b, d = stbf.next()
        c.op("dve", lambda: nc.vector.tensor_tensor(t[:P, :], tmp[0][:P, :], tmp[1][:P, :], ALU.subtract),
             reads=[tmp_b[0], tmp_b[1]], writes=[b])
        for (r0, p0, n) in dst_rows1:
            c.dma("sp", obf[r0:r0 + n, :], t[p0:p0 + n, :], d, reads=[b], pwrites=[out_b])
        t2, b2, d2 = stbf.next()
        c.op("dve", lambda: nc.vector.tensor_tensor(t2[:P, :], tmp[2][:P, :], tmp[3][:P, :], ALU.add),
             reads=[tmp_b[2], tmp_b[3]], writes=[b2])
        for (r0, p0, n) in dst_rows2:
            c.dma("sp", obf[r0:r0 + n, :], t2[p0:p0 + n, :], d2, reads=[b2], pwrites=[out_b])

    rope(kr[0][:], kr_b[0], kr[1][:], kr_b[1], 32, [(RB_KPE, 0, 32)], [(RB_KPE + 32, 0, 32)])

    rq = sb("rq", [128, TL]); rq_b = fresh(olds)
    rk = sb("rk", [128, TL]); rk_b = fresh(olds)
    col_stats(c, dn, accq, accq_b, rq, rq_b, 1536)
    col_stats(c, dn, acck, acck_b, rk, rk_b, 512)
    for i in range(12):
        c.op("dve", lambda: nc.vector.tensor_tensor(cqn[:, i, :], cqn[:, i, :], rq[:], ALU.mult),
             reads=[cqn_b[i], rq_b], writes=[cqn_b[i]])
    for i in range(4):
        c.op("dve", lambda: nc.vector.tensor_tensor(ckn[:, i, :], ckn[:, i, :], rk[:], ALU.mult),
             reads=[ckn_b[i], rk_b], writes=[ckn_b[i]])

    QS = 192 ** -0.5
    qpe = [sb(f"qpe{i}", [128, TL]) for i in range(2)]
    qpe_b = [fresh(olds), fresh(olds)]

    def qpe_h(i, j3):
        def h(tiles):
            for (ps, pb, t0, tn) in tiles:
                c.op("act", lambda: nc.scalar.activation(qpe[i][:, t0:t0 + tn], ps, AF.Identity, scale=QS), reads=[pb], pwrites=[qpe_b[i]])
            if i == 1:
                rows1 = [(RB_Q + (4 * j3 + hh) * 192 + 128, 32 * hh, 32) for hh in range(4)]
                rows2 = [(RB_Q + (4 * j3 + hh) * 192 + 160, 32 * hh, 32) for hh in range(4)]
                rope(qpe[0][:], qpe_b[0], qpe[1][:], qpe_b[1], 128, rows1, rows2)
        return h

    qg = fm_groups(0, 12, lambda i: out_fm(obf, RB_Q + i * 192, scale=QS, dtype=BF16))
    for j3 in range(3):
        qg.append((1536 + j3 * 128, 128, [(0, 128, qpe_h(0, j3))]))
        qg.append((1920 + j3 * 128, 128, [(0, 128, qpe_h(1, j3))]))
    dn.fm(w_uq, 12, cqn, cqn_b, qg)
    kg = fm_groups(0, 12, lambda i: out_fm(obf, RB_KN + i * 128, dtype=BF16))
    dn.fm(w_ukv, 4, ckn, ckn_b, kg)
    vg = tm_groups(1536, 1536, lambda o: out_tm(otb, CB_VM + o, ttb))
    dn.tm(w_ukv, 4, ckn, ckn_b, vg)
    c.wait_all("sp", [out_b])


TH = TL // 2
TTH = [(0, 512), (512, TH - 512)]


def build_p3():
    nc = bass.Bass("TRN2", target_bir_lowering=False)
    dt = lambda n, s, d=F32, k="ExternalInput": nc.dram_tensor(n, s, d, kind=k).ap()
    oT = dt("oT", [128, 32, TL], BF16)
    hT = dt("hT", [128, 32, TL])
    w_out = dt("w_out", [D, D])
    w_up = dt("w_up", [D, 2 * DFF])
    w_down = dt("w_down", [DFF, D])
    g_ffn = dt("g_ffn", [128, 32])
    g_next = dt("g_next", [128, 32])
    conv_w = dt("conv_w", [128, 172, 3])
    conv_b = dt("conv_b", [128, 172])
    h_out = dt("h_out", [128, 32, TL], F32, "ExternalOutput")
    y_out = dt("y_out", [128, 32, TL], F32, "ExternalOutput")
    h_mid = dt("h_mid", [128, 32, TL], F32, "Internal")
    act = dt("act_scr", [86, 128, TL], BF16, "Internal")
    with ExitStack() as es:
        c = Ctx(nc, es)
        emit_p3(c, oT, hT, w_out, w_up, w_down, g_ffn, g_next, conv_w, conv_b, h_out, y_out, h_mid, act)
    return nc


def emit_p3(c, oT, hT, w_out, w_up, w_down, g_ffn, g_next, conv_w, conv_b, h_out, y_out, h_mid, act, final=True):
    nc, es = c.nc, c.es
    sb = lambda n, s, d=F32, st=None: (st or es).enter_context(nc.sbuf_tensor(c.nm(n), s, d))
    dn = Dense(c)
    K = _consts(c, [("g_ffn", g_ffn, [128, 32]), ("g_next", g_next, [128, 32]),
                    ("cw", conv_w, [128, 172, 3]), ("cb", conv_b, [128, 172])])
    dn.ones = sb("ones", [128, 128]); dn.ones_b = Buf()
    c.op("dve", lambda: nc.vector.memset(dn.ones[:], 1.0), writes=[dn.ones_b])
    sq = Ring(c, "sq", 2, [128, TL], F32)
    hc = Ring(c, "hc", 2, [128, TL], F32)
    acc = sb("acc", [128, TL]); acc_b = Buf()
    rstd = sb("rstd", [128, TL]); rstd_b = Buf()
    hmid_b = [Buf() for _ in range(32)]
    act_b = [Buf() for _ in range(86)]
    hout_b = [Buf() for _ in range(32)]
    y_b = Buf()
    gf, gf_b = K["g_ffn"]
    c.op("dve", lambda: nc.vector.memset(acc[:], 0.0), writes=[acc_b])

    es_hg = ExitStack()
    hg = sb("hg", [128, 32, TL], BF16, es_hg); hg_b = [Buf() for _ in range(32)]
    with ExitStack() as es1:
        osb = sb("osb", [128, 32, TL], BF16, es1); osb_b = [Buf() for _ in range(32)]
        d_o = c.dsem("oT")
        for g in range(8):
            c.dma("sp", osb[:, g * 4:(g + 1) * 4, :], oT[:, g * 4:(g + 1) * 4, :], d_o, writes=osb_b[g * 4:(g + 1) * 4])
        Ctx.seal(d_o, osb_b)

        def h3a(m):
            def h(tiles):
                t, b, d = hc.next()
                c.dma("sp", t[:], hT[:, m, :], d, writes=[b])
                for (ps, pb, t0, tn) in tiles:
                    c.op("dve", lambda: nc.vector.tensor_tensor(t[:, t0:t0 + tn], ps, t[:, t0:t0 + tn], ALU.add),
                         reads=[pb, b], pwrites=[b])
                c.dma("sp", h_mid[:, m, :], t[:], d, reads=[b], writes=[hmid_b[m]])
                s, s_b, _ = sq.next()
                c.op("act", lambda: nc.scalar.activation(s[:], t[:], AF.Square), reads=[b], writes=[s_b])
                c.op("dve", lambda: nc.vector.tensor_tensor(acc[:], acc[:], s[:], ALU.add), reads=[acc_b, s_b], writes=[acc_b])
                c.op("act", lambda: nc.scalar.activation(hg[:, m, :], t[:], AF.Identity, scale=gf[:, m:m + 1]),
                     reads=[b, gf_b], writes=[hg_b[m]])
            return h
        dn.fm(w_out, 32, osb, osb_b, fm_groups(0, 32, h3a))
    olds = list(osb_b)
    col_stats(c, dn, acc, acc_b, rstd, rstd_b, D)

    cw, cw_b = K["cw"]
    cb, cb_b = K["cb"]
    with ExitStack() as es2:
        ug = Ring(c, "ug", 2, [128, TL], F32, es2)
        uv = Ring(c, "uv", 2, [128, TL], F32, es2)
        cvg = Ring(c, "cvg", 2, [128, TL], F32, es2)
        cvv = Ring(c, "cvv", 2, [128, TL], F32, es2)
        ab = Ring(c, "ab", 2, [128, TL], BF16, es2)
        for r in (ug, uv, cvg, cvv, ab):
            r.b = [fresh(olds) for _ in r.b]
        for i in range(2):
            c.op("dve", lambda: nc.vector.memset(ab.t[i][:], 0.0), writes=[ab.b[i]])
        gate_cv = {}

        def conv(u, u_b, ch, ring):
            cv, cv_b, _ = ring.next()
            n = TL - 2
            c.op("act", lambda: nc.scalar.activation(cv[:, 2:], u[:, 2:], AF.Identity, bias=cb[:, ch:ch + 1], scale=cw[:, ch, 2:3]),
                 reads=[u_b, cb_b, cw_b], writes=[cv_b])
            c.op("dve", lambda: nc.vector.scalar_tensor_tensor(cv[:, 2:], u[:, 1:1 + n], cw[:, ch, 1:2], cv[:, 2:], ALU.mult, ALU.add),
                 reads=[u_b, cw_b, cv_b], writes=[cv_b])
            c.op("dve", lambda: nc.vector.scalar_tensor_tensor(cv[:, 2:], u[:, 0:n], cw[:, ch, 0:1], cv[:, 2:], ALU.mult, ALU.add),
                 reads=[u_b, cw_b, cv_b], writes=[cv_b])
            return cv, cv_b

        def hup(m, is_val):
            def h(tiles):
                u, u_b, _ = (uv if is_val else ug).next()
                for (ps, pb, t0, tn) in tiles:
                    c.op("dve", lambda: nc.vector.tensor_tensor(u[:, t0:t0 + tn], ps, rstd[:, t0:t0 + tn], ALU.mult),
                         reads=[pb, rstd_b], pwrites=[u_b])
                if not is_val:
                    cv, cv_b = conv(u, u_b, m, cvg)
                    c.op("act", lambda: nc.scalar.activation(cv[:, 2:], cv[:, 2:], AF.Silu), reads=[cv_b], writes=[cv_b])
                    gate_cv[m] = (cv, cv_b)
                else:
                    cv, cv_b = conv(u, u_b, 86 + m, cvv)
                    g, g_b = gate_cv.pop(m)
                    a, a_b, a_d = ab.next()
                    c.op("dve", lambda: nc.vector.tensor_tensor(a[:, 2:], g[:, 2:], cv[:, 2:], ALU.mult),
                         reads=[g_b, cv_b], pwrites=[a_b])
                    c.dma("sp", act[m], a[:], a_d, reads=[a_b], writes=[act_b[m]])
            return h

        groups = []
        for m0 in range(0, 86, 2):
            groups += [(m0 * 128, 256, [(0, 128, hup(m0, False)), (128, 128, hup(m0 + 1, False))]),
                       (DFF + m0 * 128, 256, [(0, 128, hup(m0, True)), (128, 128, hup(m0 + 1, True))])]
        dn.fm(w_up, 32, hg, hg_b, groups)
        olds = olds + hg_b + ug.b + uv.b + cvg.b + cvv.b + ab.b
    es_hg.close()

    accf = sb("accf", [128, TL]); accf_b = fresh(olds)
    c.op("dve", lambda: nc.vector.memset(accf[:], 0.0), writes=[accf_b])
    asb = sb("asb", [128, 43, TL], BF16); asb_b = [fresh(olds) for _ in range(43)]
    d_a = c.dsem("asb")
    actv = act.rearrange("m p t -> p m t")
    for part in range(2):
        for g in range(0, 43, 8):
            g1 = min(43, g + 8)
            c.dma("sp", asb[:, g:g1, :], actv[:, part * 43 + g:part * 43 + g1, :], d_a,
                  reads=act_b[part * 43 + g:part * 43 + g1], writes=asb_b[g:g1])
        Ctx.seal(d_a, asb_b)
        for m in range(32):
            pb = dn.it % 2
            dn.it += 1
            pst = [(dn.psA[pb], dn.bA[pb]), (dn.psB[pb], dn.bB[pb]), (dn.psC, dn.bC)]
            wt, wb = dn.load(w_down, 43, m * 128, 128, row0=part * 43 * 128)
            for k in range(43):
                last = k == 42
                for j, (t0, tn) in enumerate(TT):
                    ps, pbuf = pst[j]
                    c.op("pe", lambda: nc.tensor.matmul(ps[:, :tn], wt[:, k, 0:128], asb[:, k, t0:t0 + tn], start=(k == 0), stop=last),
                         reads=[wb, asb_b[k]], writes=[pbuf], inc=last)
            t, b, d = hc.next()
            if part == 0:
                c.dma("sp", t[:], h_mid[:, m, :], d, reads=[hmid_b[m]], writes=[b])
            else:
                c.dma("sp", t[:], h_out[:, m, :], d, reads=[hout_b[m]], writes=[b])
            for j, (t0, tn) in enumerate(TT):
                ps, pbuf = pst[j]
                c.op("dve", lambda: nc.vector.tensor_tensor(t[:, t0:t0 + tn], ps[:, :tn], t[:, t0:t0 + tn], ALU.add),
                     reads=[pbuf, b], pwrites=[b])
            c.dma("sp", h_out[:, m, :], t[:], d, reads=[b], writes=[hout_b[m]])
            if part == 1 and final:
                s_, s_b, _ = sq.next()
                c.op("act", lambda: nc.scalar.activation(s_[:], t[:], AF.Square), reads=[b], writes=[s_b])
                c.op("dve", lambda: nc.vector.tensor_tensor(accf[:], accf[:], s_[:], ALU.add), reads=[accf_b, s_b], writes=[accf_b])
    if not final:
        c.wait_all("sp", hout_b)
        return
    col_stats(c, dn, accf, accf_b, rstd, rstd_b, D)
    gn, gn_b = K["g_next"]
    for m in range(32):
        t, b, d = hc.next()
        c.dma("sp", t[:], h_out[:, m, :], d, reads=[hout_b[m]], writes=[b])
        c.op("dve", lambda: nc.vector.scalar_tensor_tensor(t[:], t[:], gn[:, m:m + 1], rstd[:], ALU.mult, ALU.mult),
             reads=[b, gn_b, rstd_b], writes=[b])
        c.dma("sp", y_out[:, m, :], t[:], d, reads=[b], pwrites=[y_b])
    c.wait_all("sp", [y_b] + hout_b)


QT = [(i * 512, 512) for i in range(8)] + [(4096, 16)]
NKC = 33
A_Q, A_KN, A_KPE, A_FQ, A_FK, NA = 0, 576, 960, 1024, 1408, 1792
V_VM, V_FV, V_GV, NV = 0, 384, 768, 1024
F_GQ, F_GK, F_GZ, F_GR, F_FZ, NF = 0, 128, 256, 273, 529, 532
GCH = [(0, 16)] + [(16 + 64 * i, 64) for i in range(64)]


def build_p2():
    nc = bass.Bass("TRN2", target_bir_lowering=False)
    dt = lambda n, s, d=F32, k="ExternalInput": nc.dram_tensor(n, s, d, kind=k).ap()
    a = dict(
        abf=dt("abf", [NA, NT], BF16), vbf=dt("vbf", [NT, NV], BF16), f32=dt("f32", [NF, NT]),
        gkm=dt("gkm", [NT, 128]), wg2=dt("wg2", [17, 128]), fbf=dt("fbf", [3, 1]),
        gmla=dt("gmla", [128, 3]), gfox=dt("gfox", [128, 3]), ggla=dt("ggla", [128, 2]),
        mask4=dt("mask4", [128, 4, 512], BF16), tri01=dt("tri01", [64, 64]),
        tris64=dt("tris64", [64, 65]), tris16=dt("tris16", [16, 17]), sus=dt("sus", [64, 64]),
        oT=dt("oT", [1024, NT], BF16, "ExternalOutput"),
        aug=dt("aug_scr", [3, 12, NT], BF16, "Internal"))
    with ExitStack() as es:
        c = Ctx(nc, es)
        emit_p2(c, **a)
    return nc


def emit_p2(c, abf, vbf, f32, gkm, wg2, fbf, gmla, gfox, ggla, mask4, tri01, tris64, tris16, sus, oT, aug,
            mid_hook=None, rows_done=None):
    nc, es = c.nc, c.es
    sb = lambda n, s, d=F32, st=None: (st or es).enter_context(nc.sbuf_tensor(c.nm(n), s, d))
    K = _consts(c, [("wg2", wg2, [17, 128]), ("fbf", fbf, [3, 1]), ("gmla", gmla, [128, 3]), ("gfox", gfox, [128, 3]),
                    ("ggla", ggla, [128, 2]), ("tri01", tri01, [64, 64]), ("tris64", tris64, [64, 65]),
                    ("tris16", tris16, [16, 17]), ("sus", sus, [64, 64])])
    mk = sb("mk_sb", [128, 4, 512], BF16); mk_b = Buf()
    c.dma("sp", mk[:], mask4, c.dsem("mk"), writes=[mk_b])
    ones = sb("ones", [128, 512]); ones_b = Buf()
    c.op("dve", lambda: nc.vector.memset(ones[:], 1.0), writes=[ones_b])
    onesb = sb("onesb", [128, 128], BF16); onesb_b = Buf()
    c.op("dve", lambda: nc.vector.memset(onesb[:], 1.0), writes=[onesb_b])
    P = [es.enter_context(nc.psum_tensor(c.nm(f"pp{i}"), [128, 512], F32)) for i in range(7)]
    Pb = [Buf() for _ in range(7)]
    out_b = Buf()
    blk_b = [Buf() for _ in range(8)]
    stb = Ring(c, "stb", 3, [128, 512], BF16)
    olds = []

    def head_norm_out(o_parts, gain, gain_b, gcol0, n, q0, row0, extra=None):
        nf = 128 * len(o_parts)
        for i, (o, o_b) in enumerate(o_parts):
            s, s_b, _ = sqr.next()
            c.op("act", lambda: nc.scalar.activation(s[:, :n], o, AF.Square), reads=[o_b], writes=[s_b])
            c.op("pe", lambda: nc.tensor.matmul(P[6][:, :n], ones[:, :128], s[:, :n], start=(i == 0), stop=(i == len(o_parts) - 1)),
                 reads=[s_b, ones_b], writes=[Pb[6]])
        r, r_b, _ = rsr.next()
        c.op("dve", lambda: nc.vector.tensor_scalar(r[:, :n], P[6][:, :n], 1.0 / nf, EPS, ALU.mult, ALU.add), reads=[Pb[6]], writes=[r_b])
        c.op("act", lambda: nc.scalar.activation(r[:, :n], r[:, :n], AF.Sqrt), reads=[r_b], writes=[r_b])
        c.op("dve", lambda: nc.vector.reciprocal(r[:, :n], r[:, :n]), reads=[r_b], writes=[r_b])
        for i, (o, o_b) in enumerate(o_parts):
            t, b, d = stb.next()
            if extra is None:
                c.op("dve", lambda: nc.vector.scalar_tensor_tensor(t[:, :n], o, gain[:, gcol0 + i:gcol0 + i + 1], r[:, :n], ALU.mult, ALU.mult),
                     reads=[o_b, gain_b, r_b], writes=[b])
            else:
                ex, ex_b = extra[i]
                c.op("dve", lambda: nc.vector.scalar_tensor_tensor(o, o, gain[:, gcol0 + i:gcol0 + i + 1], r[:, :n], ALU.mult, ALU.mult),
                     reads=[o_b, gain_b, r_b], writes=[o_b])
                c.op("dve", lambda: nc.vector.tensor_tensor(t[:, :n], o, ex, ALU.mult), reads=[o_b, ex_b], writes=[b])
            c.dma("sp", oT[row0 + i * 128:row0 + (i + 1) * 128, q0:q0 + n], t[:, :n], d, reads=[b],
                  pwrites=[out_b, blk_b[row0 // 128 + i]])

    sqr = Ring(c, "sqr", 2, [128, 512], F32)
    rsr = Ring(c, "rsr", 2, [128, 512], F32)

    with ExitStack() as eg:
        wg, wg_b = K["wg2"]
        tri, tri_b = K["tri01"]
        ts64, ts64_b = K["tris64"]
        ts16, ts16_b = K["tris16"]
        su, su_b = K["sus"]
        gv = sb("gv", [64, 65, 256], BF16, eg); gv_b = Buf()
        d_gv = c.dsem("gv")
        c.dma("sp", gv[:16, 0, :], vbf[0:16, V_GV:V_GV + 256], d_gv, writes=[gv_b])
        for i in range(4):
            c.dma("sp", gv[:, 1 + 16 * i:17 + 16 * i, :],
                  vbf[16 + 1024 * i:16 + 1024 * (i + 1), V_GV:V_GV + 256].rearrange("(n p) d -> p n d", p=64), d_gv, pwrites=[gv_b])
        qdec = sb("qdec", [128, NT], BF16, eg); qdec_b = [Buf() for _ in range(65)]
        kst = sb("kst", [64, 65, 128], BF16, eg); kst_b = [Buf() for _ in range(65)]
        Aall = sb("Aall", [64, 65, 64], BF16, eg); A_b = [Buf() for _ in range(65)]
        dec = sb("dec", [128, 65], F32, eg); dec_b = [Buf() for _ in range(65)]
        oall = sb("oall_sb", [128, 2, NT], F32, eg); oall_b = [Buf() for _ in range(9)]
        S = sb("S", [128, 256], F32, eg); S_b = Buf()
        Sbf = sb("Sbf", [128, 256], BF16, eg); Sbf_b = Buf()
        gzr = Ring(c, "gzr", 2, [17, 512], F32, eg)
        gqr = Ring(c, "gqr", 2, [128, 512], F32, eg)
        gkr = Ring(c, "gkr", 2, [128, 512], F32, eg)
        gkmr = Ring(c, "gkmr", 2, [64, 8, 128], F32, eg)
        spr = Ring(c, "spr", 2, [64, 8, 128], F32, eg)
        e4 = Ring(c, "e4", 2, [128, 64], F32, eg)
        ek = Ring(c, "ek", 2, [64, 128], F32, eg)
        kdr = Ring(c, "kdr", 2, [128, 64], BF16, eg)
        supers = [(0, 16, [0])] + [(16 + 512 * j, 512, list(range(1 + 8 * j, 9 + 8 * j))) for j in range(8)]
        for (r0, rn, chunks) in supers:
            gz, gz_b, gz_d = gzr.next()
            c.dma("sp", gz[:, :rn], f32[F_GZ:F_GZ + 17, r0:r0 + rn], gz_d, writes=[gz_b])
            gq, gq_b, gq_d = gqr.next()
            c.dma("sp", gq[:, :rn], f32[F_GQ:F_GQ + 128, r0:r0 + rn], gq_d, writes=[gq_b])
            gk, gk_b, gk_d = gkr.next()
            c.dma("sp", gk[:, :rn], f32[F_GK:F_GK + 128, r0:r0 + rn], gk_d, writes=[gk_b])
            gm, gm_b, gm_d = gkmr.next()
            C = GCH[chunks[0]][1]
            nch = len(chunks)
            c.dma("sp", gm[:C, :nch, :], gkm[r0:r0 + rn, :].rearrange("(n p) d -> p n d", p=C), gm_d, writes=[gm_b])
            sp, sp_b, _ = spr.next()
            for g4 in range(0, nch, 4):
                n4 = min(4, nch - g4)
                for i in range(n4):
                    o0 = (g4 + i) * C
                    c.op("pe", lambda: nc.tensor.matmul(P[0][:C, i * 128:(i + 1) * 128], gz[:, o0:o0 + C], wg[:], start=True, stop=True),
                         reads=[gz_b, wg_b], writes=[Pb[0]])
                c.op("act", lambda: nc.scalar.activation(sp[:C, g4:g4 + n4, :], P[0][:C, :n4 * 128].rearrange("p (a b) -> p a b", b=128), AF.Exp, scale=-1.0),
                     reads=[Pb[0]], pwrites=[sp_b])
            c.op("act", lambda: nc.scalar.activation(sp[:C, :nch, :], sp[:C, :nch, :], AF.Ln, bias=1.0), reads=[sp_b], writes=[sp_b])
            for i, n in enumerate(chunks):
                s0, C = GCH[n]
                l0 = i * C
                tsx, tsx_b = (ts64, ts64_b) if C == 64 else (ts16, ts16_b)
                c.op("pe", lambda: nc.tensor.matmul(P[1][:, :C + 1], sp[:C, i, :], tsx[:C, :C + 1], start=True, stop=True),
                     reads=[sp_b, tsx_b], writes=[Pb[1]])
                c.op("pe", lambda: nc.tensor.matmul(P[2][:C, :128], su[:C, :C], sp[:C, i, :], start=True, stop=True),
                     reads=[sp_b, su_b], writes=[Pb[2]])
                eb, eb_b, _ = e4.next()
                c.op("act", lambda: nc.scalar.activation(eb[:, :C], P[1][:, :C], AF.Exp), reads=[Pb[1]], writes=[eb_b])
                c.op("dve", lambda: nc.vector.tensor_tensor(qdec[:, s0:s0 + C], gq[:, l0:l0 + C], eb[:, :C], ALU.mult),
                     reads=[gq_b, eb_b], writes=[qdec_b[n]])
                en, en_b, _ = e4.next()
                c.op("act", lambda: nc.scalar.activation(en[:, :C], P[1][:, :C], AF.Exp, scale=-1.0), reads=[Pb[1]], writes=[en_b])
                kd, kd_b, _ = kdr.next()
                c.op("dve", lambda: nc.vector.tensor_tensor(kd[:, :C], gk[:, l0:l0 + C], en[:, :C], ALU.mult),
                     reads=[gk_b, en_b], writes=[kd_b])
                c.op("act", lambda: nc.scalar.activation(dec[:, n:n + 1], P[1][:, C:C + 1], AF.Exp), reads=[Pb[1]], writes=[dec_b[n]])
                ekt, ek_b, _ = ek.next()
                c.op("act", lambda: nc.scalar.activation(ekt[:C, :], P[2][:C, :128], AF.Exp), reads=[Pb[2]], writes=[ek_b])
                c.op("dve", lambda: nc.vector.tensor_tensor(kst[:C, n, :], gm[:C, i, :], ekt[:C, :], ALU.mult),
                     reads=[gm_b, ek_b], writes=[kst_b[n]])
                c.op("pe", lambda: nc.tensor.matmul(P[3][:C, :C], kd[:, :C], qdec[:, s0:s0 + C], start=True, stop=True),
                     reads=[kd_b, qdec_b[n]], writes=[Pb[3]])
                c.op("dve", lambda: nc.vector.tensor_tensor(Aall[:C, n, :C], P[3][:C, :C], tri[:C, :C], ALU.mult),
                     reads=[Pb[3], tri_b], writes=[A_b[n]])
        c.op("dve", lambda: nc.vector.memset(S[:], 0.0), writes=[S_b])
        for n, (s0, C) in enumerate(GCH):
            po, po_b = P[4 + n % 2], Pb[4 + n % 2]
            for half in range(2):
                c.op("pe", lambda: nc.tensor.matmul(po[:, half * 64:half * 64 + C], gv[:C, n, half * 128:(half + 1) * 128], Aall[:C, n, :C],
                                                    start=True, stop=(n == 0)),
                     reads=[gv_b, A_b[n]], writes=[po_b])
                if n > 0:
                    c.op("pe", lambda: nc.tensor.matmul(po[:, half * 64:half * 64 + C], Sbf[:, half * 128:(half + 1) * 128], qdec[:, s0:s0 + C],
                                                        start=False, stop=True),
                         reads=[Sbf_b, qdec_b[n]], writes=[po_b])
            ti = 0 if n == 0 else 0 + (s0 // 512)
            for half in range(2):
                c.op("act", lambda: nc.scalar.copy(oall[:, half, s0:s0 + C], po[:, half * 64:half * 64 + C]),
                     reads=[po_b], pwrites=[oall_b[min(8, s0 // 512)], oall_b[min(8, (s0 + C - 1) // 512)]])
            c.op("pe", lambda: nc.tensor.matmul(P[0][:, :256], kst[:C, n, :], gv[:C, n, :], start=True, stop=True),
                 reads=[kst_b[n], gv_b], writes=[Pb[0]])
            c.op("dve", lambda: nc.vector.scalar_tensor_tensor(S[:], S[:], dec[:, n:n + 1], P[0][:, :256], ALU.mult, ALU.add),
                 reads=[S_b, dec_b[n], Pb[0]], writes=[S_b])
            c.op("act", lambda: nc.scalar.copy(Sbf[:], S[:]), reads=[S_b], writes=[Sbf_b])
        gg, gg_b = K["ggla"]
        grr = Ring(c, "grr", 2, [128, 2, 512], F32, eg)
        for ti, (q0, n) in enumerate(QT):
            gr, gr_b, gr_d = grr.next()
            c.dma("sp", gr[:, :, :n], f32[F_GR:F_GR + 256, q0:q0 + n].rearrange("(h p) t -> p h t", p=128), gr_d, writes=[gr_b])
            head_norm_out([(oall[:, hh, q0:q0 + n], oall_b[ti]) for hh in range(2)], gg, gg_b, 0, n, q0, 384,
                          extra=[(gr[:, hh, :n], gr_b) for hh in range(2)])
        olds = [gv_b, Sbf_b, S_b] + qdec_b + kst_b + A_b + dec_b + oall_b + gzr.b + gqr.b + gkr.b + gkmr.b + spr.b + e4.b + ek.b + kdr.b + grr.b
        if rows_done is not None:
            rows_done(384, 256, blk_b[3:5])
    if mid_hook is not None:
        c.barrier(exclude=getattr(c, "soft", ()))
        with ExitStack() as eh:
            keep = c.es
            c.es = eh
            mid_hook()
            c.es = keep
        c.barrier(exclude=getattr(c, "soft", ()))

    with ExitStack() as ea:
        qn = Ring(c, "qn", 2, [128, NT], BF16, ea)
        qp = Ring(c, "qp", 2, [64, NT], BF16, ea)
        kn = Ring(c, "kn", 2, [128, NT], BF16, ea)
        vv = Ring(c, "vv", 2, [128, NKC, 128], BF16, ea)
        kpe = sb("kpe", [64, NT], BF16, ea); kpe_b = fresh(olds)
        pT = Ring(c, "pT", 3, [128, 512], BF16, ea)
        osb = Ring(c, "osb", 2, [128, 512], F32, ea)
        rl = Ring(c, "rl", 2, [128, 512], F32, ea)
        exr = Ring(c, "exr", 2, [128, 512], F32, ea)
        mx = sb("mx", [128, 4], F32, ea); mx_b = fresh(olds)
        negm = sb("negm", [128, 1], F32, ea); negm_b = fresh(olds)
        nfb = sb("nfb", [3, 1], F32, ea); nfb_b = fresh(olds)
        aq = Ring(c, "aq", 1, [6, NT], BF16, ea)
        ak = Ring(c, "ak", 1, [6, NT], BF16, ea)
        for r in (qn, qp, kn, vv, pT, osb, rl, exr, aq, ak):
            r.b = [fresh(olds) for _ in r.b]
        c.dma("sp", kpe[:], abf[A_KPE:A_KPE + 64, :], c.dsem("kpe"), writes=[kpe_b])
        aug_b = Buf()
        with ExitStack() as ef:
            fz = sb("fz", [3, NT], F32, ef); fz_b = fresh(olds)
            cs = sb("cs", [3, NT], F32, ef); cs_b = fresh(olds)
            spl = Ring(c, "spl", 2, [3, NT], BF16, ef)
            spl.b = [fresh(olds) for _ in spl.b]
            fb, fb_b = K["fbf"]
            c.dma("sp", fz[:], f32[F_FZ:F_FZ + 3, :], c.dsem("fz"), writes=[fz_b])
            t1, t1_b, t1_d = spl.next()
            c.op("dve", lambda: nc.vector.memset(t1[:], 1.0), writes=[t1_b])
            for r in range(3, 9):
                c.dma("sp", aug[:, r, :], t1[:], t1_d, reads=[t1_b], pwrites=[aug_b])
            c.op("dve", lambda: nc.vector.tensor_scalar(nfb[:], fb[:], -1.0, None, ALU.mult), reads=[fb_b], writes=[nfb_b])
            c.op("act", lambda: nc.scalar.activation(fz[:], fz[:], AF.Exp, bias=nfb[:], scale=-1.0), reads=[fz_b, nfb_b], writes=[fz_b])
            c.op("act", lambda: nc.scalar.activation(fz[:], fz[:], AF.Ln, bias=1.0), reads=[fz_b], writes=[fz_b])
            c.op("dve", lambda: nc.vector.tensor_scalar(fz[:], fz[:], -1.0, None, ALU.mult), reads=[fz_b], writes=[fz_b])
            for j, (q0, n) in enumerate(QT):
                init = 0.0 if j == 0 else cs[:, q0 - 1:q0]
                c.op("dve", lambda: nc.vector.tensor_tensor_scan(cs[:, q0:q0 + n], ones[:3, :n], fz[:, q0:q0 + n], init, ALU.mult, ALU.add),
                     reads=[ones_b, fz_b, cs_b], writes=[cs_b])
            for i in range(3):
                t1, t1_b, t1_d = spl.next()
                c.op("dve", lambda: nc.vector.tensor_copy(t1[:], cs[:]), reads=[cs_b], writes=[t1_b])
                c.dma("sp", aug[:, i, :], t1[:], t1_d, reads=[t1_b], pwrites=[aug_b])
                if i < 2:
                    c.op("dve", lambda: nc.vector.tensor_tensor(cs[:], cs[:], t1[:], ALU.subtract), reads=[cs_b, t1_b], writes=[cs_b])
                t2, t2_b, t2_d = spl.next()
                c.op("act", lambda: nc.scalar.mul(t2[:], t1[:], -1.0), reads=[t1_b], writes=[t2_b])
                c.dma("sp", aug[:, 9 + i, :], t2[:], t2_d, reads=[t2_b], pwrites=[aug_b])
            olds = olds + [fz_b, cs_b] + spl.b

        heads = [("mla", h) for h in range(3)] + [("fox", h) for h in range(3)]
        for (kind, h) in heads:
            q_t, q_b, q_d = qn.next()
            k_t, k_b, k_d = kn.next()
            v_t, v_b, v_d = vv.next()
            parts = []
            if kind == "mla":
                qrow, krow, vcol, orow = A_Q + h * 192, A_KN + h * 128, V_VM + h * 128, h * 128
                gain, gain_b = K["gmla"]
                p_t, p_b, p_d = qp.next()
                c.dma("sp", p_t[:], abf[qrow + 128:qrow + 192, :], p_d, writes=[p_b])
                parts = [(k_t, k_b, q_t, q_b, 128), (kpe, kpe_b, p_t, p_b, 64)]
            else:
                qrow, krow, vcol, orow = A_FQ + h * 128, A_FK + h * 128, V_FV + h * 128, 640 + h * 128
                gain, gain_b = K["gfox"]
                a_q, a_qb, a_qd = aq.next()
                a_k, a_kb, a_kd = ak.next()
                c.dma("sp", a_q[:], aug[h, 0:6, :], a_qd, reads=[aug_b], writes=[a_qb])
                c.dma("sp", a_k[:], aug[h, 6:12, :], a_kd, reads=[aug_b], writes=[a_kb])
                parts = [(k_t, k_b, q_t, q_b, 128), (a_k, a_kb, a_q, a_qb, 6)]
            c.dma("sp", q_t[:], abf[qrow:qrow + 128, :], q_d, writes=[q_b])
            c.dma("sp", k_t[:], abf[krow:krow + 128, :], k_d, writes=[k_b])
            for i in range(4):
                c.dma("sp", v_t[:, 8 * i:8 * i + 8, :], vbf[1024 * i:1024 * (i + 1), vcol:vcol + 128].rearrange("(n p) d -> p n d", p=128),
                      v_d, writes=[v_b] if i == 0 else [], pwrites=[v_b] if i else [])
            c.dma("sp", v_t[:16, 32, :], vbf[4096:4112, vcol:vcol + 128], v_d, pwrites=[v_b])
            c.op("dve", lambda: nc.vector.memset(mx[:], 0.0), writes=[mx_b])
            for side in range(2):
                plist = [(p[2], p[3], p[4]) if side == 0 else (p[0], p[1], p[4]) for p in parts if p[4] > 6]
                for (q0, n) in QT:
                    for i, (t_, b_, kp) in enumerate(plist):
                        s, s_b, _ = sqr.next()
                        c.op("act", lambda: nc.scalar.activation(s[:kp, :n], t_[:kp, q0:q0 + n], AF.Square), reads=[b_], writes=[s_b])
                        c.op("pe", lambda: nc.tensor.matmul(P[6][:, :n], ones[:kp, :128], s[:kp, :n], start=(i == 0), stop=(i == len(plist) - 1)),
                             reads=[s_b, ones_b], writes=[Pb[6]])
                    c.op("dve", lambda: nc.vector.reduce_max(mx[:, 2:3], P[6][:, :n], axis=AX.X), reads=[Pb[6]], writes=[mx_b])
                    c.op("dve", lambda: nc.vector.tensor_tensor(mx[:, side:side + 1], mx[:, side:side + 1], mx[:, 2:3], ALU.max),
                         reads=[mx_b], writes=[mx_b])
            c.op("dve", lambda: nc.vector.tensor_tensor(mx[:, 3:4], mx[:, 0:1], mx[:, 1:2], ALU.mult), reads=[mx_b], writes=[mx_b])
            c.op("act", lambda: nc.scalar.activation(negm[:], mx[:, 3:4], AF.Sqrt), reads=[mx_b], writes=[negm_b])
            c.op("dve", lambda: nc.vector.tensor_scalar(negm[:], negm[:], -1.0, None, ALU.mult), reads=[negm_b], writes=[negm_b])
            for ti, (q0, n) in enumerate(QT):
                last_c = min(4 * ti + 3, NKC - 1)
                po, po_b = P[2 + ti % 2], Pb[2 + ti % 2]
                pl, pl_b = P[4 + ti % 2], Pb[4 + ti % 2]
                pend = None

                def pv(kc_, kn2, p2, p2_b):
                    c.op("pe", lambda: nc.tensor.matmul(po[:, :n], v_t[:kn2, kc_, :], p2[:kn2, :n], start=(kc_ == 0), stop=(kc_ == last_c)),
                         reads=[v_b, p2_b], writes=[po_b], inc=(kc_ == last_c))
                    c.op("pe", lambda: nc.tensor.matmul(pl[:, :n], onesb[:kn2, :], p2[:kn2, :n], start=(kc_ == 0), stop=(kc_ == last_c)),
                         reads=[onesb_b, p2_b], writes=[pl_b], inc=True)

                for kc in range(last_c + 1):
                    k0 = kc * 128
                    kn_ = min(128, NT - k0)
                    ps, ps_b = P[kc % 2], Pb[kc % 2]
                    for i, (kt_, kb_, qt_, qb_, kp) in enumerate(parts):
                        c.op("pe", lambda: nc.tensor.matmul(ps[:kn_, :n], kt_[:kp, k0:k0 + kn_], qt_[:kp, q0:q0 + n],
                                                            start=(i == 0), stop=(i == len(parts) - 1)),
                             reads=[kb_, qb_], writes=[ps_b], inc=(i == len(parts) - 1))
                    if pend is not None:
                        pv(*pend)
                    p_, p_b2, _ = pT.next()
                    r = kc - 4 * ti
                    if r >= 0 and kind == "fox":
                        x_, x_b, _ = exr.next()
                        c.op("dve", lambda: nc.vector.tensor_scalar(x_[:kn_, :n], ps[:kn_, :n], negm[:kn_, :], 0.0, ALU.add, ALU.min),
                             reads=[ps_b, negm_b], writes=[x_b])
                        c.op("act", lambda: nc.scalar.activation(p_[:kn_, :n], x_[:kn_, :n], AF.Exp), reads=[x_b], writes=[p_b2])
                    else:
                        c.op("act", lambda: nc.scalar.activation(p_[:kn_, :n], ps[:kn_, :n], AF.Exp, bias=negm[:kn_, :]),
                             reads=[ps_b, negm_b], writes=[p_b2])
                    if r >= 0:
                        c.op("dve", lambda: nc.vector.tensor_tensor(p_[:kn_, :n], p_[:kn_, :n], mk[:kn_, r, :n], ALU.mult),
                             reads=[p_b2, mk_b], writes=[p_b2])
                    pend = (kc, kn_, p_, p_b2)
                pv(*pend)
                r_, r_b, _ = rl.next()
                c.op("dve", lambda: nc.vector.reciprocal(r_[:, :n], pl[:, :n]), reads=[pl_b], writes=[r_b])
                o_, o_b, _ = osb.next()
                c.op("dve", lambda: nc.vector.tensor_tensor(o_[:, :n], po[:, :n], r_[:, :n], ALU.mult), reads=[po_b, r_b], writes=[o_b])
                head_norm_out([(o_[:, :n], o_b)], gain, gain_b, h, n, q0, orow)
            if rows_done is not None:
                rows_done(orow, 128, [blk_b[orow // 128]])
    c.wait_all("sp", [out_b])


_PROG = {}


def _prog(name, builder):
    if name not in _PROG:
        _PROG[name] = builder()
    return _PROG[name]


def _fmaj(a):
    T = a.shape[0]
    return np.ascontiguousarray(a.T.reshape(-1, 128, T).transpose(1, 0, 2))


def _unfm(a):
    return a.transpose(1, 0, 2).reshape(-1, a.shape[2]).T


def _pcol(v, n):
    return np.ascontiguousarray(np.asarray(v).reshape(n, 128).T)


def _p2_consts():
    kk = np.arange(128)[:, None]
    qq = np.arange(512)[None, :]
    mask4 = np.stack([(qq >= r * 128 + kk) for r in range(4)], 1).astype(NPBF)
    s = np.arange(64)[:, None]
    t = np.arange(64)[None, :]
    m16 = np.float32(-1.0 / 16.0)
    tri01 = (s <= t).astype(np.float32)
    tris64 = np.concatenate([(s <= t) * m16, np.full((64, 1), m16)], 1).astype(np.float32)
    tris16 = np.ascontiguousarray(np.concatenate([tris64[:16, :16], tris64[:16, 64:65]], 1))
    sus = ((s > t) * m16).astype(np.float32)
    return dict(mask4=mask4, tri01=tri01, tris64=tris64, tris16=tris16, sus=sus)


def _core_cols(g4):
    s = NMETA + g4 * OWN
    return np.concatenate([np.arange(s - 2, s + OWN), np.array([0, 0]), np.arange(0, NMETA)])


def kernel_unfused(x, meta_tokens, attn_norm, w_in, mla_q_norm, mla_w_uq, mla_kv_norm, mla_w_ukv,
           gla_w_gate2, gla_b_gate, fox_b_f, out_norm_mla, out_norm_gla, out_norm_fox,
           w_out, ffn_norm, ffn_w_up, ffn_conv_w, ffn_conv_b, ffn_w_down, final_norm):
    f32 = np.float32
    x = np.asarray(x, f32)
    B = x.shape[0]
    cores = list(range(8))
    h = np.concatenate([np.broadcast_to(np.asarray(meta_tokens, f32)[None], (B, NMETA, D)), x], axis=1)
    pos = np.arange(NT, dtype=f32)
    inv = (f32(1.0) / (f32(10000.0) ** (np.arange(0, 64, 2, dtype=f32) / f32(64)))).astype(f32)
    ang = (pos[:, None] * inv[None, :]).astype(f32)
    cosT, sinT = np.cos(ang).astype(f32).T, np.sin(ang).astype(f32).T
    zero_cols = np.array([2 + OWN, 3 + OWN])
    p2c = _p2_consts()
    p1, p2, p3 = _prog("p1", build_p1), _prog("p2", build_p2), _prog("p3", build_p3)
    y_final = None
    for l in range(2):
        uq3 = np.asarray(mla_w_uq[l]).reshape(1536, 12, 192)
        w_uq_p = np.ascontiguousarray(np.concatenate(
            [uq3[:, :, :128].reshape(1536, -1), uq3[:, :, 128:160].reshape(1536, -1), uq3[:, :, 160:].reshape(1536, -1)], 1))
        kv3 = np.asarray(mla_w_ukv[l]).reshape(512, 12, 256)
        w_ukv_p = np.ascontiguousarray(np.concatenate([kv3[:, :, :128].reshape(512, -1), kv3[:, :, 128:].reshape(512, -1)], 1))
        hTs = []
        maps = []
        for core in cores:
            b, g4 = divmod(core, 4)
            cols = _core_cols(g4)
            Hc = h[b][cols]
            Hc[zero_cols] = 0
            hT = _fmaj(Hc)
            hTs.append(hT)
            cs = cosT[:, cols].copy(); sn = sinT[:, cols].copy()
            maps.append(dict(hT=hT, w_in=np.asarray(w_in[l]), w_uq=w_uq_p, w_ukv=w_ukv_p,
                             g_attn=_pcol(attn_norm[l], 32), g_q=_pcol(mla_q_norm[l], 12), g_kv=_pcol(mla_kv_norm[l], 4),
                             cos4=np.ascontiguousarray(np.tile(cs, (4, 1))), sin4=np.ascontiguousarray(np.tile(sn, (4, 1)))))
        r1 = run_bass_kernel_spmd(p1, maps, core_ids=cores).results
        del maps
        maps = []
        for b in range(B):
            def gather_fm(name):
                parts = [r1[b * 4][name][:, 4 + OWN:4 + OWN + NMETA]] + [r1[b * 4 + g][name][:, 2:2 + OWN] for g in range(4)]
                return np.concatenate(parts, axis=1)

            def gather_tm(name):
                parts = [r1[b * 4][name][4 + OWN:4 + OWN + NMETA]] + [r1[b * 4 + g][name][2:2 + OWN] for g in range(4)]
                return np.concatenate(parts, axis=0)
            obf, o32, otf, otb = gather_fm("obf"), gather_fm("o32"), gather_tm("otf"), gather_tm("otb")
            for g in range(4):
                abf = np.concatenate([obf[RB_Q + 3 * g * 192:RB_Q + 3 * (g + 1) * 192],
                                      obf[RB_KN + 3 * g * 128:RB_KN + 3 * (g + 1) * 128],
                                      obf[RB_KPE:RB_KPE + 64],
                                      obf[RB_FQ + 3 * g * 128:RB_FQ + 3 * (g + 1) * 128],
                                      obf[RB_FK + 3 * g * 128:RB_FK + 3 * (g + 1) * 128]], 0)
                vbf = np.concatenate([otb[:, CB_VM + 3 * g * 128:CB_VM + 3 * (g + 1) * 128],
                                      otb[:, CB_FV + 3 * g * 128:CB_FV + 3 * (g + 1) * 128],
                                      otb[:, CB_GV + g * 256:CB_GV + (g + 1) * 256]], 1)
                ff = np.concatenate([o32[R32_GQ + g * 128:R32_GQ + (g + 1) * 128],
                                     o32[R32_GK + g * 128:R32_GK + (g + 1) * 128],
                                     o32[R32_GZ:R32_GZ + 16], np.ones((1, NT), f32),
                                     o32[R32_GR + g * 256:R32_GR + (g + 1) * 256],
                                     o32[R32_FZ + 3 * g:R32_FZ + 3 * (g + 1)]], 0)
                wg2 = np.concatenate([np.asarray(gla_w_gate2[l])[:, g * 128:(g + 1) * 128],
                                      np.asarray(gla_b_gate[l])[None, g * 128:(g + 1) * 128]], 0).astype(f32)
                maps.append(dict(
                    abf=np.ascontiguousarray(abf), vbf=np.ascontiguousarray(vbf), f32=np.ascontiguousarray(ff),
                    gkm=np.ascontiguousarray(otf[:, g * 128:(g + 1) * 128]), wg2=np.ascontiguousarray(wg2),
                    fbf=np.ascontiguousarray(np.asarray(fox_b_f[l], f32)[3 * g:3 * g + 3, None]),
                    gmla=np.ascontiguousarray(np.asarray(out_norm_mla[l], f32).reshape(12, 128)[3 * g:3 * g + 3].T),
                    gfox=np.ascontiguousarray(np.asarray(out_norm_fox[l], f32).reshape(12, 128)[3 * g:3 * g + 3].T),
                    ggla=np.ascontiguousarray(np.asarray(out_norm_gla[l], f32).reshape(4, 2, 128)[g].T),
                    **p2c))
        del r1
        r2 = run_bass_kernel_spmd(p2, maps, core_ids=cores).results
        del maps
        maps = []
        for b in range(B):
            om = np.empty((D, NT), NPBF)
            for g in range(4):
                o = r2[b * 4 + g]["oT"]
                om[3 * g * 128:3 * (g + 1) * 128] = o[0:384]
                om[1536 + g * 256:1536 + (g + 1) * 256] = o[384:640]
                om[2560 + 3 * g * 128:2560 + 3 * (g + 1) * 128] = o[640:1024]
            for g4 in range(4):
                cols = _core_cols(g4)
                oc = om[:, cols]
                oc[:, zero_cols] = 0
                oTc = np.ascontiguousarray(oc.reshape(32, 128, TL).transpose(1, 0, 2))
                gn = final_norm if l == 1 else attn_norm[1]
                cw = np.asarray(ffn_conv_w[l], f32)
                maps.append(dict(oT=oTc, hT=hTs[b * 4 + g4], w_out=np.asarray(w_out[l]), w_up=np.asarray(ffn_w_up[l]),
                                 w_down=np.asarray(ffn_w_down[l]), g_ffn=_pcol(ffn_norm[l], 32), g_next=_pcol(gn, 32),
                                 conv_w=np.ascontiguousarray(cw.T.reshape(172, 128, 3).transpose(1, 0, 2)),
                                 conv_b=_pcol(ffn_conv_b[l], 172)))
        del r2
        r3 = run_bass_kernel_spmd(p3, maps, core_ids=cores).results
        del maps
        for core in cores:
            b, g4 = divmod(core, 4)
            s = NMETA + g4 * OWN
            ho = _unfm(r3[core]["h_out"])
            h[b, s:s + OWN] = ho[2:2 + OWN]
            if g4 == 0:
                h[b, 0:NMETA] = ho[4 + OWN:4 + OWN + NMETA]
        if l == 1:
            y_final = np.empty((B, SEQ, D), f32)
            for core in cores:
                b, g4 = divmod(core, 4)
                y_final[b, g4 * OWN:(g4 + 1) * OWN] = _unfm(r3[core]["y_out"])[2:2 + OWN]
        del r3
    return y_final


class Gath:
    def __init__(self, nc, name, R, C, dtype, esz):
        rp = (1 << 20) // (C * esz)
        if rp >= 64:
            rp = (rp // 64) * 64
        self.C = C
        self.pieces = [(r0, min(rp, R - r0)) for r0 in range(0, R, rp)]
        self.g = [nc.dram_tensor(f"{name}_g{i}", [4 * n, C], dtype, kind="Internal").ap() for i, (r0, n) in enumerate(self.pieces)]
        self.buf = Buf()

    def gather(self, c, X, cs, reads=()):
        for (r0, n), g in zip(self.pieces, self.g):
            c.allgather(g, X[r0:r0 + n, :], cs, reads=reads, writes=[self.buf])

    def gather_where(self, c, X, cs, pred, reads):
        for (r0, pn), g in zip(self.pieces, self.g):
            if pred(r0):
                c.allgather(g, X[r0:r0 + pn, :], cs, reads=reads, writes=[self.buf])

    def gather_rows(self, c, X, cs, row0, n, reads):
        for (r0, pn), g in zip(self.pieces, self.g):
            if r0 >= row0 and r0 + pn <= row0 + n:
                c.allgather(g, X[r0:r0 + pn, :], cs, reads=reads, writes=[self.buf])

    def segs(self, row0, n):
        out = []
        for (r0, pn), g in zip(self.pieces, self.g):
            a, b = max(row0, r0), min(row0 + n, r0 + pn)
            if a < b:
                out.append((g.rearrange("(r n) c -> r n c", r=4)[:, a - r0:b - r0, :], a - row0, b - a))
        return out


def emit_select1(c, sel, G32, GBF, GTF, GTBg, GTBr, abf, vbf, f32, gkm, dst_b, part):
    nc, es = c.nc, c.es
    K = _consts(c, [("sel", sel, [128, 4])])
    sl, sl_b = K["sel"]
    fm_jobs = [(GBF, abf, BF16, [(A_Q, RB_Q, 576, 576), (A_KN, RB_KN, 384, 384), (A_KPE, RB_KPE, 64, 0),
                                 (A_FQ, RB_FQ, 384, 384), (A_FK, RB_FK, 384, 384)], "sb")] if part == "B" else \
              [(G32, f32, F32, [(F_GQ, R32_GQ, 128, 128), (F_GK, R32_GK, 128, 128), (F_GZ, R32_GZ, 16, 0),
                                (F_GR, R32_GR, 256, 256), (F_FZ, R32_FZ, 3, 3)], "sf")]
    for (G, dst, dtype, jobs, tag) in fm_jobs:
        cand = Ring(c, "cand" + tag, 4, [128, NT], dtype)
        accr = Ring(c, "acc" + tag, 2, [128, NT], dtype)
        for (d0, s0, nrows, stride) in jobs:
            for r0 in range(0, nrows, 128):
                n = min(128, nrows - r0)
                a, a_b, a_d = accr.next()
                for g in range(4):
                    t, b, d = cand.next()
                    srow = s0 + g * stride + r0
                    first = True
                    for (gv, p0, ln) in G.segs(srow, n):
                        c.dma("sp" if g % 2 == 0 else "act", t[p0:p0 + ln, NMETA:].rearrange("p (r t) -> p r t", r=4),
                              gv[:, :, 2:2 + OWN].rearrange("r p t -> p r t"), d, reads=[G.buf],
                              writes=[b] if first else [], pwrites=[] if first else [b])
                        first = False
                        c.dma("sp" if g % 2 == 0 else "act", t[p0:p0 + ln, :NMETA], gv[0, :, 4 + OWN:4 + OWN + NMETA], d, reads=[G.buf], pwrites=[b])
                    if g == 0:
                        c.op("dve", lambda: nc.vector.tensor_scalar(a[:n, :], t[:n, :], sl[:n, 0:1], None, ALU.mult),
                             reads=[b, sl_b], writes=[a_b])
                    else:
                        c.op("dve", lambda: nc.vector.scalar_tensor_tensor(a[:n, :], t[:n, :], sl[:n, g:g + 1], a[:n, :], ALU.mult, ALU.add),
                             reads=[b, sl_b, a_b], writes=[a_b])
                c.dma("sp", dst[d0 + r0:d0 + r0 + n, :], a[:n, :], a_d, reads=[a_b], pwrites=[dst_b])
    if part == "A":
        on = es.enter_context(nc.sbuf_tensor(c.nm("ones_row"), [1, NT], F32)); on_b = Buf()
        c.op("dve", lambda: nc.vector.memset(on[:], 1.0), writes=[on_b])
        c.dma("sp", f32[F_GZ + 16:F_GZ + 17, :], on[:], c.dsem("onr"), reads=[on_b], pwrites=[dst_b])
    chunks = [(0, 4 + OWN, NMETA, 0)] + [(r, 2 + 128 * i, 128, NMETA + OWN * r + 128 * i) for r in range(4) for i in range(8)]
    if part == "A":
        jobs = [(GTBg, 1024, BF16, vbf, [(V_GV, 0, 256, 256)], "tg"), (GTF, 512, F32, gkm, [(0, 0, 128, 128)], "tk")]
    else:
        jobs = [(GTBr, 3072, BF16, vbf, [(V_VM, CB_VM - CB_FV, 384, 384), (V_FV, 0, 384, 384)], "tr")]
    for (G, width, dtype, dst, blocks, tag) in jobs:
        candt = Ring(c, "cand" + tag, 3, [128, width], dtype)
        acct = Ring(c, "acc" + tag, 2, [128, 768], dtype)
        for (r, srow, n, drow) in chunks:
            a, a_b, a_d = acct.next()
            t, b, d = candt.next()
            first = True
            for (gv, p0, ln) in G.segs(srow, n):
                c.dma("sp", t[p0:p0 + ln, :], gv[r, :, :], d, reads=[G.buf], writes=[b] if first else [], pwrites=[] if first else [b])
                first = False
            for g in range(4):
                for bi, (dc, sc0, w, stride) in enumerate(blocks):
                    sc = sc0 + stride * g
                    ao = sum(bb[2] for bb in blocks[:bi])
                    if g == 0:
                        c.op("dve", lambda: nc.vector.tensor_scalar(a[:n, ao:ao + w], t[:n, sc:sc + w], sl[:n, 0:1], None, ALU.mult),
                             reads=[b, sl_b], pwrites=[a_b])
                    else:
                        c.op("dve", lambda: nc.vector.scalar_tensor_tensor(a[:n, ao:ao + w], t[:n, sc:sc + w], sl[:n, g:g + 1], a[:n, ao:ao + w], ALU.mult, ALU.add),
                             reads=[b, sl_b, a_b], pwrites=[a_b])
            for bi, (dc, sc0, w, stride) in enumerate(blocks):
                ao = sum(bb[2] for bb in blocks[:bi])
                c.dma("sp", dst[drow:drow + n, dc:dc + w], a[:n, ao:ao + w], a_d, reads=[a_b], pwrites=[dst_b])


def _omix_src(kc):
    if kc < 12:
        return kc // 3, (kc % 3) * 128
    if kc < 20:
        return (kc - 12) // 2, 384 + ((kc - 12) % 2) * 128
    return (kc - 20) // 3, 640 + ((kc - 20) % 3) * 128


def emit_select2(c, sel, GO, oT3, dst_b):
    nc, es = c.nc, c.es
    K = _consts(c, [("sel", sel, [128, 4])])
    sl, sl_b = K["sel"]
    cand = Ring(c, "cand2", 4, [128, 2 + OWN], BF16)
    accr = Ring(c, "acc2", 2, [128, TL], BF16)
    for i in range(2):
        c.op("dve", lambda: nc.vector.memset(accr.t[i][:], 0.0), writes=[accr.b[i]])
    for kc in range(32):
        rk, row0 = _omix_src(kc)
        a, a_b, a_d = accr.next()
        segs = GO.segs(row0, 128)
        for (gv, p0, ln) in segs:
            c.dma("sp", a[p0:p0 + ln, 4 + OWN:], gv[rk, :, 0:NMETA], a_d, reads=[GO.buf], pwrites=[a_b])
        for dd in range(4):
            t, b, d = cand.next()
            s0 = NMETA + OWN * dd - 2
            first = True
            for (gv, p0, ln) in segs:
                c.dma("sp" if dd % 2 == 0 else "act", t[p0:p0 + ln, :], gv[rk, :, s0:s0 + 2 + OWN], d, reads=[GO.buf],
                      writes=[b] if first else [], pwrites=[] if first else [b])
                first = False
            if dd == 0:
                c.op("dve", lambda: nc.vector.tensor_scalar(a[:, :2 + OWN], t[:], sl[:, 0:1], None, ALU.mult), reads=[b, sl_b], pwrites=[a_b])
            else:
                c.op("dve", lambda: nc.vector.scalar_tensor_tensor(a[:, :2 + OWN], t[:], sl[:, dd:dd + 1], a[:, :2 + OWN], ALU.mult, ALU.add),
                     reads=[b, sl_b, a_b], pwrites=[a_b])
        c.dma("sp", oT3[:, kc, :], a[:], a_d, reads=[a_b], pwrites=[dst_b])


def emit_halo(c, selh, hbuf, h_b, tail, g_tail, cs):
    nc, es = c.nc, c.es
    K = _consts(c, [("selh", selh, [128, 5])])
    sh, sh_b = K["selh"]
    tail_b, gt_b = Buf(), Buf()
    d = c.dsem("halo")
    c.dma("sp", tail.rearrange("p (k t) -> p k t", t=2), hbuf[:, :, OWN:OWN + 2], d, reads=[h_b], writes=[tail_b])
    c.allgather(g_tail, tail, cs, reads=[tail_b], writes=[gt_b])
    cnd = es.enter_context(nc.sbuf_tensor(c.nm("hcand"), [128, 5, 64], F32)); cnd_b = Buf()
    d2 = c.dsem("halo2")
    c.dma("sp", cnd[:, 0:4, :], g_tail.rearrange("(r p) n -> p r n", r=4), d2, reads=[gt_b], writes=[cnd_b])
    c.dma("sp", cnd[:, 4, :].rearrange("p (k t) -> p k t", t=2), hbuf[:, :, TL - 2:TL], d2, reads=[h_b], pwrites=[cnd_b])
    Ctx.seal(d2, [cnd_b])
    acc = es.enter_context(nc.sbuf_tensor(c.nm("hacc"), [128, 64], F32)); acc_b = Buf()
    zz = es.enter_context(nc.sbuf_tensor(c.nm("hzero"), [128, 64], F32)); zz_b = Buf()
    c.op("dve", lambda: nc.vector.memset(zz[:], 0.0), writes=[zz_b])
    c.op("dve", lambda: nc.vector.tensor_scalar(acc[:], cnd[:, 0, :], sh[:, 0:1], None, ALU.mult), reads=[cnd_b, sh_b], writes=[acc_b])
    for i in range(1, 5):
        c.op("dve", lambda: nc.vector.scalar_tensor_tensor(acc[:], cnd[:, i, :], sh[:, i:i + 1], acc[:], ALU.mult, ALU.add),
             reads=[cnd_b, sh_b, acc_b], writes=[acc_b])
    d3 = c.dsem("halo3")
    c.dma("sp", hbuf[:, :, 0:2], acc[:].rearrange("p (k t) -> p k t", t=2), d3, reads=[acc_b, tail_b, cnd_b], pwrites=[h_b])
    c.dma("sp", hbuf[:, :, 2 + OWN:4 + OWN], zz[:].rearrange("p (k t) -> p k t", t=2), d3, reads=[zz_b], pwrites=[h_b])


def build_fused():
    nc = bass.Bass("TRN2", target_bir_lowering=False)
    dt = lambda n, s, d=F32, k="ExternalInput": nc.dram_tensor(n, s, d, kind=k).ap()
    I = lambda n, s, d=F32: nc.dram_tensor(n, s, d, kind="Internal").ap()
    hT0 = dt("hT0", [128, 32, TL])
    cos4, sin4 = dt("cos4", [128, TL]), dt("sin4", [128, TL])
    sel, selh = dt("sel", [128, 4]), dt("selh", [128, 5])
    w_in = dt("w_in", [2, D, DIN]); w_uq = dt("w_uq", [2, 1536, 2304]); w_ukv = dt("w_ukv", [2, 512, 3072])
    w_out = dt("w_out", [2, D, D]); w_up = dt("w_up", [2, D, 2 * DFF]); w_down = dt("w_down", [2, DFF, D])
    g_attn, g_q, g_kv = dt("g_attn", [2, 128, 32]), dt("g_q", [2, 128, 12]), dt("g_kv", [2, 128, 4])
    g_ffn, g_next = dt("g_ffn", [2, 128, 32]), dt("g_next", [2, 128, 32])
    conv_w, conv_b = dt("conv_w", [2, 128, 172, 3]), dt("conv_b", [2, 128, 172])
    wg2, fbf = dt("wg2", [2, 17, 128]), dt("fbf", [2, 3, 1])
    gmla, gfox, ggla = dt("gmla", [2, 128, 3]), dt("gfox", [2, 128, 3]), dt("ggla", [2, 128, 2])
    mask4 = dt("mask4", [128, 4, 512], BF16)
    tri01, tris64, tris16, sus = dt("tri01", [64, 64]), dt("tris64", [64, 65]), dt("tris16", [16, 17]), dt("sus", [64, 64])
    y_out = dt("y_out", [128, 32, TL], F32, "ExternalOutput")
    o32, obf, otf, otb = I("o32", [N32, TL]), I("obf", [NB, TL], BF16), I("otf", [TL, 512]), I("otb", [TL, NCB], BF16)
    G32, GBF = Gath(nc, "o32", N32, TL, F32, 4), Gath(nc, "obf", NB, TL, BF16, 2)
    GTF = Gath(nc, "otf", TL, 512, F32, 4)
    otbG, otbR = I("otbG", [TL, 1024], BF16), I("otbR", [TL, 3072], BF16)
    GTBg, GTBr = Gath(nc, "otbG", TL, 1024, BF16, 2), Gath(nc, "otbR", TL, 3072, BF16, 2)
    GO = Gath(nc, "oT2", 1024, NT, BF16, 2)
    abf, vbf, f32, gkm = I("abf", [NA, NT], BF16), I("vbf", [NT, NV], BF16), I("f32s", [NF, NT]), I("gkm", [NT, 128])
    aug = I("aug_scr", [3, 12, NT], BF16)
    oT2, oT3 = I("oT2", [1024, NT], BF16), I("oT3", [128, 32, TL], BF16)
    h_mid, act = I("h_mid", [128, 32, TL]), I("act_scr", [86, 128, TL], BF16)
    hA, hB = I("hA", [128, 32, TL]), I("hB", [128, 32, TL])
    tail, g_tail = I("tail", [128, 64]), I("g_tail", [4 * 128, 64])
    with ExitStack() as es:
        c = Ctx(nc, es)
        cs = c.dsem("coll")
        c.phase_dsems.remove(cs)

        def phase(fn, exclude=()):
            with ExitStack() as pes:
                c.es = pes
                fn()
                c.es = c.sem_es
            c.end_phase(exclude)

        csA, csB, cs2 = c.dsem("collA"), c.dsem("collB"), c.dsem("coll2")
        for x_ in (csA, csB, cs2):
            c.phase_dsems.remove(x_)
        c.soft = {id(cs2.h)}
        hcur = hT0
        for l in range(2):
            def early(ob):
                for (G_, x_) in ((G32, o32), (GTF, otf), (GTBg, otbG)):
                    G_.gather(c, x_, csA, reads=[ob])
                GBF.gather_where(c, obf, csB, lambda r0: r0 >= RB_KPE, [ob])
            phase(lambda: emit_p1(c, hcur, w_in[l], w_uq[l], w_ukv[l], g_attn[l], g_q[l], g_kv[l], cos4, sin4, o32, obf, otf, otb,
                                  otb_split=(otbG, otbR), after_win=early), exclude={id(csA.h), id(csB.h)})
            GBF.gather_where(c, obf, csB, lambda r0: r0 < RB_KPE, [])
            GTBr.gather(c, otbR, csB)
            db = Buf()
            phase(lambda: emit_select1(c, sel, G32, GBF, GTF, GTBg, GTBr, abf, vbf, f32, gkm, db, "A"), exclude={id(csB.h)})
            phase(lambda: emit_p2(c, abf, vbf, f32, gkm, wg2[l], fbf[l], gmla[l], gfox[l], ggla[l], mask4, tri01, tris64, tris16, sus, oT2, aug,
                                  mid_hook=lambda: emit_select1(c, sel, G32, GBF, GTF, GTBg, GTBr, abf, vbf, f32, gkm, db, "B"),
                                  rows_done=lambda r0, n, bufs: GO.gather_rows(c, oT2, cs2, r0, n, bufs)))
            db2 = Buf()
            phase(lambda: emit_select2(c, sel, GO, oT3, db2))
            hnext = y_out if False else (hA if l == 0 else hB)
            phase(lambda: emit_p3(c, oT3, hcur, w_out[l], w_up[l], w_down[l], g_ffn[l], g_next[l], conv_w[l], conv_b[l],
                                  hnext, y_out, h_mid, act, final=(l == 1)))
            if l == 0:
                hb_ = Buf()
                phase(lambda: emit_halo(c, selh, hnext, hb_, tail, g_tail, cs))
            hcur = hnext
        c.barrier()
        print("fused program instructions:", c.n_inst)
    return nc


def kernel(x, meta_tokens, attn_norm, w_in, mla_q_norm, mla_w_uq, mla_kv_norm, mla_w_ukv,
           gla_w_gate2, gla_b_gate, fox_b_f, out_norm_mla, out_norm_gla, out_norm_fox,
           w_out, ffn_norm, ffn_w_up, ffn_conv_w, ffn_conv_b, ffn_w_down, final_norm):
    f32 = np.float32
    x = np.asarray(x, f32)
    B = x.shape[0]
    cores = list(range(8))
    h = np.concatenate([np.broadcast_to(np.asarray(meta_tokens, f32)[None], (B, NMETA, D)), x], axis=1)
    pos = np.arange(NT, dtype=f32)
    inv = (f32(1.0) / (f32(10000.0) ** (np.arange(0, 64, 2, dtype=f32) / f32(64)))).astype(f32)
    ang = (pos[:, None] * inv[None, :]).astype(f32)
    cosT, sinT = np.cos(ang).astype(f32).T, np.sin(ang).astype(f32).T
    zero_cols = np.array([2 + OWN, 3 + OWN])
    A = lambda v: np.asarray(v, f32)
    uq3 = A(mla_w_uq).reshape(2, 1536, 12, 192)
    w_uq_p = np.ascontiguousarray(np.concatenate(
        [uq3[..., :128].reshape(2, 1536, -1), uq3[..., 128:160].reshape(2, 1536, -1), uq3[..., 160:].reshape(2, 1536, -1)], 2))
    kv3 = A(mla_w_ukv).reshape(2, 512, 12, 256)
    w_ukv_p = np.ascontiguousarray(np.concatenate([kv3[..., :128].reshape(2, 512, -1), kv3[..., 128:].reshape(2, 512, -1)], 2))
    cw = A(ffn_conv_w)
    shared = dict(
        w_in=A(w_in), w_uq=w_uq_p, w_ukv=w_ukv_p, w_out=A(w_out), w_up=A(ffn_w_up), w_down=A(ffn_w_down),
        g_attn=np.stack([_pcol(attn_norm[l], 32) for l in range(2)]),
        g_q=np.stack([_pcol(mla_q_norm[l], 12) for l in range(2)]),
        g_kv=np.stack([_pcol(mla_kv_norm[l], 4) for l in range(2)]),
        g_ffn=np.stack([_pcol(ffn_norm[l], 32) for l in range(2)]),
        g_next=np.stack([_pcol(attn_norm[1], 32), _pcol(final_norm, 32)]),
        conv_w=np.stack([np.ascontiguousarray(cw[l].T.reshape(172, 128, 3).transpose(1, 0, 2)) for l in range(2)]),
        conv_b=np.stack([_pcol(ffn_conv_b[l], 172) for l in range(2)]),
        **_p2_consts())
    maps = []
    for core in cores:
        b, g = divmod(core, 4)
        cols = _core_cols(g)
        Hc = h[b][cols]
        Hc[zero_cols] = 0
        sel = np.zeros((128, 4), f32); sel[:, g] = 1
        selh = np.zeros((128, 5), f32); selh[:, 4 if g == 0 else g - 1] = 1
        m = dict(shared)
        m.update(
            hT0=_fmaj(Hc),
            cos4=np.ascontiguousarray(np.tile(cosT[:, cols], (4, 1))), sin4=np.ascontiguousarray(np.tile(sinT[:, cols], (4, 1))),
            sel=sel, selh=selh,
            wg2=np.stack([np.concatenate([A(gla_w_gate2[l])[:, g * 128:(g + 1) * 128], A(gla_b_gate[l])[None, g * 128:(g + 1) * 128]], 0)
                          for l in range(2)]),
            fbf=np.stack([A(fox_b_f[l])[3 * g:3 * g + 3, None] for l in range(2)]),
            gmla=np.stack([np.ascontiguousarray(A(out_norm_mla[l]).reshape(12, 128)[3 * g:3 * g + 3].T) for l in range(2)]),
            gfox=np.stack([np.ascontiguousarray(A(out_norm_fox[l]).reshape(12, 128)[3 * g:3 * g + 3].T) for l in range(2)]),
            ggla=np.stack([np.ascontiguousarray(A(out_norm_gla[l]).reshape(4, 2, 128)[g].T) for l in range(2)]))
        maps.append(m)
    res = run_bass_kernel_spmd(_prog("fused", build_fused), maps, core_ids=cores).results
    y = np.empty((B, SEQ, D), f32)
    for core in cores:
        b, g = divmod(core, 4)
        y[b, g * OWN:(g + 1) * OWN] = _unfm(res[core]["y_out"])[2:2 + OWN]
    return y
```
